# Optimizing a Trainium2 kernel written in Bass

```python
import math
import jax, jax.numpy as jnp
from jax import lax
import numpy as np


D_MODEL = 1024
BATCH = 4
SEQ = 8192
DEPTH = 4

N_MIXERS = 2
N_A = (DEPTH + 1) // 2
N_B = DEPTH // 2
A_HEADS = 16
A_QK_DIM = 64
A_V_DIM = 64
A_KV_RANK = 256
IDX_HEADS = 8
IDX_DIM = 64
TOPK_MAX = 256
A_SPLITS = [A_HEADS * A_QK_DIM,
            A_HEADS * A_QK_DIM + A_KV_RANK,
            A_HEADS * A_QK_DIM + A_KV_RANK + IDX_HEADS * IDX_DIM,
            A_HEADS * A_QK_DIM + A_KV_RANK + IDX_HEADS * IDX_DIM + IDX_DIM]
A_IN = A_SPLITS[-1] + IDX_HEADS
B_HEADS = 8
B_HEAD_DIM = 64
B_QK = 2 * B_HEADS * B_HEAD_DIM
B_IN = 3 * B_QK
REL_BUCKETS = 32
REL_MAX_DIST = 128
BIAS_HEADS = A_HEADS
D_FF = 4 * D_MODEL
Q_BLOCK = 128
LN_EPS = 1e-5
NEG = -1e30
DN_ALPHA = (2 * DEPTH) ** 0.25
DN_BETA = (8 * DEPTH) ** -0.25

kernel_name = "hybrid_dsa_diffattn_deepnorm_adaln"


def _layer_norm(x, g, b):
    xf = x.astype(jnp.float32)
    mu = jnp.mean(xf, -1, keepdims=True)
    var = jnp.mean(jnp.square(xf - mu), -1, keepdims=True)
    return ((xf - mu) * lax.rsqrt(var + LN_EPS)).astype(x.dtype) * g + b


def _rms_norm(x, g):
    xf = x.astype(jnp.float32)
    return (xf * lax.rsqrt(jnp.mean(xf * xf, -1, keepdims=True) + LN_EPS)).astype(x.dtype) * g


def _rel_bucket(dist):
    n = jnp.maximum(dist, 0)
    max_exact = REL_BUCKETS // 2
    nf = jnp.maximum(n, 1).astype(jnp.float32)
    large = max_exact + (jnp.log(nf / max_exact) / math.log(REL_MAX_DIST / max_exact)
                         * (REL_BUCKETS - max_exact)).astype(jnp.int32)
    large = jnp.minimum(large, REL_BUCKETS - 1)
    return jnp.where(n < max_exact, n, large)


def _to_blocks(a):
    B, S = a.shape[:2]
    return jnp.moveaxis(a.reshape(B, S // Q_BLOCK, Q_BLOCK, *a.shape[2:]), 1, 0)


def _from_blocks(a):
    nb, B, Q = a.shape[:3]
    return jnp.moveaxis(a, 0, 1).reshape(B, nb * Q, *a.shape[3:])


def _dsa_mixer(h, w_in, kv_norm, w_uk, w_uv, w_o, rel_bias):
    B, S, _ = h.shape
    topk = min(TOPK_MAX, S // 4)
    proj = h @ w_in
    q, ckv, iq, ik, iw = jnp.split(proj, A_SPLITS, axis=-1)
    q = q.reshape(B, S, A_HEADS, A_QK_DIM)
    ckv = _rms_norm(ckv, kv_norm)
    iq = iq.reshape(B, S, IDX_HEADS, IDX_DIM)
    iw = iw * (IDX_HEADS ** -0.5 * IDX_DIM ** -0.5)
    pos = jnp.arange(S, dtype=jnp.int32)

    def block(args):
        qb, iqb, iwb, tpos = args
        idx_logits = jnp.einsum('bqhd,bsd->bhqs', iqb, ik).astype(jnp.float32)
        score = jnp.einsum('bhqs,bqh->bqs', jax.nn.relu(idx_logits), iwb.astype(jnp.float32))
        causal = pos[None, :] <= tpos[:, None]
        score = jnp.where(causal[None], score, -jnp.inf)
        _, sel = lax.top_k(score, topk)
        valid = sel <= tpos[None, :, None]
        kv_sel = jax.vmap(lambda cb, ib: cb[ib])(ckv, sel)
        q_lat = jnp.einsum('bqhd,hdr->bqhr', qb, w_uk) * (A_QK_DIM ** -0.5)
        logits = jnp.einsum('bqhr,bqkr->bhqk', q_lat, kv_sel).astype(jnp.float32)
        bias = rel_bias[_rel_bucket(tpos[None, :, None] - sel)]
        logits = logits + jnp.transpose(bias, (0, 3, 1, 2))
        logits = jnp.where(valid[:, None], logits, NEG)
        p = jax.nn.softmax(logits, axis=-1).astype(h.dtype)
        o_lat = jnp.einsum('bhqk,bqkr->bqhr', p, kv_sel)
        return jnp.einsum('bqhr,hrd->bqhd', o_lat, w_uv)

    o = lax.map(block, (_to_blocks(q), _to_blocks(iq), _to_blocks(iw),
                        pos.reshape(S // Q_BLOCK, Q_BLOCK)))
    o = _from_blocks(o)
    return o.reshape(B, S, A_HEADS * A_V_DIM) @ w_o


def _diff_mixer(h, w_in, lam, subln_g, w_o, rel_bias, layer_idx):
    B, S, _ = h.shape
    lam_init = 0.8 - 0.6 * math.exp(-0.3 * layer_idx)
    proj = h @ w_in
    q, k, v = jnp.split(proj, 3, axis=-1)
    q = q.reshape(B, S, B_HEADS, 2, B_HEAD_DIM) * (B_HEAD_DIM ** -0.5)
    k = k.reshape(B, S, B_HEADS, 2, B_HEAD_DIM)
    v = v.reshape(B, S, B_HEADS, 2 * B_HEAD_DIM)
    lam_f = lam.astype(jnp.float32)
    lam_full = (jnp.exp(jnp.sum(lam_f[0] * lam_f[1])) - jnp.exp(jnp.sum(lam_f[2] * lam_f[3]))
                + lam_init)
    pos = jnp.arange(S, dtype=jnp.int32)

    def block(args):
        qb, tpos = args
        logits = jnp.einsum('bqhmd,bshmd->bhmqs', qb, k).astype(jnp.float32)
        bias = rel_bias[_rel_bucket(tpos[:, None] - pos[None, :])]
        bias = jnp.transpose(bias.reshape(Q_BLOCK, S, B_HEADS, 2), (2, 3, 0, 1))
        causal = pos[None, :] <= tpos[:, None]
        logits = jnp.where(causal, logits + bias, NEG)
        p = jax.nn.softmax(logits, axis=-1)
        a = p[:, :, 0] - lam_full * p[:, :, 1]
        return jnp.einsum('bhqs,bshe->bqhe', a.astype(h.dtype), v)

    o = _from_blocks(lax.map(block, (_to_blocks(q), pos.reshape(S // Q_BLOCK, Q_BLOCK))))
    o = _rms_norm(o, subln_g) * (1.0 - lam_init)
    return o.reshape(B, S, B_HEADS * 2 * B_HEAD_DIM) @ w_o


def _sqrelu_mlp(h, w1, w2):
    return jnp.square(jax.nn.relu(h @ w1)) @ w2


def setup_inputs(seed: int = 0) -> dict:
    key = jax.random.key(seed)
    ks = jax.random.split(key, 20)
    nrm = lambda k, shape, s: jax.random.normal(k, shape, jnp.float32) * s
    D = D_MODEL
    b_w_in = nrm(ks[10], (N_B, D, B_IN), D ** -0.5)
    b_w_in = b_w_in.at[..., 2 * B_QK:].multiply(DN_BETA)
    return {
        'x': nrm(ks[0], (BATCH, SEQ, D), 1.0),
        'c': nrm(ks[1], (BATCH, D), 1.0),
        'rel_bias': nrm(ks[2], (REL_BUCKETS, BIAS_HEADS), 0.5),
        'ada_w': nrm(ks[3], (DEPTH, D, 6 * D), 0.1 * D ** -0.5),
        'ada_b': nrm(ks[4], (DEPTH, 6 * D), 0.01),
        'ln_g': 1.0 + nrm(ks[5], (DEPTH, 2, D), 0.02),
        'ln_b': nrm(ks[6], (DEPTH, 2, D), 0.02),
        'a_w_in': nrm(ks[7], (N_A, D, A_IN), D ** -0.5),
        'a_kv_norm': 1.0 + nrm(ks[8], (N_A, A_KV_RANK), 0.02),
        'a_w_uk': nrm(ks[9], (N_A, A_HEADS, A_QK_DIM, A_KV_RANK), A_KV_RANK ** -0.5),
        'a_w_uv': nrm(ks[11], (N_A, A_HEADS, A_KV_RANK, A_V_DIM), DN_BETA * A_KV_RANK ** -0.5),
        'a_w_o': nrm(ks[12], (N_A, A_HEADS * A_V_DIM, D), DN_BETA * (A_HEADS * A_V_DIM) ** -0.5),
        'b_w_in': b_w_in,
        'b_lambda': nrm(ks[13], (N_B, 4, B_HEAD_DIM), 0.1),
        'b_subln': 1.0 + nrm(ks[14], (N_B, 2 * B_HEAD_DIM), 0.02),
        'b_w_o': nrm(ks[15], (N_B, B_QK, D), DN_BETA * B_QK ** -0.5),
        'mlp_w1': nrm(ks[16], (DEPTH, D, D_FF), D ** -0.5),
        'mlp_w2': nrm(ks[17], (DEPTH, D_FF, D), DN_BETA * D_FF ** -0.5),
    }


def reference(x, c, rel_bias, ada_w, ada_b, ln_g, ln_b,
              a_w_in, a_kv_norm, a_w_uk, a_w_uv, a_w_o,
              b_w_in, b_lambda, b_subln, b_w_o,
              mlp_w1, mlp_w2):
    mod = jnp.einsum('bd,lde->lbe', jax.nn.silu(c), ada_w) + ada_b[:, None]
    for i in range(DEPTH):
        sh_t, sc_t, g_t, sh_c, sc_c, g_c = [m[:, None] for m in jnp.split(mod[i], 6, axis=-1)]
        h = x * (1.0 + sc_t) + sh_t
        j = i // N_MIXERS
        if i % N_MIXERS == 0:
            y = _dsa_mixer(h, a_w_in[j], a_kv_norm[j], a_w_uk[j], a_w_uv[j], a_w_o[j], rel_bias)
        else:
            y = _diff_mixer(h, b_w_in[j], b_lambda[j], b_subln[j], b_w_o[j], rel_bias, i)
        x = _layer_norm(DN_ALPHA * x + (1.0 + g_t) * y, ln_g[i, 0], ln_b[i, 0])
        h = x * (1.0 + sc_c) + sh_c
        y = _sqrelu_mlp(h, mlp_w1[i], mlp_w2[i])
        x = _layer_norm(DN_ALPHA * x + (1.0 + g_c) * y, ln_g[i, 1], ln_b[i, 1])
    return x
```

```python
import numpy as np
from contextlib import ExitStack
import concourse.bass as bass
import concourse.mybir as mybir
from concourse.bass_utils import run_bass_kernel_spmd

F32 = mybir.dt.float32
BF16 = mybir.dt.bfloat16
U8 = mybir.dt.uint8
AF = mybir.ActivationFunctionType
ALU = mybir.AluOpType
AX = mybir.AxisListType

D = 1024
SEQ = 8192
NB = 4
DEPTH = 4
TL = 8192
NLB = 64
DFF = 4096
LN_EPS = 1e-5
DN_ALPHA = (2 * DEPTH) ** 0.25
A_IN = 1864
NEGM = -30000.0

DBG = {}
STATS = {}
ENGS = ["tensor", "vector", "scalar", "gpsimd", "sync"]
EPOCH = 30000
EPOCH_DMA = 1800


class Op:
    __slots__ = ("eng", "fn", "chan", "waits", "needs_inc", "val", "epoch", "isdma", "noattach")


class Prog:
    _uid = [0]

    def __init__(self, nc, es):
        self.nc = nc
        self.es = es
        Prog._uid[0] += 1
        self.uid = Prog._uid[0]
        self.ops = {e: [] for e in ENGS}
        self.chan_ops = {}
        self.lastw = {}
        self.readers = {}
        self.n = 0

    def sb(self, name, shape, dt):
        return self.es.enter_context(self.nc.sbuf_tensor("%s_u%d" % (name, self.uid), list(shape), dt))

    def ps(self, name, shape, dt=F32):
        esz = 4 if dt == F32 else 2
        n = 1
        for d in shape[1:]:
            n *= d
        per_bank = 2048 // esz
        tot = ((n + per_bank - 1) // per_bank) * per_bank
        h = self.es.enter_context(self.nc.psum_tensor("%s_u%d" % (name, self.uid), [128, tot], dt))
        ap = h[0:shape[0], 0:n]
        if len(shape) == 3:
            ap = ap.rearrange("p (a b) -> p a b", a=shape[1])
        return ap

    def add(self, eng, fn, reads=(), writes=(), dma=None, noattach=False):
        op = Op()
        op.noattach = noattach or (eng == "tensor")
        op.eng = eng
        op.fn = fn
        op.isdma = dma is not None
        op.chan = ("dma", dma) if dma is not None else eng
        op.needs_inc = op.isdma
        deps = []
        for k in reads:
            w = self.lastw.get(k)
            if w is not None:
                deps.append(w)
        for k in writes:
            w = self.lastw.get(k)
            if w is not None:
                deps.append(w)
            rd = self.readers.get(k)
            if rd:
                deps.extend(rd.values())
        waits = []
        seen = set()
        for d in deps:
            if id(d) in seen:
                continue
            seen.add(id(d))
            if (not d.isdma) and d.chan == eng and eng == "tensor":
                continue
            d.needs_inc = True
            waits.append(d)
        op.waits = waits
        for k in reads:
            self.readers.setdefault(k, {})[op.chan] = op
        for k in writes:
            self.lastw[k] = op
            self.readers[k] = {}
        self.ops[eng].append(op)
        self.chan_ops.setdefault(op.chan, []).append(op)
        self.n += 1
        return op

    def dma(self, eng, chan, out, in_, reads=(), writes=()):
        return self.add(eng, lambda e: e.dma_start(out=out, in_=in_), reads, writes, dma=chan)

    def finish(self):
        nc = self.nc
        sems = {}
        for chan, lst in self.chan_ops.items():
            cnt = 0
            ep = EPOCH_DMA if chan[0] == "dma" else EPOCH
            for op in lst:
                if op.needs_inc:
                    op.epoch = cnt // ep
                    op.val = cnt % ep + 1
                    cnt += 1
                    key = (chan, op.epoch)
                    if key not in sems:
                        sems[key] = nc.alloc_semaphore(name="s%d_u%d" % (len(sems), self.uid))
        self.nsems = len(sems)
        with nc.Block() as block:
            self._emit(block, sems)
        nc.clear_and_free_semaphores(list(sems.values()))
        nc.all_engine_barrier()

    def _emit(self, block, sems):

        def emit(eng_name):
            ops = self.ops[eng_name]

            def body(e):
                waited = {}
                for op in ops:
                    need = {}
                    for d in op.waits:
                        cur = waited.get(d.chan)
                        if cur is not None and (cur[0] > d.epoch or (cur[0] == d.epoch and cur[1] >= d.val)):
                            continue
                        prev = need.get(d.chan)
                        if prev is None or (d.epoch, d.val) > (prev.epoch, prev.val):
                            need[d.chan] = d
                    need = list(need.values())
                    attach = None
                    if need and op.fn is not None and not op.noattach:
                        attach = need.pop()
                    for d in need:
                        e.wait_ge(sems[(d.chan, d.epoch)], d.val * (16 if d.isdma else 1))
                        waited[d.chan] = (d.epoch, d.val)
                        STATS[eng_name + "_wait"] = STATS.get(eng_name + "_wait", 0) + 1
                    if op.fn is None:
                        continue
                    ins = op.fn(e)
                    if attach is not None:
                        ins._wait_ge(sems[(attach.chan, attach.epoch)], attach.val * (16 if attach.isdma else 1))
                        waited[attach.chan] = (attach.epoch, attach.val)
                    STATS[eng_name] = STATS.get(eng_name, 0) + 1
                    if op.needs_inc:
                        ins.then_inc(sems[(op.chan, op.epoch)], 16 if op.isdma else 1)
            return body

        for en in ENGS:
            if self.ops[en]:
                getattr(block, en)(emit(en))


class Ctx:
    pass


def _rot(lst, i):
    return lst[i % len(lst)]


def load_mod_cols(P, pfx, modT_ap, l, which):
    nc = P.nc
    mt = P.sb(pfx + "modc", [128, 16], F32)
    base = l * 48 + which * 24
    P.dma("sync", pfx + "modc", mt[:, :], modT_ap[:, base:base + 16], reads=["modT"], writes=[pfx + "modc"])
    P.add("vector", lambda e: e.tensor_scalar(out=mt[:, 8:16], in0=mt[:, 8:16], scalar1=1.0, scalar2=None,
                                             op0=ALU.add), reads=[pfx + "modc"], writes=[pfx + "modc"])
    return mt


def load_bcast(P, name, src_ap_1d, n):
    t = P.sb(name, [128, n], F32)
    P.dma("sync", name, t[:, :], src_ap_1d.partition_broadcast(128), reads=["modrow"], writes=[name])
    return t


def make_ident(P, pfx, ident_ap):
    idf = P.sb(pfx + "idf", [128, 128], F32)
    P.dma("sync", pfx + "idf", idf[:, :], ident_ap, writes=[pfx + "idf"])
    idb = P.sb(pfx + "idb", [128, 128], BF16)
    P.add("vector", lambda e: e.tensor_copy(out=idb[:, :], in_=idf[:, :]), reads=[pfx + "idf"], writes=[pfx + "idb"])
    return idf, idb


def emit_hT(P, pfx, xblk_tiles, nblk, idf, modc, hT, col0, ps_tr_list, cnt):
    for c in range(8):
        pt, pk = _rot(ps_tr_list, cnt[0])
        cnt[0] += 1
        for b in range(nblk):
            xt, xk = xblk_tiles[b]
            P.add("tensor", (lambda e, pt=pt, xt=xt, b=b, c=c: e.transpose(
                out=pt[:, b * 128:(b + 1) * 128], in_=xt[:, c * 128:(c + 1) * 128], identity=idf[:, :])),
                reads=[xk, pfx + "idf"], writes=[pk])
        P.add("scalar", (lambda e, pt=pt, c=c: e.activation(
            out=hT[:, c, col0:col0 + nblk * 128], in_=pt[:, 0:nblk * 128], func=AF.Identity,
            bias=modc[:, c:c + 1], scale=modc[:, 8 + c:9 + c])),
            reads=[pk, pfx + "modc"], writes=[pfx + "hT"])


def emit_resid_ln(P, pfx, ps_y, ps_key, xt, xkey, gbc, lng, lnb, ot, okey, small, skey):
    nc = P.nc
    st, mv, rs = small
    P.add("vector", lambda e: e.tensor_tensor(out=ot[:, :].rearrange("p (a f) -> p a f", a=2), in0=ps_y[:, :, :],
                                             in1=gbc[:, :].rearrange("p (a f) -> p a f", a=2), op=ALU.mult),
          reads=[ps_key, pfx + "gbc"], writes=[okey])
    P.add("vector", lambda e: e.scalar_tensor_tensor(out=ot[:, :], in0=xt, scalar=float(DN_ALPHA), in1=ot[:, :],
                                                    op0=ALU.mult, op1=ALU.add),
          reads=[xkey], writes=[okey])
    P.add("vector", lambda e: e.bn_stats(out=st[:, 0, :], in_=ot[:, 0:512]), reads=[okey], writes=[skey])
    P.add("vector", lambda e: e.bn_stats(out=st[:, 1, :], in_=ot[:, 512:1024]), reads=[okey], writes=[skey])
    P.add("vector", lambda e: e.bn_aggr(out=mv[:, :], in_=st[:, :, :]), reads=[skey], writes=[skey])
    P.add("scalar", lambda e: e.activation(out=rs[:, 0:1], in_=mv[:, 1:2], func=AF.Sqrt, bias=P.epsc[:, 0:1], scale=1.0),
          reads=[skey, "epsc"], writes=[skey + "r"])
    P.add("vector", lambda e: e.reciprocal(out=rs[:, 1:2], in_=rs[:, 0:1]), reads=[skey + "r"], writes=[skey + "r2"])
    P.add("vector", lambda e: e.tensor_scalar(out=rs[:, 2:3], in0=mv[:, 0:1], scalar1=rs[:, 1:2], scalar2=-1.0,
                                             op0=ALU.mult, op1=ALU.mult), reads=[skey + "r2"], writes=[skey + "r2"])
    P.add("scalar", lambda e: e.activation(out=ot[:, :], in_=ot[:, :], func=AF.Identity, bias=rs[:, 2:3],
                                           scale=rs[:, 1:2]), reads=[skey + "r2", okey], writes=[okey])
    P.add("vector", lambda e: e.tensor_tensor(out=ot[:, :], in0=ot[:, :], in1=lng[:, :], op=ALU.mult),
          reads=[okey, pfx + "lng"], writes=[okey])
    P.add("gpsimd", lambda e: e.tensor_tensor(out=ot[:, :], in0=ot[:, :], in1=lnb[:, :], op=ALU.add),
          reads=[okey, pfx + "lnb"], writes=[okey])


def make_epsc(P):
    t = P.sb("epsc", [128, 1], F32)
    P.add("vector", lambda e: e.memset(t[:, :], LN_EPS), writes=["epsc"])
    P.epsc = t


def phase_mod(P, A, NL=DEPTH):
    nc = P.nc
    pfx = "md_"
    cT = P.sb(pfx + "cT", [128, 8], F32)
    P.dma("sync", pfx + "c", cT[:, :], A["ccol"], writes=[pfx + "cT"])
    sT = P.sb(pfx + "sT", [128, 8], F32)
    P.add("scalar", lambda e: e.activation(out=sT[:, :], in_=cT[:, :], func=AF.Silu), reads=[pfx + "cT"], writes=[pfx + "sT"])
    bT = P.sb(pfx + "bT", [128, NL * 48], F32)
    P.dma("sync", pfx + "b", bT[:, :], A["ada_bT"], writes=[pfx + "bT"])
    brow = P.sb(pfx + "brow", [1, NL * 2048], F32)
    for l in range(NL):
        for gi in range(2):
            P.dma("sync", pfx + "br", brow[:, (l * 2 + gi) * 1024:(l * 2 + gi + 1) * 1024],
                  A["ada_b"][l:l + 1, (2 + 3 * gi) * 1024:(3 + 3 * gi) * 1024], writes=[pfx + "brow"])
    modsb = P.sb(pfx + "modsb", [128, NL * 48], F32)
    rowsb = P.sb(pfx + "rowsb", [1, NL * 2048], F32)
    wp = [P.sb(pfx + "wp%d" % i, [128, 8, 512], F32) for i in range(3)]
    psc = P.ps(pfx + "psc", [128, NL * 48], F32)
    psr = [P.ps(pfx + "psr%d" % i, [1, 512], F32) for i in range(2)]
    P.add("vector", lambda e: e.memset(modsb[:, :], 0.0), writes=[pfx + "modsb"])
    P.add("vector", lambda e: e.memset(rowsb[:, :], 0.0), writes=[pfx + "rowsb"])
    it = 0
    for l in range(NL):
        for pc in range(12):
            w = wp[it % 3]
            wk = pfx + "wp%d" % (it % 3)
            src = A["ada_w"][l, :, pc * 512:(pc + 1) * 512].rearrange("(kk p) f -> p kk f", p=128)
            qeng = "sync" if it % 2 == 0 else "gpsimd"
            P.dma(qeng, wk + qeng, w[:, :, :], src, writes=[wk])
            seg = pc // 2
            if seg in (2, 5):
                pr = psr[it % 2]
                prk = pfx + "psr%d" % (it % 2)
                for kk in range(8):
                    P.add("tensor", (lambda e, pr=pr, w=w, kk=kk: e.matmul(
                        pr[:, :], sT[:, kk:kk + 1], w[:, kk, :], start=(kk == 0), stop=(kk == 7))),
                        reads=[wk, pfx + "sT"], writes=[prk])
                o0 = (l * 2 + seg // 3) * 1024 + (pc % 2) * 512
                P.add("vector", (lambda e, pr=pr, o0=o0: e.tensor_tensor(
                    out=rowsb[:, o0:o0 + 512], in0=pr[:, :], in1=brow[:, o0:o0 + 512], op=ALU.add)),
                    reads=[prk, pfx + "brow"], writes=[pfx + "rowsb"])
            else:
                for q in range(4):
                    col = l * 48 + pc * 4 + q
                    for kk in range(8):
                        P.add("tensor", (lambda e, w=w, kk=kk, q=q, col=col: e.matmul(
                            psc[:, col:col + 1], w[:, kk, q * 128:(q + 1) * 128], sT[:, kk:kk + 1],
                            start=(kk == 0), stop=(kk == 7))),
                            reads=[wk, pfx + "sT"], writes=[pfx + "psc"])
            it += 1
    for l in range(NL):
        for c0 in (l * 48, l * 48 + 24):
            P.add("vector", (lambda e, c0=c0: e.tensor_tensor(out=modsb[:, c0:c0 + 16], in0=psc[:, c0:c0 + 16],
                                                              in1=bT[:, c0:c0 + 16], op=ALU.add)),
                  reads=[pfx + "psc", pfx + "bT"], writes=[pfx + "modsb"])
    P.dma("sync", pfx + "o1", A["modT"], modsb[:, :], reads=[pfx + "modsb"], writes=["modT"])
    P.dma("sync", pfx + "o2", A["modrow"].rearrange("(o l) g f -> o (l g f)", o=1), rowsb[:, :], reads=[pfx + "rowsb"],
          writes=["modrow"])


def phase_mlp(P, A, l, xin, xin_key, xout, xout_key, pfx):
    nc = P.nc
    TT = 256
    NT = TL // TT
    idf, idb = make_ident(P, pfx, A["ident"])
    modc = load_mod_cols(P, pfx, A["modT"], l, 1)
    gbc = load_bcast(P, pfx + "gbc", A["modrow"][l, 1, :], 1024)
    P.add("vector", lambda e: e.tensor_scalar(out=gbc[:, :], in0=gbc[:, :], scalar1=1.0, scalar2=None, op0=ALU.add),
          reads=[pfx + "gbc"], writes=[pfx + "gbc"])
    lng = P.sb(pfx + "lng", [128, 1024], F32)
    P.dma("sync", pfx + "lng", lng[:, :], A["ln_g"][l, 1, :].partition_broadcast(128), writes=[pfx + "lng"])
    lnb = P.sb(pfx + "lnb", [128, 1024], F32)
    P.dma("sync", pfx + "lnb", lnb[:, :], A["ln_b"][l, 1, :].partition_broadcast(128), writes=[pfx + "lnb"])
    w1b = P.sb(pfx + "w1b", [128, 8, DFF], BF16)
    w2b = P.sb(pfx + "w2b", [128, 32, D], BF16)
    for kc in range(8):
        P.dma("gpsimd", pfx + "w1", w1b[:, kc, :], A["mlp_w1"][l, kc * 128:(kc + 1) * 128, :], writes=[pfx + "w1b"])
    w2v = A["mlp_w2"][l].rearrange("(kc p) f -> p kc f", p=128)
    for g in range(8):
        P.dma("gpsimd", pfx + "w2", w2b[:, g * 4:(g + 1) * 4, :], w2v[:, g * 4:(g + 1) * 4, :], writes=[pfx + "w2b"])
    xtr = [P.sb(pfx + "xtr%d" % i, [128, 1024], F32) for i in range(2)]
    xep = [P.sb(pfx + "xep%d" % i, [128, 1024], F32) for i in range(2)]
    ots = [P.sb(pfx + "ot%d" % i, [128, 1024], F32) for i in range(3)]
    hT = P.sb(pfx + "hT", [128, 8, TT], BF16)
    aT = P.sb(pfx + "aT", [128, 32, TT], BF16)
    rts = [P.sb(pfx + "rt%d" % i, [128, TT], F32) for i in range(3)]
    smalls = [(P.sb(pfx + "st%d" % i, [128, 2, 6], F32), P.sb(pfx + "mv%d" % i, [128, 2], F32),
               P.sb(pfx + "rs%d" % i, [128, 3], F32)) for i in range(3)]
    ps_tr = [(P.ps(pfx + "ptr%d" % i, [128, 512], F32), pfx + "ptr%d" % i) for i in range(2)]
    ps_a = [P.ps(pfx + "pa%d" % i, [128, 512], F32) for i in range(2)]
    ps_y = [P.ps(pfx + "py%d" % i, [128, 2, 512], F32) for i in range(2)]
    cnt = [0]
    nblk = TT // 128
    xcnt = [0]
    ecnt = [0]

    def do_tr(t):
        tiles = []
        for b in range(nblk):
            i = xcnt[0] % 2
            xcnt[0] += 1
            r0 = t * TT + b * 128
            P.dma("sync", pfx + "xtr%d" % i, xtr[i][:, :], xin[r0:r0 + 128, :], reads=[xin_key], writes=[pfx + "xtr%d" % i])
            tiles.append((xtr[i], pfx + "xtr%d" % i))
        emit_hT(P, pfx, tiles, nblk, idf, modc, hT, 0, ps_tr, cnt)

    def do_w1(t):
        for fc in range(32):
            pa = ps_a[fc % 2]
            pak = pfx + "pa%d" % (fc % 2)
            for kc in range(8):
                P.add("tensor", (lambda e, pa=pa, kc=kc, fc=fc: e.matmul(
                    pa[:, 0:TT], w1b[:, kc, fc * 128:(fc + 1) * 128], hT[:, kc, :], start=(kc == 0), stop=(kc == 7))),
                    reads=[pfx + "w1b", pfx + "hT"], writes=[pak])
            rt = rts[fc % 3]
            rk = pfx + "rt%d" % (fc % 3)
            P.add("scalar", (lambda e, pa=pa, rt=rt: e.activation(out=rt[:, :], in_=pa[:, 0:TT], func=AF.Relu)),
                  reads=[pak], writes=[rk])
            P.add("gpsimd", (lambda e, rt=rt, fc=fc: e.tensor_tensor(out=aT[:, fc, :], in0=rt[:, :], in1=rt[:, :], op=ALU.mult)),
                  reads=[rk], writes=[pfx + "aT"])

    def do_w2(t):
        for b in range(nblk):
            i = ecnt[0]
            ecnt[0] += 1
            py = ps_y[i % 2]
            pyk = pfx + "py%d" % (i % 2)
            for half in range(2):
                for fc in range(32):
                    P.add("tensor", (lambda e, py=py, half=half, fc=fc, b=b: e.matmul(
                        py[:, half, :], aT[:, fc, b * 128:(b + 1) * 128], w2b[:, fc, half * 512:(half + 1) * 512],
                        start=(fc == 0), stop=(fc == 31))),
                        reads=[pfx + "aT", pfx + "w2b"], writes=[pyk])
            r0 = t * TT + b * 128
            xe = xep[i % 2]
            xek = pfx + "xep%d" % (i % 2)
            P.dma("sync", xek, xe[:, :], xin[r0:r0 + 128, :], reads=[xin_key], writes=[xek])
            ot = ots[i % 3]
            ok = pfx + "ot%d" % (i % 3)
            emit_resid_ln(P, pfx, py, pyk, xe[:, :], xek, gbc, lng, lnb, ot, ok, smalls[i % 3], pfx + "sm%d" % (i % 3))
            P.dma("gpsimd", ok + "o", xout[r0:r0 + 128, :], ot[:, :], reads=[ok], writes=[xout_key])

    NT = DBG.get("NT", NT)
    do_tr(0)
    for t in range(NT):
        do_w1(t)
        if t + 1 < NT:
            do_tr(t + 1)
        do_w2(t)


def phase_outln(P, A, l, w_ap, oT, oT_key, xin, xin_key, xout, xout_key, pfx):
    gbc = load_bcast(P, pfx + "gbc", A["modrow"][l, 0, :], 1024)
    P.add("vector", lambda e: e.tensor_scalar(out=gbc[:, :], in0=gbc[:, :], scalar1=1.0, scalar2=None, op0=ALU.add),
          reads=[pfx + "gbc"], writes=[pfx + "gbc"])
    lng = P.sb(pfx + "lng", [128, 1024], F32)
    P.dma("sync", pfx + "lng", lng[:, :], A["ln_g"][l, 0, :].partition_broadcast(128), writes=[pfx + "lng"])
    lnb = P.sb(pfx + "lnb", [128, 1024], F32)
    P.dma("sync", pfx + "lnb", lnb[:, :], A["ln_b"][l, 0, :].partition_broadcast(128), writes=[pfx + "lnb"])
    wob = P.sb(pfx + "wob", [128, 8, D], BF16)
    P.dma("gpsimd", pfx + "wo", wob[:, :, :], w_ap.rearrange("(c p) f -> p c f", p=128), writes=[pfx + "wob"])
    ots = [P.sb(pfx + "ot%d" % i, [128, 1024], F32) for i in range(3)]
    xep = [P.sb(pfx + "xep%d" % i, [128, 1024], F32) for i in range(3)]
    oTt = [P.sb(pfx + "oTt%d" % i, [128, 8, 512], BF16) for i in range(2)]
    smalls = [(P.sb(pfx + "st%d" % i, [128, 2, 6], F32), P.sb(pfx + "mv%d" % i, [128, 2], F32),
               P.sb(pfx + "rs%d" % i, [128, 3], F32)) for i in range(3)]
    ps_y = [P.ps(pfx + "py%d" % i, [128, 2, 512], F32) for i in range(2)]
    oTv = oT.rearrange("(c p) t -> p c t", p=128)
    i = 0
    for t in range(DBG.get("OT", TL // 512)):
        ob = oTt[t % 2]
        obk = pfx + "oTt%d" % (t % 2)
        P.dma("sync", obk, ob[:, :, :], oTv[:, :, t * 512:(t + 1) * 512], reads=[oT_key], writes=[obk])
        for b in range(4):
            py = ps_y[i % 2]
            pyk = pfx + "py%d" % (i % 2)
            for half in range(2):
                for c in range(8):
                    P.add("tensor", (lambda e, py=py, half=half, c=c, b=b, ob=ob: e.matmul(
                        py[:, half, :], ob[:, c, b * 128:(b + 1) * 128], wob[:, c, half * 512:(half + 1) * 512],
                        start=(c == 0), stop=(c == 7))), reads=[obk, pfx + "wob"], writes=[pyk])
            r0 = t * 512 + b * 128
            xe = xep[i % 3]
            xek = pfx + "xep%d" % (i % 3)
            P.dma("sync", xek, xe[:, :], xin[r0:r0 + 128, :], reads=[xin_key], writes=[xek])
            ot = ots[i % 3]
            ok = pfx + "ot%d" % (i % 3)
            emit_resid_ln(P, pfx, py, pyk, xe[:, :], xek, gbc, lng, lnb, ot, ok, smalls[i % 3], pfx + "sm%d" % (i % 3))
            P.dma("gpsimd", ok + "o", xout[r0:r0 + 128, :], ot[:, :], reads=[ok], writes=[xout_key])
            i += 1


def phase_bproj(P, A, l, xin, xin_key, pfx):
    j = l // 2
    idf, idb = make_ident(P, pfx, A["ident"])
    modc = load_mod_cols(P, pfx, A["modT"], l, 0)
    wb = P.sb(pfx + "wb", [128, 8, 3072], BF16)
    wv = A["b_w_in"][j].rearrange("(c p) f -> p c f", p=128)
    for c in range(8):
        P.dma("gpsimd", pfx + "w", wb[:, c, :], wv[:, c, :], writes=[pfx + "wb"])
    xtr = [P.sb(pfx + "xtr%d" % i, [128, 1024], F32) for i in range(6)]
    hT = P.sb(pfx + "hT", [128, 8, 512], BF16)
    stg = [P.sb(pfx + "stg%d" % i, [128, 512], BF16) for i in range(4)]
    ps_tr = [(P.ps(pfx + "ptr%d" % i, [128, 512], F32), pfx + "ptr%d" % i) for i in range(2)]
    ps_o = [P.ps(pfx + "po%d" % i, [128, 512], F32) for i in range(4)]
    cnt = [0]
    xc = 0
    oc = 0
    for t in range(TL // 512):
        tiles = []
        for b in range(4):
            i = xc % 6
            xc += 1
            r0 = t * 512 + b * 128
            P.dma("sync", pfx + "xtr%d" % i, xtr[i][:, :], xin[r0:r0 + 128, :], reads=[xin_key], writes=[pfx + "xtr%d" % i])
            tiles.append((xtr[i], pfx + "xtr%d" % i))
        emit_hT(P, pfx, tiles, 4, idf, modc, hT, 0, ps_tr, cnt)
        for fo in range(16):
            po = ps_o[oc % 4]
            pok = pfx + "po%d" % (oc % 4)
            sg = stg[oc % 4]
            sgk = pfx + "stg%d" % (oc % 4)
            oc += 1
            for c in range(8):
                P.add("tensor", (lambda e, po=po, c=c, fo=fo: e.matmul(
                    po[:, :], wb[:, c, fo * 128:(fo + 1) * 128], hT[:, c, :], start=(c == 0), stop=(c == 7))),
                    reads=[pfx + "wb", pfx + "hT"], writes=[pok])
            sc = 0.125 if fo < 8 else 1.0
            P.add("scalar", (lambda e, po=po, sg=sg, sc=sc: e.activation(out=sg[:, :], in_=po[:, :], func=AF.Copy, scale=sc)),
                  reads=[pok], writes=[sgk])
            dst = A["qT"] if fo < 8 else A["kT"]
            dk = "qT" if fo < 8 else "kT"
            fr = (fo % 8) * 128
            P.dma("gpsimd", sgk + "o", dst[fr:fr + 128, t * 512:(t + 1) * 512], sg[:, :], reads=[sgk], writes=[dk])
        for b in range(4):
            for half in range(2):
                po = ps_o[oc % 4]
                pok = pfx + "po%d" % (oc % 4)
                sg = stg[oc % 4]
                sgk = pfx + "stg%d" % (oc % 4)
                oc += 1
                for c in range(8):
                    P.add("tensor", (lambda e, po=po, c=c, b=b, half=half: e.matmul(
                        po[:, :], hT[:, c, b * 128:(b + 1) * 128], wb[:, c, 2048 + half * 512:2048 + (half + 1) * 512],
                        start=(c == 0), stop=(c == 7))), reads=[pfx + "wb", pfx + "hT"], writes=[pok])
                P.add("vector", (lambda e, po=po, sg=sg: e.tensor_copy(out=sg[:, :], in_=po[:, :])), reads=[pok], writes=[sgk])
                r0 = t * 512 + b * 128
                P.dma("gpsimd", sgk + "o", A["v"][r0:r0 + 128, half * 512:(half + 1) * 512], sg[:, :], reads=[sgk], writes=["v"])


def phase_battn(P, A, l, pfx):
    j = l // 2
    import math
    lam_init = 0.8 - 0.6 * math.exp(-0.3 * DBG.get("true_l", l))
    idf, idb = make_ident(P, pfx, A["ident"])
    onesb = P.sb(pfx + "onesb", [128, 128], BF16)
    P.add("vector", lambda e: e.memset(onesb[:, :], 1.0), writes=[pfx + "onesb"])
    onesf = P.sb(pfx + "onesf", [128, 128], F32)
    P.add("vector", lambda e: e.memset(onesf[:, :], 1.0), writes=[pfx + "onesf"])
    b31 = P.sb(pfx + "b31", [128, 16], F32)
    P.dma("sync", pfx + "b31", b31[:, :], A["rel_bias"][31, :].partition_broadcast(128), writes=[pfx + "b31"])
    BT = P.sb(pfx + "BT", [128, 16, 256], F32)
    P.dma("sync", pfx + "BT", BT[:, :, :], A["biasT"].rearrange("c s q -> s c q"), writes=[pfx + "BT"])
    for c in range(16):
        P.add("vector", (lambda e, c=c: e.tensor_scalar(out=BT[:, c, :], in0=BT[:, c, :], scalar1=b31[:, c:c + 1],
                                                        scalar2=None, op0=ALU.subtract)),
              reads=[pfx + "BT", pfx + "b31"], writes=[pfx + "BT"])
    lamb = P.sb(pfx + "lamb", [128, 256], F32)
    P.dma("sync", pfx + "lamb", lamb[:, :], A["b_lambda"][j].rearrange("a d -> (a d)").partition_broadcast(128),
          writes=[pfx + "lamb"])
    lt = P.sb(pfx + "lt", [128, 128], F32)
    ls = P.sb(pfx + "ls", [128, 4], F32)
    P.add("vector", lambda e: e.tensor_tensor(out=lt[:, 0:64], in0=lamb[:, 0:64], in1=lamb[:, 64:128], op=ALU.mult),
          reads=[pfx + "lamb"], writes=[pfx + "lt"])
    P.add("vector", lambda e: e.tensor_tensor(out=lt[:, 64:128], in0=lamb[:, 128:192], in1=lamb[:, 192:256], op=ALU.mult),
          reads=[pfx + "lamb"], writes=[pfx + "lt"])
    P.add("vector", lambda e: e.reduce_sum(out=ls[:, 0:1], in_=lt[:, 0:64], axis=AX.X), reads=[pfx + "lt"], writes=[pfx + "ls"])
    P.add("vector", lambda e: e.reduce_sum(out=ls[:, 1:2], in_=lt[:, 64:128], axis=AX.X), reads=[pfx + "lt"], writes=[pfx + "ls"])
    P.add("scalar", lambda e: e.activation(out=ls[:, 2:4], in_=ls[:, 0:2], func=AF.Exp), reads=[pfx + "ls"], writes=[pfx + "ls2"])
    neglam = P.sb(pfx + "neglam", [128, 1], F32)
    P.add("vector", lambda e: e.tensor_scalar(out=neglam[:, :], in0=ls[:, 3:4], scalar1=float(lam_init), scalar2=ls[:, 2:3],
                                             op0=ALU.subtract, op1=ALU.subtract), reads=[pfx + "ls2"], writes=[pfx + "neglam"])
    sg = P.sb(pfx + "sg", [128, 1], F32)
    P.dma("sync", pfx + "sg", sg[:, :], A["b_subln"][j].rearrange("(p o) -> p o", o=1), writes=[pfx + "sg"])
    P.add("vector", lambda e: e.tensor_scalar(out=sg[:, :], in0=sg[:, :], scalar1=float(1.0 - lam_init), scalar2=None,
                                             op0=ALU.mult), reads=[pfx + "sg"], writes=[pfx + "sg"])
    kTh = [P.sb(pfx + "kTh%d" % i, [128, TL], BF16) for i in range(2)]
    qTh = [P.sb(pfx + "qTh%d" % i, [128, TL], BF16) for i in range(2)]
    Vh = [P.sb(pfx + "Vh%d" % i, [128, NLB, 128], BF16) for i in range(2)]
    PT = [P.sb(pfx + "PT%d" % i, [128, 512], BF16) for i in range(4)]
    r0t = P.sb(pfx + "r0t", [128, 512], F32)
    r1t = P.sb(pfx + "r1t", [128, 512], F32)
    o0t = P.sb(pfx + "o0t", [128, 512], F32)
    o1t = P.sb(pfx + "o1t", [128, 512], F32)
    sqt = P.sb(pfx + "sqt", [128, 512], F32)
    oTs = [P.sb(pfx + "oTs%d" % i, [128, 512], BF16) for i in range(2)]
    ps_s = [P.ps(pfx + "pss%d" % i, [128, 512], F32) for i in range(2)]
    acc_o = [P.ps(pfx + "acco%d" % i, [128, 512], F32) for i in range(2)]
    acc_s = [P.ps(pfx + "accs%d" % i, [128, 512], F32) for i in range(2)]
    ps_ms = P.ps(pfx + "psms", [128, 512], F32)
    st = {"sc": 0, "pc": 0}
    oc = 0
    vv = A["v"].rearrange("(blk p) f -> p blk f", p=128)
    for h in range(DBG.get("heads", 8)):
        kt = kTh[h % 2]; ktk = pfx + "kTh%d" % (h % 2)
        qt = qTh[h % 2]; qtk = pfx + "qTh%d" % (h % 2)
        vt = Vh[h % 2]; vtk = pfx + "Vh%d" % (h % 2)
        P.dma("sync", ktk, kt[:, :], A["kT"][h * 128:(h + 1) * 128, :], reads=["kT"], writes=[ktk])
        P.dma("sync", qtk, qt[:, :], A["qT"][h * 128:(h + 1) * 128, :], reads=["qT"], writes=[qtk])
        for g in range(4):
            P.dma("sync", vtk, vt[:, g * 16:(g + 1) * 16, :], vv[:, g * 16:(g + 1) * 16, h * 128:(h + 1) * 128], reads=["v"], writes=[vtk])
        for J in range(DBG.get("J", TL // 512)):
            ns = 4 * J + 4

            def qk_b(m, s_):
                col = 2 * h + m
                k = s_ - 4 * J
                c0 = max(0, k) * 128
                near = k >= -1
                ps = ps_s[st["sc"] % 2]; psk = pfx + "pss%d" % (st["sc"] % 2)
                st["sc"] += 1
                P.add("tensor", (lambda e, ps=ps, kt=kt, qt=qt, m=m, s_=s_, c0=c0, J=J, near=near: e.matmul(
                    ps[:, c0:512], kt[64 * m:64 * m + 64, s_ * 128:(s_ + 1) * 128],
                    qt[64 * m:64 * m + 64, J * 512 + c0:J * 512 + 512], start=True, stop=(not near))),
                    reads=[ktk, qtk], writes=[psk])
                if near:
                    if k == -1:
                        P.add("tensor", (lambda e, ps=ps, col=col: e.matmul(
                            ps[:, 0:128], idf[:, :], BT[:, col, 128:256], start=False, stop=True)),
                            reads=[pfx + "idf", pfx + "BT"], writes=[psk])
                    else:
                        w = min(256, 512 - c0)
                        P.add("tensor", (lambda e, ps=ps, col=col, c0=c0, w=w: e.matmul(
                            ps[:, c0:c0 + w], idf[:, :], BT[:, col, 0:w], start=False, stop=True)),
                            reads=[pfx + "idf", pfx + "BT"], writes=[psk])
                pt = PT[st["pc"] % 4]; ptk = pfx + "PT%d" % (st["pc"] % 4)
                st["pc"] += 1
                P.add("scalar", (lambda e, ps=ps, pt=pt, c0=c0, col=col: e.activation(
                    out=pt[:, c0:512], in_=ps[:, c0:512], func=AF.Exp, bias=b31[:, col:col + 1], scale=1.0)),
                    reads=[psk, pfx + "b31"], writes=[ptk])
                return pt, ptk, c0

            def pv_b(m, s_, pre):
                pt, ptk, c0 = pre
                ao = acc_o[m]; aok = pfx + "acco%d" % m
                as_ = acc_s[m]; ask = pfx + "accs%d" % m
                P.add("tensor", (lambda e, ao=ao, vt=vt, pt=pt, s_=s_, c0=c0, ns=ns: e.matmul(
                    ao[:, c0:512], vt[:, s_, :], pt[:, c0:512], start=(s_ == 0), stop=(s_ == ns - 1))),
                    reads=[vtk, ptk], writes=[aok])
                P.add("tensor", (lambda e, as_=as_, pt=pt, s_=s_, c0=c0, ns=ns: e.matmul(
                    as_[:, c0:512], onesb[:, :], pt[:, c0:512], start=(s_ == 0), stop=(s_ == ns - 1))),
                    reads=[pfx + "onesb", ptk], writes=[ask])

            pairs = [(m, s_) for m in range(2) for s_ in range(ns)]
            pre = qk_b(*pairs[0])
            for i_, (m, s_) in enumerate(pairs):
                nxt = qk_b(*pairs[i_ + 1]) if i_ + 1 < len(pairs) else None
                pv_b(m, s_, pre)
                pre = nxt
            P.add("vector", lambda e: e.reciprocal(out=r0t[:, :], in_=acc_s[0][:, :]), reads=[pfx + "accs0"], writes=[pfx + "r0t"])
            P.add("vector", lambda e: e.reciprocal(out=r1t[:, :], in_=acc_s[1][:, :]), reads=[pfx + "accs1"], writes=[pfx + "r1t"])
            P.add("vector", lambda e: e.tensor_tensor(out=o0t[:, :], in0=acc_o[0][:, :], in1=r0t[:, :], op=ALU.mult),
                  reads=[pfx + "acco0", pfx + "r0t"], writes=[pfx + "o0t"])
            P.add("vector", lambda e: e.tensor_tensor(out=o1t[:, :], in0=acc_o[1][:, :], in1=r1t[:, :], op=ALU.mult),
                  reads=[pfx + "acco1", pfx + "r1t"], writes=[pfx + "o1t"])
            P.add("vector", lambda e: e.scalar_tensor_tensor(out=o0t[:, :], in0=o1t[:, :], scalar=neglam[:, 0:1], in1=o0t[:, :],
                                                            op0=ALU.mult, op1=ALU.add),
                  reads=[pfx + "o1t", pfx + "neglam"], writes=[pfx + "o0t"])
            P.add("gpsimd", lambda e: e.tensor_tensor(out=sqt[:, :], in0=o0t[:, :], in1=o0t[:, :], op=ALU.mult),
                  reads=[pfx + "o0t"], writes=[pfx + "sqt"])
            P.add("tensor", lambda e: e.matmul(ps_ms[:, :], onesf[:, :], sqt[:, :], start=True, stop=True),
                  reads=[pfx + "onesf", pfx + "sqt"], writes=[pfx + "psms"])
            P.add("scalar", lambda e: e.activation(out=r0t[:, :], in_=ps_ms[:, :], func=AF.Sqrt, bias=P.epsc[:, 0:1], scale=1.0 / 128.0),
                  reads=[pfx + "psms", "epsc"], writes=[pfx + "r0t"])
            P.add("vector", lambda e: e.reciprocal(out=r1t[:, :], in_=r0t[:, :]), reads=[pfx + "r0t"], writes=[pfx + "r1t"])
            P.add("vector", lambda e: e.tensor_tensor(out=o0t[:, :], in0=o0t[:, :], in1=r1t[:, :], op=ALU.mult),
                  reads=[pfx + "r1t"], writes=[pfx + "o0t"])
            os_ = oTs[oc % 2]; osk = pfx + "oTs%d" % (oc % 2)
            oc += 1
            P.add("vector", (lambda e, os_=os_: e.tensor_scalar(out=os_[:, :], in0=o0t[:, :], scalar1=sg[:, 0:1], scalar2=None,
                                                               op0=ALU.mult)), reads=[pfx + "o0t", pfx + "sg"], writes=[osk])
            P.dma("gpsimd", osk + "o", A["oT"][h * 128:(h + 1) * 128, J * 512:(J + 1) * 512], os_[:, :], reads=[osk], writes=["oT"])


def phase_aproj(P, A, l, xin, xin_key, pfx):
    j = l // 2
    idf, idb = make_ident(P, pfx, A["ident"])
    modc = load_mod_cols(P, pfx, A["modT"], l, 0)
    wb = P.sb(pfx + "wb", [128, 8, A_IN], BF16)
    wv = A["a_w_in"][j].rearrange("(c p) f -> p c f", p=128)
    for c in range(8):
        P.dma("gpsimd", pfx + "w", wb[:, c, :], wv[:, c, :], writes=[pfx + "wb"])
    wuk = P.sb(pfx + "wuk", [128, 8, 256], BF16)
    P.dma("gpsimd", pfx + "wuk", wuk[:, :, :], A["a_w_uk"][j].rearrange("(hp two) d r -> (two d) hp r", two=2),
          writes=[pfx + "wuk"])
    wuvb = P.sb(pfx + "wuvb", [128, 2, 16, 64], BF16)
    for rc in range(2):
        P.dma("gpsimd", pfx + "wuvb", wuvb[:, rc, :, :], A["a_w_uv"][j][:, rc * 128:(rc + 1) * 128, :].rearrange("h p e -> p h e"),
              writes=[pfx + "wuvb"])
    kvn = P.sb(pfx + "kvn", [128, 256], F32)
    P.dma("sync", pfx + "kvn", kvn[:, :], A["a_kv_norm"][j].partition_broadcast(128), writes=[pfx + "kvn"])
    xtr = [P.sb(pfx + "xtr%d" % i, [128, 1024], F32) for i in range(6)]
    hT = P.sb(pfx + "hT", [128, 8, 512], BF16)
    stg = [P.sb(pfx + "stg%d" % i, [128, 512], BF16) for i in range(4)]
    cst = [P.sb(pfx + "cst%d" % i, [128, 256], BF16) for i in range(2)]
    iwst = [P.sb(pfx + "iwst%d" % i, [128, 8], F32) for i in range(2)]
    cTs = [P.sb(pfx + "cTs%d" % i, [128, 2, 512], BF16) for i in range(2)]
    vst = [P.sb(pfx + "vst%d" % i, [128, 16, 65], BF16) for i in range(2)]
    for i in range(2):
        P.add("vector", (lambda e, i=i: e.memset(vst[i][:, :, 64:65], 1.0)), writes=[pfx + "vst%d" % i])
    wukT = P.sb(pfx + "wukT", [128, 2, 1024], BF16)
    junk = P.sb(pfx + "junk", [128, 256], F32)
    sm = [P.sb(pfx + "sm%d" % i, [128, 3], F32) for i in range(2)]
    ps_tr = [(P.ps(pfx + "ptr%d" % i, [128, 512], F32), pfx + "ptr%d" % i) for i in range(2)]
    ps_o = [P.ps(pfx + "po%d" % i, [128, 512], F32) for i in range(3)]
    ps_c = [P.ps(pfx + "pc%d" % i, [128, 264], F32) for i in range(2)]
    ps_t = P.ps(pfx + "pt", [128, 2, 128], BF16)
    cnt = [0]
    st8 = {"xc": 0, "oc": 0, "sc": 0, "bc": 0, "vc": 0}
    for hp in range(8):
        for rc in range(2):
            P.add("tensor", (lambda e, hp=hp, rc=rc: e.transpose(out=ps_t[:, rc, :], in_=wuk[:, hp, rc * 128:(rc + 1) * 128],
                                                                identity=idb[:, :])), reads=[pfx + "wuk", pfx + "idb"], writes=[pfx + "pt"])
        P.add("scalar", (lambda e, hp=hp: e.activation(out=wukT[:, :, hp * 128:(hp + 1) * 128], in_=ps_t[:, :, :], func=AF.Copy)),
              reads=[pfx + "pt"], writes=[pfx + "wukT"])

    def evac(po_ap, pok, width, dst_ap, dkey, parts=128, scale=None):
        i = st8["sc"] % 4
        st8["sc"] += 1
        sg = stg[i]
        sgk = pfx + "stg%d" % i
        if scale is not None or st8["sc"] % 2 == 0:
            P.add("scalar", (lambda e: e.activation(out=sg[0:parts, 0:width], in_=po_ap, func=AF.Copy,
                                                    scale=(1.0 if scale is None else scale))), reads=[pok], writes=[sgk])
        else:
            P.add("vector", (lambda e: e.tensor_copy(out=sg[0:parts, 0:width], in_=po_ap)), reads=[pok], writes=[sgk])
        P.dma("gpsimd", sgk + "o", dst_ap, sg[0:parts, 0:width], reads=[sgk], writes=[dkey])

    def next_po():
        po = ps_o[st8["oc"] % 3]
        pok = pfx + "po%d" % (st8["oc"] % 3)
        st8["oc"] += 1
        return po, pok

    for t in range(DBG.get("T", TL // 512)):
        tiles = []
        for b in range(4):
            i = st8["xc"] % 6
            st8["xc"] += 1
            r0 = t * 512 + b * 128
            P.dma("sync", pfx + "xtr%d" % i, xtr[i][:, :], xin[r0:r0 + 128, :], reads=[xin_key], writes=[pfx + "xtr%d" % i])
            tiles.append((xtr[i], pfx + "xtr%d" % i))
        emit_hT(P, pfx, tiles, 4, idf, modc, hT, 0, ps_tr, cnt)
        tsl = slice(t * 512, (t + 1) * 512)

        def proj_fm(col0, ncol):
            po, pok = next_po()
            for c in range(8):
                P.add("tensor", (lambda e, po=po, c=c: e.matmul(po[0:ncol, :], wb[:, c, col0:col0 + ncol], hT[:, c, :],
                                                               start=(c == 0), stop=(c == 7))),
                      reads=[pfx + "wb", pfx + "hT"], writes=[pok])
            return po, pok

        for fo in range(8):
            po, pok = proj_fm(fo * 128, 128)
            evac(po[:, :], pok, 512, A["qT"][fo * 128:(fo + 1) * 128, tsl], "qT", scale=0.125)
        for fo in range(4):
            po, pok = proj_fm(1280 + fo * 128, 128)
            evac(po[:, :], pok, 512, A["iqT"][fo * 128:(fo + 1) * 128, tsl], "iqT")
        po, pok = proj_fm(1792, 64)
        evac(po[0:64, :], pok, 512, A["ikT"][0:64, tsl], "ikT", parts=64)
        ct = cTs[t % 2]
        ctk = pfx + "cTs%d" % (t % 2)
        for b in range(4):
            bi = st8["bc"]
            st8["bc"] += 1
            pc = ps_c[bi % 2]
            pck = pfx + "pc%d" % (bi % 2)
            for c in range(8):
                P.add("tensor", (lambda e, pc=pc, c=c, b=b: e.matmul(pc[:, 0:256], hT[:, c, b * 128:(b + 1) * 128],
                                                                    wb[:, c, 1024:1280], start=(c == 0), stop=(c == 7))),
                      reads=[pfx + "wb", pfx + "hT"], writes=[pck])
            for c in range(8):
                P.add("tensor", (lambda e, pc=pc, c=c, b=b: e.matmul(pc[:, 256:264], hT[:, c, b * 128:(b + 1) * 128],
                                                                    wb[:, c, 1856:1864], start=(c == 0), stop=(c == 7))),
                      reads=[pfx + "wb", pfx + "hT"], writes=[pck])
            s3 = sm[bi % 2]
            s3k = pfx + "sm%d" % (bi % 2)
            P.add("scalar", (lambda e, pc=pc, s3=s3: e.activation(out=junk[:, :], in_=pc[:, 0:256], func=AF.Square,
                                                                 accum_out=s3[:, 0:1])), reads=[pck], writes=[pfx + "junk", s3k],
                  noattach=True)
            P.add("scalar", (lambda e, s3=s3: e.activation(out=s3[:, 1:2], in_=s3[:, 0:1], func=AF.Sqrt, bias=P.epsc[:, 0:1],
                                                          scale=1.0 / 256.0)), reads=[s3k, "epsc"], writes=[s3k])
            P.add("vector", (lambda e, s3=s3: e.reciprocal(out=s3[:, 2:3], in_=s3[:, 1:2])), reads=[s3k], writes=[s3k])
            cs = cst[bi % 2]
            csk = pfx + "cst%d" % (bi % 2)
            P.add("vector", (lambda e, pc=pc, s3=s3, cs=cs: e.scalar_tensor_tensor(
                out=cs[:, :], in0=pc[:, 0:256], scalar=s3[:, 2:3], in1=kvn[:, :], op0=ALU.mult, op1=ALU.mult)),
                reads=[pck, s3k, pfx + "kvn"], writes=[csk])
            r0 = t * 512 + b * 128
            iws = iwst[bi % 2]
            iwk = pfx + "iwst%d" % (bi % 2)
            P.add("vector", (lambda e, pc=pc, iws=iws: e.tensor_scalar(out=iws[:, :], in0=pc[:, 256:264],
                                                                      scalar1=float(8 ** -0.5 * 64 ** -0.5), scalar2=None,
                                                                      op0=ALU.mult)), reads=[pck], writes=[iwk])
            P.dma("gpsimd", iwk + "o", A["iw"][r0:r0 + 128, :], iws[:, :], reads=[iwk], writes=["iw"])
            for rc in range(2):
                P.add("tensor", (lambda e, cs=cs, rc=rc: e.transpose(out=ps_t[:, rc, :], in_=cs[:, rc * 128:(rc + 1) * 128],
                                                                    identity=idb[:, :])), reads=[csk, pfx + "idb"], writes=[pfx + "pt"])
            P.add("scalar", (lambda e, ct=ct, b=b: e.activation(out=ct[:, :, b * 128:(b + 1) * 128], in_=ps_t[:, :, :], func=AF.Copy)),
                  reads=[pfx + "pt"], writes=[ctk])
        for hp in range(8):
            po, pok = next_po()
            for rc in range(2):
                P.add("tensor", (lambda e, po=po, hp=hp, rc=rc, ct=ct: e.matmul(po[:, :], wukT[:, rc, hp * 128:(hp + 1) * 128],
                                                                               ct[:, rc, :], start=(rc == 0), stop=(rc == 1))),
                      reads=[pfx + "wukT", ctk], writes=[pok])
            evac(po[:, :], pok, 512, A["kT"][hp * 128:(hp + 1) * 128, tsl], "kT")
        for b in range(4):
            vs = vst[st8["vc"] % 2]
            vsk = pfx + "vst%d" % (st8["vc"] % 2)
            st8["vc"] += 1
            for half in range(2):
                po, pok = next_po()
                for rc in range(2):
                    P.add("tensor", (lambda e, po=po, b=b, rc=rc, half=half, ct=ct: e.matmul(
                        po[:, :], ct[:, rc, b * 128:(b + 1) * 128], wuvb[:, rc, half * 8:(half + 1) * 8, :],
                        start=(rc == 0), stop=(rc == 1))), reads=[pfx + "wuvb", ctk], writes=[pok])
                if half == 0:
                    P.add("scalar", (lambda e, po=po, vs=vs, half=half: e.activation(
                        out=vs[:, half * 8:(half + 1) * 8, 0:64], in_=po[:, :].rearrange("p (h e) -> p h e", h=8), func=AF.Copy)),
                        reads=[pok], writes=[vsk])
                else:
                    P.add("vector", (lambda e, po=po, vs=vs, half=half: e.tensor_copy(
                        out=vs[:, half * 8:(half + 1) * 8, 0:64], in_=po[:, :].rearrange("p (h e) -> p h e", h=8))),
                        reads=[pok], writes=[vsk])
            r0 = t * 512 + b * 128
            P.dma("gpsimd", vsk + "o", A["vaug"][r0:r0 + 128, :], vs[:, :, :].rearrange("p h e -> p (h e)"), reads=[vsk], writes=["vaug"])


N_IT = 20
TOPK = 256


def phase_aattn(P, A, l, pfx):
    j = l // 2
    idf, idb = make_ident(P, pfx, A["ident"])
    onesb = P.sb(pfx + "onesb", [128, 128], BF16)
    P.add("vector", lambda e: e.memset(onesb[:, :], 1.0), writes=[pfx + "onesb"])
    onesf = P.sb(pfx + "onesf", [128, 128], F32)
    P.add("vector", lambda e: e.memset(onesf[:, :], 1.0), writes=[pfx + "onesf"])
    b31 = P.sb(pfx + "b31", [128, 16], F32)
    P.dma("sync", pfx + "b31", b31[:, :], A["rel_bias"][31, :].partition_broadcast(128), writes=[pfx + "b31"])
    BTf = P.sb(pfx + "BTf", [128, 256], F32)
    BT = P.sb(pfx + "BT", [128, 16, 256], BF16)
    for c in range(16):
        P.dma("sync", pfx + "BTf", BTf[:, :], A["biasT"][c], writes=[pfx + "BTf"])
        P.add("vector", (lambda e, c=c: e.tensor_scalar(out=BT[:, c, :], in0=BTf[:, :], scalar1=b31[:, c:c + 1],
                                                        scalar2=None, op0=ALU.subtract)),
              reads=[pfx + "BTf", pfx + "b31"], writes=[pfx + "BT"])
    cmask = P.sb(pfx + "cmask", [128, 128], F32)
    P.dma("sync", pfx + "cmask", cmask[:, :], A["cmask"], writes=[pfx + "cmask"])
    pw = P.sb(pfx + "pw", [128, N_IT + 1], F32)
    for i in range(N_IT + 1):
        P.add("vector", (lambda e, i=i: e.memset(pw[:, i:i + 1], float(2.0 ** -i))), writes=[pfx + "pw"])
    ik2 = P.sb(pfx + "ik2", [128, TL // 2], BF16)
    P.dma("sync", pfx + "ik2", ik2[0:64, :], A["ikT"][:, 0:TL // 2], reads=["ikT"], writes=[pfx + "ik2"])
    P.dma("sync", pfx + "ik2", ik2[64:128, :], A["ikT"][:, TL // 2:TL], reads=["ikT"], writes=[pfx + "ik2"])
    kTp = [P.sb(pfx + "kTp%d" % i, [128, TL], BF16) for i in range(2)]
    vp = [P.sb(pfx + "vp%d" % i, [128, NLB, 2, 65], BF16) for i in range(2)]
    qTp = [P.sb(pfx + "qTp%d" % i, [128, 256], BF16) for i in range(2)]
    otile = [P.sb(pfx + "otile%d" % i, [128, 1024], BF16) for i in range(2)]
    oTs = P.sb(pfx + "oTs", [128, 8, 256], BF16)
    rcp = P.sb(pfx + "rcp", [128, 4], F32)
    for i in range(2):
        P.add("gpsimd", (lambda e, i=i: e.memset(otile[i][:, :], 0.0)), writes=[pfx + "otile%d" % i])
    scores = P.sb(pfx + "scores", [128, TL], F32)
    junk = P.sb(pfx + "junk", [128, TL], U8)
    NM = P.sb(pfx + "NM", [128, NLB, 256], BF16)
    iqt = [P.sb(pfx + "iqt%d" % i, [128, 8, 128], BF16) for i in range(2)]
    iwt = [P.sb(pfx + "iwt%d" % i, [128, 8], F32) for i in range(2)]
    rts = [P.sb(pfx + "rt%d" % i, [128, 512], F32) for i in range(3)]
    PT = [P.sb(pfx + "PT%d" % i, [128, 256], BF16) for i in range(4)]
    bs = P.sb(pfx + "bs", [128, 8 + N_IT + 1], F32)
    nd = P.sb(pfx + "nd", [128, 128], F32)
    ps_i = [P.ps(pfx + "pi%d" % i, [128, 512], F32) for i in range(2)]
    ps_m = ps_i
    ps_s = [P.ps(pfx + "pss%d" % i, [128, 256], F32) for i in range(2)]
    acc_o = [P.ps(pfx + "acco%d" % i, [128, 65], F32) for i in range(2)]
    ps_tt = P.ps(pfx + "ptt", [128, 2, 128], BF16)
    iqv = A["iqT"].rearrange("(h d) t -> d h t", d=64)
    vav = A["vaug"].rearrange("(blk p) (h e) -> p blk h e", p=128, e=65)
    st8 = {"ic": 0, "rc": 0, "mc": 0, "sc": 0, "pc": 0, "oc": 0}
    skey = pfx + "scores"
    bkey = pfx + "bs"

    def idx(qb):
        nk = (qb + 1) * 128
        it = iqt[qb % 2]; itk = pfx + "iqt%d" % (qb % 2)
        P.dma("sync", itk, it[0:64, :, :], iqv[:, :, qb * 128:(qb + 1) * 128], reads=["iqT"], writes=[itk])
        P.dma("sync", itk, it[64:128, :, :], iqv[:, :, qb * 128:(qb + 1) * 128], reads=["iqT"], writes=[itk])
        wt = iwt[qb % 2]; wtk = pfx + "iwt%d" % (qb % 2)
        P.dma("sync", wtk, wt[:, :], A["iw"][qb * 128:(qb + 1) * 128, :], reads=["iw"], writes=[wtk])
        nst = (nk + 511) // 512
        for st in range(nst):
            wd = min(512, nk - st * 512)
            half = (st * 512) // (TL // 2)
            kc0 = st * 512 - half * (TL // 2)
            for h in range(8):
                pi = ps_i[st8["ic"] % 2]; pik = pfx + "pi%d" % (st8["ic"] % 2)
                st8["ic"] += 1
                P.add("tensor", (lambda e, pi=pi, it=it, h=h, half=half, kc0=kc0, wd=wd: e.matmul(
                    pi[:, 0:wd], it[64 * half:64 * half + 64, h, :], ik2[64 * half:64 * half + 64, kc0:kc0 + wd],
                    start=True, stop=True)), reads=[itk, pfx + "ik2"], writes=[pik])
                rt = rts[st8["rc"] % 3]; rk = pfx + "rt%d" % (st8["rc"] % 3)
                st8["rc"] += 1
                P.add("scalar", (lambda e, pi=pi, rt=rt, wd=wd: e.activation(out=rt[:, 0:wd], in_=pi[:, 0:wd], func=AF.Relu)),
                      reads=[pik], writes=[rk])
                sl = slice(st * 512, st * 512 + wd)
                if h == 0:
                    P.add("vector", (lambda e, rt=rt, wt=wt, sl=sl, wd=wd: e.tensor_scalar(
                        out=scores[:, sl], in0=rt[:, 0:wd], scalar1=wt[:, 0:1], scalar2=None, op0=ALU.mult)),
                        reads=[rk, wtk], writes=[skey])
                else:
                    P.add("vector", (lambda e, rt=rt, wt=wt, sl=sl, wd=wd, h=h: e.scalar_tensor_tensor(
                        out=scores[:, sl], in0=rt[:, 0:wd], scalar=wt[:, h:h + 1], in1=scores[:, sl],
                        op0=ALU.mult, op1=ALU.add)), reads=[rk, wtk, skey], writes=[skey])
        V = lambda fn, rd, wr, na=False: P.add("vector", fn, reads=rd, writes=wr, noattach=na)
        V(lambda e: e.tensor_reduce(out=bs[:, 0:1], in_=scores[:, 0:nk], axis=AX.X, op=ALU.max, apply_absolute_value=True),
          [skey], [bkey])
        V(lambda e: e.tensor_scalar(out=bs[:, 1:2], in0=bs[:, 0:1], scalar1=1.0, scalar2=None, op0=ALU.add), [bkey], [bkey])
        V(lambda e: e.tensor_scalar(out=bs[:, 8:8 + N_IT + 1], in0=pw[:, :], scalar1=bs[:, 1:2], scalar2=None, op0=ALU.mult),
          [bkey, pfx + "pw"], [bkey])
        V(lambda e: e.tensor_tensor(out=scores[:, qb * 128:(qb + 1) * 128], in0=scores[:, qb * 128:(qb + 1) * 128],
                                    in1=cmask[:, :], op=ALU.add), [skey, pfx + "cmask"], [skey])
        V(lambda e: e.tensor_scalar(out=bs[:, 2:3], in0=bs[:, 1:2], scalar1=-1.0, scalar2=None, op0=ALU.mult), [bkey], [bkey])
        V(lambda e: e.tensor_tensor(out=bs[:, 3:4], in0=bs[:, 2:3], in1=bs[:, 8:9], op=ALU.add), [bkey], [bkey])
        def one_iter(i):
            V(lambda e: e.tensor_scalar(out=junk[:, 0:nk], in0=scores[:, 0:nk], scalar1=bs[:, 3:4], scalar2=None,
                                        op0=ALU.is_ge, op1=ALU.add, accum_out=bs[:, 4:5]), [skey, bkey], [pfx + "junk", bkey], True)
            V((lambda e, i=i: e.tensor_scalar(out=bs[:, 5:6], in0=bs[:, 4:5], scalar1=float(TOPK), scalar2=bs[:, 8 + i:9 + i],
                                              op0=ALU.is_ge, op1=ALU.mult)), [bkey], [bkey])
            V(lambda e: e.tensor_tensor(out=bs[:, 2:3], in0=bs[:, 2:3], in1=bs[:, 5:6], op=ALU.add), [bkey], [bkey])
            V((lambda e, i=i: e.tensor_tensor(out=bs[:, 3:4], in0=bs[:, 2:3], in1=bs[:, 9 + i:10 + i], op=ALU.add)), [bkey], [bkey])
            if i == N_IT - 1:
                V(lambda e: e.tensor_scalar(out=nd[:, :], in0=idf[:, :], scalar1=bs[:, 2:3], scalar2=-1.0, op0=ALU.mult,
                                            op1=ALU.mult), [bkey, pfx + "idf"], [pfx + "nd"])
        return [(lambda i=i: one_iter(i)) for i in range(N_IT)]

    def nm_gen(qb):
        jj = qb % 2
        for sb in range(qb + 1):
            mi = st8["mc"] % 2
            st8["mc"] += 1
            pmk = pfx + "pi%d" % mi
            P.add("tensor", (lambda e, mi=mi, sb=sb: e.matmul(ps_m[mi][:, 0:128], scores[:, sb * 128:(sb + 1) * 128], idf[:, :],
                                                             start=True, stop=False)), reads=[skey, pfx + "idf"], writes=[pmk])
            P.add("tensor", (lambda e, mi=mi: e.matmul(ps_m[mi][:, 0:128], onesf[:, :], nd[:, :], start=False, stop=True)),
                  reads=[pfx + "onesf", pfx + "nd"], writes=[pmk])
            P.add("vector", (lambda e, mi=mi, sb=sb, jj=jj: e.tensor_scalar(
                out=NM[:, sb, jj * 128:(jj + 1) * 128], in0=ps_m[mi][:, 0:128], scalar1=0.0, scalar2=float(NEGM),
                op0=ALU.is_lt, op1=ALU.mult)), reads=[pmk], writes=[pfx + "NM"])

    def attn(J, steps=()):
        steps = list(steps)
        ns = 2 * J + 2
        nk = ns * 128
        nh = DBG.get("aheads", 16)

        def load_pair(hp):
            kt = kTp[hp % 2]; ktk = pfx + "kTp%d" % (hp % 2)
            vt = vp[hp % 2]; vtk = pfx + "vp%d" % (hp % 2)
            qt = qTp[hp % 2]; qtk = pfx + "qTp%d" % (hp % 2)
            P.dma("sync", ktk, kt[:, 0:nk], A["kT"][hp * 128:(hp + 1) * 128, 0:nk], reads=["kT"], writes=[ktk])
            P.dma("sync", vtk, vt[:, 0:ns, :, :], vav[:, 0:ns, 2 * hp:2 * hp + 2, :], reads=["vaug"], writes=[vtk])
            P.dma("sync", qtk, qt[:, :], A["qT"][hp * 128:(hp + 1) * 128, J * 256:(J + 1) * 256], reads=["qT"], writes=[qtk])

        def qk(h, s):
            hp, hh = h // 2, h % 2
            kt = kTp[hp % 2]; ktk = pfx + "kTp%d" % (hp % 2)
            qt = qTp[hp % 2]; qtk = pfx + "qTp%d" % (hp % 2)
            k = s - 2 * J
            c0 = max(0, k) * 128
            near = k >= -1
            si = st8["sc"] % 2
            st8["sc"] += 1
            psk = pfx + "pss%d" % si
            P.add("tensor", (lambda e, si=si, s=s, hh=hh, c0=c0, kt=kt, qt=qt: e.matmul(
                ps_s[si][:, c0:256], kt[64 * hh:64 * hh + 64, s * 128:(s + 1) * 128], qt[64 * hh:64 * hh + 64, c0:256],
                start=True, stop=False)), reads=[ktk, qtk], writes=[psk])
            P.add("tensor", (lambda e, si=si, s=s, c0=c0, near=near: e.matmul(
                ps_s[si][:, c0:256], idb[:, :], NM[:, s, c0:256], start=False, stop=(not near))),
                reads=[pfx + "idb", pfx + "NM"], writes=[psk])
            if near:
                if k == -1:
                    P.add("tensor", (lambda e, si=si, h=h: e.matmul(ps_s[si][:, 0:128], idb[:, :], BT[:, h, 128:256],
                                                                   start=False, stop=True)),
                          reads=[pfx + "idb", pfx + "BT"], writes=[psk])
                else:
                    w = 256 - c0
                    P.add("tensor", (lambda e, si=si, h=h, c0=c0, w=w: e.matmul(ps_s[si][:, c0:c0 + w], idb[:, :], BT[:, h, 0:w],
                                                                               start=False, stop=True)),
                          reads=[pfx + "idb", pfx + "BT"], writes=[psk])
            pt = PT[st8["pc"] % 4]; ptk = pfx + "PT%d" % (st8["pc"] % 4)
            st8["pc"] += 1
            P.add("scalar", (lambda e, si=si, pt=pt, c0=c0, h=h: e.activation(
                out=pt[:, c0:256], in_=ps_s[si][:, c0:256], func=AF.Exp, bias=b31[:, h:h + 1], scale=1.0)),
                reads=[psk, pfx + "b31"], writes=[ptk])
            return pt, ptk, c0

        def pv(h, s, pre):
            pt, ptk, c0 = pre
            hp, hh = h // 2, h % 2
            vt = vp[hp % 2]; vtk = pfx + "vp%d" % (hp % 2)
            for jj in range(2):
                if jj * 128 < c0:
                    continue
                last = (s == ns - 1) if jj == 1 else (s == ns - 2)
                P.add("tensor", (lambda e, pt=pt, jj=jj, s=s, hh=hh, vt=vt, last=last: e.matmul(
                    acc_o[jj][:, :], pt[:, jj * 128:(jj + 1) * 128], vt[:, s, hh, :],
                    start=(s == 0), stop=last)), reads=[vtk, ptk], writes=[pfx + "acco%d" % jj])

        def finish_head(h):
            for jj in range(2):
                P.add("vector", (lambda e, jj=jj: e.reciprocal(out=rcp[:, jj:jj + 1], in_=acc_o[jj][:, 64:65])),
                      reads=[pfx + "acco%d" % jj], writes=[pfx + "rcp%d" % jj])
                P.add("vector", (lambda e, jj=jj, h=h: e.tensor_scalar(out=otile[jj][:, h * 64:(h + 1) * 64], in0=acc_o[jj][:, 0:64],
                                                                      scalar1=rcp[:, jj:jj + 1], scalar2=None, op0=ALU.mult)),
                      reads=[pfx + "acco%d" % jj, pfx + "rcp%d" % jj], writes=[pfx + "otile%d" % jj])
            nstep = 2 if h < 4 else 1
            if h == nh - 1:
                nstep = len(steps)
            for _ in range(min(nstep, len(steps))):
                steps.pop(0)()

        load_pair(0)
        pairs = [(h, s) for h in range(nh) for s in range(ns)]
        pre = qk(*pairs[0])
        for i, (h, s) in enumerate(pairs):
            if s == 0 and h % 2 == 0 and h + 2 < nh:
                load_pair(h // 2 + 1)
            nxt = qk(*pairs[i + 1]) if i + 1 < len(pairs) else None
            pv(h, s, pre)
            pre = nxt
            if s == ns - 1:
                finish_head(h)
        for jj in range(2):
            for c in range(8):
                P.add("tensor", (lambda e, jj=jj, c=c: e.transpose(out=ps_tt[:, c % 2, :], in_=otile[jj][:, c * 128:(c + 1) * 128],
                                                                  identity=idb[:, :])),
                      reads=[pfx + "otile%d" % jj, pfx + "idb"], writes=[pfx + "ptt"])
                if c % 2 == 1:
                    P.add("scalar", (lambda e, jj=jj, c=c: e.activation(out=oTs[:, c - 1:c + 1, jj * 128:(jj + 1) * 128],
                                                                       in_=ps_tt[:, :, :], func=AF.Copy)),
                          reads=[pfx + "ptt"], writes=[pfx + "oTs"])
        P.dma("gpsimd", pfx + "oTso", A["oT"].rearrange("(c p) t -> p c t", p=128)[:, :, J * 256:(J + 1) * 256], oTs[:, :, :],
              reads=[pfx + "oTs"], writes=["oT"])

    NJ = DBG.get("AJ", TL // 256)
    for st_ in idx(0):
        st_()
    nm_gen(0)
    for st_ in idx(1):
        st_()
    nm_gen(1)
    for J in range(NJ):
        steps = idx(2 * J + 2) if J + 1 < NJ else []
        attn(J, steps)
        if J + 1 < NJ:
            nm_gen(2 * J + 2)
            for st_ in idx(2 * J + 3):
                st_()
            nm_gen(2 * J + 3)


def _new_nc():
    return bass.Bass("TRN2", target_bir_lowering=False)


def build_single(phase_name, l=0):
    nc = _new_nc()
    A = {}
    ins, outs = [], []

    def din(name, shape, dt=F32):
        A[name] = nc.dram_tensor(name, list(shape), dt, kind="ExternalInput").ap()
        ins.append(name)

    def dout(name, shape, dt=F32):
        A[name] = nc.dram_tensor(name, list(shape), dt, kind="ExternalOutput").ap()
        outs.append(name)

    with ExitStack() as es:
        P = Prog(nc, es)
        make_epsc(P)
        if phase_name == "mod":
            din("ccol", [128, 8]); din("ada_w", [DEPTH, D, 6 * D]); din("ada_b", [DEPTH, 6 * D]); din("ada_bT", [128, 4 * 48])
            dout("modT", [128, 4 * 48]); dout("modrow", [DEPTH, 2, D])
            phase_mod(P, A)
            P.add("sync", None, reads=["modT", "modrow"])
        elif phase_name == "mlp":
            din("ident", [128, 128]); din("modT", [128, 4 * 48]); din("modrow", [DEPTH, 2, D])
            din("ln_g", [DEPTH, 2, D]); din("ln_b", [DEPTH, 2, D])
            din("mlp_w1", [DEPTH, D, DFF]); din("mlp_w2", [DEPTH, DFF, D]); din("xin", [TL, D])
            dout("xout", [TL, D])
            phase_mlp(P, A, l, A["xin"], "xin", A["xout"], "xout", "ml_")
            P.add("gpsimd", None, reads=["xout"])
        elif phase_name == "bproj":
            din("ident", [128, 128]); din("modT", [128, 4 * 48]); din("b_w_in", [2, D, 3072]); din("xin", [TL, D])
            dout("qT", [D, TL], BF16); dout("kT", [D, TL], BF16); dout("v", [TL, D], BF16)
            phase_bproj(P, A, l, A["xin"], "xin", "bp_")
            P.add("gpsimd", None, reads=["qT", "kT", "v"])
        elif phase_name == "battn":
            din("ident", [128, 128]); din("rel_bias", [32, 16]); din("biasT", [16, 128, 256]); din("b_lambda", [2, 4, 64])
            din("b_subln", [2, 128]); din("qT", [D, TL], BF16); din("kT", [D, TL], BF16); din("v", [TL, D], BF16)
            dout("oT", [D, TL], BF16)
            phase_battn(P, A, l, "ba_")
            P.add("gpsimd", None, reads=["oT"])
        elif phase_name == "aproj":
            din("ident", [128, 128]); din("modT", [128, 4 * 48]); din("a_w_in", [2, D, A_IN]); din("a_w_uk", [2, 16, 64, 256])
            din("a_kv_norm", [2, 256]); din("xin", [TL, D])
            din("a_w_uv", [2, 16, 256, 64])
            dout("qT", [D, TL], BF16); dout("kT", [D, TL], BF16); dout("vaug", [TL, 16 * 65], BF16)
            dout("iqT", [512, TL], BF16); dout("ikT", [64, TL], BF16); dout("iw", [TL, 8], F32)
            phase_aproj(P, A, l, A["xin"], "xin", "ap_")
            P.add("gpsimd", None, reads=["qT", "kT", "vaug", "iqT", "ikT", "iw"])
        elif phase_name == "aattn":
            din("ident", [128, 128]); din("rel_bias", [32, 16]); din("biasT", [16, 128, 256]); din("cmask", [128, 128])
            din("qT", [D, TL], BF16); din("kT", [D, TL], BF16); din("vaug", [TL, 16 * 65], BF16)
            din("iqT", [512, TL], BF16); din("ikT", [64, TL], BF16); din("iw", [TL, 8], F32)
            dout("oT", [D, TL], BF16)
            phase_aattn(P, A, l, "aa_")
            P.add("gpsimd", None, reads=["oT"])
        elif phase_name == "outln":
            din("modrow", [DEPTH, 2, D]); din("ln_g", [DEPTH, 2, D]); din("ln_b", [DEPTH, 2, D])
            din("w_o", [D, D]); din("oT", [D, TL], BF16); din("xin", [TL, D])
            dout("xout", [TL, D])
            phase_outln(P, A, l, A["w_o"], A["oT"], "oT", A["xin"], "xin", A["xout"], "xout", "ol_")
            P.add("gpsimd", None, reads=["xout"])
        else:
            raise ValueError(phase_name)
        P.finish()
        nops = P.n
    return nc, ins, outs, nops


def _to_local(a_b, hf):
    s = a_b.shape
    return np.ascontiguousarray(a_b.reshape(32, 2, 128, *s[1:])[:, hf].reshape(TL, *s[1:]))


def _from_local(parts):
    s = parts[0].shape
    o = np.empty((32, 2, 128) + s[1:], parts[0].dtype)
    for hf in range(2):
        o[:, hf] = parts[hf].reshape(32, 128, *s[1:])
    return o.reshape(SEQ, *s[1:])


def run_phase(nc, ins, in_maps):
    res = run_bass_kernel_spmd(nc, [{k: m[k] for k in ins} for m in in_maps], core_ids=list(range(len(in_maps))))
    return res.results


def rel_bucket_np(dist):
    import math
    n = np.maximum(dist, 0)
    nf = np.maximum(n, 1).astype(np.float32)
    large = 16 + (np.log(nf / np.float32(16)) / np.float32(math.log(128 / 16)) * np.float32(16)).astype(np.int32)
    large = np.minimum(large, 31)
    return np.where(n < 16, n, large)


def make_biasT(rel_bias):
    s_ = np.arange(128)[:, None]
    q_ = np.arange(128)[None, :]
    dd = q_ - s_
    bd = rel_bucket_np(dd)
    bp = rel_bucket_np(dd + 128)
    out = np.empty((16, 128, 256), np.float32)
    for c in range(16):
        diag = rel_bias[:, c][bd]
        out[c, :, 0:128] = np.where(dd >= 0, diag, np.float32(NEGM))
        out[c, :, 128:256] = rel_bias[:, c][bp]
    return out


def build_fused(single=None):
    nc = _new_nc()
    A = {}
    NL = DEPTH if single is None else 1
    NM_ = 2 if single is None else 1

    def din(name, shape, dt=F32):
        A[name] = nc.dram_tensor(name, list(shape), dt, kind="ExternalInput").ap()

    def dint(name, shape, dt=F32):
        A[name] = nc.dram_tensor(name, list(shape), dt, kind="Internal").ap()

    din("x", [TL, D]); din("ccol", [128, 8]); din("ada_w", [NL, D, 6 * D]); din("ada_b", [NL, 6 * D])
    din("ada_bT", [128, NL * 48]); din("ident", [128, 128]); din("rel_bias", [32, 16]); din("biasT", [16, 128, 256])
    din("ln_g", [NL, 2, D]); din("ln_b", [NL, 2, D])
    if single is None or single % 2 == 0:
        din("cmask", [128, 128])
        din("a_w_in", [NM_, D, A_IN]); din("a_kv_norm", [NM_, 256]); din("a_w_uk", [NM_, 16, 64, 256])
        din("a_w_uv", [NM_, 16, 256, 64]); din("a_w_o", [NM_, D, D])
    if single is None or single % 2 == 1:
        din("b_w_in", [NM_, D, 3072]); din("b_lambda", [NM_, 4, 64]); din("b_subln", [NM_, 128]); din("b_w_o", [NM_, D, D])
    din("mlp_w1", [NL, D, DFF]); din("mlp_w2", [NL, DFF, D])
    A["out"] = nc.dram_tensor("out", [TL, D], F32, kind="ExternalOutput").ap()
    dint("modT", [128, NL * 48]); dint("modrow", [NL, 2, D]); dint("xA", [TL, D]); dint("xB", [TL, D])
    dint("vaug", [TL, 16 * 65], BF16)
    dint("iqT", [512, TL], BF16); dint("ikT", [64, TL], BF16); dint("iw", [TL, 8], F32)
    dint("qT", [D, TL], BF16); dint("kT", [D, TL], BF16); dint("v", [TL, D], BF16); dint("oT", [D, TL], BF16)
    nops = [0]

    def block(fn, outkeys):
        with ExitStack() as es:
            P = Prog(nc, es)
            make_epsc(P)
            fn(P)
            P.add("gpsimd", None, reads=outkeys)
            P.finish()
            nops[0] += P.n

    block(lambda P: phase_mod(P, A, NL), ["modT", "modrow"])
    xcur = "x"
    llist = DBG.get("llist", list(range(DBG.get("layers", DEPTH))))
    if single is not None:
        llist = [0]
        DBG["true_l"] = single
    else:
        DBG.pop("true_l", None)
    for l in llist:
        j = l // 2
        kind_a = (l % 2 == 0) if single is None else (single % 2 == 0)
        if kind_a:
            block(lambda P: phase_aproj(P, A, l, A[xcur], "xin", "ap_"), ["qT", "kT", "vaug", "iqT", "ikT", "iw"])
            block(lambda P: phase_aattn(P, A, l, "aa_"), ["oT"])
            w_o = A["a_w_o"][j]
        else:
            block(lambda P: phase_bproj(P, A, l, A[xcur], "xin", "bp_"), ["qT", "kT", "v"])
            block(lambda P: phase_battn(P, A, l, "ba_"), ["oT"])
            w_o = A["b_w_o"][j]
        block(lambda P: phase_outln(P, A, l, w_o, A["oT"], "oT", A[xcur], "xin", A["xA"], "xout", "ol_"), ["xout"])
        last = (l == llist[-1])
        dst = "out" if last else "xB"
        block(lambda P: phase_mlp(P, A, l, A["xA"], "xin", A[dst], "xout", "ml_"), ["xout"])
        xcur = "xB"
    return nc, nops[0]


_CACHE = {}
FUSED = True


def kernel(**inputs):
    f32 = lambda a: np.ascontiguousarray(np.asarray(a, dtype=np.float32))
    x = f32(inputs["x"])
    c = f32(inputs["c"])
    rel_bias = f32(inputs["rel_bias"])
    W = {k: f32(inputs[k]) for k in ["ada_w", "ada_b", "ln_g", "ln_b", "a_w_in", "a_kv_norm", "a_w_uk", "a_w_uv", "a_w_o",
                                     "b_w_in", "b_lambda", "b_subln", "b_w_o", "mlp_w1", "mlp_w2"]}
    const = {
        "ident": np.eye(128, dtype=np.float32),
        "rel_bias": rel_bias,
        "biasT": make_biasT(rel_bias),
    }
    cmask = np.where(np.arange(128)[None, :] <= np.arange(128)[:, None], 0.0, -1e30).astype(np.float32)
    ccols = [np.ascontiguousarray(c[b].reshape(8, 128).T) for b in range(NB)]
    if FUSED:
        if "nc" not in _CACHE:
            _CACHE["nc"] = build_fused()
        nc, nops = _CACHE["nc"]
        shared = dict(const)
        shared.update(W)
        shared["ada_bT"] = np.ascontiguousarray(W["ada_b"].reshape(4 * 48, 128).T)
        shared["cmask"] = cmask
        in_maps = []
        for b in range(NB):
            m = dict(shared)
            m["x"] = x[b]
            m["ccol"] = ccols[b]
            in_maps.append(m)
        res = run_bass_kernel_spmd(nc, in_maps, core_ids=list(range(NB)))
        return np.stack([np.asarray(res.results[b]["out"], dtype=np.float32) for b in range(NB)], axis=0)
    xs = [x[b] for b in range(NB)]
    for L in range(DEPTH):
        key = "nc%d" % (L % 2)
        if ("L", L) not in _CACHE:
            _CACHE[("L", L)] = build_fused(single=L)
        nc, nops = _CACHE[("L", L)]
        j = L // 2
        shared = dict(const)
        for k in ["ada_w", "ada_b", "ln_g", "ln_b", "mlp_w1", "mlp_w2"]:
            shared[k] = np.ascontiguousarray(W[k][L:L + 1])
        shared["ada_bT"] = np.ascontiguousarray(W["ada_b"][L].reshape(48, 128).T)
        if L % 2 == 0:
            shared["cmask"] = cmask
            for k in ["a_w_in", "a_kv_norm", "a_w_uk", "a_w_uv", "a_w_o"]:
                shared[k] = np.ascontiguousarray(W[k][j:j + 1])
        else:
            for k in ["b_w_in", "b_lambda", "b_subln", "b_w_o"]:
                shared[k] = np.ascontiguousarray(W[k][j:j + 1])
        in_maps = []
        for b in range(NB):
            m = dict(shared)
            m["x"] = xs[b]
            m["ccol"] = ccols[b]
            in_maps.append(m)
        res = run_bass_kernel_spmd(nc, in_maps, core_ids=list(range(NB)))
        xs = [np.asarray(res.results[b]["out"], dtype=np.float32) for b in range(NB)]
    return np.stack(xs, axis=0)
```

```python
import numpy as np
from contextlib import ExitStack
import concourse.bass as bass
import concourse.mybir as mybir
from concourse.bass_utils import run_bass_kernel_spmd

F32 = mybir.dt.float32
BF16 = mybir.dt.bfloat16
U8 = mybir.dt.uint8
AF = mybir.ActivationFunctionType
ALU = mybir.AluOpType
AX = mybir.AxisListType

D = 1024
SEQ = 8192
NB = 4
DEPTH = 4
TL = 8192
NLB = 64
DFF = 4096
LN_EPS = 1e-5
DN_ALPHA = (2 * DEPTH) ** 0.25
A_IN = 1864
NEGM = -30000.0

DBG = {}
STATS = {}
ENGS = ["tensor", "vector", "scalar", "gpsimd", "sync"]
EPOCH = 30000
EPOCH_DMA = 1800


class Op:
    __slots__ = ("eng", "fn", "chan", "waits", "needs_inc", "val", "epoch", "isdma", "noattach")


class Prog:
    _uid = [0]

    def __init__(self, nc, es):
        self.nc = nc
        self.es = es
        Prog._uid[0] += 1
        self.uid = Prog._uid[0]
        self.ops = {e: [] for e in ENGS}
        self.chan_ops = {}
        self.lastw = {}
        self.readers = {}
        self.n = 0

    def sb(self, name, shape, dt):
        return self.es.enter_context(self.nc.sbuf_tensor("%s_u%d" % (name, self.uid), list(shape), dt))

    def ps(self, name, shape, dt=F32):
        esz = 4 if dt == F32 else 2
        n = 1
        for d in shape[1:]:
            n *= d
        per_bank = 2048 // esz
        tot = ((n + per_bank - 1) // per_bank) * per_bank
        h = self.es.enter_context(self.nc.psum_tensor("%s_u%d" % (name, self.uid), [128, tot], dt))
        ap = h[0:shape[0], 0:n]
        if len(shape) == 3:
            ap = ap.rearrange("p (a b) -> p a b", a=shape[1])
        return ap

    def add(self, eng, fn, reads=(), writes=(), dma=None, noattach=False, pe_attach=False):
        op = Op()
        op.noattach = noattach or (eng == "tensor" and not pe_attach)
        op.eng = eng
        op.fn = fn
        op.isdma = dma is not None
        op.chan = ("dma", dma) if dma is not None else eng
        op.needs_inc = op.isdma
        deps = []
        for k in reads:
            w = self.lastw.get(k)
            if w is not None:
                deps.append(w)
        for k in writes:
            w = self.lastw.get(k)
            if w is not None:
                deps.append(w)
            rd = self.readers.get(k)
            if rd:
                deps.extend(rd.values())
        waits = []
        seen = set()
        for d in deps:
            if id(d) in seen:
                continue
            seen.add(id(d))
            if (not d.isdma) and d.chan == eng and eng == "tensor":
                continue
            d.needs_inc = True
            waits.append(d)
        op.waits = waits
        for k in reads:
            self.readers.setdefault(k, {})[op.chan] = op
        for k in writes:
            self.lastw[k] = op
            self.readers[k] = {}
        self.ops[eng].append(op)
        self.chan_ops.setdefault(op.chan, []).append(op)
        self.n += 1
        return op

    def dma(self, eng, chan, out, in_, reads=(), writes=()):
        return self.add(eng, lambda e: e.dma_start(out=out, in_=in_), reads, writes, dma=chan)

    def finish(self):
        nc = self.nc
        sems = {}
        for chan, lst in self.chan_ops.items():
            cnt = 0
            ep = EPOCH_DMA if chan[0] == "dma" else EPOCH
            for op in lst:
                if op.needs_inc:
                    op.epoch = cnt // ep
                    op.val = cnt % ep + 1
                    cnt += 1
                    key = (chan, op.epoch)
                    if key not in sems:
                        sems[key] = nc.alloc_semaphore(name="s%d_u%d" % (len(sems), self.uid))
        self.nsems = len(sems)
        with nc.Block() as block:
            self._emit(block, sems)
        nc.clear_and_free_semaphores(list(sems.values()))
        nc.all_engine_barrier()

    def _emit(self, block, sems):

        def emit(eng_name):
            ops = self.ops[eng_name]

            def body(e):
                waited = {}
                for op in ops:
                    need = {}
                    for d in op.waits:
                        cur = waited.get(d.chan)
                        if cur is not None and (cur[0] > d.epoch or (cur[0] == d.epoch and cur[1] >= d.val)):
                            continue
                        prev = need.get(d.chan)
                        if prev is None or (d.epoch, d.val) > (prev.epoch, prev.val):
                            need[d.chan] = d
                    need = list(need.values())
                    attach = None
                    if need and op.fn is not None and not op.noattach:
                        attach = need.pop()
                    for d in need:
                        e.wait_ge(sems[(d.chan, d.epoch)], d.val * (16 if d.isdma else 1))
                        waited[d.chan] = (d.epoch, d.val)
                        STATS[eng_name + "_wait"] = STATS.get(eng_name + "_wait", 0) + 1
                    if op.fn is None:
                        continue
                    ins = op.fn(e)
                    if attach is not None:
                        ins._wait_ge(sems[(attach.chan, attach.epoch)], attach.val * (16 if attach.isdma else 1))
                        waited[attach.chan] = (attach.epoch, attach.val)
                    STATS[eng_name] = STATS.get(eng_name, 0) + 1
                    if op.needs_inc:
                        ins.then_inc(sems[(op.chan, op.epoch)], 16 if op.isdma else 1)
            return body

        for en in ENGS:
            if self.ops[en]:
                getattr(block, en)(emit(en))


class Ctx:
    pass


def _rot(lst, i):
    return lst[i % len(lst)]


def load_mod_cols(P, pfx, modT_ap, l, which):
    nc = P.nc
    mt = P.sb(pfx + "modc", [128, 16], F32)
    base = l * 48 + which * 24
    P.dma("sync", pfx + "modc", mt[:, :], modT_ap[:, base:base + 16], reads=["modT"], writes=[pfx + "modc"])
    P.add("vector", lambda e: e.tensor_scalar(out=mt[:, 8:16], in0=mt[:, 8:16], scalar1=1.0, scalar2=None,
                                             op0=ALU.add), reads=[pfx + "modc"], writes=[pfx + "modc"])
    return mt


def load_bcast(P, name, src_ap_1d, n):
    t = P.sb(name, [128, n], F32)
    P.dma("sync", name, t[:, :], src_ap_1d.partition_broadcast(128), reads=["modrow"], writes=[name])
    return t


def make_ident(P, pfx, ident_ap):
    idf = P.sb(pfx + "idf", [128, 128], F32)
    P.dma("sync", pfx + "idf", idf[:, :], ident_ap, writes=[pfx + "idf"])
    idb = P.sb(pfx + "idb", [128, 128], BF16)
    P.add("vector", lambda e: e.tensor_copy(out=idb[:, :], in_=idf[:, :]), reads=[pfx + "idf"], writes=[pfx + "idb"])
    return idf, idb


def emit_hT(P, pfx, xblk_tiles, nblk, idf, modc, hT, col0, ps_tr_list, cnt):
    for c in range(8):
        pt, pk = _rot(ps_tr_list, cnt[0])
        cnt[0] += 1
        for b in range(nblk):
            xt, xk = xblk_tiles[b]
            P.add("tensor", (lambda e, pt=pt, xt=xt, b=b, c=c: e.transpose(
                out=pt[:, b * 128:(b + 1) * 128], in_=xt[:, c * 128:(c + 1) * 128], identity=idf[:, :])),
                reads=[xk, pfx + "idf"], writes=[pk])
        P.add("scalar", (lambda e, pt=pt, c=c: e.activation(
            out=hT[:, c, col0:col0 + nblk * 128], in_=pt[:, 0:nblk * 128], func=AF.Identity,
            bias=modc[:, c:c + 1], scale=modc[:, 8 + c:9 + c])),
            reads=[pk, pfx + "modc"], writes=[pfx + "hT"])


def emit_resid_ln(P, pfx, ps_y, ps_key, xt, xkey, gbc, lng, lnb, ot, okey, small, skey):
    nc = P.nc
    st, mv, rs = small
    P.add("vector", lambda e: e.tensor_tensor(out=ot[:, :].rearrange("p (a f) -> p a f", a=2), in0=ps_y[:, :, :],
                                             in1=gbc[:, :].rearrange("p (a f) -> p a f", a=2), op=ALU.mult),
          reads=[ps_key, pfx + "gbc"], writes=[okey])
    P.add("vector", lambda e: e.scalar_tensor_tensor(out=ot[:, :], in0=xt, scalar=float(DN_ALPHA), in1=ot[:, :],
                                                    op0=ALU.mult, op1=ALU.add),
          reads=[xkey], writes=[okey])
    P.add("vector", lambda e: e.bn_stats(out=st[:, 0, :], in_=ot[:, 0:512]), reads=[okey], writes=[skey])
    P.add("vector", lambda e: e.bn_stats(out=st[:, 1, :], in_=ot[:, 512:1024]), reads=[okey], writes=[skey])
    P.add("vector", lambda e: e.bn_aggr(out=mv[:, :], in_=st[:, :, :]), reads=[skey], writes=[skey])
    P.add("scalar", lambda e: e.activation(out=rs[:, 0:1], in_=mv[:, 1:2], func=AF.Sqrt, bias=P.epsc[:, 0:1], scale=1.0),
          reads=[skey, "epsc"], writes=[skey + "r"])
    P.add("vector", lambda e: e.reciprocal(out=rs[:, 1:2], in_=rs[:, 0:1]), reads=[skey + "r"], writes=[skey + "r2"])
    P.add("vector", lambda e: e.tensor_scalar(out=rs[:, 2:3], in0=mv[:, 0:1], scalar1=rs[:, 1:2], scalar2=-1.0,
                                             op0=ALU.mult, op1=ALU.mult), reads=[skey + "r2"], writes=[skey + "r2"])
    P.add("scalar", lambda e: e.activation(out=ot[:, :], in_=ot[:, :], func=AF.Identity, bias=rs[:, 2:3],
                                           scale=rs[:, 1:2]), reads=[skey + "r2", okey], writes=[okey])
    P.add("vector", lambda e: e.tensor_tensor(out=ot[:, :], in0=ot[:, :], in1=lng[:, :], op=ALU.mult),
          reads=[okey, pfx + "lng"], writes=[okey])
    P.add("gpsimd", lambda e: e.tensor_tensor(out=ot[:, :], in0=ot[:, :], in1=lnb[:, :], op=ALU.add),
          reads=[okey, pfx + "lnb"], writes=[okey])


def make_epsc(P):
    t = P.sb("epsc", [128, 1], F32)
    P.add("vector", lambda e: e.memset(t[:, :], LN_EPS), writes=["epsc"])
    P.epsc = t


def phase_mod(P, A, NL=DEPTH):
    nc = P.nc
    pfx = "md_"
    cT = P.sb(pfx + "cT", [128, 8], F32)
    P.dma("sync", pfx + "c", cT[:, :], A["ccol"], writes=[pfx + "cT"])
    sT = P.sb(pfx + "sT", [128, 8], F32)
    P.add("scalar", lambda e: e.activation(out=sT[:, :], in_=cT[:, :], func=AF.Silu), reads=[pfx + "cT"], writes=[pfx + "sT"])
    bT = P.sb(pfx + "bT", [128, NL * 48], F32)
    P.dma("sync", pfx + "b", bT[:, :], A["ada_bT"], writes=[pfx + "bT"])
    brow = P.sb(pfx + "brow", [1, NL * 2048], F32)
    for l in range(NL):
        for gi in range(2):
            P.dma("sync", pfx + "br", brow[:, (l * 2 + gi) * 1024:(l * 2 + gi + 1) * 1024],
                  A["ada_b"][l:l + 1, (2 + 3 * gi) * 1024:(3 + 3 * gi) * 1024], writes=[pfx + "brow"])
    modsb = P.sb(pfx + "modsb", [128, NL * 48], F32)
    rowsb = P.sb(pfx + "rowsb", [1, NL * 2048], F32)
    wp = [P.sb(pfx + "wp%d" % i, [128, 8, 512], F32) for i in range(3)]
    psc = P.ps(pfx + "psc", [128, NL * 48], F32)
    psr = [P.ps(pfx + "psr%d" % i, [1, 512], F32) for i in range(2)]
    P.add("vector", lambda e: e.memset(modsb[:, :], 0.0), writes=[pfx + "modsb"])
    P.add("vector", lambda e: e.memset(rowsb[:, :], 0.0), writes=[pfx + "rowsb"])
    it = 0
    for l in range(NL):
        for pc in range(12):
            w = wp[it % 3]
            wk = pfx + "wp%d" % (it % 3)
            src = A["ada_w"][l, :, pc * 512:(pc + 1) * 512].rearrange("(kk p) f -> p kk f", p=128)
            qeng = "sync" if it % 2 == 0 else "gpsimd"
            P.dma(qeng, wk + qeng, w[:, :, :], src, writes=[wk])
            seg = pc // 2
            if seg in (2, 5):
                pr = psr[it % 2]
                prk = pfx + "psr%d" % (it % 2)
                for kk in range(8):
                    P.add("tensor", (lambda e, pr=pr, w=w, kk=kk: e.matmul(
                        pr[:, :], sT[:, kk:kk + 1], w[:, kk, :], start=(kk == 0), stop=(kk == 7))),
                        reads=[wk, pfx + "sT"], writes=[prk])
                o0 = (l * 2 + seg // 3) * 1024 + (pc % 2) * 512
                P.add("vector", (lambda e, pr=pr, o0=o0: e.tensor_tensor(
                    out=rowsb[:, o0:o0 + 512], in0=pr[:, :], in1=brow[:, o0:o0 + 512], op=ALU.add)),
                    reads=[prk, pfx + "brow"], writes=[pfx + "rowsb"])
            else:
                for q in range(4):
                    col = l * 48 + pc * 4 + q
                    for kk in range(8):
                        P.add("tensor", (lambda e, w=w, kk=kk, q=q, col=col: e.matmul(
                            psc[:, col:col + 1], w[:, kk, q * 128:(q + 1) * 128], sT[:, kk:kk + 1],
                            start=(kk == 0), stop=(kk == 7))),
                            reads=[wk, pfx + "sT"], writes=[pfx + "psc"])
            it += 1
    for l in range(NL):
        for c0 in (l * 48, l * 48 + 24):
            P.add("vector", (lambda e, c0=c0: e.tensor_tensor(out=modsb[:, c0:c0 + 16], in0=psc[:, c0:c0 + 16],
                                                              in1=bT[:, c0:c0 + 16], op=ALU.add)),
                  reads=[pfx + "psc", pfx + "bT"], writes=[pfx + "modsb"])
    P.dma("sync", pfx + "o1", A["modT"], modsb[:, :], reads=[pfx + "modsb"], writes=["modT"])
    P.dma("sync", pfx + "o2", A["modrow"].rearrange("(o l) g f -> o (l g f)", o=1), rowsb[:, :], reads=[pfx + "rowsb"],
          writes=["modrow"])


def phase_mlp(P, A, l, xin, xin_key, xout, xout_key, pfx):
    nc = P.nc
    TT = 256
    NT = TL // TT
    idf, idb = make_ident(P, pfx, A["ident"])
    modc = load_mod_cols(P, pfx, A["modT"], l, 1)
    gbc = load_bcast(P, pfx + "gbc", A["modrow"][l, 1, :], 1024)
    P.add("vector", lambda e: e.tensor_scalar(out=gbc[:, :], in0=gbc[:, :], scalar1=1.0, scalar2=None, op0=ALU.add),
          reads=[pfx + "gbc"], writes=[pfx + "gbc"])
    lng = P.sb(pfx + "lng", [128, 1024], F32)
    P.dma("sync", pfx + "lng", lng[:, :], A["ln_g"][l, 1, :].partition_broadcast(128), writes=[pfx + "lng"])
    lnb = P.sb(pfx + "lnb", [128, 1024], F32)
    P.dma("sync", pfx + "lnb", lnb[:, :], A["ln_b"][l, 1, :].partition_broadcast(128), writes=[pfx + "lnb"])
    w1b = P.sb(pfx + "w1b", [128, 8, DFF], BF16)
    w2b = P.sb(pfx + "w2b", [128, 32, D], BF16)
    for kc in range(8):
        P.dma("gpsimd", pfx + "w1", w1b[:, kc, :], A["mlp_w1"][l, kc * 128:(kc + 1) * 128, :], writes=[pfx + "w1b"])
    w2v = A["mlp_w2"][l].rearrange("(kc p) f -> p kc f", p=128)
    for g in range(8):
        P.dma("gpsimd", pfx + "w2", w2b[:, g * 4:(g + 1) * 4, :], w2v[:, g * 4:(g + 1) * 4, :], writes=[pfx + "w2b"])
    xtr = [P.sb(pfx + "xtr%d" % i, [128, 1024], F32) for i in range(2)]
    xep = [P.sb(pfx + "xep%d" % i, [128, 1024], F32) for i in range(2)]
    ots = [P.sb(pfx + "ot%d" % i, [128, 1024], F32) for i in range(3)]
    hT = P.sb(pfx + "hT", [128, 8, TT], BF16)
    aT = P.sb(pfx + "aT", [128, 32, TT], BF16)
    rts = [P.sb(pfx + "rt%d" % i, [128, TT], F32) for i in range(3)]
    smalls = [(P.sb(pfx + "st%d" % i, [128, 2, 6], F32), P.sb(pfx + "mv%d" % i, [128, 2], F32),
               P.sb(pfx + "rs%d" % i, [128, 3], F32)) for i in range(3)]
    ps_tr = [(P.ps(pfx + "ptr%d" % i, [128, 512], F32), pfx + "ptr%d" % i) for i in range(2)]
    ps_a = [P.ps(pfx + "pa%d" % i, [128, 512], F32) for i in range(2)]
    ps_y = [P.ps(pfx + "py%d" % i, [128, 2, 512], F32) for i in range(2)]
    cnt = [0]
    nblk = TT // 128
    xcnt = [0]
    ecnt = [0]

    def do_tr(t):
        tiles = []
        for b in range(nblk):
            i = xcnt[0] % 2
            xcnt[0] += 1
            r0 = t * TT + b * 128
            P.dma("sync", pfx + "xtr%d" % i, xtr[i][:, :], xin[r0:r0 + 128, :], reads=[xin_key], writes=[pfx + "xtr%d" % i])
            tiles.append((xtr[i], pfx + "xtr%d" % i))
        emit_hT(P, pfx, tiles, nblk, idf, modc, hT, 0, ps_tr, cnt)

    def do_w1(t):
        for fc in range(32):
            pa = ps_a[fc % 2]
            pak = pfx + "pa%d" % (fc % 2)
            for kc in range(8):
                P.add("tensor", (lambda e, pa=pa, kc=kc, fc=fc: e.matmul(
                    pa[:, 0:TT], w1b[:, kc, fc * 128:(fc + 1) * 128], hT[:, kc, :], start=(kc == 0), stop=(kc == 7))),
                    reads=[pfx + "w1b", pfx + "hT"], writes=[pak])
            rt = rts[fc % 3]
            rk = pfx + "rt%d" % (fc % 3)
            P.add("scalar", (lambda e, pa=pa, rt=rt: e.activation(out=rt[:, :], in_=pa[:, 0:TT], func=AF.Relu)),
                  reads=[pak], writes=[rk])
            P.add("gpsimd", (lambda e, rt=rt, fc=fc: e.tensor_tensor(out=aT[:, fc, :], in0=rt[:, :], in1=rt[:, :], op=ALU.mult)),
                  reads=[rk], writes=[pfx + "aT"])

    def do_w2(t):
        for b in range(nblk):
            i = ecnt[0]
            ecnt[0] += 1
            py = ps_y[i % 2]
            pyk = pfx + "py%d" % (i % 2)
            for half in range(2):
                for fc in range(32):
                    P.add("tensor", (lambda e, py=py, half=half, fc=fc, b=b: e.matmul(
                        py[:, half, :], aT[:, fc, b * 128:(b + 1) * 128], w2b[:, fc, half * 512:(half + 1) * 512],
                        start=(fc == 0), stop=(fc == 31))),
                        reads=[pfx + "aT", pfx + "w2b"], writes=[pyk])
            r0 = t * TT + b * 128
            xe = xep[i % 2]
            xek = pfx + "xep%d" % (i % 2)
            P.dma("sync", xek, xe[:, :], xin[r0:r0 + 128, :], reads=[xin_key], writes=[xek])
            ot = ots[i % 3]
            ok = pfx + "ot%d" % (i % 3)
            emit_resid_ln(P, pfx, py, pyk, xe[:, :], xek, gbc, lng, lnb, ot, ok, smalls[i % 3], pfx + "sm%d" % (i % 3))
            P.dma("gpsimd", ok + "o", xout[r0:r0 + 128, :], ot[:, :], reads=[ok], writes=[xout_key])

    NT = DBG.get("NT", NT)
    do_tr(0)
    for t in range(NT):
        do_w1(t)
        if t + 1 < NT:
            do_tr(t + 1)
        do_w2(t)


def phase_outln(P, A, l, w_ap, oT, oT_key, xin, xin_key, xout, xout_key, pfx):
    gbc = load_bcast(P, pfx + "gbc", A["modrow"][l, 0, :], 1024)
    P.add("vector", lambda e: e.tensor_scalar(out=gbc[:, :], in0=gbc[:, :], scalar1=1.0, scalar2=None, op0=ALU.add),
          reads=[pfx + "gbc"], writes=[pfx + "gbc"])
    lng = P.sb(pfx + "lng", [128, 1024], F32)
    P.dma("sync", pfx + "lng", lng[:, :], A["ln_g"][l, 0, :].partition_broadcast(128), writes=[pfx + "lng"])
    lnb = P.sb(pfx + "lnb", [128, 1024], F32)
    P.dma("sync", pfx + "lnb", lnb[:, :], A["ln_b"][l, 0, :].partition_broadcast(128), writes=[pfx + "lnb"])
    wob = P.sb(pfx + "wob", [128, 8, D], BF16)
    P.dma("gpsimd", pfx + "wo", wob[:, :, :], w_ap.rearrange("(c p) f -> p c f", p=128), writes=[pfx + "wob"])
    ots = [P.sb(pfx + "ot%d" % i, [128, 1024], F32) for i in range(3)]
    xep = [P.sb(pfx + "xep%d" % i, [128, 1024], F32) for i in range(3)]
    oTt = [P.sb(pfx + "oTt%d" % i, [128, 8, 512], BF16) for i in range(2)]
    smalls = [(P.sb(pfx + "st%d" % i, [128, 2, 6], F32), P.sb(pfx + "mv%d" % i, [128, 2], F32),
               P.sb(pfx + "rs%d" % i, [128, 3], F32)) for i in range(3)]
    ps_y = [P.ps(pfx + "py%d" % i, [128, 2, 512], F32) for i in range(2)]
    oTv = oT.rearrange("(c p) t -> p c t", p=128)
    i = 0
    for t in range(DBG.get("OT", TL // 512)):
        ob = oTt[t % 2]
        obk = pfx + "oTt%d" % (t % 2)
        P.dma("sync", obk, ob[:, :, :], oTv[:, :, t * 512:(t + 1) * 512], reads=[oT_key], writes=[obk])
        for b in range(4):
            py = ps_y[i % 2]
            pyk = pfx + "py%d" % (i % 2)
            for half in range(2):
                for c in range(8):
                    P.add("tensor", (lambda e, py=py, half=half, c=c, b=b, ob=ob: e.matmul(
                        py[:, half, :], ob[:, c, b * 128:(b + 1) * 128], wob[:, c, half * 512:(half + 1) * 512],
                        start=(c == 0), stop=(c == 7))), reads=[obk, pfx + "wob"], writes=[pyk])
            r0 = t * 512 + b * 128
            xe = xep[i % 3]
            xek = pfx + "xep%d" % (i % 3)
            P.dma("sync", xek, xe[:, :], xin[r0:r0 + 128, :], reads=[xin_key], writes=[xek])
            ot = ots[i % 3]
            ok = pfx + "ot%d" % (i % 3)
            emit_resid_ln(P, pfx, py, pyk, xe[:, :], xek, gbc, lng, lnb, ot, ok, smalls[i % 3], pfx + "sm%d" % (i % 3))
            P.dma("gpsimd", ok + "o", xout[r0:r0 + 128, :], ot[:, :], reads=[ok], writes=[xout_key])
            i += 1


def phase_bproj(P, A, l, xin, xin_key, pfx):
    j = l // 2
    idf, idb = make_ident(P, pfx, A["ident"])
    modc = load_mod_cols(P, pfx, A["modT"], l, 0)
    wb = P.sb(pfx + "wb", [128, 8, 3072], BF16)
    wv = A["b_w_in"][j].rearrange("(c p) f -> p c f", p=128)
    for c in range(8):
        P.dma("gpsimd", pfx + "w", wb[:, c, :], wv[:, c, :], writes=[pfx + "wb"])
    xtr = [P.sb(pfx + "xtr%d" % i, [128, 1024], F32) for i in range(6)]
    hT = P.sb(pfx + "hT", [128, 8, 512], BF16)
    stg = [P.sb(pfx + "stg%d" % i, [128, 512], BF16) for i in range(4)]
    ps_tr = [(P.ps(pfx + "ptr%d" % i, [128, 512], F32), pfx + "ptr%d" % i) for i in range(2)]
    ps_o = [P.ps(pfx + "po%d" % i, [128, 512], F32) for i in range(4)]
    cnt = [0]
    xc = 0
    oc = 0
    for t in range(TL // 512):
        tiles = []
        for b in range(4):
            i = xc % 6
            xc += 1
            r0 = t * 512 + b * 128
            P.dma("sync", pfx + "xtr%d" % i, xtr[i][:, :], xin[r0:r0 + 128, :], reads=[xin_key], writes=[pfx + "xtr%d" % i])
            tiles.append((xtr[i], pfx + "xtr%d" % i))
        emit_hT(P, pfx, tiles, 4, idf, modc, hT, 0, ps_tr, cnt)
        for fo in range(16):
            po = ps_o[oc % 4]
            pok = pfx + "po%d" % (oc % 4)
            sg = stg[oc % 4]
            sgk = pfx + "stg%d" % (oc % 4)
            oc += 1
            for c in range(8):
                P.add("tensor", (lambda e, po=po, c=c, fo=fo: e.matmul(
                    po[:, :], wb[:, c, fo * 128:(fo + 1) * 128], hT[:, c, :], start=(c == 0), stop=(c == 7))),
                    reads=[pfx + "wb", pfx + "hT"], writes=[pok])
            sc = 0.125 if fo < 8 else 1.0
            P.add("scalar", (lambda e, po=po, sg=sg, sc=sc: e.activation(out=sg[:, :], in_=po[:, :], func=AF.Copy, scale=sc)),
                  reads=[pok], writes=[sgk])
            dst = A["qT"] if fo < 8 else A["kT"]
            dk = "qT" if fo < 8 else "kT"
            fr = (fo % 8) * 128
            P.dma("gpsimd", sgk + "o", dst[fr:fr + 128, t * 512:(t + 1) * 512], sg[:, :], reads=[sgk], writes=[dk])
        for b in range(4):
            for half in range(2):
                po = ps_o[oc % 4]
                pok = pfx + "po%d" % (oc % 4)
                sg = stg[oc % 4]
                sgk = pfx + "stg%d" % (oc % 4)
                oc += 1
                for c in range(8):
                    P.add("tensor", (lambda e, po=po, c=c, b=b, half=half: e.matmul(
                        po[:, :], hT[:, c, b * 128:(b + 1) * 128], wb[:, c, 2048 + half * 512:2048 + (half + 1) * 512],
                        start=(c == 0), stop=(c == 7))), reads=[pfx + "wb", pfx + "hT"], writes=[pok])
                P.add("vector", (lambda e, po=po, sg=sg: e.tensor_copy(out=sg[:, :], in_=po[:, :])), reads=[pok], writes=[sgk])
                r0 = t * 512 + b * 128
                P.dma("gpsimd", sgk + "o", A["v"][r0:r0 + 128, half * 512:(half + 1) * 512], sg[:, :], reads=[sgk], writes=["v"])


def phase_battn(P, A, l, pfx):
    j = l // 2
    import math
    lam_init = 0.8 - 0.6 * math.exp(-0.3 * DBG.get("true_l", l))
    idf, idb = make_ident(P, pfx, A["ident"])
    onesb = P.sb(pfx + "onesb", [128, 128], BF16)
    P.add("vector", lambda e: e.memset(onesb[:, :], 1.0), writes=[pfx + "onesb"])
    onesf = P.sb(pfx + "onesf", [128, 128], F32)
    P.add("vector", lambda e: e.memset(onesf[:, :], 1.0), writes=[pfx + "onesf"])
    b31 = P.sb(pfx + "b31", [128, 16], F32)
    P.dma("sync", pfx + "b31", b31[:, :], A["rel_bias"][31, :].partition_broadcast(128), writes=[pfx + "b31"])
    BT = P.sb(pfx + "BT", [128, 16, 256], F32)
    P.dma("sync", pfx + "BT", BT[:, :, :], A["biasT"].rearrange("c s q -> s c q"), writes=[pfx + "BT"])
    for c in range(16):
        P.add("vector", (lambda e, c=c: e.tensor_scalar(out=BT[:, c, :], in0=BT[:, c, :], scalar1=b31[:, c:c + 1],
                                                        scalar2=None, op0=ALU.subtract)),
              reads=[pfx + "BT", pfx + "b31"], writes=[pfx + "BT"])
    lamb = P.sb(pfx + "lamb", [128, 256], F32)
    P.dma("sync", pfx + "lamb", lamb[:, :], A["b_lambda"][j].rearrange("a d -> (a d)").partition_broadcast(128),
          writes=[pfx + "lamb"])
    lt = P.sb(pfx + "lt", [128, 128], F32)
    ls = P.sb(pfx + "ls", [128, 4], F32)
    P.add("vector", lambda e: e.tensor_tensor(out=lt[:, 0:64], in0=lamb[:, 0:64], in1=lamb[:, 64:128], op=ALU.mult),
          reads=[pfx + "lamb"], writes=[pfx + "lt"])
    P.add("vector", lambda e: e.tensor_tensor(out=lt[:, 64:128], in0=lamb[:, 128:192], in1=lamb[:, 192:256], op=ALU.mult),
          reads=[pfx + "lamb"], writes=[pfx + "lt"])
    P.add("vector", lambda e: e.reduce_sum(out=ls[:, 0:1], in_=lt[:, 0:64], axis=AX.X), reads=[pfx + "lt"], writes=[pfx + "ls"])
    P.add("vector", lambda e: e.reduce_sum(out=ls[:, 1:2], in_=lt[:, 64:128], axis=AX.X), reads=[pfx + "lt"], writes=[pfx + "ls"])
    P.add("scalar", lambda e: e.activation(out=ls[:, 2:4], in_=ls[:, 0:2], func=AF.Exp), reads=[pfx + "ls"], writes=[pfx + "ls2"])
    neglam = P.sb(pfx + "neglam", [128, 1], F32)
    P.add("vector", lambda e: e.tensor_scalar(out=neglam[:, :], in0=ls[:, 3:4], scalar1=float(lam_init), scalar2=ls[:, 2:3],
                                             op0=ALU.subtract, op1=ALU.subtract), reads=[pfx + "ls2"], writes=[pfx + "neglam"])
    sg = P.sb(pfx + "sg", [128, 1], F32)
    P.dma("sync", pfx + "sg", sg[:, :], A["b_subln"][j].rearrange("(p o) -> p o", o=1), writes=[pfx + "sg"])
    P.add("vector", lambda e: e.tensor_scalar(out=sg[:, :], in0=sg[:, :], scalar1=float(1.0 - lam_init), scalar2=None,
                                             op0=ALU.mult), reads=[pfx + "sg"], writes=[pfx + "sg"])
    kTh = [P.sb(pfx + "kTh%d" % i, [128, TL], BF16) for i in range(2)]
    qTh = [P.sb(pfx + "qTh%d" % i, [128, TL], BF16) for i in range(2)]
    Vh = [P.sb(pfx + "Vh%d" % i, [128, NLB, 128], BF16) for i in range(2)]
    PT = [P.sb(pfx + "PT%d" % i, [128, 512], BF16) for i in range(6)]
    r0t = P.sb(pfx + "r0t", [128, 512], F32)
    r1t = P.sb(pfx + "r1t", [128, 512], F32)
    o0t = P.sb(pfx + "o0t", [128, 512], F32)
    o1t = P.sb(pfx + "o1t", [128, 512], F32)
    sqt = P.sb(pfx + "sqt", [128, 512], F32)
    oTs = [P.sb(pfx + "oTs%d" % i, [128, 512], BF16) for i in range(2)]
    ps_s = [P.ps(pfx + "pss%d" % i, [128, 512], F32) for i in range(3)]
    acc_o = [P.ps(pfx + "acco%d" % i, [128, 512], F32) for i in range(2)]
    acc_s = [P.ps(pfx + "accs%d" % i, [128, 512], F32) for i in range(2)]
    ps_ms = P.ps(pfx + "psms", [128, 512], F32)
    st = {"sc": 0, "pc": 0}
    oc = 0
    vv = A["v"].rearrange("(blk p) f -> p blk f", p=128)
    for h in range(DBG.get("heads", 8)):
        kt = kTh[h % 2]; ktk = pfx + "kTh%d" % (h % 2)
        qt = qTh[h % 2]; qtk = pfx + "qTh%d" % (h % 2)
        vt = Vh[h % 2]; vtk = pfx + "Vh%d" % (h % 2)
        P.dma("sync", ktk, kt[:, :], A["kT"][h * 128:(h + 1) * 128, :], reads=["kT"], writes=[ktk])
        P.dma("sync", qtk, qt[:, :], A["qT"][h * 128:(h + 1) * 128, :], reads=["qT"], writes=[qtk])
        for g in range(4):
            P.dma("sync", vtk, vt[:, g * 16:(g + 1) * 16, :], vv[:, g * 16:(g + 1) * 16, h * 128:(h + 1) * 128], reads=["v"], writes=[vtk])
        for J in range(DBG.get("J", TL // 512)):
            ns = 4 * J + 4

            def qk_b(m, s_):
                col = 2 * h + m
                k = s_ - 4 * J
                c0 = max(0, k) * 128
                near = k >= -1
                ps = ps_s[st["sc"] % 3]; psk = pfx + "pss%d" % (st["sc"] % 3)
                st["sc"] += 1
                P.add("tensor", (lambda e, ps=ps, kt=kt, qt=qt, m=m, s_=s_, c0=c0, J=J, near=near: e.matmul(
                    ps[:, c0:512], kt[64 * m:64 * m + 64, s_ * 128:(s_ + 1) * 128],
                    qt[64 * m:64 * m + 64, J * 512 + c0:J * 512 + 512], start=True, stop=(not near))),
                    reads=[ktk, qtk], writes=[psk], pe_attach=(s_ > 0))
                if near:
                    if k == -1:
                        P.add("tensor", (lambda e, ps=ps, col=col: e.matmul(
                            ps[:, 0:128], idf[:, :], BT[:, col, 128:256], start=False, stop=True)),
                            reads=[pfx + "idf", pfx + "BT"], writes=[psk])
                    else:
                        w = min(256, 512 - c0)
                        P.add("tensor", (lambda e, ps=ps, col=col, c0=c0, w=w: e.matmul(
                            ps[:, c0:c0 + w], idf[:, :], BT[:, col, 0:w], start=False, stop=True)),
                            reads=[pfx + "idf", pfx + "BT"], writes=[psk])
                pt = PT[st["pc"] % 6]; ptk = pfx + "PT%d" % (st["pc"] % 6)
                st["pc"] += 1
                P.add("scalar", (lambda e, ps=ps, pt=pt, c0=c0, col=col: e.activation(
                    out=pt[:, c0:512], in_=ps[:, c0:512], func=AF.Exp, bias=b31[:, col:col + 1], scale=1.0)),
                    reads=[psk, pfx + "b31"], writes=[ptk])
                return pt, ptk, c0

            def pv_b(m, s_, pre):
                pt, ptk, c0 = pre
                ao = acc_o[m]; aok = pfx + "acco%d" % m
                as_ = acc_s[m]; ask = pfx + "accs%d" % m
                P.add("tensor", (lambda e, ao=ao, vt=vt, pt=pt, s_=s_, c0=c0, ns=ns: e.matmul(
                    ao[:, c0:512], vt[:, s_, :], pt[:, c0:512], start=(s_ == 0), stop=(s_ == ns - 1))),
                    reads=[vtk, ptk], writes=[aok], pe_attach=(s_ > 0))
                P.add("tensor", (lambda e, as_=as_, pt=pt, s_=s_, c0=c0, ns=ns: e.matmul(
                    as_[:, c0:512], onesb[:, :], pt[:, c0:512], start=(s_ == 0), stop=(s_ == ns - 1))),
                    reads=[pfx + "onesb", ptk], writes=[ask])

            pairs = [(m, s_) for m in range(2) for s_ in range(ns)]
            LA = 2
            queue = [qk_b(*pairs[i_]) for i_ in range(min(LA, len(pairs)))]
            for i_, (m, s_) in enumerate(pairs):
                if i_ + LA < len(pairs):
                    queue.append(qk_b(*pairs[i_ + LA]))
                pv_b(m, s_, queue.pop(0))
            P.add("vector", lambda e: e.reciprocal(out=r0t[:, :], in_=acc_s[0][:, :]), reads=[pfx + "accs0"], writes=[pfx + "r0t"])
            P.add("vector", lambda e: e.reciprocal(out=r1t[:, :], in_=acc_s[1][:, :]), reads=[pfx + "accs1"], writes=[pfx + "r1t"])
            P.add("vector", lambda e: e.tensor_tensor(out=o0t[:, :], in0=acc_o[0][:, :], in1=r0t[:, :], op=ALU.mult),
                  reads=[pfx + "acco0", pfx + "r0t"], writes=[pfx + "o0t"])
            P.add("vector", lambda e: e.tensor_tensor(out=o1t[:, :], in0=acc_o[1][:, :], in1=r1t[:, :], op=ALU.mult),
                  reads=[pfx + "acco1", pfx + "r1t"], writes=[pfx + "o1t"])
            P.add("vector", lambda e: e.scalar_tensor_tensor(out=o0t[:, :], in0=o1t[:, :], scalar=neglam[:, 0:1], in1=o0t[:, :],
                                                            op0=ALU.mult, op1=ALU.add),
                  reads=[pfx + "o1t", pfx + "neglam"], writes=[pfx + "o0t"])
            P.add("gpsimd", lambda e: e.tensor_tensor(out=sqt[:, :], in0=o0t[:, :], in1=o0t[:, :], op=ALU.mult),
                  reads=[pfx + "o0t"], writes=[pfx + "sqt"])
            P.add("tensor", lambda e: e.matmul(ps_ms[:, :], onesf[:, :], sqt[:, :], start=True, stop=True),
                  reads=[pfx + "onesf", pfx + "sqt"], writes=[pfx + "psms"])
            P.add("scalar", lambda e: e.activation(out=r0t[:, :], in_=ps_ms[:, :], func=AF.Sqrt, bias=P.epsc[:, 0:1], scale=1.0 / 128.0),
                  reads=[pfx + "psms", "epsc"], writes=[pfx + "r0t"])
            P.add("vector", lambda e: e.reciprocal(out=r1t[:, :], in_=r0t[:, :]), reads=[pfx + "r0t"], writes=[pfx + "r1t"])
            P.add("vector", lambda e: e.tensor_tensor(out=o0t[:, :], in0=o0t[:, :], in1=r1t[:, :], op=ALU.mult),
                  reads=[pfx + "r1t"], writes=[pfx + "o0t"])
            os_ = oTs[oc % 2]; osk = pfx + "oTs%d" % (oc % 2)
            oc += 1
            P.add("vector", (lambda e, os_=os_: e.tensor_scalar(out=os_[:, :], in0=o0t[:, :], scalar1=sg[:, 0:1], scalar2=None,
                                                               op0=ALU.mult)), reads=[pfx + "o0t", pfx + "sg"], writes=[osk])
            P.dma("gpsimd", osk + "o", A["oT"][h * 128:(h + 1) * 128, J * 512:(J + 1) * 512], os_[:, :], reads=[osk], writes=["oT"])


def phase_aproj(P, A, l, xin, xin_key, pfx):
    j = l // 2
    idf, idb = make_ident(P, pfx, A["ident"])
    modc = load_mod_cols(P, pfx, A["modT"], l, 0)
    wb = P.sb(pfx + "wb", [128, 8, A_IN], BF16)
    wv = A["a_w_in"][j].rearrange("(c p) f -> p c f", p=128)
    for c in range(8):
        P.dma("gpsimd", pfx + "w", wb[:, c, :], wv[:, c, :], writes=[pfx + "wb"])
    wuk = P.sb(pfx + "wuk", [128, 8, 256], BF16)
    P.dma("gpsimd", pfx + "wuk", wuk[:, :, :], A["a_w_uk"][j].rearrange("(hp two) d r -> (two d) hp r", two=2),
          writes=[pfx + "wuk"])
    wuvb = P.sb(pfx + "wuvb", [128, 2, 16, 64], BF16)
    for rc in range(2):
        P.dma("gpsimd", pfx + "wuvb", wuvb[:, rc, :, :], A["a_w_uv"][j][:, rc * 128:(rc + 1) * 128, :].rearrange("h p e -> p h e"),
              writes=[pfx + "wuvb"])
    kvn = P.sb(pfx + "kvn", [128, 256], F32)
    P.dma("sync", pfx + "kvn", kvn[:, :], A["a_kv_norm"][j].partition_broadcast(128), writes=[pfx + "kvn"])
    xtr = [P.sb(pfx + "xtr%d" % i, [128, 1024], F32) for i in range(6)]
    hT = P.sb(pfx + "hT", [128, 8, 512], BF16)
    stg = [P.sb(pfx + "stg%d" % i, [128, 512], BF16) for i in range(4)]
    cst = [P.sb(pfx + "cst%d" % i, [128, 256], BF16) for i in range(2)]
    iwst = [P.sb(pfx + "iwst%d" % i, [128, 8], F32) for i in range(2)]
    cTs = [P.sb(pfx + "cTs%d" % i, [128, 2, 512], BF16) for i in range(2)]
    vst = [P.sb(pfx + "vst%d" % i, [128, 16, 65], BF16) for i in range(2)]
    for i in range(2):
        P.add("vector", (lambda e, i=i: e.memset(vst[i][:, :, 64:65], 1.0)), writes=[pfx + "vst%d" % i])
    wukT = P.sb(pfx + "wukT", [128, 2, 1024], BF16)
    junk = P.sb(pfx + "junk", [128, 256], F32)
    sm = [P.sb(pfx + "sm%d" % i, [128, 3], F32) for i in range(2)]
    ps_tr = [(P.ps(pfx + "ptr%d" % i, [128, 512], F32), pfx + "ptr%d" % i) for i in range(2)]
    ps_o = [P.ps(pfx + "po%d" % i, [128, 512], F32) for i in range(3)]
    ps_c = [P.ps(pfx + "pc%d" % i, [128, 264], F32) for i in range(2)]
    ps_t = P.ps(pfx + "pt", [128, 2, 128], BF16)
    cnt = [0]
    st8 = {"xc": 0, "oc": 0, "sc": 0, "bc": 0, "vc": 0}
    for hp in range(8):
        for rc in range(2):
            P.add("tensor", (lambda e, hp=hp, rc=rc: e.transpose(out=ps_t[:, rc, :], in_=wuk[:, hp, rc * 128:(rc + 1) * 128],
                                                                identity=idb[:, :])), reads=[pfx + "wuk", pfx + "idb"], writes=[pfx + "pt"])
        P.add("scalar", (lambda e, hp=hp: e.activation(out=wukT[:, :, hp * 128:(hp + 1) * 128], in_=ps_t[:, :, :], func=AF.Copy)),
              reads=[pfx + "pt"], writes=[pfx + "wukT"])

    def evac(po_ap, pok, width, dst_ap, dkey, parts=128, scale=None):
        i = st8["sc"] % 4
        st8["sc"] += 1
        sg = stg[i]
        sgk = pfx + "stg%d" % i
        if scale is not None or st8["sc"] % 2 == 0:
            P.add("scalar", (lambda e: e.activation(out=sg[0:parts, 0:width], in_=po_ap, func=AF.Copy,
                                                    scale=(1.0 if scale is None else scale))), reads=[pok], writes=[sgk])
        else:
            P.add("vector", (lambda e: e.tensor_copy(out=sg[0:parts, 0:width], in_=po_ap)), reads=[pok], writes=[sgk])
        P.dma("gpsimd", sgk + "o", dst_ap, sg[0:parts, 0:width], reads=[sgk], writes=[dkey])

    def next_po():
        po = ps_o[st8["oc"] % 3]
        pok = pfx + "po%d" % (st8["oc"] % 3)
        st8["oc"] += 1
        return po, pok

    for t in range(DBG.get("T", TL // 512)):
        tiles = []
        for b in range(4):
            i = st8["xc"] % 6
            st8["xc"] += 1
            r0 = t * 512 + b * 128
            P.dma("sync", pfx + "xtr%d" % i, xtr[i][:, :], xin[r0:r0 + 128, :], reads=[xin_key], writes=[pfx + "xtr%d" % i])
            tiles.append((xtr[i], pfx + "xtr%d" % i))
        emit_hT(P, pfx, tiles, 4, idf, modc, hT, 0, ps_tr, cnt)
        tsl = slice(t * 512, (t + 1) * 512)

        def proj_fm(col0, ncol):
            po, pok = next_po()
            for c in range(8):
                P.add("tensor", (lambda e, po=po, c=c: e.matmul(po[0:ncol, :], wb[:, c, col0:col0 + ncol], hT[:, c, :],
                                                               start=(c == 0), stop=(c == 7))),
                      reads=[pfx + "wb", pfx + "hT"], writes=[pok])
            return po, pok

        for fo in range(8):
            po, pok = proj_fm(fo * 128, 128)
            evac(po[:, :], pok, 512, A["qT"][fo * 128:(fo + 1) * 128, tsl], "qT", scale=0.125)
        for fo in range(4):
            po, pok = proj_fm(1280 + fo * 128, 128)
            evac(po[:, :], pok, 512, A["iqT"][fo * 128:(fo + 1) * 128, tsl], "iqT")
        po, pok = proj_fm(1792, 64)
        evac(po[0:64, :], pok, 512, A["ikT"][0:64, tsl], "ikT", parts=64)
        ct = cTs[t % 2]
        ctk = pfx + "cTs%d" % (t % 2)
        for b in range(4):
            bi = st8["bc"]
            st8["bc"] += 1
            pc = ps_c[bi % 2]
            pck = pfx + "pc%d" % (bi % 2)
            for c in range(8):
                P.add("tensor", (lambda e, pc=pc, c=c, b=b: e.matmul(pc[:, 0:256], hT[:, c, b * 128:(b + 1) * 128],
                                                                    wb[:, c, 1024:1280], start=(c == 0), stop=(c == 7))),
                      reads=[pfx + "wb", pfx + "hT"], writes=[pck])
            for c in range(8):
                P.add("tensor", (lambda e, pc=pc, c=c, b=b: e.matmul(pc[:, 256:264], hT[:, c, b * 128:(b + 1) * 128],
                                                                    wb[:, c, 1856:1864], start=(c == 0), stop=(c == 7))),
                      reads=[pfx + "wb", pfx + "hT"], writes=[pck])
            s3 = sm[bi % 2]
            s3k = pfx + "sm%d" % (bi % 2)
            P.add("scalar", (lambda e, pc=pc, s3=s3: e.activation(out=junk[:, :], in_=pc[:, 0:256], func=AF.Square,
                                                                 accum_out=s3[:, 0:1])), reads=[pck], writes=[pfx + "junk", s3k],
                  noattach=True)
            P.add("scalar", (lambda e, s3=s3: e.activation(out=s3[:, 1:2], in_=s3[:, 0:1], func=AF.Sqrt, bias=P.epsc[:, 0:1],
                                                          scale=1.0 / 256.0)), reads=[s3k, "epsc"], writes=[s3k])
            P.add("vector", (lambda e, s3=s3: e.reciprocal(out=s3[:, 2:3], in_=s3[:, 1:2])), reads=[s3k], writes=[s3k])
            cs = cst[bi % 2]
            csk = pfx + "cst%d" % (bi % 2)
            P.add("vector", (lambda e, pc=pc, s3=s3, cs=cs: e.scalar_tensor_tensor(
                out=cs[:, :], in0=pc[:, 0:256], scalar=s3[:, 2:3], in1=kvn[:, :], op0=ALU.mult, op1=ALU.mult)),
                reads=[pck, s3k, pfx + "kvn"], writes=[csk])
            r0 = t * 512 + b * 128
            iws = iwst[bi % 2]
            iwk = pfx + "iwst%d" % (bi % 2)
            P.add("vector", (lambda e, pc=pc, iws=iws: e.tensor_scalar(out=iws[:, :], in0=pc[:, 256:264],
                                                                      scalar1=float(8 ** -0.5 * 64 ** -0.5), scalar2=None,
                                                                      op0=ALU.mult)), reads=[pck], writes=[iwk])
            P.dma("gpsimd", iwk + "o", A["iw"][r0:r0 + 128, :], iws[:, :], reads=[iwk], writes=["iw"])
            for rc in range(2):
                P.add("tensor", (lambda e, cs=cs, rc=rc: e.transpose(out=ps_t[:, rc, :], in_=cs[:, rc * 128:(rc + 1) * 128],
                                                                    identity=idb[:, :])), reads=[csk, pfx + "idb"], writes=[pfx + "pt"])
            P.add("scalar", (lambda e, ct=ct, b=b: e.activation(out=ct[:, :, b * 128:(b + 1) * 128], in_=ps_t[:, :, :], func=AF.Copy)),
                  reads=[pfx + "pt"], writes=[ctk])
        for hp in range(8):
            po, pok = next_po()
            for rc in range(2):
                P.add("tensor", (lambda e, po=po, hp=hp, rc=rc, ct=ct: e.matmul(po[:, :], wukT[:, rc, hp * 128:(hp + 1) * 128],
                                                                               ct[:, rc, :], start=(rc == 0), stop=(rc == 1))),
                      reads=[pfx + "wukT", ctk], writes=[pok])
            evac(po[:, :], pok, 512, A["kT"][hp * 128:(hp + 1) * 128, tsl], "kT")
        for b in range(4):
            vs = vst[st8["vc"] % 2]
            vsk = pfx + "vst%d" % (st8["vc"] % 2)
            st8["vc"] += 1
            for half in range(2):
                po, pok = next_po()
                for rc in range(2):
                    P.add("tensor", (lambda e, po=po, b=b, rc=rc, half=half, ct=ct: e.matmul(
                        po[:, :], ct[:, rc, b * 128:(b + 1) * 128], wuvb[:, rc, half * 8:(half + 1) * 8, :],
                        start=(rc == 0), stop=(rc == 1))), reads=[pfx + "wuvb", ctk], writes=[pok])
                if half == 0:
                    P.add("scalar", (lambda e, po=po, vs=vs, half=half: e.activation(
                        out=vs[:, half * 8:(half + 1) * 8, 0:64], in_=po[:, :].rearrange("p (h e) -> p h e", h=8), func=AF.Copy)),
                        reads=[pok], writes=[vsk])
                else:
                    P.add("vector", (lambda e, po=po, vs=vs, half=half: e.tensor_copy(
                        out=vs[:, half * 8:(half + 1) * 8, 0:64], in_=po[:, :].rearrange("p (h e) -> p h e", h=8))),
                        reads=[pok], writes=[vsk])
            r0 = t * 512 + b * 128
            P.dma("gpsimd", vsk + "o", A["vaug"][r0:r0 + 128, :], vs[:, :, :].rearrange("p h e -> p (h e)"), reads=[vsk], writes=["vaug"])


N_IT = 20
TOPK = 256


def phase_aattn(P, A, l, pfx):
    j = l // 2
    idf, idb = make_ident(P, pfx, A["ident"])
    onesb = P.sb(pfx + "onesb", [128, 128], BF16)
    P.add("vector", lambda e: e.memset(onesb[:, :], 1.0), writes=[pfx + "onesb"])
    onesf = P.sb(pfx + "onesf", [128, 128], F32)
    P.add("vector", lambda e: e.memset(onesf[:, :], 1.0), writes=[pfx + "onesf"])
    b31 = P.sb(pfx + "b31", [128, 16], F32)
    P.dma("sync", pfx + "b31", b31[:, :], A["rel_bias"][31, :].partition_broadcast(128), writes=[pfx + "b31"])
    BTf = P.sb(pfx + "BTf", [128, 256], F32)
    BT = P.sb(pfx + "BT", [128, 16, 256], BF16)
    for c in range(16):
        P.dma("sync", pfx + "BTf", BTf[:, :], A["biasT"][c], writes=[pfx + "BTf"])
        P.add("vector", (lambda e, c=c: e.tensor_scalar(out=BT[:, c, :], in0=BTf[:, :], scalar1=b31[:, c:c + 1],
                                                        scalar2=None, op0=ALU.subtract)),
              reads=[pfx + "BTf", pfx + "b31"], writes=[pfx + "BT"])
    cmask = P.sb(pfx + "cmask", [128, 128], F32)
    P.dma("sync", pfx + "cmask", cmask[:, :], A["cmask"], writes=[pfx + "cmask"])
    pw = P.sb(pfx + "pw", [128, N_IT + 1], F32)
    for i in range(N_IT + 1):
        P.add("vector", (lambda e, i=i: e.memset(pw[:, i:i + 1], float(2.0 ** -i))), writes=[pfx + "pw"])
    ik2 = P.sb(pfx + "ik2", [128, TL // 2], BF16)
    P.dma("sync", pfx + "ik2", ik2[0:64, :], A["ikT"][:, 0:TL // 2], reads=["ikT"], writes=[pfx + "ik2"])
    P.dma("sync", pfx + "ik2", ik2[64:128, :], A["ikT"][:, TL // 2:TL], reads=["ikT"], writes=[pfx + "ik2"])
    kTp = [P.sb(pfx + "kTp%d" % i, [128, TL], BF16) for i in range(2)]
    vp = [P.sb(pfx + "vp%d" % i, [128, NLB, 2, 65], BF16) for i in range(2)]
    qTp = [P.sb(pfx + "qTp%d" % i, [128, 256], BF16) for i in range(2)]
    otile = [P.sb(pfx + "otile%d" % i, [128, 1024], BF16) for i in range(2)]
    oTs = P.sb(pfx + "oTs", [128, 8, 256], BF16)
    rcp = P.sb(pfx + "rcp", [128, 4], F32)
    for i in range(2):
        P.add("gpsimd", (lambda e, i=i: e.memset(otile[i][:, :], 0.0)), writes=[pfx + "otile%d" % i])
    scores = P.sb(pfx + "scores", [128, TL], F32)
    junk = P.sb(pfx + "junk", [128, TL], U8)
    NM = P.sb(pfx + "NM", [128, NLB, 256], BF16)
    iqt = [P.sb(pfx + "iqt%d" % i, [128, 8, 128], BF16) for i in range(2)]
    iwt = [P.sb(pfx + "iwt%d" % i, [128, 8], F32) for i in range(2)]
    rts = [P.sb(pfx + "rt%d" % i, [128, 512], F32) for i in range(3)]
    PT = [P.sb(pfx + "PT%d" % i, [128, 256], BF16) for i in range(6)]
    bs = P.sb(pfx + "bs", [128, 8 + N_IT + 1], F32)
    nd = P.sb(pfx + "nd", [128, 128], F32)
    ps_i = [P.ps(pfx + "pi%d" % i, [128, 512], F32) for i in range(2)]
    ps_m = ps_i
    ps_s = [P.ps(pfx + "pss%d" % i, [128, 256], F32) for i in range(3)]
    acc_o = [P.ps(pfx + "acco%d" % i, [128, 65], F32) for i in range(2)]
    ps_tt = P.ps(pfx + "ptt", [128, 2, 128], BF16)
    iqv = A["iqT"].rearrange("(h d) t -> d h t", d=64)
    vav = A["vaug"].rearrange("(blk p) (h e) -> p blk h e", p=128, e=65)
    st8 = {"ic": 0, "rc": 0, "mc": 0, "sc": 0, "pc": 0, "oc": 0}
    skey = pfx + "scores"
    bkey = pfx + "bs"

    def idx(qb):
        nk = (qb + 1) * 128
        it = iqt[qb % 2]; itk = pfx + "iqt%d" % (qb % 2)
        P.dma("sync", itk, it[0:64, :, :], iqv[:, :, qb * 128:(qb + 1) * 128], reads=["iqT"], writes=[itk])
        P.dma("sync", itk, it[64:128, :, :], iqv[:, :, qb * 128:(qb + 1) * 128], reads=["iqT"], writes=[itk])
        wt = iwt[qb % 2]; wtk = pfx + "iwt%d" % (qb % 2)
        P.dma("sync", wtk, wt[:, :], A["iw"][qb * 128:(qb + 1) * 128, :], reads=["iw"], writes=[wtk])
        nst = (nk + 511) // 512
        for st in range(nst):
            wd = min(512, nk - st * 512)
            half = (st * 512) // (TL // 2)
            kc0 = st * 512 - half * (TL // 2)
            for h in range(8):
                pi = ps_i[st8["ic"] % 2]; pik = pfx + "pi%d" % (st8["ic"] % 2)
                st8["ic"] += 1
                P.add("tensor", (lambda e, pi=pi, it=it, h=h, half=half, kc0=kc0, wd=wd: e.matmul(
                    pi[:, 0:wd], it[64 * half:64 * half + 64, h, :], ik2[64 * half:64 * half + 64, kc0:kc0 + wd],
                    start=True, stop=True)), reads=[itk, pfx + "ik2"], writes=[pik])
                rt = rts[st8["rc"] % 3]; rk = pfx + "rt%d" % (st8["rc"] % 3)
                st8["rc"] += 1
                P.add("scalar", (lambda e, pi=pi, rt=rt, wd=wd: e.activation(out=rt[:, 0:wd], in_=pi[:, 0:wd], func=AF.Relu)),
                      reads=[pik], writes=[rk])
                sl = slice(st * 512, st * 512 + wd)
                if h == 0:
                    P.add("vector", (lambda e, rt=rt, wt=wt, sl=sl, wd=wd: e.tensor_scalar(
                        out=scores[:, sl], in0=rt[:, 0:wd], scalar1=wt[:, 0:1], scalar2=None, op0=ALU.mult)),
                        reads=[rk, wtk], writes=[skey])
                else:
                    P.add("vector", (lambda e, rt=rt, wt=wt, sl=sl, wd=wd, h=h: e.scalar_tensor_tensor(
                        out=scores[:, sl], in0=rt[:, 0:wd], scalar=wt[:, h:h + 1], in1=scores[:, sl],
                        op0=ALU.mult, op1=ALU.add)), reads=[rk, wtk, skey], writes=[skey])
        V = lambda fn, rd, wr, na=False: P.add("vector", fn, reads=rd, writes=wr, noattach=na)
        V(lambda e: e.tensor_reduce(out=bs[:, 0:1], in_=scores[:, 0:nk], axis=AX.X, op=ALU.max, apply_absolute_value=True),
          [skey], [bkey])
        V(lambda e: e.tensor_scalar(out=bs[:, 1:2], in0=bs[:, 0:1], scalar1=1.0, scalar2=None, op0=ALU.add), [bkey], [bkey])
        V(lambda e: e.tensor_scalar(out=bs[:, 8:8 + N_IT + 1], in0=pw[:, :], scalar1=bs[:, 1:2], scalar2=None, op0=ALU.mult),
          [bkey, pfx + "pw"], [bkey])
        V(lambda e: e.tensor_tensor(out=scores[:, qb * 128:(qb + 1) * 128], in0=scores[:, qb * 128:(qb + 1) * 128],
                                    in1=cmask[:, :], op=ALU.add), [skey, pfx + "cmask"], [skey])
        V(lambda e: e.tensor_scalar(out=bs[:, 2:3], in0=bs[:, 1:2], scalar1=-1.0, scalar2=None, op0=ALU.mult), [bkey], [bkey])
        V(lambda e: e.tensor_tensor(out=bs[:, 3:4], in0=bs[:, 2:3], in1=bs[:, 8:9], op=ALU.add), [bkey], [bkey])
        def one_iter(i):
            V(lambda e: e.tensor_scalar(out=junk[:, 0:nk], in0=scores[:, 0:nk], scalar1=bs[:, 3:4], scalar2=None,
                                        op0=ALU.is_ge, op1=ALU.add, accum_out=bs[:, 4:5]), [skey, bkey], [pfx + "junk", bkey], True)
            V((lambda e, i=i: e.tensor_scalar(out=bs[:, 5:6], in0=bs[:, 4:5], scalar1=float(TOPK), scalar2=bs[:, 8 + i:9 + i],
                                              op0=ALU.is_ge, op1=ALU.mult)), [bkey], [bkey])
            V(lambda e: e.tensor_tensor(out=bs[:, 2:3], in0=bs[:, 2:3], in1=bs[:, 5:6], op=ALU.add), [bkey], [bkey])
            V((lambda e, i=i: e.tensor_tensor(out=bs[:, 3:4], in0=bs[:, 2:3], in1=bs[:, 9 + i:10 + i], op=ALU.add)), [bkey], [bkey])
            if i == N_IT - 1:
                V(lambda e: e.tensor_scalar(out=nd[:, :], in0=idf[:, :], scalar1=bs[:, 2:3], scalar2=-1.0, op0=ALU.mult,
                                            op1=ALU.mult), [bkey, pfx + "idf"], [pfx + "nd"])
        return [(lambda i=i: one_iter(i)) for i in range(N_IT)]

    def nm_gen(qb):
        jj = qb % 2
        for sb in range(qb + 1):
            mi = st8["mc"] % 2
            st8["mc"] += 1
            pmk = pfx + "pi%d" % mi
            P.add("tensor", (lambda e, mi=mi, sb=sb: e.matmul(ps_m[mi][:, 0:128], scores[:, sb * 128:(sb + 1) * 128], idf[:, :],
                                                             start=True, stop=False)), reads=[skey, pfx + "idf"], writes=[pmk])
            P.add("tensor", (lambda e, mi=mi: e.matmul(ps_m[mi][:, 0:128], onesf[:, :], nd[:, :], start=False, stop=True)),
                  reads=[pfx + "onesf", pfx + "nd"], writes=[pmk])
            P.add("vector", (lambda e, mi=mi, sb=sb, jj=jj: e.tensor_scalar(
                out=NM[:, sb, jj * 128:(jj + 1) * 128], in0=ps_m[mi][:, 0:128], scalar1=0.0, scalar2=float(NEGM),
                op0=ALU.is_lt, op1=ALU.mult)), reads=[pmk], writes=[pfx + "NM"])

    def attn(J, steps=()):
        steps = list(steps)
        ns = 2 * J + 2
        nk = ns * 128
        nh = DBG.get("aheads", 16)

        def load_pair(hp):
            kt = kTp[hp % 2]; ktk = pfx + "kTp%d" % (hp % 2)
            vt = vp[hp % 2]; vtk = pfx + "vp%d" % (hp % 2)
            qt = qTp[hp % 2]; qtk = pfx + "qTp%d" % (hp % 2)
            P.dma("sync", ktk, kt[:, 0:nk], A["kT"][hp * 128:(hp + 1) * 128, 0:nk], reads=["kT"], writes=[ktk])
            P.dma("sync", vtk, vt[:, 0:ns, :, :], vav[:, 0:ns, 2 * hp:2 * hp + 2, :], reads=["vaug"], writes=[vtk])
            P.dma("sync", qtk, qt[:, :], A["qT"][hp * 128:(hp + 1) * 128, J * 256:(J + 1) * 256], reads=["qT"], writes=[qtk])

        def qk(h, s):
            hp, hh = h // 2, h % 2
            kt = kTp[hp % 2]; ktk = pfx + "kTp%d" % (hp % 2)
            qt = qTp[hp % 2]; qtk = pfx + "qTp%d" % (hp % 2)
            k = s - 2 * J
            c0 = max(0, k) * 128
            near = k >= -1
            si = st8["sc"] % 3
            st8["sc"] += 1
            psk = pfx + "pss%d" % si
            P.add("tensor", (lambda e, si=si, s=s, hh=hh, c0=c0, kt=kt, qt=qt: e.matmul(
                ps_s[si][:, c0:256], kt[64 * hh:64 * hh + 64, s * 128:(s + 1) * 128], qt[64 * hh:64 * hh + 64, c0:256],
                start=True, stop=False)), reads=[ktk, qtk], writes=[psk], pe_attach=(s > 0))
            P.add("tensor", (lambda e, si=si, s=s, c0=c0, near=near: e.matmul(
                ps_s[si][:, c0:256], idb[:, :], NM[:, s, c0:256], start=False, stop=(not near))),
                reads=[pfx + "idb", pfx + "NM"], writes=[psk])
            if near:
                if k == -1:
                    P.add("tensor", (lambda e, si=si, h=h: e.matmul(ps_s[si][:, 0:128], idb[:, :], BT[:, h, 128:256],
                                                                   start=False, stop=True)),
                          reads=[pfx + "idb", pfx + "BT"], writes=[psk])
                else:
                    w = 256 - c0
                    P.add("tensor", (lambda e, si=si, h=h, c0=c0, w=w: e.matmul(ps_s[si][:, c0:c0 + w], idb[:, :], BT[:, h, 0:w],
                                                                               start=False, stop=True)),
                          reads=[pfx + "idb", pfx + "BT"], writes=[psk])
            pt = PT[st8["pc"] % 6]; ptk = pfx + "PT%d" % (st8["pc"] % 6)
            st8["pc"] += 1
            P.add("scalar", (lambda e, si=si, pt=pt, c0=c0, h=h: e.activation(
                out=pt[:, c0:256], in_=ps_s[si][:, c0:256], func=AF.Exp, bias=b31[:, h:h + 1], scale=1.0)),
                reads=[psk, pfx + "b31"], writes=[ptk])
            return pt, ptk, c0

        def pv(h, s, pre):
            pt, ptk, c0 = pre
            hp, hh = h // 2, h % 2
            vt = vp[hp % 2]; vtk = pfx + "vp%d" % (hp % 2)
            for jj in range(2):
                if jj * 128 < c0:
                    continue
                last = (s == ns - 1) if jj == 1 else (s == ns - 2)
                P.add("tensor", (lambda e, pt=pt, jj=jj, s=s, hh=hh, vt=vt, last=last: e.matmul(
                    acc_o[jj][:, :], pt[:, jj * 128:(jj + 1) * 128], vt[:, s, hh, :],
                    start=(s == 0), stop=last)), reads=[vtk, ptk], writes=[pfx + "acco%d" % jj])

        def finish_head(h):
            for jj in range(2):
                P.add("vector", (lambda e, jj=jj: e.reciprocal(out=rcp[:, jj:jj + 1], in_=acc_o[jj][:, 64:65])),
                      reads=[pfx + "acco%d" % jj], writes=[pfx + "rcp%d" % jj])
                P.add("vector", (lambda e, jj=jj, h=h: e.tensor_scalar(out=otile[jj][:, h * 64:(h + 1) * 64], in0=acc_o[jj][:, 0:64],
                                                                      scalar1=rcp[:, jj:jj + 1], scalar2=None, op0=ALU.mult)),
                      reads=[pfx + "acco%d" % jj, pfx + "rcp%d" % jj], writes=[pfx + "otile%d" % jj])
            nstep = 2 if h < 4 else 1
            if h == nh - 1:
                nstep = len(steps)
            for _ in range(min(nstep, len(steps))):
                steps.pop(0)()

        load_pair(0)
        pairs = [(h, s) for h in range(nh) for s in range(ns)]
        LA = 2
        queue = [qk(*pairs[i]) for i in range(min(LA, len(pairs)))]
        for i, (h, s) in enumerate(pairs):
            if s == 0 and h % 2 == 0 and h + 2 < nh:
                load_pair(h // 2 + 1)
            if i + LA < len(pairs):
                queue.append(qk(*pairs[i + LA]))
            pv(h, s, queue.pop(0))
            if s == ns - 1:
                finish_head(h)
        for jj in range(2):
            for c in range(8):
                P.add("tensor", (lambda e, jj=jj, c=c: e.transpose(out=ps_tt[:, c % 2, :], in_=otile[jj][:, c * 128:(c + 1) * 128],
                                                                  identity=idb[:, :])),
                      reads=[pfx + "otile%d" % jj, pfx + "idb"], writes=[pfx + "ptt"])
                if c % 2 == 1:
                    P.add("scalar", (lambda e, jj=jj, c=c: e.activation(out=oTs[:, c - 1:c + 1, jj * 128:(jj + 1) * 128],
                                                                       in_=ps_tt[:, :, :], func=AF.Copy)),
                          reads=[pfx + "ptt"], writes=[pfx + "oTs"])
        P.dma("gpsimd", pfx + "oTso", A["oT"].rearrange("(c p) t -> p c t", p=128)[:, :, J * 256:(J + 1) * 256], oTs[:, :, :],
              reads=[pfx + "oTs"], writes=["oT"])

    NJ = DBG.get("AJ", TL // 256)
    for st_ in idx(0):
        st_()
    nm_gen(0)
    for st_ in idx(1):
        st_()
    nm_gen(1)
    for J in range(NJ):
        steps = idx(2 * J + 2) if J + 1 < NJ else []
        attn(J, steps)
        if J + 1 < NJ:
            nm_gen(2 * J + 2)
            for st_ in idx(2 * J + 3):
                st_()
            nm_gen(2 * J + 3)


def _new_nc():
    return bass.Bass("TRN2", target_bir_lowering=False)


def build_single(phase_name, l=0):
    nc = _new_nc()
    A = {}
    ins, outs = [], []

    def din(name, shape, dt=F32):
        A[name] = nc.dram_tensor(name, list(shape), dt, kind="ExternalInput").ap()
        ins.append(name)

    def dout(name, shape, dt=F32):
        A[name] = nc.dram_tensor(name, list(shape), dt, kind="ExternalOutput").ap()
        outs.append(name)

    with ExitStack() as es:
        P = Prog(nc, es)
        make_epsc(P)
        if phase_name == "mod":
            din("ccol", [128, 8]); din("ada_w", [DEPTH, D, 6 * D]); din("ada_b", [DEPTH, 6 * D]); din("ada_bT", [128, 4 * 48])
            dout("modT", [128, 4 * 48]); dout("modrow", [DEPTH, 2, D])
            phase_mod(P, A)
            P.add("sync", None, reads=["modT", "modrow"])
        elif phase_name == "mlp":
            din("ident", [128, 128]); din("modT", [128, 4 * 48]); din("modrow", [DEPTH, 2, D])
            din("ln_g", [DEPTH, 2, D]); din("ln_b", [DEPTH, 2, D])
            din("mlp_w1", [DEPTH, D, DFF]); din("mlp_w2", [DEPTH, DFF, D]); din("xin", [TL, D])
            dout("xout", [TL, D])
            phase_mlp(P, A, l, A["xin"], "xin", A["xout"], "xout", "ml_")
            P.add("gpsimd", None, reads=["xout"])
        elif phase_name == "bproj":
            din("ident", [128, 128]); din("modT", [128, 4 * 48]); din("b_w_in", [2, D, 3072]); din("xin", [TL, D])
            dout("qT", [D, TL], BF16); dout("kT", [D, TL], BF16); dout("v", [TL, D], BF16)
            phase_bproj(P, A, l, A["xin"], "xin", "bp_")
            P.add("gpsimd", None, reads=["qT", "kT", "v"])
        elif phase_name == "battn":
            din("ident", [128, 128]); din("rel_bias", [32, 16]); din("biasT", [16, 128, 256]); din("b_lambda", [2, 4, 64])
            din("b_subln", [2, 128]); din("qT", [D, TL], BF16); din("kT", [D, TL], BF16); din("v", [TL, D], BF16)
            dout("oT", [D, TL], BF16)
            phase_battn(P, A, l, "ba_")
            P.add("gpsimd", None, reads=["oT"])
        elif phase_name == "aproj":
            din("ident", [128, 128]); din("modT", [128, 4 * 48]); din("a_w_in", [2, D, A_IN]); din("a_w_uk", [2, 16, 64, 256])
            din("a_kv_norm", [2, 256]); din("xin", [TL, D])
            din("a_w_uv", [2, 16, 256, 64])
            dout("qT", [D, TL], BF16); dout("kT", [D, TL], BF16); dout("vaug", [TL, 16 * 65], BF16)
            dout("iqT", [512, TL], BF16); dout("ikT", [64, TL], BF16); dout("iw", [TL, 8], F32)
            phase_aproj(P, A, l, A["xin"], "xin", "ap_")
            P.add("gpsimd", None, reads=["qT", "kT", "vaug", "iqT", "ikT", "iw"])
        elif phase_name == "aattn":
            din("ident", [128, 128]); din("rel_bias", [32, 16]); din("biasT", [16, 128, 256]); din("cmask", [128, 128])
            din("qT", [D, TL], BF16); din("kT", [D, TL], BF16); din("vaug", [TL, 16 * 65], BF16)
            din("iqT", [512, TL], BF16); din("ikT", [64, TL], BF16); din("iw", [TL, 8], F32)
            dout("oT", [D, TL], BF16)
            phase_aattn(P, A, l, "aa_")
            P.add("gpsimd", None, reads=["oT"])
        elif phase_name == "outln":
            din("modrow", [DEPTH, 2, D]); din("ln_g", [DEPTH, 2, D]); din("ln_b", [DEPTH, 2, D])
            din("w_o", [D, D]); din("oT", [D, TL], BF16); din("xin", [TL, D])
            dout("xout", [TL, D])
            phase_outln(P, A, l, A["w_o"], A["oT"], "oT", A["xin"], "xin", A["xout"], "xout", "ol_")
            P.add("gpsimd", None, reads=["xout"])
        else:
            raise ValueError(phase_name)
        P.finish()
        nops = P.n
    return nc, ins, outs, nops


def _to_local(a_b, hf):
    s = a_b.shape
    return np.ascontiguousarray(a_b.reshape(32, 2, 128, *s[1:])[:, hf].reshape(TL, *s[1:]))


def _from_local(parts):
    s = parts[0].shape
    o = np.empty((32, 2, 128) + s[1:], parts[0].dtype)
    for hf in range(2):
        o[:, hf] = parts[hf].reshape(32, 128, *s[1:])
    return o.reshape(SEQ, *s[1:])


def run_phase(nc, ins, in_maps):
    res = run_bass_kernel_spmd(nc, [{k: m[k] for k in ins} for m in in_maps], core_ids=list(range(len(in_maps))))
    return res.results


def rel_bucket_np(dist):
    import math
    n = np.maximum(dist, 0)
    nf = np.maximum(n, 1).astype(np.float32)
    large = 16 + (np.log(nf / np.float32(16)) / np.float32(math.log(128 / 16)) * np.float32(16)).astype(np.int32)
    large = np.minimum(large, 31)
    return np.where(n < 16, n, large)


def make_biasT(rel_bias):
    s_ = np.arange(128)[:, None]
    q_ = np.arange(128)[None, :]
    dd = q_ - s_
    bd = rel_bucket_np(dd)
    bp = rel_bucket_np(dd + 128)
    out = np.empty((16, 128, 256), np.float32)
    for c in range(16):
        diag = rel_bias[:, c][bd]
        out[c, :, 0:128] = np.where(dd >= 0, diag, np.float32(NEGM))
        out[c, :, 128:256] = rel_bias[:, c][bp]
    return out


def build_fused(single=None):
    nc = _new_nc()
    A = {}
    NL = DEPTH if single is None else 1
    NM_ = 2 if single is None else 1

    def din(name, shape, dt=F32):
        A[name] = nc.dram_tensor(name, list(shape), dt, kind="ExternalInput").ap()

    def dint(name, shape, dt=F32):
        A[name] = nc.dram_tensor(name, list(shape), dt, kind="Internal").ap()

    din("x", [TL, D]); din("ccol", [128, 8]); din("ada_w", [NL, D, 6 * D]); din("ada_b", [NL, 6 * D])
    din("ada_bT", [128, NL * 48]); din("ident", [128, 128]); din("rel_bias", [32, 16]); din("biasT", [16, 128, 256])
    din("ln_g", [NL, 2, D]); din("ln_b", [NL, 2, D])
    if single is None or single % 2 == 0:
        din("cmask", [128, 128])
        din("a_w_in", [NM_, D, A_IN]); din("a_kv_norm", [NM_, 256]); din("a_w_uk", [NM_, 16, 64, 256])
        din("a_w_uv", [NM_, 16, 256, 64]); din("a_w_o", [NM_, D, D])
    if single is None or single % 2 == 1:
        din("b_w_in", [NM_, D, 3072]); din("b_lambda", [NM_, 4, 64]); din("b_subln", [NM_, 128]); din("b_w_o", [NM_, D, D])
    din("mlp_w1", [NL, D, DFF]); din("mlp_w2", [NL, DFF, D])
    A["out"] = nc.dram_tensor("out", [TL, D], F32, kind="ExternalOutput").ap()
    dint("modT", [128, NL * 48]); dint("modrow", [NL, 2, D]); dint("xA", [TL, D]); dint("xB", [TL, D])
    dint("vaug", [TL, 16 * 65], BF16)
    dint("iqT", [512, TL], BF16); dint("ikT", [64, TL], BF16); dint("iw", [TL, 8], F32)
    dint("qT", [D, TL], BF16); dint("kT", [D, TL], BF16); dint("v", [TL, D], BF16); dint("oT", [D, TL], BF16)
    nops = [0]

    def block(fn, outkeys):
        with ExitStack() as es:
            P = Prog(nc, es)
            make_epsc(P)
            fn(P)
            P.add("gpsimd", None, reads=outkeys)
            P.finish()
            nops[0] += P.n

    block(lambda P: phase_mod(P, A, NL), ["modT", "modrow"])
    xcur = "x"
    llist = DBG.get("llist", list(range(DBG.get("layers", DEPTH))))
    if single is not None:
        llist = [0]
        DBG["true_l"] = single
    else:
        DBG.pop("true_l", None)
    for l in llist:
        j = l // 2
        kind_a = (l % 2 == 0) if single is None else (single % 2 == 0)
        if kind_a:
            block(lambda P: phase_aproj(P, A, l, A[xcur], "xin", "ap_"), ["qT", "kT", "vaug", "iqT", "ikT", "iw"])
            block(lambda P: phase_aattn(P, A, l, "aa_"), ["oT"])
            w_o = A["a_w_o"][j]
        else:
            block(lambda P: phase_bproj(P, A, l, A[xcur], "xin", "bp_"), ["qT", "kT", "v"])
            block(lambda P: phase_battn(P, A, l, "ba_"), ["oT"])
            w_o = A["b_w_o"][j]
        block(lambda P: phase_outln(P, A, l, w_o, A["oT"], "oT", A[xcur], "xin", A["xA"], "xout", "ol_"), ["xout"])
        last = (l == llist[-1])
        dst = "out" if last else "xB"
        block(lambda P: phase_mlp(P, A, l, A["xA"], "xin", A[dst], "xout", "ml_"), ["xout"])
        xcur = "xB"
    return nc, nops[0]


_CACHE = {}
FUSED = True


def kernel(**inputs):
    f32 = lambda a: np.ascontiguousarray(np.asarray(a, dtype=np.float32))
    x = f32(inputs["x"])
    c = f32(inputs["c"])
    rel_bias = f32(inputs["rel_bias"])
    W = {k: f32(inputs[k]) for k in ["ada_w", "ada_b", "ln_g", "ln_b", "a_w_in", "a_kv_norm", "a_w_uk", "a_w_uv", "a_w_o",
                                     "b_w_in", "b_lambda", "b_subln", "b_w_o", "mlp_w1", "mlp_w2"]}
    const = {
        "ident": np.eye(128, dtype=np.float32),
        "rel_bias": rel_bias,
        "biasT": make_biasT(rel_bias),
    }
    cmask = np.where(np.arange(128)[None, :] <= np.arange(128)[:, None], 0.0, -1e30).astype(np.float32)
    ccols = [np.ascontiguousarray(c[b].reshape(8, 128).T) for b in range(NB)]
    if FUSED:
        if "nc" not in _CACHE:
            _CACHE["nc"] = build_fused()
        nc, nops = _CACHE["nc"]
        shared = dict(const)
        shared.update(W)
        shared["ada_bT"] = np.ascontiguousarray(W["ada_b"].reshape(4 * 48, 128).T)
        shared["cmask"] = cmask
        in_maps = []
        for b in range(NB):
            m = dict(shared)
            m["x"] = x[b]
            m["ccol"] = ccols[b]
            in_maps.append(m)
        res = run_bass_kernel_spmd(nc, in_maps, core_ids=list(range(NB)))
        return np.stack([np.asarray(res.results[b]["out"], dtype=np.float32) for b in range(NB)], axis=0)
    xs = [x[b] for b in range(NB)]
    for L in range(DEPTH):
        key = "nc%d" % (L % 2)
        if ("L", L) not in _CACHE:
            _CACHE[("L", L)] = build_fused(single=L)
        nc, nops = _CACHE[("L", L)]
        j = L // 2
        shared = dict(const)
        for k in ["ada_w", "ada_b", "ln_g", "ln_b", "mlp_w1", "mlp_w2"]:
            shared[k] = np.ascontiguousarray(W[k][L:L + 1])
        shared["ada_bT"] = np.ascontiguousarray(W["ada_b"][L].reshape(48, 128).T)
        if L % 2 == 0:
            shared["cmask"] = cmask
            for k in ["a_w_in", "a_kv_norm", "a_w_uk", "a_w_uv", "a_w_o"]:
                shared[k] = np.ascontiguousarray(W[k][j:j + 1])
        else:
            for k in ["b_w_in", "b_lambda", "b_subln", "b_w_o"]:
                shared[k] = np.ascontiguousarray(W[k][j:j + 1])
        in_maps = []
        for b in range(NB):
            m = dict(shared)
            m["x"] = xs[b]
            m["ccol"] = ccols[b]
            in_maps.append(m)
        res = run_bass_kernel_spmd(nc, in_maps, core_ids=list(range(NB)))
        xs = [np.asarray(res.results[b]["out"], dtype=np.float32) for b in range(NB)]
    return np.stack(xs, axis=0)
```

```python
import numpy as np
from contextlib import ExitStack
import concourse.bass as bass
import concourse.mybir as mybir
from concourse.bass_utils import run_bass_kernel_spmd

F32 = mybir.dt.float32
BF16 = mybir.dt.bfloat16
U8 = mybir.dt.uint8
AF = mybir.ActivationFunctionType
ALU = mybir.AluOpType
AX = mybir.AxisListType

D = 1024
SEQ = 8192
NB = 4
DEPTH = 4
TL = 8192
NLB = 64
DFF = 4096
LN_EPS = 1e-5
DN_ALPHA = (2 * DEPTH) ** 0.25
A_IN = 1864
NEGM = -30000.0

DBG = {}
STATS = {}
ENGS = ["tensor", "vector", "scalar", "gpsimd", "sync"]
EPOCH = 30000
EPOCH_DMA = 1800


class Op:
    __slots__ = ("eng", "fn", "chan", "waits", "needs_inc", "val", "epoch", "isdma", "noattach")


class Prog:
    _uid = [0]

    def __init__(self, nc, es):
        self.nc = nc
        self.es = es
        Prog._uid[0] += 1
        self.uid = Prog._uid[0]
        self.ops = {e: [] for e in ENGS}
        self.chan_ops = {}
        self.lastw = {}
        self.readers = {}
        self.n = 0
        self.outkeys = []

    def sb(self, name, shape, dt):
        return self.es.enter_context(self.nc.sbuf_tensor("%s_u%d" % (name, self.uid), list(shape), dt))

    def ps(self, name, shape, dt=F32):
        esz = 4 if dt == F32 else 2
        n = 1
        for d in shape[1:]:
            n *= d
        per_bank = 2048 // esz
        tot = ((n + per_bank - 1) // per_bank) * per_bank
        h = self.es.enter_context(self.nc.psum_tensor("%s_u%d" % (name, self.uid), [128, tot], dt))
        ap = h[0:shape[0], 0:n]
        if len(shape) == 3:
            ap = ap.rearrange("p (a b) -> p a b", a=shape[1])
        return ap

    def add(self, eng, fn, reads=(), writes=(), dma=None, noattach=False, pe_attach=False):
        op = Op()
        op.noattach = noattach or (eng == "tensor" and not pe_attach)
        op.eng = eng
        op.fn = fn
        op.isdma = dma is not None
        op.chan = ("dma", dma) if dma is not None else eng
        op.needs_inc = op.isdma
        deps = []
        for k in reads:
            w = self.lastw.get(k)
            if w is not None:
                deps.append(w)
        for k in writes:
            w = self.lastw.get(k)
            if w is not None:
                deps.append(w)
            rd = self.readers.get(k)
            if rd:
                deps.extend(rd.values())
        waits = []
        seen = set()
        for d in deps:
            if id(d) in seen:
                continue
            seen.add(id(d))
            if (not d.isdma) and d.chan == eng and eng == "tensor":
                continue
            d.needs_inc = True
            waits.append(d)
        op.waits = waits
        for k in reads:
            self.readers.setdefault(k, {})[op.chan] = op
        for k in writes:
            self.lastw[k] = op
            self.readers[k] = {}
        self.ops[eng].append(op)
        self.chan_ops.setdefault(op.chan, []).append(op)
        self.n += 1
        return op

    DRAM_OUT = ("qT", "kT", "v", "vaug", "iqT", "ikT", "iw", "oT", "xout", "modT", "modrow")

    def dma(self, eng, chan, out, in_, reads=(), writes=()):
        w2 = []
        for k in writes:
            if k in Prog.DRAM_OUT:
                k = "%s#%d" % (k, len(self.outkeys))
                self.outkeys.append(k)
            w2.append(k)
        return self.add(eng, lambda e: e.dma_start(out=out, in_=in_), reads, w2, dma=chan)

    def final_wait(self, eng="gpsimd"):
        self.add(eng, None, reads=list(self.outkeys))

    def finish(self):
        nc = self.nc
        sems = {}
        for chan, lst in self.chan_ops.items():
            cnt = 0
            ep = EPOCH_DMA if chan[0] == "dma" else EPOCH
            for op in lst:
                if op.needs_inc:
                    op.epoch = cnt // ep
                    op.val = cnt % ep + 1
                    cnt += 1
                    key = (chan, op.epoch)
                    if key not in sems:
                        sems[key] = nc.alloc_semaphore(name="s%d_u%d" % (len(sems), self.uid))
        self.nsems = len(sems)
        with nc.Block() as block:
            self._emit(block, sems)
        nc.clear_and_free_semaphores(list(sems.values()))
        nc.all_engine_barrier()

    def _emit(self, block, sems):

        def emit(eng_name):
            ops = self.ops[eng_name]

            def body(e):
                waited = {}
                for op in ops:
                    need = {}
                    for d in op.waits:
                        cur = waited.get(d.chan)
                        if cur is not None and (cur[0] > d.epoch or (cur[0] == d.epoch and cur[1] >= d.val)):
                            continue
                        prev = need.get(d.chan)
                        if prev is None or (d.epoch, d.val) > (prev.epoch, prev.val):
                            need[d.chan] = d
                    need = list(need.values())
                    attach = None
                    if need and op.fn is not None and not op.noattach:
                        attach = need.pop()
                    for d in need:
                        e.wait_ge(sems[(d.chan, d.epoch)], d.val * (16 if d.isdma else 1))
                        waited[d.chan] = (d.epoch, d.val)
                        STATS[eng_name + "_wait"] = STATS.get(eng_name + "_wait", 0) + 1
                    if op.fn is None:
                        continue
                    ins = op.fn(e)
                    if attach is not None:
                        ins._wait_ge(sems[(attach.chan, attach.epoch)], attach.val * (16 if attach.isdma else 1))
                        waited[attach.chan] = (attach.epoch, attach.val)
                    STATS[eng_name] = STATS.get(eng_name, 0) + 1
                    if op.needs_inc:
                        ins.then_inc(sems[(op.chan, op.epoch)], 16 if op.isdma else 1)
            return body

        for en in ENGS:
            if self.ops[en]:
                getattr(block, en)(emit(en))


class Ctx:
    pass


def _rot(lst, i):
    return lst[i % len(lst)]


def load_mod_cols(P, pfx, modT_ap, l, which):
    nc = P.nc
    mt = P.sb(pfx + "modc", [128, 16], F32)
    base = l * 48 + which * 24
    P.dma("sync", pfx + "modc", mt[:, :], modT_ap[:, base:base + 16], reads=["modT"], writes=[pfx + "modc"])
    P.add("vector", lambda e: e.tensor_scalar(out=mt[:, 8:16], in0=mt[:, 8:16], scalar1=1.0, scalar2=None,
                                             op0=ALU.add), reads=[pfx + "modc"], writes=[pfx + "modc"])
    return mt


def load_bcast(P, name, src_ap_1d, n):
    t = P.sb(name, [128, n], F32)
    P.dma("sync", name, t[:, :], src_ap_1d.partition_broadcast(128), reads=["modrow"], writes=[name])
    return t


def make_ident(P, pfx, ident_ap):
    idf = P.sb(pfx + "idf", [128, 128], F32)
    P.dma("sync", pfx + "idf", idf[:, :], ident_ap, writes=[pfx + "idf"])
    idb = P.sb(pfx + "idb", [128, 128], BF16)
    P.add("vector", lambda e: e.tensor_copy(out=idb[:, :], in_=idf[:, :]), reads=[pfx + "idf"], writes=[pfx + "idb"])
    return idf, idb


def emit_hT(P, pfx, xblk_tiles, nblk, idf, modc, hT, col0, ps_tr_list, cnt):
    for c in range(8):
        pt, pk = _rot(ps_tr_list, cnt[0])
        cnt[0] += 1
        for b in range(nblk):
            xt, xk = xblk_tiles[b]
            P.add("tensor", (lambda e, pt=pt, xt=xt, b=b, c=c: e.transpose(
                out=pt[:, b * 128:(b + 1) * 128], in_=xt[:, c * 128:(c + 1) * 128], identity=idf[:, :])),
                reads=[xk, pfx + "idf"], writes=[pk])
        P.add("scalar", (lambda e, pt=pt, c=c: e.activation(
            out=hT[:, c, col0:col0 + nblk * 128], in_=pt[:, 0:nblk * 128], func=AF.Identity,
            bias=modc[:, c:c + 1], scale=modc[:, 8 + c:9 + c])),
            reads=[pk, pfx + "modc"], writes=[pfx + "hT"])


def emit_resid_ln(P, pfx, ps_y, ps_key, xt, xkey, gbc, lng, lnb, ot, okey, small, skey):
    nc = P.nc
    st, mv, rs = small
    P.add("vector", lambda e: e.tensor_tensor(out=ot[:, :].rearrange("p (a f) -> p a f", a=2), in0=ps_y[:, :, :],
                                             in1=gbc[:, :].rearrange("p (a f) -> p a f", a=2), op=ALU.mult),
          reads=[ps_key, pfx + "gbc"], writes=[okey])
    P.add("vector", lambda e: e.scalar_tensor_tensor(out=ot[:, :], in0=xt, scalar=float(DN_ALPHA), in1=ot[:, :],
                                                    op0=ALU.mult, op1=ALU.add),
          reads=[xkey], writes=[okey])
    P.add("vector", lambda e: e.bn_stats(out=st[:, 0, :], in_=ot[:, 0:512]), reads=[okey], writes=[skey])
    P.add("vector", lambda e: e.bn_stats(out=st[:, 1, :], in_=ot[:, 512:1024]), reads=[okey], writes=[skey])
    P.add("vector", lambda e: e.bn_aggr(out=mv[:, :], in_=st[:, :, :]), reads=[skey], writes=[skey])
    P.add("scalar", lambda e: e.activation(out=rs[:, 0:1], in_=mv[:, 1:2], func=AF.Sqrt, bias=P.epsc[:, 0:1], scale=1.0),
          reads=[skey, "epsc"], writes=[skey + "r"])
    P.add("vector", lambda e: e.reciprocal(out=rs[:, 1:2], in_=rs[:, 0:1]), reads=[skey + "r"], writes=[skey + "r2"])
    P.add("vector", lambda e: e.tensor_scalar(out=rs[:, 2:3], in0=mv[:, 0:1], scalar1=rs[:, 1:2], scalar2=-1.0,
                                             op0=ALU.mult, op1=ALU.mult), reads=[skey + "r2"], writes=[skey + "r2"])
    P.add("scalar", lambda e: e.activation(out=ot[:, :], in_=ot[:, :], func=AF.Identity, bias=rs[:, 2:3],
                                           scale=rs[:, 1:2]), reads=[skey + "r2", okey], writes=[okey])
    P.add("vector", lambda e: e.tensor_tensor(out=ot[:, :], in0=ot[:, :], in1=lng[:, :], op=ALU.mult),
          reads=[okey, pfx + "lng"], writes=[okey])
    P.add("gpsimd", lambda e: e.tensor_tensor(out=ot[:, :], in0=ot[:, :], in1=lnb[:, :], op=ALU.add),
          reads=[okey, pfx + "lnb"], writes=[okey])


def make_epsc(P):
    t = P.sb("epsc", [128, 1], F32)
    P.add("vector", lambda e: e.memset(t[:, :], LN_EPS), writes=["epsc"])
    P.epsc = t


def phase_mod(P, A, NL=DEPTH):
    nc = P.nc
    pfx = "md_"
    cT = P.sb(pfx + "cT", [128, 8], F32)
    P.dma("sync", pfx + "c", cT[:, :], A["ccol"], writes=[pfx + "cT"])
    sT = P.sb(pfx + "sT", [128, 8], F32)
    P.add("scalar", lambda e: e.activation(out=sT[:, :], in_=cT[:, :], func=AF.Silu), reads=[pfx + "cT"], writes=[pfx + "sT"])
    bT = P.sb(pfx + "bT", [128, NL * 48], F32)
    P.dma("sync", pfx + "b", bT[:, :], A["ada_bT"], writes=[pfx + "bT"])
    brow = P.sb(pfx + "brow", [1, NL * 2048], F32)
    for l in range(NL):
        for gi in range(2):
            P.dma("sync", pfx + "br", brow[:, (l * 2 + gi) * 1024:(l * 2 + gi + 1) * 1024],
                  A["ada_b"][l:l + 1, (2 + 3 * gi) * 1024:(3 + 3 * gi) * 1024], writes=[pfx + "brow"])
    modsb = P.sb(pfx + "modsb", [128, NL * 48], F32)
    rowsb = P.sb(pfx + "rowsb", [1, NL * 2048], F32)
    wp = [P.sb(pfx + "wp%d" % i, [128, 8, 512], F32) for i in range(3)]
    psc = P.ps(pfx + "psc", [128, NL * 48], F32)
    psr = [P.ps(pfx + "psr%d" % i, [1, 512], F32) for i in range(2)]
    P.add("vector", lambda e: e.memset(modsb[:, :], 0.0), writes=[pfx + "modsb"])
    P.add("vector", lambda e: e.memset(rowsb[:, :], 0.0), writes=[pfx + "rowsb"])
    it = 0
    for l in range(NL):
        for pc in range(12):
            w = wp[it % 3]
            wk = pfx + "wp%d" % (it % 3)
            src = A["ada_w"][l, :, pc * 512:(pc + 1) * 512].rearrange("(kk p) f -> p kk f", p=128)
            qeng = "sync" if it % 2 == 0 else "gpsimd"
            P.dma(qeng, wk + qeng, w[:, :, :], src, writes=[wk])
            seg = pc // 2
            if seg in (2, 5):
                pr = psr[it % 2]
                prk = pfx + "psr%d" % (it % 2)
                for kk in range(8):
                    P.add("tensor", (lambda e, pr=pr, w=w, kk=kk: e.matmul(
                        pr[:, :], sT[:, kk:kk + 1], w[:, kk, :], start=(kk == 0), stop=(kk == 7))),
                        reads=[wk, pfx + "sT"], writes=[prk])
                o0 = (l * 2 + seg // 3) * 1024 + (pc % 2) * 512
                P.add("vector", (lambda e, pr=pr, o0=o0: e.tensor_tensor(
                    out=rowsb[:, o0:o0 + 512], in0=pr[:, :], in1=brow[:, o0:o0 + 512], op=ALU.add)),
                    reads=[prk, pfx + "brow"], writes=[pfx + "rowsb"])
            else:
                for q in range(4):
                    col = l * 48 + pc * 4 + q
                    for kk in range(8):
                        P.add("tensor", (lambda e, w=w, kk=kk, q=q, col=col: e.matmul(
                            psc[:, col:col + 1], w[:, kk, q * 128:(q + 1) * 128], sT[:, kk:kk + 1],
                            start=(kk == 0), stop=(kk == 7))),
                            reads=[wk, pfx + "sT"], writes=[pfx + "psc"])
            it += 1
    for l in range(NL):
        for c0 in (l * 48, l * 48 + 24):
            P.add("vector", (lambda e, c0=c0: e.tensor_tensor(out=modsb[:, c0:c0 + 16], in0=psc[:, c0:c0 + 16],
                                                              in1=bT[:, c0:c0 + 16], op=ALU.add)),
                  reads=[pfx + "psc", pfx + "bT"], writes=[pfx + "modsb"])
    P.dma("sync", pfx + "o1", A["modT"], modsb[:, :], reads=[pfx + "modsb"], writes=["modT"])
    P.dma("sync", pfx + "o2", A["modrow"].rearrange("(o l) g f -> o (l g f)", o=1), rowsb[:, :], reads=[pfx + "rowsb"],
          writes=["modrow"])


def phase_mlp(P, A, l, xin, xin_key, xout, xout_key, pfx):
    nc = P.nc
    TT = 256
    NT = TL // TT
    idf, idb = make_ident(P, pfx, A["ident"])
    modc = load_mod_cols(P, pfx, A["modT"], l, 1)
    gbc = load_bcast(P, pfx + "gbc", A["modrow"][l, 1, :], 1024)
    P.add("vector", lambda e: e.tensor_scalar(out=gbc[:, :], in0=gbc[:, :], scalar1=1.0, scalar2=None, op0=ALU.add),
          reads=[pfx + "gbc"], writes=[pfx + "gbc"])
    lng = P.sb(pfx + "lng", [128, 1024], F32)
    P.dma("sync", pfx + "lng", lng[:, :], A["ln_g"][l, 1, :].partition_broadcast(128), writes=[pfx + "lng"])
    lnb = P.sb(pfx + "lnb", [128, 1024], F32)
    P.dma("sync", pfx + "lnb", lnb[:, :], A["ln_b"][l, 1, :].partition_broadcast(128), writes=[pfx + "lnb"])
    w1b = P.sb(pfx + "w1b", [128, 8, DFF], BF16)
    w2b = P.sb(pfx + "w2b", [128, 32, D], BF16)
    for kc in range(8):
        P.dma("gpsimd", pfx + "w1", w1b[:, kc, :], A["mlp_w1"][l, kc * 128:(kc + 1) * 128, :], writes=[pfx + "w1b"])
    w2v = A["mlp_w2"][l].rearrange("(kc p) f -> p kc f", p=128)
    for g in range(8):
        P.dma("gpsimd", pfx + "w2", w2b[:, g * 4:(g + 1) * 4, :], w2v[:, g * 4:(g + 1) * 4, :], writes=[pfx + "w2b"])
    xtr = [P.sb(pfx + "xtr%d" % i, [128, 1024], F32) for i in range(2)]
    xep = [P.sb(pfx + "xep%d" % i, [128, 1024], F32) for i in range(2)]
    ots = [P.sb(pfx + "ot%d" % i, [128, 1024], F32) for i in range(3)]
    hT = P.sb(pfx + "hT", [128, 8, TT], BF16)
    aT = P.sb(pfx + "aT", [128, 32, TT], BF16)
    rts = [P.sb(pfx + "rt%d" % i, [128, TT], F32) for i in range(3)]
    smalls = [(P.sb(pfx + "st%d" % i, [128, 2, 6], F32), P.sb(pfx + "mv%d" % i, [128, 2], F32),
               P.sb(pfx + "rs%d" % i, [128, 3], F32)) for i in range(3)]
    ps_tr = [(P.ps(pfx + "ptr%d" % i, [128, 512], F32), pfx + "ptr%d" % i) for i in range(2)]
    ps_a = [P.ps(pfx + "pa%d" % i, [128, 512], F32) for i in range(2)]
    ps_y = [P.ps(pfx + "py%d" % i, [128, 2, 512], F32) for i in range(2)]
    cnt = [0]
    nblk = TT // 128
    xcnt = [0]
    ecnt = [0]

    def do_tr(t):
        tiles = []
        for b in range(nblk):
            i = xcnt[0] % 2
            xcnt[0] += 1
            r0 = t * TT + b * 128
            P.dma("sync", pfx + "xtr%d" % i, xtr[i][:, :], xin[r0:r0 + 128, :], reads=[xin_key], writes=[pfx + "xtr%d" % i])
            tiles.append((xtr[i], pfx + "xtr%d" % i))
        emit_hT(P, pfx, tiles, nblk, idf, modc, hT, 0, ps_tr, cnt)

    def do_w1(t):
        for fc in range(32):
            pa = ps_a[fc % 2]
            pak = pfx + "pa%d" % (fc % 2)
            for kc in range(8):
                P.add("tensor", (lambda e, pa=pa, kc=kc, fc=fc: e.matmul(
                    pa[:, 0:TT], w1b[:, kc, fc * 128:(fc + 1) * 128], hT[:, kc, :], start=(kc == 0), stop=(kc == 7))),
                    reads=[pfx + "w1b", pfx + "hT"], writes=[pak])
            rt = rts[fc % 3]
            rk = pfx + "rt%d" % (fc % 3)
            P.add("scalar", (lambda e, pa=pa, rt=rt: e.activation(out=rt[:, :], in_=pa[:, 0:TT], func=AF.Relu)),
                  reads=[pak], writes=[rk])
            P.add("gpsimd", (lambda e, rt=rt, fc=fc: e.tensor_tensor(out=aT[:, fc, :], in0=rt[:, :], in1=rt[:, :], op=ALU.mult)),
                  reads=[rk], writes=[pfx + "aT"])

    def do_w2(t):
        for b in range(nblk):
            i = ecnt[0]
            ecnt[0] += 1
            py = ps_y[i % 2]
            pyk = pfx + "py%d" % (i % 2)
            for half in range(2):
                for fc in range(32):
                    P.add("tensor", (lambda e, py=py, half=half, fc=fc, b=b: e.matmul(
                        py[:, half, :], aT[:, fc, b * 128:(b + 1) * 128], w2b[:, fc, half * 512:(half + 1) * 512],
                        start=(fc == 0), stop=(fc == 31))),
                        reads=[pfx + "aT", pfx + "w2b"], writes=[pyk])
            r0 = t * TT + b * 128
            xe = xep[i % 2]
            xek = pfx + "xep%d" % (i % 2)
            P.dma("sync", xek, xe[:, :], xin[r0:r0 + 128, :], reads=[xin_key], writes=[xek])
            ot = ots[i % 3]
            ok = pfx + "ot%d" % (i % 3)
            emit_resid_ln(P, pfx, py, pyk, xe[:, :], xek, gbc, lng, lnb, ot, ok, smalls[i % 3], pfx + "sm%d" % (i % 3))
            P.dma("gpsimd", ok + "o", xout[r0:r0 + 128, :], ot[:, :], reads=[ok], writes=[xout_key])

    NT = DBG.get("NT", NT)
    do_tr(0)
    for t in range(NT):
        do_w1(t)
        if t + 1 < NT:
            do_tr(t + 1)
        do_w2(t)


def phase_outln(P, A, l, w_ap, oT, oT_key, xin, xin_key, xout, xout_key, pfx):
    gbc = load_bcast(P, pfx + "gbc", A["modrow"][l, 0, :], 1024)
    P.add("vector", lambda e: e.tensor_scalar(out=gbc[:, :], in0=gbc[:, :], scalar1=1.0, scalar2=None, op0=ALU.add),
          reads=[pfx + "gbc"], writes=[pfx + "gbc"])
    lng = P.sb(pfx + "lng", [128, 1024], F32)
    P.dma("sync", pfx + "lng", lng[:, :], A["ln_g"][l, 0, :].partition_broadcast(128), writes=[pfx + "lng"])
    lnb = P.sb(pfx + "lnb", [128, 1024], F32)
    P.dma("sync", pfx + "lnb", lnb[:, :], A["ln_b"][l, 0, :].partition_broadcast(128), writes=[pfx + "lnb"])
    wob = P.sb(pfx + "wob", [128, 8, D], BF16)
    P.dma("gpsimd", pfx + "wo", wob[:, :, :], w_ap.rearrange("(c p) f -> p c f", p=128), writes=[pfx + "wob"])
    ots = [P.sb(pfx + "ot%d" % i, [128, 1024], F32) for i in range(3)]
    xep = [P.sb(pfx + "xep%d" % i, [128, 1024], F32) for i in range(3)]
    oTt = [P.sb(pfx + "oTt%d" % i, [128, 8, 512], BF16) for i in range(2)]
    smalls = [(P.sb(pfx + "st%d" % i, [128, 2, 6], F32), P.sb(pfx + "mv%d" % i, [128, 2], F32),
               P.sb(pfx + "rs%d" % i, [128, 3], F32)) for i in range(3)]
    ps_y = [P.ps(pfx + "py%d" % i, [128, 2, 512], F32) for i in range(2)]
    oTv = oT.rearrange("(c p) t -> p c t", p=128)
    i = 0
    for t in range(DBG.get("OT", TL // 512)):
        ob = oTt[t % 2]
        obk = pfx + "oTt%d" % (t % 2)
        P.dma("sync", obk, ob[:, :, :], oTv[:, :, t * 512:(t + 1) * 512], reads=[oT_key], writes=[obk])
        for b in range(4):
            py = ps_y[i % 2]
            pyk = pfx + "py%d" % (i % 2)
            for half in range(2):
                for c in range(8):
                    P.add("tensor", (lambda e, py=py, half=half, c=c, b=b, ob=ob: e.matmul(
                        py[:, half, :], ob[:, c, b * 128:(b + 1) * 128], wob[:, c, half * 512:(half + 1) * 512],
                        start=(c == 0), stop=(c == 7))), reads=[obk, pfx + "wob"], writes=[pyk])
            r0 = t * 512 + b * 128
            xe = xep[i % 3]
            xek = pfx + "xep%d" % (i % 3)
            P.dma("sync", xek, xe[:, :], xin[r0:r0 + 128, :], reads=[xin_key], writes=[xek])
            ot = ots[i % 3]
            ok = pfx + "ot%d" % (i % 3)
            emit_resid_ln(P, pfx, py, pyk, xe[:, :], xek, gbc, lng, lnb, ot, ok, smalls[i % 3], pfx + "sm%d" % (i % 3))
            P.dma("gpsimd", ok + "o", xout[r0:r0 + 128, :], ot[:, :], reads=[ok], writes=[xout_key])
            i += 1


def phase_bproj(P, A, l, xin, xin_key, pfx):
    j = l // 2
    idf, idb = make_ident(P, pfx, A["ident"])
    modc = load_mod_cols(P, pfx, A["modT"], l, 0)
    wb = P.sb(pfx + "wb", [128, 8, 3072], BF16)
    wv = A["b_w_in"][j].rearrange("(c p) f -> p c f", p=128)
    for c in range(8):
        P.dma("gpsimd", pfx + "w", wb[:, c, :], wv[:, c, :], writes=[pfx + "wb"])
    xtr = [P.sb(pfx + "xtr%d" % i, [128, 1024], F32) for i in range(6)]
    hT = P.sb(pfx + "hT", [128, 8, 512], BF16)
    stg = [P.sb(pfx + "stg%d" % i, [128, 512], BF16) for i in range(4)]
    ps_tr = [(P.ps(pfx + "ptr%d" % i, [128, 512], F32), pfx + "ptr%d" % i) for i in range(2)]
    ps_o = [P.ps(pfx + "po%d" % i, [128, 512], F32) for i in range(4)]
    cnt = [0]
    xc = 0
    oc = 0
    for t in range(TL // 512):
        tiles = []
        for b in range(4):
            i = xc % 6
            xc += 1
            r0 = t * 512 + b * 128
            P.dma("sync", pfx + "xtr%d" % i, xtr[i][:, :], xin[r0:r0 + 128, :], reads=[xin_key], writes=[pfx + "xtr%d" % i])
            tiles.append((xtr[i], pfx + "xtr%d" % i))
        emit_hT(P, pfx, tiles, 4, idf, modc, hT, 0, ps_tr, cnt)
        for fo in range(16):
            po = ps_o[oc % 4]
            pok = pfx + "po%d" % (oc % 4)
            sg = stg[oc % 4]
            sgk = pfx + "stg%d" % (oc % 4)
            oc += 1
            for c in range(8):
                P.add("tensor", (lambda e, po=po, c=c, fo=fo: e.matmul(
                    po[:, :], wb[:, c, fo * 128:(fo + 1) * 128], hT[:, c, :], start=(c == 0), stop=(c == 7))),
                    reads=[pfx + "wb", pfx + "hT"], writes=[pok])
            sc = 0.125 if fo < 8 else 1.0
            P.add("scalar", (lambda e, po=po, sg=sg, sc=sc: e.activation(out=sg[:, :], in_=po[:, :], func=AF.Copy, scale=sc)),
                  reads=[pok], writes=[sgk])
            dst = A["qT"] if fo < 8 else A["kT"]
            dk = "qT" if fo < 8 else "kT"
            fr = (fo % 8) * 128
            P.dma("gpsimd", sgk + "o", dst[fr:fr + 128, t * 512:(t + 1) * 512], sg[:, :], reads=[sgk], writes=[dk])
        for b in range(4):
            for half in range(2):
                po = ps_o[oc % 4]
                pok = pfx + "po%d" % (oc % 4)
                sg = stg[oc % 4]
                sgk = pfx + "stg%d" % (oc % 4)
                oc += 1
                for c in range(8):
                    P.add("tensor", (lambda e, po=po, c=c, b=b, half=half: e.matmul(
                        po[:, :], hT[:, c, b * 128:(b + 1) * 128], wb[:, c, 2048 + half * 512:2048 + (half + 1) * 512],
                        start=(c == 0), stop=(c == 7))), reads=[pfx + "wb", pfx + "hT"], writes=[pok])
                P.add("vector", (lambda e, po=po, sg=sg: e.tensor_copy(out=sg[:, :], in_=po[:, :])), reads=[pok], writes=[sgk])
                r0 = t * 512 + b * 128
                P.dma("gpsimd", sgk + "o", A["v"][r0:r0 + 128, half * 512:(half + 1) * 512], sg[:, :], reads=[sgk], writes=["v"])


def phase_battn(P, A, l, pfx):
    j = l // 2
    import math
    lam_init = 0.8 - 0.6 * math.exp(-0.3 * DBG.get("true_l", l))
    idf, idb = make_ident(P, pfx, A["ident"])
    onesb = P.sb(pfx + "onesb", [128, 128], BF16)
    P.add("vector", lambda e: e.memset(onesb[:, :], 1.0), writes=[pfx + "onesb"])
    onesf = P.sb(pfx + "onesf", [128, 128], F32)
    P.add("vector", lambda e: e.memset(onesf[:, :], 1.0), writes=[pfx + "onesf"])
    b31 = P.sb(pfx + "b31", [128, 16], F32)
    P.dma("sync", pfx + "b31", b31[:, :], A["rel_bias"][31, :].partition_broadcast(128), writes=[pfx + "b31"])
    BT = P.sb(pfx + "BT", [128, 16, 256], F32)
    P.dma("sync", pfx + "BT", BT[:, :, :], A["biasT"].rearrange("c s q -> s c q"), writes=[pfx + "BT"])
    for c in range(16):
        P.add("vector", (lambda e, c=c: e.tensor_scalar(out=BT[:, c, :], in0=BT[:, c, :], scalar1=b31[:, c:c + 1],
                                                        scalar2=None, op0=ALU.subtract)),
              reads=[pfx + "BT", pfx + "b31"], writes=[pfx + "BT"])
    lamb = P.sb(pfx + "lamb", [128, 256], F32)
    P.dma("sync", pfx + "lamb", lamb[:, :], A["b_lambda"][j].rearrange("a d -> (a d)").partition_broadcast(128),
          writes=[pfx + "lamb"])
    lt = P.sb(pfx + "lt", [128, 128], F32)
    ls = P.sb(pfx + "ls", [128, 4], F32)
    P.add("vector", lambda e: e.tensor_tensor(out=lt[:, 0:64], in0=lamb[:, 0:64], in1=lamb[:, 64:128], op=ALU.mult),
          reads=[pfx + "lamb"], writes=[pfx + "lt"])
    P.add("vector", lambda e: e.tensor_tensor(out=lt[:, 64:128], in0=lamb[:, 128:192], in1=lamb[:, 192:256], op=ALU.mult),
          reads=[pfx + "lamb"], writes=[pfx + "lt"])
    P.add("vector", lambda e: e.reduce_sum(out=ls[:, 0:1], in_=lt[:, 0:64], axis=AX.X), reads=[pfx + "lt"], writes=[pfx + "ls"])
    P.add("vector", lambda e: e.reduce_sum(out=ls[:, 1:2], in_=lt[:, 64:128], axis=AX.X), reads=[pfx + "lt"], writes=[pfx + "ls"])
    P.add("scalar", lambda e: e.activation(out=ls[:, 2:4], in_=ls[:, 0:2], func=AF.Exp), reads=[pfx + "ls"], writes=[pfx + "ls2"])
    neglam = P.sb(pfx + "neglam", [128, 1], F32)
    P.add("vector", lambda e: e.tensor_scalar(out=neglam[:, :], in0=ls[:, 3:4], scalar1=float(lam_init), scalar2=ls[:, 2:3],
                                             op0=ALU.subtract, op1=ALU.subtract), reads=[pfx + "ls2"], writes=[pfx + "neglam"])
    sg = P.sb(pfx + "sg", [128, 1], F32)
    P.dma("sync", pfx + "sg", sg[:, :], A["b_subln"][j].rearrange("(p o) -> p o", o=1), writes=[pfx + "sg"])
    P.add("vector", lambda e: e.tensor_scalar(out=sg[:, :], in0=sg[:, :], scalar1=float(1.0 - lam_init), scalar2=None,
                                             op0=ALU.mult), reads=[pfx + "sg"], writes=[pfx + "sg"])
    kTh = [P.sb(pfx + "kTh%d" % i, [128, TL], BF16) for i in range(2)]
    qTh = [P.sb(pfx + "qTh%d" % i, [128, TL], BF16) for i in range(2)]
    Vh = [P.sb(pfx + "Vh%d" % i, [128, NLB, 128], BF16) for i in range(2)]
    PT = [P.sb(pfx + "PT%d" % i, [128, 512], BF16) for i in range(6)]
    r0t = P.sb(pfx + "r0t", [128, 512], F32)
    r1t = P.sb(pfx + "r1t", [128, 512], F32)
    o0t = P.sb(pfx + "o0t", [128, 512], F32)
    o1t = P.sb(pfx + "o1t", [128, 512], F32)
    sqt = P.sb(pfx + "sqt", [128, 512], F32)
    oTs = [P.sb(pfx + "oTs%d" % i, [128, 512], BF16) for i in range(2)]
    ps_s = [P.ps(pfx + "pss%d" % i, [128, 512], F32) for i in range(3)]
    acc_o = [P.ps(pfx + "acco%d" % i, [128, 512], F32) for i in range(2)]
    acc_s = [P.ps(pfx + "accs%d" % i, [128, 512], F32) for i in range(2)]
    ps_ms = P.ps(pfx + "psms", [128, 512], F32)
    st = {"sc": 0, "pc": 0}
    oc = 0
    vv = A["v"].rearrange("(blk p) f -> p blk f", p=128)
    for h in range(DBG.get("heads", 8)):
        kt = kTh[h % 2]; ktk = pfx + "kTh%d" % (h % 2)
        qt = qTh[h % 2]; qtk = pfx + "qTh%d" % (h % 2)
        vt = Vh[h % 2]; vtk = pfx + "Vh%d" % (h % 2)
        P.dma("sync", ktk, kt[:, :], A["kT"][h * 128:(h + 1) * 128, :], reads=["kT"], writes=[ktk])
        P.dma("sync", qtk, qt[:, :], A["qT"][h * 128:(h + 1) * 128, :], reads=["qT"], writes=[qtk])
        for g in range(4):
            P.dma("sync", vtk, vt[:, g * 16:(g + 1) * 16, :], vv[:, g * 16:(g + 1) * 16, h * 128:(h + 1) * 128], reads=["v"], writes=[vtk])
        for J in range(DBG.get("J", TL // 512)):
            ns = 4 * J + 4

            def qk_b(m, s_):
                col = 2 * h + m
                k = s_ - 4 * J
                c0 = max(0, k) * 128
                near = k >= -1
                ps = ps_s[st["sc"] % 3]; psk = pfx + "pss%d" % (st["sc"] % 3)
                st["sc"] += 1
                P.add("tensor", (lambda e, ps=ps, kt=kt, qt=qt, m=m, s_=s_, c0=c0, J=J, near=near: e.matmul(
                    ps[:, c0:512], kt[64 * m:64 * m + 64, s_ * 128:(s_ + 1) * 128],
                    qt[64 * m:64 * m + 64, J * 512 + c0:J * 512 + 512], start=True, stop=(not near))),
                    reads=[ktk, qtk], writes=[psk], pe_attach=(s_ > 0))
                if near:
                    if k == -1:
                        P.add("tensor", (lambda e, ps=ps, col=col: e.matmul(
                            ps[:, 0:128], idf[:, :], BT[:, col, 128:256], start=False, stop=True)),
                            reads=[pfx + "idf", pfx + "BT"], writes=[psk])
                    else:
                        w = min(256, 512 - c0)
                        P.add("tensor", (lambda e, ps=ps, col=col, c0=c0, w=w: e.matmul(
                            ps[:, c0:c0 + w], idf[:, :], BT[:, col, 0:w], start=False, stop=True)),
                            reads=[pfx + "idf", pfx + "BT"], writes=[psk])
                pt = PT[st["pc"] % 6]; ptk = pfx + "PT%d" % (st["pc"] % 6)
                st["pc"] += 1
                P.add("scalar", (lambda e, ps=ps, pt=pt, c0=c0, col=col: e.activation(
                    out=pt[:, c0:512], in_=ps[:, c0:512], func=AF.Exp, bias=b31[:, col:col + 1], scale=1.0)),
                    reads=[psk, pfx + "b31"], writes=[ptk])
                return pt, ptk, c0

            def pv_b(m, s_, pre):
                pt, ptk, c0 = pre
                ao = acc_o[m]; aok = pfx + "acco%d" % m
                as_ = acc_s[m]; ask = pfx + "accs%d" % m
                P.add("tensor", (lambda e, ao=ao, vt=vt, pt=pt, s_=s_, c0=c0, ns=ns: e.matmul(
                    ao[:, c0:512], vt[:, s_, :], pt[:, c0:512], start=(s_ == 0), stop=(s_ == ns - 1))),
                    reads=[vtk, ptk], writes=[aok], pe_attach=(s_ > 0))
                P.add("tensor", (lambda e, as_=as_, pt=pt, s_=s_, c0=c0, ns=ns: e.matmul(
                    as_[:, c0:512], onesb[:, :], pt[:, c0:512], start=(s_ == 0), stop=(s_ == ns - 1))),
                    reads=[pfx + "onesb", ptk], writes=[ask])

            pairs = [(m, s_) for m in range(2) for s_ in range(ns)]
            LA = 2
            queue = [qk_b(*pairs[i_]) for i_ in range(min(LA, len(pairs)))]
            for i_, (m, s_) in enumerate(pairs):
                if i_ + LA < len(pairs):
                    queue.append(qk_b(*pairs[i_ + LA]))
                pv_b(m, s_, queue.pop(0))
            P.add("vector", lambda e: e.reciprocal(out=r0t[:, :], in_=acc_s[0][:, :]), reads=[pfx + "accs0"], writes=[pfx + "r0t"])
            P.add("vector", lambda e: e.reciprocal(out=r1t[:, :], in_=acc_s[1][:, :]), reads=[pfx + "accs1"], writes=[pfx + "r1t"])
            P.add("vector", lambda e: e.tensor_tensor(out=o0t[:, :], in0=acc_o[0][:, :], in1=r0t[:, :], op=ALU.mult),
                  reads=[pfx + "acco0", pfx + "r0t"], writes=[pfx + "o0t"])
            P.add("vector", lambda e: e.tensor_tensor(out=o1t[:, :], in0=acc_o[1][:, :], in1=r1t[:, :], op=ALU.mult),
                  reads=[pfx + "acco1", pfx + "r1t"], writes=[pfx + "o1t"])
            P.add("vector", lambda e: e.scalar_tensor_tensor(out=o0t[:, :], in0=o1t[:, :], scalar=neglam[:, 0:1], in1=o0t[:, :],
                                                            op0=ALU.mult, op1=ALU.add),
                  reads=[pfx + "o1t", pfx + "neglam"], writes=[pfx + "o0t"])
            P.add("gpsimd", lambda e: e.tensor_tensor(out=sqt[:, :], in0=o0t[:, :], in1=o0t[:, :], op=ALU.mult),
                  reads=[pfx + "o0t"], writes=[pfx + "sqt"])
            P.add("tensor", lambda e: e.matmul(ps_ms[:, :], onesf[:, :], sqt[:, :], start=True, stop=True),
                  reads=[pfx + "onesf", pfx + "sqt"], writes=[pfx + "psms"])
            P.add("scalar", lambda e: e.activation(out=r0t[:, :], in_=ps_ms[:, :], func=AF.Sqrt, bias=P.epsc[:, 0:1], scale=1.0 / 128.0),
                  reads=[pfx + "psms", "epsc"], writes=[pfx + "r0t"])
            P.add("vector", lambda e: e.reciprocal(out=r1t[:, :], in_=r0t[:, :]), reads=[pfx + "r0t"], writes=[pfx + "r1t"])
            P.add("vector", lambda e: e.tensor_tensor(out=o0t[:, :], in0=o0t[:, :], in1=r1t[:, :], op=ALU.mult),
                  reads=[pfx + "r1t"], writes=[pfx + "o0t"])
            os_ = oTs[oc % 2]; osk = pfx + "oTs%d" % (oc % 2)
            oc += 1
            P.add("vector", (lambda e, os_=os_: e.tensor_scalar(out=os_[:, :], in0=o0t[:, :], scalar1=sg[:, 0:1], scalar2=None,
                                                               op0=ALU.mult)), reads=[pfx + "o0t", pfx + "sg"], writes=[osk])
            P.dma("gpsimd", osk + "o", A["oT"][h * 128:(h + 1) * 128, J * 512:(J + 1) * 512], os_[:, :], reads=[osk], writes=["oT"])


def phase_aproj(P, A, l, xin, xin_key, pfx):
    j = l // 2
    idf, idb = make_ident(P, pfx, A["ident"])
    modc = load_mod_cols(P, pfx, A["modT"], l, 0)
    wb = P.sb(pfx + "wb", [128, 8, A_IN], BF16)
    wv = A["a_w_in"][j].rearrange("(c p) f -> p c f", p=128)
    for c in range(8):
        P.dma("gpsimd", pfx + "w", wb[:, c, :], wv[:, c, :], writes=[pfx + "wb"])
    wuk = P.sb(pfx + "wuk", [128, 8, 256], BF16)
    P.dma("gpsimd", pfx + "wuk", wuk[:, :, :], A["a_w_uk"][j].rearrange("(hp two) d r -> (two d) hp r", two=2),
          writes=[pfx + "wuk"])
    wuvb = P.sb(pfx + "wuvb", [128, 2, 16, 64], BF16)
    for rc in range(2):
        P.dma("gpsimd", pfx + "wuvb", wuvb[:, rc, :, :], A["a_w_uv"][j][:, rc * 128:(rc + 1) * 128, :].rearrange("h p e -> p h e"),
              writes=[pfx + "wuvb"])
    kvn = P.sb(pfx + "kvn", [128, 256], F32)
    P.dma("sync", pfx + "kvn", kvn[:, :], A["a_kv_norm"][j].partition_broadcast(128), writes=[pfx + "kvn"])
    xtr = [P.sb(pfx + "xtr%d" % i, [128, 1024], F32) for i in range(6)]
    hT = P.sb(pfx + "hT", [128, 8, 512], BF16)
    stg = [P.sb(pfx + "stg%d" % i, [128, 512], BF16) for i in range(4)]
    cst = [P.sb(pfx + "cst%d" % i, [128, 256], BF16) for i in range(2)]
    iwst = [P.sb(pfx + "iwst%d" % i, [128, 8], F32) for i in range(2)]
    cTs = [P.sb(pfx + "cTs%d" % i, [128, 2, 512], BF16) for i in range(2)]
    vst = [P.sb(pfx + "vst%d" % i, [128, 16, 65], BF16) for i in range(2)]
    for i in range(2):
        P.add("vector", (lambda e, i=i: e.memset(vst[i][:, :, 64:65], 1.0)), writes=[pfx + "vst%d" % i])
    wukT = P.sb(pfx + "wukT", [128, 2, 1024], BF16)
    junk = P.sb(pfx + "junk", [128, 256], F32)
    sm = [P.sb(pfx + "sm%d" % i, [128, 3], F32) for i in range(2)]
    ps_tr = [(P.ps(pfx + "ptr%d" % i, [128, 512], F32), pfx + "ptr%d" % i) for i in range(2)]
    ps_o = [P.ps(pfx + "po%d" % i, [128, 512], F32) for i in range(3)]
    ps_c = [P.ps(pfx + "pc%d" % i, [128, 264], F32) for i in range(2)]
    ps_t = P.ps(pfx + "pt", [128, 2, 128], BF16)
    cnt = [0]
    st8 = {"xc": 0, "oc": 0, "sc": 0, "bc": 0, "vc": 0}
    for hp in range(8):
        for rc in range(2):
            P.add("tensor", (lambda e, hp=hp, rc=rc: e.transpose(out=ps_t[:, rc, :], in_=wuk[:, hp, rc * 128:(rc + 1) * 128],
                                                                identity=idb[:, :])), reads=[pfx + "wuk", pfx + "idb"], writes=[pfx + "pt"])
        P.add("scalar", (lambda e, hp=hp: e.activation(out=wukT[:, :, hp * 128:(hp + 1) * 128], in_=ps_t[:, :, :], func=AF.Copy)),
              reads=[pfx + "pt"], writes=[pfx + "wukT"])

    def evac(po_ap, pok, width, dst_ap, dkey, parts=128, scale=None):
        i = st8["sc"] % 4
        st8["sc"] += 1
        sg = stg[i]
        sgk = pfx + "stg%d" % i
        if scale is not None or st8["sc"] % 2 == 0:
            P.add("scalar", (lambda e: e.activation(out=sg[0:parts, 0:width], in_=po_ap, func=AF.Copy,
                                                    scale=(1.0 if scale is None else scale))), reads=[pok], writes=[sgk])
        else:
            P.add("vector", (lambda e: e.tensor_copy(out=sg[0:parts, 0:width], in_=po_ap)), reads=[pok], writes=[sgk])
        P.dma("gpsimd", sgk + "o", dst_ap, sg[0:parts, 0:width], reads=[sgk], writes=[dkey])

    def next_po():
        po = ps_o[st8["oc"] % 3]
        pok = pfx + "po%d" % (st8["oc"] % 3)
        st8["oc"] += 1
        return po, pok

    for t in range(DBG.get("T", TL // 512)):
        tiles = []
        for b in range(4):
            i = st8["xc"] % 6
            st8["xc"] += 1
            r0 = t * 512 + b * 128
            P.dma("sync", pfx + "xtr%d" % i, xtr[i][:, :], xin[r0:r0 + 128, :], reads=[xin_key], writes=[pfx + "xtr%d" % i])
            tiles.append((xtr[i], pfx + "xtr%d" % i))
        emit_hT(P, pfx, tiles, 4, idf, modc, hT, 0, ps_tr, cnt)
        tsl = slice(t * 512, (t + 1) * 512)

        def proj_fm(col0, ncol):
            po, pok = next_po()
            for c in range(8):
                P.add("tensor", (lambda e, po=po, c=c: e.matmul(po[0:ncol, :], wb[:, c, col0:col0 + ncol], hT[:, c, :],
                                                               start=(c == 0), stop=(c == 7))),
                      reads=[pfx + "wb", pfx + "hT"], writes=[pok])
            return po, pok

        for fo in range(8):
            po, pok = proj_fm(fo * 128, 128)
            evac(po[:, :], pok, 512, A["qT"][fo * 128:(fo + 1) * 128, tsl], "qT", scale=0.125)
        for fo in range(4):
            po, pok = proj_fm(1280 + fo * 128, 128)
            evac(po[:, :], pok, 512, A["iqT"][fo * 128:(fo + 1) * 128, tsl], "iqT")
        po, pok = proj_fm(1792, 64)
        evac(po[0:64, :], pok, 512, A["ikT"][0:64, tsl], "ikT", parts=64)
        ct = cTs[t % 2]
        ctk = pfx + "cTs%d" % (t % 2)
        for b in range(4):
            bi = st8["bc"]
            st8["bc"] += 1
            pc = ps_c[bi % 2]
            pck = pfx + "pc%d" % (bi % 2)
            for c in range(8):
                P.add("tensor", (lambda e, pc=pc, c=c, b=b: e.matmul(pc[:, 0:256], hT[:, c, b * 128:(b + 1) * 128],
                                                                    wb[:, c, 1024:1280], start=(c == 0), stop=(c == 7))),
                      reads=[pfx + "wb", pfx + "hT"], writes=[pck])
            for c in range(8):
                P.add("tensor", (lambda e, pc=pc, c=c, b=b: e.matmul(pc[:, 256:264], hT[:, c, b * 128:(b + 1) * 128],
                                                                    wb[:, c, 1856:1864], start=(c == 0), stop=(c == 7))),
                      reads=[pfx + "wb", pfx + "hT"], writes=[pck])
            s3 = sm[bi % 2]
            s3k = pfx + "sm%d" % (bi % 2)
            P.add("scalar", (lambda e, pc=pc, s3=s3: e.activation(out=junk[:, :], in_=pc[:, 0:256], func=AF.Square,
                                                                 accum_out=s3[:, 0:1])), reads=[pck], writes=[pfx + "junk", s3k],
                  noattach=True)
            P.add("scalar", (lambda e, s3=s3: e.activation(out=s3[:, 1:2], in_=s3[:, 0:1], func=AF.Sqrt, bias=P.epsc[:, 0:1],
                                                          scale=1.0 / 256.0)), reads=[s3k, "epsc"], writes=[s3k])
            P.add("vector", (lambda e, s3=s3: e.reciprocal(out=s3[:, 2:3], in_=s3[:, 1:2])), reads=[s3k], writes=[s3k])
            cs = cst[bi % 2]
            csk = pfx + "cst%d" % (bi % 2)
            P.add("vector", (lambda e, pc=pc, s3=s3, cs=cs: e.scalar_tensor_tensor(
                out=cs[:, :], in0=pc[:, 0:256], scalar=s3[:, 2:3], in1=kvn[:, :], op0=ALU.mult, op1=ALU.mult)),
                reads=[pck, s3k, pfx + "kvn"], writes=[csk])
            r0 = t * 512 + b * 128
            iws = iwst[bi % 2]
            iwk = pfx + "iwst%d" % (bi % 2)
            P.add("vector", (lambda e, pc=pc, iws=iws: e.tensor_scalar(out=iws[:, :], in0=pc[:, 256:264],
                                                                      scalar1=float(8 ** -0.5 * 64 ** -0.5), scalar2=None,
                                                                      op0=ALU.mult)), reads=[pck], writes=[iwk])
            P.dma("gpsimd", iwk + "o", A["iw"][r0:r0 + 128, :], iws[:, :], reads=[iwk], writes=["iw"])
            for rc in range(2):
                P.add("tensor", (lambda e, cs=cs, rc=rc: e.transpose(out=ps_t[:, rc, :], in_=cs[:, rc * 128:(rc + 1) * 128],
                                                                    identity=idb[:, :])), reads=[csk, pfx + "idb"], writes=[pfx + "pt"])
            P.add("scalar", (lambda e, ct=ct, b=b: e.activation(out=ct[:, :, b * 128:(b + 1) * 128], in_=ps_t[:, :, :], func=AF.Copy)),
                  reads=[pfx + "pt"], writes=[ctk])
        for hp in range(8):
            po, pok = next_po()
            for rc in range(2):
                P.add("tensor", (lambda e, po=po, hp=hp, rc=rc, ct=ct: e.matmul(po[:, :], wukT[:, rc, hp * 128:(hp + 1) * 128],
                                                                               ct[:, rc, :], start=(rc == 0), stop=(rc == 1))),
                      reads=[pfx + "wukT", ctk], writes=[pok])
            evac(po[:, :], pok, 512, A["kT"][hp * 128:(hp + 1) * 128, tsl], "kT")
        for b in range(4):
            vs = vst[st8["vc"] % 2]
            vsk = pfx + "vst%d" % (st8["vc"] % 2)
            st8["vc"] += 1
            for half in range(2):
                po, pok = next_po()
                for rc in range(2):
                    P.add("tensor", (lambda e, po=po, b=b, rc=rc, half=half, ct=ct: e.matmul(
                        po[:, :], ct[:, rc, b * 128:(b + 1) * 128], wuvb[:, rc, half * 8:(half + 1) * 8, :],
                        start=(rc == 0), stop=(rc == 1))), reads=[pfx + "wuvb", ctk], writes=[pok])
                if half == 0:
                    P.add("scalar", (lambda e, po=po, vs=vs, half=half: e.activation(
                        out=vs[:, half * 8:(half + 1) * 8, 0:64], in_=po[:, :].rearrange("p (h e) -> p h e", h=8), func=AF.Copy)),
                        reads=[pok], writes=[vsk])
                else:
                    P.add("vector", (lambda e, po=po, vs=vs, half=half: e.tensor_copy(
                        out=vs[:, half * 8:(half + 1) * 8, 0:64], in_=po[:, :].rearrange("p (h e) -> p h e", h=8))),
                        reads=[pok], writes=[vsk])
            r0 = t * 512 + b * 128
            P.dma("gpsimd", vsk + "o", A["vaug"][r0:r0 + 128, :], vs[:, :, :].rearrange("p h e -> p (h e)"), reads=[vsk], writes=["vaug"])


N_IT = 20
TOPK = 256


def phase_aattn(P, A, l, pfx):
    j = l // 2
    idf, idb = make_ident(P, pfx, A["ident"])
    onesb = P.sb(pfx + "onesb", [128, 128], BF16)
    P.add("vector", lambda e: e.memset(onesb[:, :], 1.0), writes=[pfx + "onesb"])
    onesf = P.sb(pfx + "onesf", [128, 128], F32)
    P.add("vector", lambda e: e.memset(onesf[:, :], 1.0), writes=[pfx + "onesf"])
    b31 = P.sb(pfx + "b31", [128, 16], F32)
    P.dma("sync", pfx + "b31", b31[:, :], A["rel_bias"][31, :].partition_broadcast(128), writes=[pfx + "b31"])
    BTf = P.sb(pfx + "BTf", [128, 256], F32)
    BT = P.sb(pfx + "BT", [128, 16, 256], BF16)
    for c in range(16):
        P.dma("sync", pfx + "BTf", BTf[:, :], A["biasT"][c], writes=[pfx + "BTf"])
        P.add("vector", (lambda e, c=c: e.tensor_scalar(out=BT[:, c, :], in0=BTf[:, :], scalar1=b31[:, c:c + 1],
                                                        scalar2=None, op0=ALU.subtract)),
              reads=[pfx + "BTf", pfx + "b31"], writes=[pfx + "BT"])
    cmask = P.sb(pfx + "cmask", [128, 128], F32)
    P.dma("sync", pfx + "cmask", cmask[:, :], A["cmask"], writes=[pfx + "cmask"])
    pw = P.sb(pfx + "pw", [128, N_IT + 1], F32)
    for i in range(N_IT + 1):
        P.add("vector", (lambda e, i=i: e.memset(pw[:, i:i + 1], float(2.0 ** -i))), writes=[pfx + "pw"])
    ik2 = P.sb(pfx + "ik2", [128, TL // 2], BF16)
    P.dma("sync", pfx + "ik2", ik2[0:64, :], A["ikT"][:, 0:TL // 2], reads=["ikT"], writes=[pfx + "ik2"])
    P.dma("sync", pfx + "ik2", ik2[64:128, :], A["ikT"][:, TL // 2:TL], reads=["ikT"], writes=[pfx + "ik2"])
    kTp = [P.sb(pfx + "kTp%d" % i, [128, TL], BF16) for i in range(2)]
    vp = [P.sb(pfx + "vp%d" % i, [128, NLB, 2, 65], BF16) for i in range(2)]
    qTp = [P.sb(pfx + "qTp%d" % i, [128, 256], BF16) for i in range(2)]
    otile = [P.sb(pfx + "otile%d" % i, [128, 1024], BF16) for i in range(2)]
    oTs = P.sb(pfx + "oTs", [128, 8, 256], BF16)
    rcp = P.sb(pfx + "rcp", [128, 4], F32)
    for i in range(2):
        P.add("gpsimd", (lambda e, i=i: e.memset(otile[i][:, :], 0.0)), writes=[pfx + "otile%d" % i])
    scores = P.sb(pfx + "scores", [128, TL], F32)
    junk = P.sb(pfx + "junk", [128, TL], U8)
    NM = P.sb(pfx + "NM", [128, NLB, 256], BF16)
    iqt = [P.sb(pfx + "iqt%d" % i, [128, 8, 128], BF16) for i in range(2)]
    iwt = [P.sb(pfx + "iwt%d" % i, [128, 8], F32) for i in range(2)]
    rts = [P.sb(pfx + "rt%d" % i, [128, 512], F32) for i in range(3)]
    PT = [P.sb(pfx + "PT%d" % i, [128, 256], BF16) for i in range(6)]
    bs = P.sb(pfx + "bs", [128, 8 + N_IT + 1], F32)
    nd = P.sb(pfx + "nd", [128, 128], F32)
    ps_i = [P.ps(pfx + "pi%d" % i, [128, 512], F32) for i in range(2)]
    dsc = P.sb(pfx + "dsc", [128, TL], BF16)
    ps_s = [P.ps(pfx + "pss%d" % i, [128, 256], F32) for i in range(3)]
    acc_o = [P.ps(pfx + "acco%d" % i, [128, 65], F32) for i in range(2)]
    ps_tt = P.ps(pfx + "ptt", [128, 2, 128], BF16)
    iqv = A["iqT"].rearrange("(h d) t -> d h t", d=64)
    vav = A["vaug"].rearrange("(blk p) (h e) -> p blk h e", p=128, e=65)
    st8 = {"ic": 0, "rc": 0, "mc": 0, "sc": 0, "pc": 0, "oc": 0}
    skey = pfx + "scores"
    bkey = pfx + "bs"

    def idx(qb):
        nk = (qb + 1) * 128
        it = iqt[qb % 2]; itk = pfx + "iqt%d" % (qb % 2)
        P.dma("sync", itk, it[0:64, :, :], iqv[:, :, qb * 128:(qb + 1) * 128], reads=["iqT"], writes=[itk])
        P.dma("sync", itk, it[64:128, :, :], iqv[:, :, qb * 128:(qb + 1) * 128], reads=["iqT"], writes=[itk])
        wt = iwt[qb % 2]; wtk = pfx + "iwt%d" % (qb % 2)
        P.dma("sync", wtk, wt[:, :], A["iw"][qb * 128:(qb + 1) * 128, :], reads=["iw"], writes=[wtk])
        nst = (nk + 511) // 512
        for st in range(nst):
            wd = min(512, nk - st * 512)
            half = (st * 512) // (TL // 2)
            kc0 = st * 512 - half * (TL // 2)
            for h in range(8):
                pi = ps_i[st8["ic"] % 2]; pik = pfx + "pi%d" % (st8["ic"] % 2)
                st8["ic"] += 1
                P.add("tensor", (lambda e, pi=pi, it=it, h=h, half=half, kc0=kc0, wd=wd: e.matmul(
                    pi[:, 0:wd], it[64 * half:64 * half + 64, h, :], ik2[64 * half:64 * half + 64, kc0:kc0 + wd],
                    start=True, stop=True)), reads=[itk, pfx + "ik2"], writes=[pik])
                rt = rts[st8["rc"] % 3]; rk = pfx + "rt%d" % (st8["rc"] % 3)
                st8["rc"] += 1
                P.add("scalar", (lambda e, pi=pi, rt=rt, wd=wd: e.activation(out=rt[:, 0:wd], in_=pi[:, 0:wd], func=AF.Relu)),
                      reads=[pik], writes=[rk])
                sl = slice(st * 512, st * 512 + wd)
                if h == 0:
                    P.add("vector", (lambda e, rt=rt, wt=wt, sl=sl, wd=wd: e.tensor_scalar(
                        out=scores[:, sl], in0=rt[:, 0:wd], scalar1=wt[:, 0:1], scalar2=None, op0=ALU.mult)),
                        reads=[rk, wtk], writes=[skey])
                else:
                    P.add("vector", (lambda e, rt=rt, wt=wt, sl=sl, wd=wd, h=h: e.scalar_tensor_tensor(
                        out=scores[:, sl], in0=rt[:, 0:wd], scalar=wt[:, h:h + 1], in1=scores[:, sl],
                        op0=ALU.mult, op1=ALU.add)), reads=[rk, wtk, skey], writes=[skey])
        V = lambda fn, rd, wr, na=False: P.add("vector", fn, reads=rd, writes=wr, noattach=na)
        V(lambda e: e.tensor_reduce(out=bs[:, 0:1], in_=scores[:, 0:nk], axis=AX.X, op=ALU.max, apply_absolute_value=True),
          [skey], [bkey])
        V(lambda e: e.tensor_scalar(out=bs[:, 1:2], in0=bs[:, 0:1], scalar1=1.0, scalar2=None, op0=ALU.add), [bkey], [bkey])
        V(lambda e: e.tensor_scalar(out=bs[:, 8:8 + N_IT + 1], in0=pw[:, :], scalar1=bs[:, 1:2], scalar2=None, op0=ALU.mult),
          [bkey, pfx + "pw"], [bkey])
        V(lambda e: e.tensor_tensor(out=scores[:, qb * 128:(qb + 1) * 128], in0=scores[:, qb * 128:(qb + 1) * 128],
                                    in1=cmask[:, :], op=ALU.add), [skey, pfx + "cmask"], [skey])
        V(lambda e: e.tensor_scalar(out=bs[:, 2:3], in0=bs[:, 1:2], scalar1=-1.0, scalar2=None, op0=ALU.mult), [bkey], [bkey])
        V(lambda e: e.tensor_tensor(out=bs[:, 3:4], in0=bs[:, 2:3], in1=bs[:, 8:9], op=ALU.add), [bkey], [bkey])
        def one_iter(i):
            V(lambda e: e.tensor_scalar(out=junk[:, 0:nk], in0=scores[:, 0:nk], scalar1=bs[:, 3:4], scalar2=None,
                                        op0=ALU.is_ge, op1=ALU.add, accum_out=bs[:, 4:5]), [skey, bkey], [pfx + "junk", bkey], True)
            V((lambda e, i=i: e.tensor_scalar(out=bs[:, 5:6], in0=bs[:, 4:5], scalar1=float(TOPK), scalar2=bs[:, 8 + i:9 + i],
                                              op0=ALU.is_ge, op1=ALU.mult)), [bkey], [bkey])
            V(lambda e: e.tensor_tensor(out=bs[:, 2:3], in0=bs[:, 2:3], in1=bs[:, 5:6], op=ALU.add), [bkey], [bkey])
            V((lambda e, i=i: e.tensor_tensor(out=bs[:, 3:4], in0=bs[:, 2:3], in1=bs[:, 9 + i:10 + i], op=ALU.add)), [bkey], [bkey])
            if i == N_IT - 1:
                V(lambda e: e.tensor_scalar(out=nd[:, :], in0=idf[:, :], scalar1=bs[:, 2:3], scalar2=-1.0, op0=ALU.mult,
                                            op1=ALU.mult), [bkey, pfx + "idf"], [pfx + "nd"])
        return [(lambda i=i: one_iter(i)) for i in range(N_IT)]

    def nm_gen(qb):
        jj = qb % 2
        nk = (qb + 1) * 128
        P.add("vector", lambda e: e.tensor_scalar(out=dsc[:, 0:nk], in0=scores[:, 0:nk], scalar1=bs[:, 2:3], scalar2=None,
                                                 op0=ALU.subtract), reads=[skey, bkey], writes=[pfx + "dsc"])
        for sb in range(qb + 1):
            mi = st8["mc"] % 2
            st8["mc"] += 1
            pmk = pfx + "ptt"
            P.add("tensor", (lambda e, mi=mi, sb=sb: e.transpose(out=ps_tt[:, mi, :], in_=dsc[:, sb * 128:(sb + 1) * 128],
                                                                identity=idb[:, :])), reads=[pfx + "dsc", pfx + "idb"], writes=[pmk])
            P.add("vector", (lambda e, mi=mi, sb=sb, jj=jj: e.tensor_scalar(
                out=NM[:, sb, jj * 128:(jj + 1) * 128], in0=ps_tt[:, mi, :], scalar1=0.0, scalar2=float(NEGM),
                op0=ALU.is_lt, op1=ALU.mult)), reads=[pmk], writes=[pfx + "NM"])

    def attn(J, steps=()):
        steps = list(steps)
        ns = 2 * J + 2
        nk = ns * 128
        nh = DBG.get("aheads", 16)

        def load_pair(hp):
            kt = kTp[hp % 2]; ktk = pfx + "kTp%d" % (hp % 2)
            vt = vp[hp % 2]; vtk = pfx + "vp%d" % (hp % 2)
            qt = qTp[hp % 2]; qtk = pfx + "qTp%d" % (hp % 2)
            P.dma("sync", ktk, kt[:, 0:nk], A["kT"][hp * 128:(hp + 1) * 128, 0:nk], reads=["kT"], writes=[ktk])
            P.dma("sync", vtk, vt[:, 0:ns, :, :], vav[:, 0:ns, 2 * hp:2 * hp + 2, :], reads=["vaug"], writes=[vtk])
            P.dma("sync", qtk, qt[:, :], A["qT"][hp * 128:(hp + 1) * 128, J * 256:(J + 1) * 256], reads=["qT"], writes=[qtk])

        def qk(h, s):
            hp, hh = h // 2, h % 2
            kt = kTp[hp % 2]; ktk = pfx + "kTp%d" % (hp % 2)
            qt = qTp[hp % 2]; qtk = pfx + "qTp%d" % (hp % 2)
            k = s - 2 * J
            c0 = max(0, k) * 128
            near = k >= -1
            si = st8["sc"] % 3
            st8["sc"] += 1
            psk = pfx + "pss%d" % si
            P.add("tensor", (lambda e, si=si, s=s, hh=hh, c0=c0, kt=kt, qt=qt: e.matmul(
                ps_s[si][:, c0:256], kt[64 * hh:64 * hh + 64, s * 128:(s + 1) * 128], qt[64 * hh:64 * hh + 64, c0:256],
                start=True, stop=False)), reads=[ktk, qtk], writes=[psk], pe_attach=(s > 0))
            P.add("tensor", (lambda e, si=si, s=s, c0=c0, near=near: e.matmul(
                ps_s[si][:, c0:256], idb[:, :], NM[:, s, c0:256], start=False, stop=(not near))),
                reads=[pfx + "idb", pfx + "NM"], writes=[psk])
            if near:
                if k == -1:
                    P.add("tensor", (lambda e, si=si, h=h: e.matmul(ps_s[si][:, 0:128], idb[:, :], BT[:, h, 128:256],
                                                                   start=False, stop=True)),
                          reads=[pfx + "idb", pfx + "BT"], writes=[psk])
                else:
                    w = 256 - c0
                    P.add("tensor", (lambda e, si=si, h=h, c0=c0, w=w: e.matmul(ps_s[si][:, c0:c0 + w], idb[:, :], BT[:, h, 0:w],
                                                                               start=False, stop=True)),
                          reads=[pfx + "idb", pfx + "BT"], writes=[psk])
            pt = PT[st8["pc"] % 6]; ptk = pfx + "PT%d" % (st8["pc"] % 6)
            st8["pc"] += 1
            P.add("scalar", (lambda e, si=si, pt=pt, c0=c0, h=h: e.activation(
                out=pt[:, c0:256], in_=ps_s[si][:, c0:256], func=AF.Exp, bias=b31[:, h:h + 1], scale=1.0)),
                reads=[psk, pfx + "b31"], writes=[ptk])
            return pt, ptk, c0

        def pv(h, s, pre):
            pt, ptk, c0 = pre
            hp, hh = h // 2, h % 2
            vt = vp[hp % 2]; vtk = pfx + "vp%d" % (hp % 2)
            for jj in range(2):
                if jj * 128 < c0:
                    continue
                last = (s == ns - 1) if jj == 1 else (s == ns - 2)
                P.add("tensor", (lambda e, pt=pt, jj=jj, s=s, hh=hh, vt=vt, last=last: e.matmul(
                    acc_o[jj][:, :], pt[:, jj * 128:(jj + 1) * 128], vt[:, s, hh, :],
                    start=(s == 0), stop=last)), reads=[vtk, ptk], writes=[pfx + "acco%d" % jj])

        def finish_head(h):
            for jj in range(2):
                P.add("vector", (lambda e, jj=jj: e.reciprocal(out=rcp[:, jj:jj + 1], in_=acc_o[jj][:, 64:65])),
                      reads=[pfx + "acco%d" % jj], writes=[pfx + "rcp%d" % jj])
                P.add("vector", (lambda e, jj=jj, h=h: e.tensor_scalar(out=otile[jj][:, h * 64:(h + 1) * 64], in0=acc_o[jj][:, 0:64],
                                                                      scalar1=rcp[:, jj:jj + 1], scalar2=None, op0=ALU.mult)),
                      reads=[pfx + "acco%d" % jj, pfx + "rcp%d" % jj], writes=[pfx + "otile%d" % jj])
            nstep = 2 if h < 4 else 1
            if h == nh - 1:
                nstep = len(steps)
            for _ in range(min(nstep, len(steps))):
                steps.pop(0)()

        load_pair(0)
        pairs = [(h, s) for h in range(nh) for s in range(ns)]
        LA = 2
        queue = [qk(*pairs[i]) for i in range(min(LA, len(pairs)))]
        for i, (h, s) in enumerate(pairs):
            if s == 0 and h % 2 == 0 and h + 2 < nh:
                load_pair(h // 2 + 1)
            if i + LA < len(pairs):
                queue.append(qk(*pairs[i + LA]))
            pv(h, s, queue.pop(0))
            if s == ns - 1:
                finish_head(h)
        for jj in range(2):
            for c in range(8):
                P.add("tensor", (lambda e, jj=jj, c=c: e.transpose(out=ps_tt[:, c % 2, :], in_=otile[jj][:, c * 128:(c + 1) * 128],
                                                                  identity=idb[:, :])),
                      reads=[pfx + "otile%d" % jj, pfx + "idb"], writes=[pfx + "ptt"])
                if c % 2 == 1:
                    P.add("scalar", (lambda e, jj=jj, c=c: e.activation(out=oTs[:, c - 1:c + 1, jj * 128:(jj + 1) * 128],
                                                                       in_=ps_tt[:, :, :], func=AF.Copy)),
                          reads=[pfx + "ptt"], writes=[pfx + "oTs"])
        P.dma("gpsimd", pfx + "oTso", A["oT"].rearrange("(c p) t -> p c t", p=128)[:, :, J * 256:(J + 1) * 256], oTs[:, :, :],
              reads=[pfx + "oTs"], writes=["oT"])

    NJ = DBG.get("AJ", TL // 256)
    for st_ in idx(0):
        st_()
    nm_gen(0)
    for st_ in idx(1):
        st_()
    nm_gen(1)
    for J in range(NJ):
        steps = idx(2 * J + 2) if J + 1 < NJ else []
        attn(J, steps)
        if J + 1 < NJ:
            nm_gen(2 * J + 2)
            for st_ in idx(2 * J + 3):
                st_()
            nm_gen(2 * J + 3)


def _new_nc():
    return bass.Bass("TRN2", target_bir_lowering=False)


def build_single(phase_name, l=0):
    nc = _new_nc()
    A = {}
    ins, outs = [], []

    def din(name, shape, dt=F32):
        A[name] = nc.dram_tensor(name, list(shape), dt, kind="ExternalInput").ap()
        ins.append(name)

    def dout(name, shape, dt=F32):
        A[name] = nc.dram_tensor(name, list(shape), dt, kind="ExternalOutput").ap()
        outs.append(name)

    with ExitStack() as es:
        P = Prog(nc, es)
        make_epsc(P)
        if phase_name == "mod":
            din("ccol", [128, 8]); din("ada_w", [DEPTH, D, 6 * D]); din("ada_b", [DEPTH, 6 * D]); din("ada_bT", [128, 4 * 48])
            dout("modT", [128, 4 * 48]); dout("modrow", [DEPTH, 2, D])
            phase_mod(P, A)
            P.final_wait()
        elif phase_name == "mlp":
            din("ident", [128, 128]); din("modT", [128, 4 * 48]); din("modrow", [DEPTH, 2, D])
            din("ln_g", [DEPTH, 2, D]); din("ln_b", [DEPTH, 2, D])
            din("mlp_w1", [DEPTH, D, DFF]); din("mlp_w2", [DEPTH, DFF, D]); din("xin", [TL, D])
            dout("xout", [TL, D])
            phase_mlp(P, A, l, A["xin"], "xin", A["xout"], "xout", "ml_")
            P.final_wait()
        elif phase_name == "bproj":
            din("ident", [128, 128]); din("modT", [128, 4 * 48]); din("b_w_in", [2, D, 3072]); din("xin", [TL, D])
            dout("qT", [D, TL], BF16); dout("kT", [D, TL], BF16); dout("v", [TL, D], BF16)
            phase_bproj(P, A, l, A["xin"], "xin", "bp_")
            P.final_wait()
        elif phase_name == "battn":
            din("ident", [128, 128]); din("rel_bias", [32, 16]); din("biasT", [16, 128, 256]); din("b_lambda", [2, 4, 64])
            din("b_subln", [2, 128]); din("qT", [D, TL], BF16); din("kT", [D, TL], BF16); din("v", [TL, D], BF16)
            dout("oT", [D, TL], BF16)
            phase_battn(P, A, l, "ba_")
            P.final_wait()
        elif phase_name == "aproj":
            din("ident", [128, 128]); din("modT", [128, 4 * 48]); din("a_w_in", [2, D, A_IN]); din("a_w_uk", [2, 16, 64, 256])
            din("a_kv_norm", [2, 256]); din("xin", [TL, D])
            din("a_w_uv", [2, 16, 256, 64])
            dout("qT", [D, TL], BF16); dout("kT", [D, TL], BF16); dout("vaug", [TL, 16 * 65], BF16)
            dout("iqT", [512, TL], BF16); dout("ikT", [64, TL], BF16); dout("iw", [TL, 8], F32)
            phase_aproj(P, A, l, A["xin"], "xin", "ap_")
            P.final_wait()
        elif phase_name == "aattn":
            din("ident", [128, 128]); din("rel_bias", [32, 16]); din("biasT", [16, 128, 256]); din("cmask", [128, 128])
            din("qT", [D, TL], BF16); din("kT", [D, TL], BF16); din("vaug", [TL, 16 * 65], BF16)
            din("iqT", [512, TL], BF16); din("ikT", [64, TL], BF16); din("iw", [TL, 8], F32)
            dout("oT", [D, TL], BF16)
            phase_aattn(P, A, l, "aa_")
            P.final_wait()
        elif phase_name == "outln":
            din("modrow", [DEPTH, 2, D]); din("ln_g", [DEPTH, 2, D]); din("ln_b", [DEPTH, 2, D])
            din("w_o", [D, D]); din("oT", [D, TL], BF16); din("xin", [TL, D])
            dout("xout", [TL, D])
            phase_outln(P, A, l, A["w_o"], A["oT"], "oT", A["xin"], "xin", A["xout"], "xout", "ol_")
            P.final_wait()
        else:
            raise ValueError(phase_name)
        P.finish()
        nops = P.n
    return nc, ins, outs, nops


def _to_local(a_b, hf):
    s = a_b.shape
    return np.ascontiguousarray(a_b.reshape(32, 2, 128, *s[1:])[:, hf].reshape(TL, *s[1:]))


def _from_local(parts):
    s = parts[0].shape
    o = np.empty((32, 2, 128) + s[1:], parts[0].dtype)
    for hf in range(2):
        o[:, hf] = parts[hf].reshape(32, 128, *s[1:])
    return o.reshape(SEQ, *s[1:])


def run_phase(nc, ins, in_maps):
    res = run_bass_kernel_spmd(nc, [{k: m[k] for k in ins} for m in in_maps], core_ids=list(range(len(in_maps))))
    return res.results


def rel_bucket_np(dist):
    import math
    n = np.maximum(dist, 0)
    nf = np.maximum(n, 1).astype(np.float32)
    large = 16 + (np.log(nf / np.float32(16)) / np.float32(math.log(128 / 16)) * np.float32(16)).astype(np.int32)
    large = np.minimum(large, 31)
    return np.where(n < 16, n, large)


def make_biasT(rel_bias):
    s_ = np.arange(128)[:, None]
    q_ = np.arange(128)[None, :]
    dd = q_ - s_
    bd = rel_bucket_np(dd)
    bp = rel_bucket_np(dd + 128)
    out = np.empty((16, 128, 256), np.float32)
    for c in range(16):
        diag = rel_bias[:, c][bd]
        out[c, :, 0:128] = np.where(dd >= 0, diag, np.float32(NEGM))
        out[c, :, 128:256] = rel_bias[:, c][bp]
    return out


def build_fused(single=None):
    nc = _new_nc()
    A = {}
    NL = DEPTH if single is None else 1
    NM_ = 2 if single is None else 1

    def din(name, shape, dt=F32):
        A[name] = nc.dram_tensor(name, list(shape), dt, kind="ExternalInput").ap()

    def dint(name, shape, dt=F32):
        A[name] = nc.dram_tensor(name, list(shape), dt, kind="Internal").ap()

    din("x", [TL, D]); din("ccol", [128, 8]); din("ada_w", [NL, D, 6 * D]); din("ada_b", [NL, 6 * D])
    din("ada_bT", [128, NL * 48]); din("ident", [128, 128]); din("rel_bias", [32, 16]); din("biasT", [16, 128, 256])
    din("ln_g", [NL, 2, D]); din("ln_b", [NL, 2, D])
    if single is None or single % 2 == 0:
        din("cmask", [128, 128])
        din("a_w_in", [NM_, D, A_IN]); din("a_kv_norm", [NM_, 256]); din("a_w_uk", [NM_, 16, 64, 256])
        din("a_w_uv", [NM_, 16, 256, 64]); din("a_w_o", [NM_, D, D])
    if single is None or single % 2 == 1:
        din("b_w_in", [NM_, D, 3072]); din("b_lambda", [NM_, 4, 64]); din("b_subln", [NM_, 128]); din("b_w_o", [NM_, D, D])
    din("mlp_w1", [NL, D, DFF]); din("mlp_w2", [NL, DFF, D])
    A["out"] = nc.dram_tensor("out", [TL, D], F32, kind="ExternalOutput").ap()
    dint("modT", [128, NL * 48]); dint("modrow", [NL, 2, D]); dint("xA", [TL, D]); dint("xB", [TL, D])
    dint("vaug", [TL, 16 * 65], BF16)
    dint("iqT", [512, TL], BF16); dint("ikT", [64, TL], BF16); dint("iw", [TL, 8], F32)
    dint("qT", [D, TL], BF16); dint("kT", [D, TL], BF16); dint("v", [TL, D], BF16); dint("oT", [D, TL], BF16)
    nops = [0]

    def block(fn, outkeys):
        with ExitStack() as es:
            P = Prog(nc, es)
            make_epsc(P)
            fn(P)
            P.final_wait()
            P.finish()
            nops[0] += P.n

    block(lambda P: phase_mod(P, A, NL), ["modT", "modrow"])
    xcur = "x"
    llist = DBG.get("llist", list(range(DBG.get("layers", DEPTH))))
    if single is not None:
        llist = [0]
        DBG["true_l"] = single
    else:
        DBG.pop("true_l", None)
    for l in llist:
        j = l // 2
        kind_a = (l % 2 == 0) if single is None else (single % 2 == 0)
        if kind_a:
            block(lambda P: phase_aproj(P, A, l, A[xcur], "xin", "ap_"), ["qT", "kT", "vaug", "iqT", "ikT", "iw"])
            block(lambda P: phase_aattn(P, A, l, "aa_"), ["oT"])
            w_o = A["a_w_o"][j]
        else:
            block(lambda P: phase_bproj(P, A, l, A[xcur], "xin", "bp_"), ["qT", "kT", "v"])
            block(lambda P: phase_battn(P, A, l, "ba_"), ["oT"])
            w_o = A["b_w_o"][j]
        block(lambda P: phase_outln(P, A, l, w_o, A["oT"], "oT", A[xcur], "xin", A["xA"], "xout", "ol_"), ["xout"])
        last = (l == llist[-1])
        dst = "out" if last else "xB"
        block(lambda P: phase_mlp(P, A, l, A["xA"], "xin", A[dst], "xout", "ml_"), ["xout"])
        xcur = "xB"
    return nc, nops[0]


_CACHE = {}
FUSED = True


def kernel(**inputs):
    f32 = lambda a: np.ascontiguousarray(np.asarray(a, dtype=np.float32))
    x = f32(inputs["x"])
    c = f32(inputs["c"])
    rel_bias = f32(inputs["rel_bias"])
    W = {k: f32(inputs[k]) for k in ["ada_w", "ada_b", "ln_g", "ln_b", "a_w_in", "a_kv_norm", "a_w_uk", "a_w_uv", "a_w_o",
                                     "b_w_in", "b_lambda", "b_subln", "b_w_o", "mlp_w1", "mlp_w2"]}
    const = {
        "ident": np.eye(128, dtype=np.float32),
        "rel_bias": rel_bias,
        "biasT": make_biasT(rel_bias),
    }
    cmask = np.where(np.arange(128)[None, :] <= np.arange(128)[:, None], 0.0, -1e30).astype(np.float32)
    ccols = [np.ascontiguousarray(c[b].reshape(8, 128).T) for b in range(NB)]
    if FUSED:
        if "nc" not in _CACHE:
            _CACHE["nc"] = build_fused()
        nc, nops = _CACHE["nc"]
        shared = dict(const)
        shared.update(W)
        shared["ada_bT"] = np.ascontiguousarray(W["ada_b"].reshape(4 * 48, 128).T)
        shared["cmask"] = cmask
        in_maps = []
        for b in range(NB):
            m = dict(shared)
            m["x"] = x[b]
            m["ccol"] = ccols[b]
            in_maps.append(m)
        res = run_bass_kernel_spmd(nc, in_maps, core_ids=list(range(NB)))
        return np.stack([np.asarray(res.results[b]["out"], dtype=np.float32) for b in range(NB)], axis=0)
    xs = [x[b] for b in range(NB)]
    for L in range(DEPTH):
        key = "nc%d" % (L % 2)
        if ("L", L) not in _CACHE:
            _CACHE[("L", L)] = build_fused(single=L)
        nc, nops = _CACHE[("L", L)]
        j = L // 2
        shared = dict(const)
        for k in ["ada_w", "ada_b", "ln_g", "ln_b", "mlp_w1", "mlp_w2"]:
            shared[k] = np.ascontiguousarray(W[k][L:L + 1])
        shared["ada_bT"] = np.ascontiguousarray(W["ada_b"][L].reshape(48, 128).T)
        if L % 2 == 0:
            shared["cmask"] = cmask
            for k in ["a_w_in", "a_kv_norm", "a_w_uk", "a_w_uv", "a_w_o"]:
                shared[k] = np.ascontiguousarray(W[k][j:j + 1])
        else:
            for k in ["b_w_in", "b_lambda", "b_subln", "b_w_o"]:
                shared[k] = np.ascontiguousarray(W[k][j:j + 1])
        in_maps = []
        for b in range(NB):
            m = dict(shared)
            m["x"] = xs[b]
            m["ccol"] = ccols[b]
            in_maps.append(m)
        res = run_bass_kernel_spmd(nc, in_maps, core_ids=list(range(NB)))
        xs = [np.asarray(res.results[b]["out"], dtype=np.float32) for b in range(NB)]
    return np.stack(xs, axis=0)
```

```python
import numpy as np
from contextlib import ExitStack
import concourse.bass as bass
import concourse.mybir as mybir
from concourse.bass_utils import run_bass_kernel_spmd

F32 = mybir.dt.float32
BF16 = mybir.dt.bfloat16
U8 = mybir.dt.uint8
AF = mybir.ActivationFunctionType
ALU = mybir.AluOpType
AX = mybir.AxisListType

D = 1024
SEQ = 8192
NB = 4
DEPTH = 4
TL = 8192
NLB = 64
DFF = 4096
LN_EPS = 1e-5
DN_ALPHA = (2 * DEPTH) ** 0.25
A_IN = 1864
NEGM = -30000.0

DBG = {}
STATS = {}
ENGS = ["tensor", "vector", "scalar", "gpsimd", "sync"]
EPOCH = 30000
EPOCH_DMA = 1800


class Op:
    __slots__ = ("eng", "fn", "chan", "waits", "needs_inc", "val", "epoch", "isdma", "noattach")


class Prog:
    _uid = [0]

    def __init__(self, nc, es):
        self.nc = nc
        self.es = es
        Prog._uid[0] += 1
        self.uid = Prog._uid[0]
        self.ops = {e: [] for e in ENGS}
        self.chan_ops = {}
        self.lastw = {}
        self.readers = {}
        self.n = 0
        self.outkeys = []

    def sb(self, name, shape, dt):
        return self.es.enter_context(self.nc.sbuf_tensor("%s_u%d" % (name, self.uid), list(shape), dt))

    def ps(self, name, shape, dt=F32):
        esz = 4 if dt == F32 else 2
        n = 1
        for d in shape[1:]:
            n *= d
        per_bank = 2048 // esz
        tot = ((n + per_bank - 1) // per_bank) * per_bank
        h = self.es.enter_context(self.nc.psum_tensor("%s_u%d" % (name, self.uid), [128, tot], dt))
        ap = h[0:shape[0], 0:n]
        if len(shape) == 3:
            ap = ap.rearrange("p (a b) -> p a b", a=shape[1])
        return ap

    def add(self, eng, fn, reads=(), writes=(), dma=None, noattach=False, pe_attach=False):
        op = Op()
        op.noattach = noattach or (eng == "tensor" and not pe_attach)
        op.eng = eng
        op.fn = fn
        op.isdma = dma is not None
        op.chan = ("dma", dma) if dma is not None else eng
        op.needs_inc = op.isdma
        deps = []
        for k in reads:
            w = self.lastw.get(k)
            if w is not None:
                deps.append(w)
        for k in writes:
            w = self.lastw.get(k)
            if w is not None:
                deps.append(w)
            rd = self.readers.get(k)
            if rd:
                deps.extend(rd.values())
        waits = []
        seen = set()
        for d in deps:
            if id(d) in seen:
                continue
            seen.add(id(d))
            if (not d.isdma) and d.chan == eng and eng == "tensor":
                continue
            d.needs_inc = True
            waits.append(d)
        op.waits = waits
        for k in reads:
            self.readers.setdefault(k, {})[op.chan] = op
        for k in writes:
            self.lastw[k] = op
            self.readers[k] = {}
        self.ops[eng].append(op)
        self.chan_ops.setdefault(op.chan, []).append(op)
        self.n += 1
        return op

    DRAM_OUT = ("qT", "kT", "v", "vaug", "iqT", "ikT", "iw", "oT", "xout", "modT", "modrow")

    def dma(self, eng, chan, out, in_, reads=(), writes=()):
        w2 = []
        for k in writes:
            if k in Prog.DRAM_OUT:
                k = "%s#%d" % (k, len(self.outkeys))
                self.outkeys.append(k)
            w2.append(k)
        return self.add(eng, lambda e: e.dma_start(out=out, in_=in_), reads, w2, dma=chan)

    def final_wait(self, eng="gpsimd"):
        self.add(eng, None, reads=list(self.outkeys))

    def finish(self):
        nc = self.nc
        sems = {}
        for chan, lst in self.chan_ops.items():
            cnt = 0
            ep = EPOCH_DMA if chan[0] == "dma" else EPOCH
            for op in lst:
                if op.needs_inc:
                    op.epoch = cnt // ep
                    op.val = cnt % ep + 1
                    cnt += 1
                    key = (chan, op.epoch)
                    if key not in sems:
                        sems[key] = nc.alloc_semaphore(name="s%d_u%d" % (len(sems), self.uid))
        self.nsems = len(sems)
        with nc.Block() as block:
            self._emit(block, sems)
        nc.clear_and_free_semaphores(list(sems.values()))
        nc.all_engine_barrier()

    def _emit(self, block, sems):

        def emit(eng_name):
            ops = self.ops[eng_name]

            def body(e):
                waited = {}
                for op in ops:
                    need = {}
                    for d in op.waits:
                        cur = waited.get(d.chan)
                        if cur is not None and (cur[0] > d.epoch or (cur[0] == d.epoch and cur[1] >= d.val)):
                            continue
                        prev = need.get(d.chan)
                        if prev is None or (d.epoch, d.val) > (prev.epoch, prev.val):
                            need[d.chan] = d
                    need = list(need.values())
                    attach = None
                    if need and op.fn is not None and not op.noattach:
                        attach = need.pop()
                    for d in need:
                        e.wait_ge(sems[(d.chan, d.epoch)], d.val * (16 if d.isdma else 1))
                        waited[d.chan] = (d.epoch, d.val)
                        STATS[eng_name + "_wait"] = STATS.get(eng_name + "_wait", 0) + 1
                    if op.fn is None:
                        continue
                    ins = op.fn(e)
                    if attach is not None:
                        ins._wait_ge(sems[(attach.chan, attach.epoch)], attach.val * (16 if attach.isdma else 1))
                        waited[attach.chan] = (attach.epoch, attach.val)
                    STATS[eng_name] = STATS.get(eng_name, 0) + 1
                    if op.needs_inc:
                        ins.then_inc(sems[(op.chan, op.epoch)], 16 if op.isdma else 1)
            return body

        for en in ENGS:
            if self.ops[en]:
                getattr(block, en)(emit(en))


class Ctx:
    pass


def _rot(lst, i):
    return lst[i % len(lst)]


def load_mod_cols(P, pfx, modT_ap, l, which):
    nc = P.nc
    mt = P.sb(pfx + "modc", [128, 16], F32)
    base = l * 48 + which * 24
    P.dma("sync", pfx + "modc", mt[:, :], modT_ap[:, base:base + 16], reads=["modT"], writes=[pfx + "modc"])
    P.add("vector", lambda e: e.tensor_scalar(out=mt[:, 8:16], in0=mt[:, 8:16], scalar1=1.0, scalar2=None,
                                             op0=ALU.add), reads=[pfx + "modc"], writes=[pfx + "modc"])
    return mt


def load_bcast(P, name, src_ap_1d, n):
    t = P.sb(name, [128, n], F32)
    P.dma("sync", name, t[:, :], src_ap_1d.partition_broadcast(128), reads=["modrow"], writes=[name])
    return t


def make_ident(P, pfx, ident_ap):
    idf = P.sb(pfx + "idf", [128, 128], F32)
    P.dma("sync", pfx + "idf", idf[:, :], ident_ap, writes=[pfx + "idf"])
    idb = P.sb(pfx + "idb", [128, 128], BF16)
    P.add("vector", lambda e: e.tensor_copy(out=idb[:, :], in_=idf[:, :]), reads=[pfx + "idf"], writes=[pfx + "idb"])
    return idf, idb


def emit_hT(P, pfx, xblk_tiles, nblk, idf, modc, hT, col0, ps_tr_list, cnt):
    for c in range(8):
        pt, pk = _rot(ps_tr_list, cnt[0])
        cnt[0] += 1
        for b in range(nblk):
            xt, xk = xblk_tiles[b]
            P.add("tensor", (lambda e, pt=pt, xt=xt, b=b, c=c: e.transpose(
                out=pt[:, b * 128:(b + 1) * 128], in_=xt[:, c * 128:(c + 1) * 128], identity=idf[:, :])),
                reads=[xk, pfx + "idf"], writes=[pk])
        P.add("scalar", (lambda e, pt=pt, c=c: e.activation(
            out=hT[:, c, col0:col0 + nblk * 128], in_=pt[:, 0:nblk * 128], func=AF.Identity,
            bias=modc[:, c:c + 1], scale=modc[:, 8 + c:9 + c])),
            reads=[pk, pfx + "modc"], writes=[pfx + "hT"])


def emit_resid_ln(P, pfx, ps_y, ps_key, xt, xkey, gbc, lng, lnb, ot, okey, small, skey):
    nc = P.nc
    st, mv, rs = small
    P.add("vector", lambda e: e.tensor_tensor(out=ot[:, :].rearrange("p (a f) -> p a f", a=2), in0=ps_y[:, :, :],
                                             in1=gbc[:, :].rearrange("p (a f) -> p a f", a=2), op=ALU.mult),
          reads=[ps_key, pfx + "gbc"], writes=[okey])
    P.add("vector", lambda e: e.scalar_tensor_tensor(out=ot[:, :], in0=xt, scalar=float(DN_ALPHA), in1=ot[:, :],
                                                    op0=ALU.mult, op1=ALU.add),
          reads=[xkey], writes=[okey])
    P.add("vector", lambda e: e.bn_stats(out=st[:, 0, :], in_=ot[:, 0:512]), reads=[okey], writes=[skey])
    P.add("vector", lambda e: e.bn_stats(out=st[:, 1, :], in_=ot[:, 512:1024]), reads=[okey], writes=[skey])
    P.add("vector", lambda e: e.bn_aggr(out=mv[:, :], in_=st[:, :, :]), reads=[skey], writes=[skey])
    P.add("scalar", lambda e: e.activation(out=rs[:, 0:1], in_=mv[:, 1:2], func=AF.Sqrt, bias=P.epsc[:, 0:1], scale=1.0),
          reads=[skey, "epsc"], writes=[skey + "r"])
    P.add("vector", lambda e: e.reciprocal(out=rs[:, 1:2], in_=rs[:, 0:1]), reads=[skey + "r"], writes=[skey + "r2"])
    P.add("vector", lambda e: e.tensor_scalar(out=rs[:, 2:3], in0=mv[:, 0:1], scalar1=rs[:, 1:2], scalar2=-1.0,
                                             op0=ALU.mult, op1=ALU.mult), reads=[skey + "r2"], writes=[skey + "r2"])
    P.add("scalar", lambda e: e.activation(out=ot[:, :], in_=ot[:, :], func=AF.Identity, bias=rs[:, 2:3],
                                           scale=rs[:, 1:2]), reads=[skey + "r2", okey], writes=[okey])
    P.add("vector", lambda e: e.tensor_tensor(out=ot[:, :], in0=ot[:, :], in1=lng[:, :], op=ALU.mult),
          reads=[okey, pfx + "lng"], writes=[okey])
    P.add("gpsimd", lambda e: e.tensor_tensor(out=ot[:, :], in0=ot[:, :], in1=lnb[:, :], op=ALU.add),
          reads=[okey, pfx + "lnb"], writes=[okey])


def make_epsc(P):
    t = P.sb("epsc", [128, 1], F32)
    P.add("vector", lambda e: e.memset(t[:, :], LN_EPS), writes=["epsc"])
    P.epsc = t


def phase_mod(P, A, NL=DEPTH):
    nc = P.nc
    pfx = "md_"
    cT = P.sb(pfx + "cT", [128, 8], F32)
    P.dma("sync", pfx + "c", cT[:, :], A["ccol"], writes=[pfx + "cT"])
    sT = P.sb(pfx + "sT", [128, 8], F32)
    P.add("scalar", lambda e: e.activation(out=sT[:, :], in_=cT[:, :], func=AF.Silu), reads=[pfx + "cT"], writes=[pfx + "sT"])
    bT = P.sb(pfx + "bT", [128, NL * 48], F32)
    P.dma("sync", pfx + "b", bT[:, :], A["ada_bT"], writes=[pfx + "bT"])
    brow = P.sb(pfx + "brow", [1, NL * 2048], F32)
    for l in range(NL):
        for gi in range(2):
            P.dma("sync", pfx + "br", brow[:, (l * 2 + gi) * 1024:(l * 2 + gi + 1) * 1024],
                  A["ada_b"][l:l + 1, (2 + 3 * gi) * 1024:(3 + 3 * gi) * 1024], writes=[pfx + "brow"])
    modsb = P.sb(pfx + "modsb", [128, NL * 48], F32)
    rowsb = P.sb(pfx + "rowsb", [1, NL * 2048], F32)
    wp = [P.sb(pfx + "wp%d" % i, [128, 8, 512], F32) for i in range(3)]
    psc = P.ps(pfx + "psc", [128, NL * 48], F32)
    psr = [P.ps(pfx + "psr%d" % i, [1, 512], F32) for i in range(2)]
    P.add("vector", lambda e: e.memset(modsb[:, :], 0.0), writes=[pfx + "modsb"])
    P.add("vector", lambda e: e.memset(rowsb[:, :], 0.0), writes=[pfx + "rowsb"])
    it = 0
    for l in range(NL):
        for pc in range(12):
            w = wp[it % 3]
            wk = pfx + "wp%d" % (it % 3)
            src = A["ada_w"][l, :, pc * 512:(pc + 1) * 512].rearrange("(kk p) f -> p kk f", p=128)
            qeng = "sync" if it % 2 == 0 else "gpsimd"
            P.dma(qeng, wk + qeng, w[:, :, :], src, writes=[wk])
            seg = pc // 2
            if seg in (2, 5):
                pr = psr[it % 2]
                prk = pfx + "psr%d" % (it % 2)
                for kk in range(8):
                    P.add("tensor", (lambda e, pr=pr, w=w, kk=kk: e.matmul(
                        pr[:, :], sT[:, kk:kk + 1], w[:, kk, :], start=(kk == 0), stop=(kk == 7))),
                        reads=[wk, pfx + "sT"], writes=[prk])
                o0 = (l * 2 + seg // 3) * 1024 + (pc % 2) * 512
                P.add("vector", (lambda e, pr=pr, o0=o0: e.tensor_tensor(
                    out=rowsb[:, o0:o0 + 512], in0=pr[:, :], in1=brow[:, o0:o0 + 512], op=ALU.add)),
                    reads=[prk, pfx + "brow"], writes=[pfx + "rowsb"])
            else:
                for q in range(4):
                    col = l * 48 + pc * 4 + q
                    for kk in range(8):
                        P.add("tensor", (lambda e, w=w, kk=kk, q=q, col=col: e.matmul(
                            psc[:, col:col + 1], w[:, kk, q * 128:(q + 1) * 128], sT[:, kk:kk + 1],
                            start=(kk == 0), stop=(kk == 7))),
                            reads=[wk, pfx + "sT"], writes=[pfx + "psc"])
            it += 1
    for l in range(NL):
        for c0 in (l * 48, l * 48 + 24):
            P.add("vector", (lambda e, c0=c0: e.tensor_tensor(out=modsb[:, c0:c0 + 16], in0=psc[:, c0:c0 + 16],
                                                              in1=bT[:, c0:c0 + 16], op=ALU.add)),
                  reads=[pfx + "psc", pfx + "bT"], writes=[pfx + "modsb"])
    P.dma("sync", pfx + "o1", A["modT"], modsb[:, :], reads=[pfx + "modsb"], writes=["modT"])
    P.dma("sync", pfx + "o2", A["modrow"].rearrange("(o l) g f -> o (l g f)", o=1), rowsb[:, :], reads=[pfx + "rowsb"],
          writes=["modrow"])


def phase_mlp(P, A, l, xin, xin_key, xout, xout_key, pfx):
    nc = P.nc
    TT = 256
    NT = TL // TT
    idf, idb = make_ident(P, pfx, A["ident"])
    modc = load_mod_cols(P, pfx, A["modT"], l, 1)
    gbc = load_bcast(P, pfx + "gbc", A["modrow"][l, 1, :], 1024)
    P.add("vector", lambda e: e.tensor_scalar(out=gbc[:, :], in0=gbc[:, :], scalar1=1.0, scalar2=None, op0=ALU.add),
          reads=[pfx + "gbc"], writes=[pfx + "gbc"])
    lng = P.sb(pfx + "lng", [128, 1024], F32)
    P.dma("sync", pfx + "lng", lng[:, :], A["ln_g"][l, 1, :].partition_broadcast(128), writes=[pfx + "lng"])
    lnb = P.sb(pfx + "lnb", [128, 1024], F32)
    P.dma("sync", pfx + "lnb", lnb[:, :], A["ln_b"][l, 1, :].partition_broadcast(128), writes=[pfx + "lnb"])
    w1b = P.sb(pfx + "w1b", [128, 8, DFF], BF16)
    w2b = P.sb(pfx + "w2b", [128, 32, D], BF16)
    for kc in range(8):
        P.dma("gpsimd", pfx + "w1", w1b[:, kc, :], A["mlp_w1"][l, kc * 128:(kc + 1) * 128, :], writes=[pfx + "w1b"])
    w2v = A["mlp_w2"][l].rearrange("(kc p) f -> p kc f", p=128)
    for g in range(8):
        P.dma("gpsimd", pfx + "w2", w2b[:, g * 4:(g + 1) * 4, :], w2v[:, g * 4:(g + 1) * 4, :], writes=[pfx + "w2b"])
    xtr = [P.sb(pfx + "xtr%d" % i, [128, 1024], F32) for i in range(2)]
    xep = [P.sb(pfx + "xep%d" % i, [128, 1024], F32) for i in range(2)]
    ots = [P.sb(pfx + "ot%d" % i, [128, 1024], F32) for i in range(3)]
    hT = P.sb(pfx + "hT", [128, 8, TT], BF16)
    aT = P.sb(pfx + "aT", [128, 32, TT], BF16)
    rts = [P.sb(pfx + "rt%d" % i, [128, TT], F32) for i in range(3)]
    smalls = [(P.sb(pfx + "st%d" % i, [128, 2, 6], F32), P.sb(pfx + "mv%d" % i, [128, 2], F32),
               P.sb(pfx + "rs%d" % i, [128, 3], F32)) for i in range(3)]
    ps_tr = [(P.ps(pfx + "ptr%d" % i, [128, 512], F32), pfx + "ptr%d" % i) for i in range(2)]
    ps_a = [P.ps(pfx + "pa%d" % i, [128, 512], F32) for i in range(2)]
    ps_y = [P.ps(pfx + "py%d" % i, [128, 2, 512], F32) for i in range(2)]
    cnt = [0]
    nblk = TT // 128
    xcnt = [0]
    ecnt = [0]

    def do_tr(t):
        tiles = []
        for b in range(nblk):
            i = xcnt[0] % 2
            xcnt[0] += 1
            r0 = t * TT + b * 128
            P.dma("sync", pfx + "xtr%d" % i, xtr[i][:, :], xin[r0:r0 + 128, :], reads=[xin_key], writes=[pfx + "xtr%d" % i])
            tiles.append((xtr[i], pfx + "xtr%d" % i))
        emit_hT(P, pfx, tiles, nblk, idf, modc, hT, 0, ps_tr, cnt)

    def do_w1(t):
        for fc in range(32):
            pa = ps_a[fc % 2]
            pak = pfx + "pa%d" % (fc % 2)
            for kc in range(8):
                P.add("tensor", (lambda e, pa=pa, kc=kc, fc=fc: e.matmul(
                    pa[:, 0:TT], w1b[:, kc, fc * 128:(fc + 1) * 128], hT[:, kc, :], start=(kc == 0), stop=(kc == 7))),
                    reads=[pfx + "w1b", pfx + "hT"], writes=[pak])
            rt = rts[fc % 3]
            rk = pfx + "rt%d" % (fc % 3)
            P.add("scalar", (lambda e, pa=pa, rt=rt: e.activation(out=rt[:, :], in_=pa[:, 0:TT], func=AF.Relu)),
                  reads=[pak], writes=[rk])
            P.add("gpsimd", (lambda e, rt=rt, fc=fc: e.tensor_tensor(out=aT[:, fc, :], in0=rt[:, :], in1=rt[:, :], op=ALU.mult)),
                  reads=[rk], writes=[pfx + "aT"])

    def do_w2(t):
        for b in range(nblk):
            i = ecnt[0]
            ecnt[0] += 1
            py = ps_y[i % 2]
            pyk = pfx + "py%d" % (i % 2)
            for half in range(2):
                for fc in range(32):
                    P.add("tensor", (lambda e, py=py, half=half, fc=fc, b=b: e.matmul(
                        py[:, half, :], aT[:, fc, b * 128:(b + 1) * 128], w2b[:, fc, half * 512:(half + 1) * 512],
                        start=(fc == 0), stop=(fc == 31))),
                        reads=[pfx + "aT", pfx + "w2b"], writes=[pyk])
            r0 = t * TT + b * 128
            xe = xep[i % 2]
            xek = pfx + "xep%d" % (i % 2)
            P.dma("sync", xek, xe[:, :], xin[r0:r0 + 128, :], reads=[xin_key], writes=[xek])
            ot = ots[i % 3]
            ok = pfx + "ot%d" % (i % 3)
            emit_resid_ln(P, pfx, py, pyk, xe[:, :], xek, gbc, lng, lnb, ot, ok, smalls[i % 3], pfx + "sm%d" % (i % 3))
            P.dma("gpsimd", ok + "o", xout[r0:r0 + 128, :], ot[:, :], reads=[ok], writes=[xout_key])

    NT = DBG.get("NT", NT)
    do_tr(0)
    for t in range(NT):
        do_w1(t)
        if t + 1 < NT:
            do_tr(t + 1)
        do_w2(t)


def phase_outln(P, A, l, w_ap, oT, oT_key, xin, xin_key, xout, xout_key, pfx):
    gbc = load_bcast(P, pfx + "gbc", A["modrow"][l, 0, :], 1024)
    P.add("vector", lambda e: e.tensor_scalar(out=gbc[:, :], in0=gbc[:, :], scalar1=1.0, scalar2=None, op0=ALU.add),
          reads=[pfx + "gbc"], writes=[pfx + "gbc"])
    lng = P.sb(pfx + "lng", [128, 1024], F32)
    P.dma("sync", pfx + "lng", lng[:, :], A["ln_g"][l, 0, :].partition_broadcast(128), writes=[pfx + "lng"])
    lnb = P.sb(pfx + "lnb", [128, 1024], F32)
    P.dma("sync", pfx + "lnb", lnb[:, :], A["ln_b"][l, 0, :].partition_broadcast(128), writes=[pfx + "lnb"])
    wob = P.sb(pfx + "wob", [128, 8, D], BF16)
    P.dma("gpsimd", pfx + "wo", wob[:, :, :], w_ap.rearrange("(c p) f -> p c f", p=128), writes=[pfx + "wob"])
    ots = [P.sb(pfx + "ot%d" % i, [128, 1024], F32) for i in range(3)]
    xep = [P.sb(pfx + "xep%d" % i, [128, 1024], F32) for i in range(3)]
    oTt = [P.sb(pfx + "oTt%d" % i, [128, 8, 512], BF16) for i in range(2)]
    smalls = [(P.sb(pfx + "st%d" % i, [128, 2, 6], F32), P.sb(pfx + "mv%d" % i, [128, 2], F32),
               P.sb(pfx + "rs%d" % i, [128, 3], F32)) for i in range(3)]
    ps_y = [P.ps(pfx + "py%d" % i, [128, 2, 512], F32) for i in range(2)]
    oTv = oT.rearrange("(c p) t -> p c t", p=128)
    i = 0
    for t in range(DBG.get("OT", TL // 512)):
        ob = oTt[t % 2]
        obk = pfx + "oTt%d" % (t % 2)
        P.dma("sync", obk, ob[:, :, :], oTv[:, :, t * 512:(t + 1) * 512], reads=[oT_key], writes=[obk])
        for b in range(4):
            py = ps_y[i % 2]
            pyk = pfx + "py%d" % (i % 2)
            for half in range(2):
                for c in range(8):
                    P.add("tensor", (lambda e, py=py, half=half, c=c, b=b, ob=ob: e.matmul(
                        py[:, half, :], ob[:, c, b * 128:(b + 1) * 128], wob[:, c, half * 512:(half + 1) * 512],
                        start=(c == 0), stop=(c == 7))), reads=[obk, pfx + "wob"], writes=[pyk])
            r0 = t * 512 + b * 128
            xe = xep[i % 3]
            xek = pfx + "xep%d" % (i % 3)
            P.dma("sync", xek, xe[:, :], xin[r0:r0 + 128, :], reads=[xin_key], writes=[xek])
            ot = ots[i % 3]
            ok = pfx + "ot%d" % (i % 3)
            emit_resid_ln(P, pfx, py, pyk, xe[:, :], xek, gbc, lng, lnb, ot, ok, smalls[i % 3], pfx + "sm%d" % (i % 3))
            P.dma("gpsimd", ok + "o", xout[r0:r0 + 128, :], ot[:, :], reads=[ok], writes=[xout_key])
            i += 1


def phase_bproj(P, A, l, xin, xin_key, pfx):
    j = l // 2
    idf, idb = make_ident(P, pfx, A["ident"])
    modc = load_mod_cols(P, pfx, A["modT"], l, 0)
    wb = P.sb(pfx + "wb", [128, 8, 3072], BF16)
    wv = A["b_w_in"][j].rearrange("(c p) f -> p c f", p=128)
    for c in range(8):
        P.dma("gpsimd", pfx + "w", wb[:, c, :], wv[:, c, :], writes=[pfx + "wb"])
    xtr = [P.sb(pfx + "xtr%d" % i, [128, 1024], F32) for i in range(6)]
    hT = P.sb(pfx + "hT", [128, 8, 512], BF16)
    stg = [P.sb(pfx + "stg%d" % i, [128, 512], BF16) for i in range(4)]
    ps_tr = [(P.ps(pfx + "ptr%d" % i, [128, 512], F32), pfx + "ptr%d" % i) for i in range(2)]
    ps_o = [P.ps(pfx + "po%d" % i, [128, 512], F32) for i in range(4)]
    cnt = [0]
    xc = 0
    oc = 0
    for t in range(TL // 512):
        tiles = []
        for b in range(4):
            i = xc % 6
            xc += 1
            r0 = t * 512 + b * 128
            P.dma("sync", pfx + "xtr%d" % i, xtr[i][:, :], xin[r0:r0 + 128, :], reads=[xin_key], writes=[pfx + "xtr%d" % i])
            tiles.append((xtr[i], pfx + "xtr%d" % i))
        emit_hT(P, pfx, tiles, 4, idf, modc, hT, 0, ps_tr, cnt)
        for fo in range(16):
            po = ps_o[oc % 4]
            pok = pfx + "po%d" % (oc % 4)
            sg = stg[oc % 4]
            sgk = pfx + "stg%d" % (oc % 4)
            oc += 1
            for c in range(8):
                P.add("tensor", (lambda e, po=po, c=c, fo=fo: e.matmul(
                    po[:, :], wb[:, c, fo * 128:(fo + 1) * 128], hT[:, c, :], start=(c == 0), stop=(c == 7))),
                    reads=[pfx + "wb", pfx + "hT"], writes=[pok])
            sc = 0.125 if fo < 8 else 1.0
            P.add("scalar", (lambda e, po=po, sg=sg, sc=sc: e.activation(out=sg[:, :], in_=po[:, :], func=AF.Copy, scale=sc)),
                  reads=[pok], writes=[sgk])
            dst = A["qT"] if fo < 8 else A["kT"]
            dk = "qT" if fo < 8 else "kT"
            fr = (fo % 8) * 128
            P.dma("gpsimd", sgk + "o", dst[fr:fr + 128, t * 512:(t + 1) * 512], sg[:, :], reads=[sgk], writes=[dk])
        for b in range(4):
            for half in range(2):
                po = ps_o[oc % 4]
                pok = pfx + "po%d" % (oc % 4)
                sg = stg[oc % 4]
                sgk = pfx + "stg%d" % (oc % 4)
                oc += 1
                for c in range(8):
                    P.add("tensor", (lambda e, po=po, c=c, b=b, half=half: e.matmul(
                        po[:, :], hT[:, c, b * 128:(b + 1) * 128], wb[:, c, 2048 + half * 512:2048 + (half + 1) * 512],
                        start=(c == 0), stop=(c == 7))), reads=[pfx + "wb", pfx + "hT"], writes=[pok])
                P.add("vector", (lambda e, po=po, sg=sg: e.tensor_copy(out=sg[:, :], in_=po[:, :])), reads=[pok], writes=[sgk])
                r0 = t * 512 + b * 128
                P.dma("gpsimd", sgk + "o", A["v"][r0:r0 + 128, half * 512:(half + 1) * 512], sg[:, :], reads=[sgk], writes=["v"])


def phase_battn(P, A, l, pfx):
    j = l // 2
    import math
    lam_init = 0.8 - 0.6 * math.exp(-0.3 * DBG.get("true_l", l))
    idf, idb = make_ident(P, pfx, A["ident"])
    onesb = P.sb(pfx + "onesb", [128, 128], BF16)
    P.add("vector", lambda e: e.memset(onesb[:, :], 1.0), writes=[pfx + "onesb"])
    onesf = P.sb(pfx + "onesf", [128, 128], F32)
    P.add("vector", lambda e: e.memset(onesf[:, :], 1.0), writes=[pfx + "onesf"])
    b31 = P.sb(pfx + "b31", [128, 16], F32)
    P.dma("sync", pfx + "b31", b31[:, :], A["rel_bias"][31, :].partition_broadcast(128), writes=[pfx + "b31"])
    BTf = P.sb(pfx + "BTf", [128, 256], F32)
    BT = P.sb(pfx + "BT", [128, 16, 256], BF16)
    for c in range(16):
        P.dma("sync", pfx + "BTf", BTf[:, :], A["biasT"][c], writes=[pfx + "BTf"])
        P.add("vector", (lambda e, c=c: e.tensor_scalar(out=BT[:, c, :], in0=BTf[:, :], scalar1=b31[:, c:c + 1],
                                                        scalar2=None, op0=ALU.subtract)),
              reads=[pfx + "BTf", pfx + "b31"], writes=[pfx + "BT"])
    lamb = P.sb(pfx + "lamb", [128, 256], F32)
    P.dma("sync", pfx + "lamb", lamb[:, :], A["b_lambda"][j].rearrange("a d -> (a d)").partition_broadcast(128),
          writes=[pfx + "lamb"])
    lt = P.sb(pfx + "lt", [128, 128], F32)
    ls = P.sb(pfx + "ls", [128, 4], F32)
    P.add("vector", lambda e: e.tensor_tensor(out=lt[:, 0:64], in0=lamb[:, 0:64], in1=lamb[:, 64:128], op=ALU.mult),
          reads=[pfx + "lamb"], writes=[pfx + "lt"])
    P.add("vector", lambda e: e.tensor_tensor(out=lt[:, 64:128], in0=lamb[:, 128:192], in1=lamb[:, 192:256], op=ALU.mult),
          reads=[pfx + "lamb"], writes=[pfx + "lt"])
    P.add("vector", lambda e: e.reduce_sum(out=ls[:, 0:1], in_=lt[:, 0:64], axis=AX.X), reads=[pfx + "lt"], writes=[pfx + "ls"])
    P.add("vector", lambda e: e.reduce_sum(out=ls[:, 1:2], in_=lt[:, 64:128], axis=AX.X), reads=[pfx + "lt"], writes=[pfx + "ls"])
    P.add("scalar", lambda e: e.activation(out=ls[:, 2:4], in_=ls[:, 0:2], func=AF.Exp), reads=[pfx + "ls"], writes=[pfx + "ls2"])
    neglam = P.sb(pfx + "neglam", [128, 1], F32)
    P.add("vector", lambda e: e.tensor_scalar(out=neglam[:, :], in0=ls[:, 3:4], scalar1=float(lam_init), scalar2=ls[:, 2:3],
                                             op0=ALU.subtract, op1=ALU.subtract), reads=[pfx + "ls2"], writes=[pfx + "neglam"])
    sg = P.sb(pfx + "sg", [128, 1], F32)
    P.dma("sync", pfx + "sg", sg[:, :], A["b_subln"][j].rearrange("(p o) -> p o", o=1), writes=[pfx + "sg"])
    P.add("vector", lambda e: e.tensor_scalar(out=sg[:, :], in0=sg[:, :], scalar1=float(1.0 - lam_init), scalar2=None,
                                             op0=ALU.mult), reads=[pfx + "sg"], writes=[pfx + "sg"])
    kTh = [P.sb(pfx + "kTh%d" % i, [128, TL], BF16) for i in range(2)]
    qTh = [P.sb(pfx + "qTh%d" % i, [128, TL], BF16) for i in range(2)]
    Vh = [P.sb(pfx + "Vh%d" % i, [128, NLB, 128], BF16) for i in range(2)]
    PT = [P.sb(pfx + "PT%d" % i, [128, 512], BF16) for i in range(6)]
    r0t = P.sb(pfx + "r0t", [128, 512], F32)
    r1t = P.sb(pfx + "r1t", [128, 512], F32)
    o0t = P.sb(pfx + "o0t", [128, 512], F32)
    o1t = P.sb(pfx + "o1t", [128, 512], F32)
    sqt = P.sb(pfx + "sqt", [128, 512], F32)
    oTs = [P.sb(pfx + "oTs%d" % i, [128, 512], BF16) for i in range(2)]
    ps_s = [P.ps(pfx + "pss%d" % i, [128, 512], F32) for i in range(3)]
    acc_o = [P.ps(pfx + "acco%d" % i, [128, 512], F32) for i in range(2)]
    acc_s = [P.ps(pfx + "accs%d" % i, [128, 512], F32) for i in range(2)]
    ps_ms = P.ps(pfx + "psms", [128, 512], F32)
    st = {"sc": 0, "pc": 0}
    oc = 0
    vv = A["v"].rearrange("(blk p) f -> p blk f", p=128)
    for h in range(DBG.get("heads", 8)):
        kt = kTh[h % 2]; ktk = pfx + "kTh%d" % (h % 2)
        qt = qTh[h % 2]; qtk = pfx + "qTh%d" % (h % 2)
        vt = Vh[h % 2]; vtk = pfx + "Vh%d" % (h % 2)
        P.dma("sync", ktk, kt[:, :], A["kT"][h * 128:(h + 1) * 128, :], reads=["kT"], writes=[ktk])
        P.dma("sync", qtk, qt[:, :], A["qT"][h * 128:(h + 1) * 128, :], reads=["qT"], writes=[qtk])
        for g in range(4):
            P.dma("sync", vtk, vt[:, g * 16:(g + 1) * 16, :], vv[:, g * 16:(g + 1) * 16, h * 128:(h + 1) * 128], reads=["v"], writes=[vtk])
        for J in range(DBG.get("J", TL // 512)):
            ns = 4 * J + 4

            def qk_b(m, s_):
                col = 2 * h + m
                k = s_ - 4 * J
                c0 = max(0, k) * 128
                near = k >= -1
                ps = ps_s[st["sc"] % 3]; psk = pfx + "pss%d" % (st["sc"] % 3)
                st["sc"] += 1
                P.add("tensor", (lambda e, ps=ps, kt=kt, qt=qt, m=m, s_=s_, c0=c0, J=J, near=near: e.matmul(
                    ps[:, c0:512], kt[64 * m:64 * m + 64, s_ * 128:(s_ + 1) * 128],
                    qt[64 * m:64 * m + 64, J * 512 + c0:J * 512 + 512], start=True, stop=(not near))),
                    reads=[ktk, qtk], writes=[psk], pe_attach=(s_ > 0))
                if near:
                    if k == -1:
                        P.add("tensor", (lambda e, ps=ps, col=col: e.matmul(
                            ps[:, 0:128], idb[:, :], BT[:, col, 128:256], start=False, stop=True)),
                            reads=[pfx + "idb", pfx + "BT"], writes=[psk])
                    else:
                        w = min(256, 512 - c0)
                        P.add("tensor", (lambda e, ps=ps, col=col, c0=c0, w=w: e.matmul(
                            ps[:, c0:c0 + w], idb[:, :], BT[:, col, 0:w], start=False, stop=True)),
                            reads=[pfx + "idb", pfx + "BT"], writes=[psk])
                pt = PT[st["pc"] % 6]; ptk = pfx + "PT%d" % (st["pc"] % 6)
                st["pc"] += 1
                P.add("scalar", (lambda e, ps=ps, pt=pt, c0=c0, col=col: e.activation(
                    out=pt[:, c0:512], in_=ps[:, c0:512], func=AF.Exp, bias=b31[:, col:col + 1], scale=1.0)),
                    reads=[psk, pfx + "b31"], writes=[ptk])
                return pt, ptk, c0

            def pv_b(m, s_, pre):
                pt, ptk, c0 = pre
                ao = acc_o[m]; aok = pfx + "acco%d" % m
                as_ = acc_s[m]; ask = pfx + "accs%d" % m
                P.add("tensor", (lambda e, ao=ao, vt=vt, pt=pt, s_=s_, c0=c0, ns=ns: e.matmul(
                    ao[:, c0:512], vt[:, s_, :], pt[:, c0:512], start=(s_ == 0), stop=(s_ == ns - 1))),
                    reads=[vtk, ptk], writes=[aok], pe_attach=(s_ > 0))
                P.add("tensor", (lambda e, as_=as_, pt=pt, s_=s_, c0=c0, ns=ns: e.matmul(
                    as_[:, c0:512], onesb[:, :], pt[:, c0:512], start=(s_ == 0), stop=(s_ == ns - 1))),
                    reads=[pfx + "onesb", ptk], writes=[ask])

            pairs = [(m, s_) for m in range(2) for s_ in range(ns)]
            LA = 2
            queue = [qk_b(*pairs[i_]) for i_ in range(min(LA, len(pairs)))]
            for i_, (m, s_) in enumerate(pairs):
                if i_ + LA < len(pairs):
                    queue.append(qk_b(*pairs[i_ + LA]))
                pv_b(m, s_, queue.pop(0))
            P.add("vector", lambda e: e.reciprocal(out=r0t[:, :], in_=acc_s[0][:, :]), reads=[pfx + "accs0"], writes=[pfx + "r0t"])
            P.add("vector", lambda e: e.reciprocal(out=r1t[:, :], in_=acc_s[1][:, :]), reads=[pfx + "accs1"], writes=[pfx + "r1t"])
            P.add("vector", lambda e: e.tensor_tensor(out=o0t[:, :], in0=acc_o[0][:, :], in1=r0t[:, :], op=ALU.mult),
                  reads=[pfx + "acco0", pfx + "r0t"], writes=[pfx + "o0t"])
            P.add("vector", lambda e: e.tensor_tensor(out=o1t[:, :], in0=acc_o[1][:, :], in1=r1t[:, :], op=ALU.mult),
                  reads=[pfx + "acco1", pfx + "r1t"], writes=[pfx + "o1t"])
            P.add("vector", lambda e: e.scalar_tensor_tensor(out=o0t[:, :], in0=o1t[:, :], scalar=neglam[:, 0:1], in1=o0t[:, :],
                                                            op0=ALU.mult, op1=ALU.add),
                  reads=[pfx + "o1t", pfx + "neglam"], writes=[pfx + "o0t"])
            P.add("gpsimd", lambda e: e.tensor_tensor(out=sqt[:, :], in0=o0t[:, :], in1=o0t[:, :], op=ALU.mult),
                  reads=[pfx + "o0t"], writes=[pfx + "sqt"])
            P.add("tensor", lambda e: e.matmul(ps_ms[:, :], onesf[:, :], sqt[:, :], start=True, stop=True),
                  reads=[pfx + "onesf", pfx + "sqt"], writes=[pfx + "psms"])
            P.add("scalar", lambda e: e.activation(out=r0t[:, :], in_=ps_ms[:, :], func=AF.Sqrt, bias=P.epsc[:, 0:1], scale=1.0 / 128.0),
                  reads=[pfx + "psms", "epsc"], writes=[pfx + "r0t"])
            P.add("vector", lambda e: e.reciprocal(out=r1t[:, :], in_=r0t[:, :]), reads=[pfx + "r0t"], writes=[pfx + "r1t"])
            P.add("vector", lambda e: e.tensor_tensor(out=o0t[:, :], in0=o0t[:, :], in1=r1t[:, :], op=ALU.mult),
                  reads=[pfx + "r1t"], writes=[pfx + "o0t"])
            os_ = oTs[oc % 2]; osk = pfx + "oTs%d" % (oc % 2)
            oc += 1
            P.add("vector", (lambda e, os_=os_: e.tensor_scalar(out=os_[:, :], in0=o0t[:, :], scalar1=sg[:, 0:1], scalar2=None,
                                                               op0=ALU.mult)), reads=[pfx + "o0t", pfx + "sg"], writes=[osk])
            P.dma("gpsimd", osk + "o", A["oT"][h * 128:(h + 1) * 128, J * 512:(J + 1) * 512], os_[:, :], reads=[osk], writes=["oT"])


def phase_aproj(P, A, l, xin, xin_key, pfx):
    j = l // 2
    idf, idb = make_ident(P, pfx, A["ident"])
    modc = load_mod_cols(P, pfx, A["modT"], l, 0)
    wb = P.sb(pfx + "wb", [128, 8, A_IN], BF16)
    wv = A["a_w_in"][j].rearrange("(c p) f -> p c f", p=128)
    for c in range(8):
        P.dma("gpsimd", pfx + "w", wb[:, c, :], wv[:, c, :], writes=[pfx + "wb"])
    wuk = P.sb(pfx + "wuk", [128, 8, 256], BF16)
    P.dma("gpsimd", pfx + "wuk", wuk[:, :, :], A["a_w_uk"][j].rearrange("(hp two) d r -> (two d) hp r", two=2),
          writes=[pfx + "wuk"])
    wuvb = P.sb(pfx + "wuvb", [128, 2, 16, 64], BF16)
    for rc in range(2):
        P.dma("gpsimd", pfx + "wuvb", wuvb[:, rc, :, :], A["a_w_uv"][j][:, rc * 128:(rc + 1) * 128, :].rearrange("h p e -> p h e"),
              writes=[pfx + "wuvb"])
    kvn = P.sb(pfx + "kvn", [128, 256], F32)
    P.dma("sync", pfx + "kvn", kvn[:, :], A["a_kv_norm"][j].partition_broadcast(128), writes=[pfx + "kvn"])
    xtr = [P.sb(pfx + "xtr%d" % i, [128, 1024], F32) for i in range(6)]
    hT = P.sb(pfx + "hT", [128, 8, 512], BF16)
    stg = [P.sb(pfx + "stg%d" % i, [128, 512], BF16) for i in range(4)]
    cst = [P.sb(pfx + "cst%d" % i, [128, 256], BF16) for i in range(2)]
    iwst = [P.sb(pfx + "iwst%d" % i, [128, 8], F32) for i in range(2)]
    cTs = [P.sb(pfx + "cTs%d" % i, [128, 2, 512], BF16) for i in range(2)]
    vst = [P.sb(pfx + "vst%d" % i, [128, 16, 65], BF16) for i in range(2)]
    for i in range(2):
        P.add("vector", (lambda e, i=i: e.memset(vst[i][:, :, 64:65], 1.0)), writes=[pfx + "vst%d" % i])
    wukT = P.sb(pfx + "wukT", [128, 2, 1024], BF16)
    junk = P.sb(pfx + "junk", [128, 256], F32)
    sm = [P.sb(pfx + "sm%d" % i, [128, 3], F32) for i in range(2)]
    ps_tr = [(P.ps(pfx + "ptr%d" % i, [128, 512], F32), pfx + "ptr%d" % i) for i in range(2)]
    ps_o = [P.ps(pfx + "po%d" % i, [128, 512], F32) for i in range(3)]
    ps_c = [P.ps(pfx + "pc%d" % i, [128, 264], F32) for i in range(2)]
    ps_t = P.ps(pfx + "pt", [128, 2, 128], BF16)
    cnt = [0]
    st8 = {"xc": 0, "oc": 0, "sc": 0, "bc": 0, "vc": 0}
    for hp in range(8):
        for rc in range(2):
            P.add("tensor", (lambda e, hp=hp, rc=rc: e.transpose(out=ps_t[:, rc, :], in_=wuk[:, hp, rc * 128:(rc + 1) * 128],
                                                                identity=idb[:, :])), reads=[pfx + "wuk", pfx + "idb"], writes=[pfx + "pt"])
        P.add("scalar", (lambda e, hp=hp: e.activation(out=wukT[:, :, hp * 128:(hp + 1) * 128], in_=ps_t[:, :, :], func=AF.Copy)),
              reads=[pfx + "pt"], writes=[pfx + "wukT"])

    def evac(po_ap, pok, width, dst_ap, dkey, parts=128, scale=None):
        i = st8["sc"] % 4
        st8["sc"] += 1
        sg = stg[i]
        sgk = pfx + "stg%d" % i
        if scale is not None or st8["sc"] % 2 == 0:
            P.add("scalar", (lambda e: e.activation(out=sg[0:parts, 0:width], in_=po_ap, func=AF.Copy,
                                                    scale=(1.0 if scale is None else scale))), reads=[pok], writes=[sgk])
        else:
            P.add("vector", (lambda e: e.tensor_copy(out=sg[0:parts, 0:width], in_=po_ap)), reads=[pok], writes=[sgk])
        P.dma("gpsimd", sgk + "o", dst_ap, sg[0:parts, 0:width], reads=[sgk], writes=[dkey])

    def next_po():
        po = ps_o[st8["oc"] % 3]
        pok = pfx + "po%d" % (st8["oc"] % 3)
        st8["oc"] += 1
        return po, pok

    for t in range(DBG.get("T", TL // 512)):
        tiles = []
        for b in range(4):
            i = st8["xc"] % 6
            st8["xc"] += 1
            r0 = t * 512 + b * 128
            P.dma("sync", pfx + "xtr%d" % i, xtr[i][:, :], xin[r0:r0 + 128, :], reads=[xin_key], writes=[pfx + "xtr%d" % i])
            tiles.append((xtr[i], pfx + "xtr%d" % i))
        emit_hT(P, pfx, tiles, 4, idf, modc, hT, 0, ps_tr, cnt)
        tsl = slice(t * 512, (t + 1) * 512)

        def proj_fm(col0, ncol):
            po, pok = next_po()
            for c in range(8):
                P.add("tensor", (lambda e, po=po, c=c: e.matmul(po[0:ncol, :], wb[:, c, col0:col0 + ncol], hT[:, c, :],
                                                               start=(c == 0), stop=(c == 7))),
                      reads=[pfx + "wb", pfx + "hT"], writes=[pok])
            return po, pok

        for fo in range(8):
            po, pok = proj_fm(fo * 128, 128)
            evac(po[:, :], pok, 512, A["qT"][fo * 128:(fo + 1) * 128, tsl], "qT", scale=0.125)
        for fo in range(4):
            po, pok = proj_fm(1280 + fo * 128, 128)
            evac(po[:, :], pok, 512, A["iqT"][fo * 128:(fo + 1) * 128, tsl], "iqT")
        po, pok = proj_fm(1792, 64)
        evac(po[0:64, :], pok, 512, A["ikT"][0:64, tsl], "ikT", parts=64)
        ct = cTs[t % 2]
        ctk = pfx + "cTs%d" % (t % 2)
        for b in range(4):
            bi = st8["bc"]
            st8["bc"] += 1
            pc = ps_c[bi % 2]
            pck = pfx + "pc%d" % (bi % 2)
            for c in range(8):
                P.add("tensor", (lambda e, pc=pc, c=c, b=b: e.matmul(pc[:, 0:256], hT[:, c, b * 128:(b + 1) * 128],
                                                                    wb[:, c, 1024:1280], start=(c == 0), stop=(c == 7))),
                      reads=[pfx + "wb", pfx + "hT"], writes=[pck])
            for c in range(8):
                P.add("tensor", (lambda e, pc=pc, c=c, b=b: e.matmul(pc[:, 256:264], hT[:, c, b * 128:(b + 1) * 128],
                                                                    wb[:, c, 1856:1864], start=(c == 0), stop=(c == 7))),
                      reads=[pfx + "wb", pfx + "hT"], writes=[pck])
            s3 = sm[bi % 2]
            s3k = pfx + "sm%d" % (bi % 2)
            P.add("scalar", (lambda e, pc=pc, s3=s3: e.activation(out=junk[:, :], in_=pc[:, 0:256], func=AF.Square,
                                                                 accum_out=s3[:, 0:1])), reads=[pck], writes=[pfx + "junk", s3k],
                  noattach=True)
            P.add("scalar", (lambda e, s3=s3: e.activation(out=s3[:, 1:2], in_=s3[:, 0:1], func=AF.Sqrt, bias=P.epsc[:, 0:1],
                                                          scale=1.0 / 256.0)), reads=[s3k, "epsc"], writes=[s3k])
            P.add("vector", (lambda e, s3=s3: e.reciprocal(out=s3[:, 2:3], in_=s3[:, 1:2])), reads=[s3k], writes=[s3k])
            cs = cst[bi % 2]
            csk = pfx + "cst%d" % (bi % 2)
            P.add("vector", (lambda e, pc=pc, s3=s3, cs=cs: e.scalar_tensor_tensor(
                out=cs[:, :], in0=pc[:, 0:256], scalar=s3[:, 2:3], in1=kvn[:, :], op0=ALU.mult, op1=ALU.mult)),
                reads=[pck, s3k, pfx + "kvn"], writes=[csk])
            r0 = t * 512 + b * 128
            iws = iwst[bi % 2]
            iwk = pfx + "iwst%d" % (bi % 2)
            P.add("vector", (lambda e, pc=pc, iws=iws: e.tensor_scalar(out=iws[:, :], in0=pc[:, 256:264],
                                                                      scalar1=float(8 ** -0.5 * 64 ** -0.5), scalar2=None,
                                                                      op0=ALU.mult)), reads=[pck], writes=[iwk])
            P.dma("gpsimd", iwk + "o", A["iw"][r0:r0 + 128, :], iws[:, :], reads=[iwk], writes=["iw"])
            for rc in range(2):
                P.add("tensor", (lambda e, cs=cs, rc=rc: e.transpose(out=ps_t[:, rc, :], in_=cs[:, rc * 128:(rc + 1) * 128],
                                                                    identity=idb[:, :])), reads=[csk, pfx + "idb"], writes=[pfx + "pt"])
            P.add("scalar", (lambda e, ct=ct, b=b: e.activation(out=ct[:, :, b * 128:(b + 1) * 128], in_=ps_t[:, :, :], func=AF.Copy)),
                  reads=[pfx + "pt"], writes=[ctk])
        for hp in range(8):
            po, pok = next_po()
            for rc in range(2):
                P.add("tensor", (lambda e, po=po, hp=hp, rc=rc, ct=ct: e.matmul(po[:, :], wukT[:, rc, hp * 128:(hp + 1) * 128],
                                                                               ct[:, rc, :], start=(rc == 0), stop=(rc == 1))),
                      reads=[pfx + "wukT", ctk], writes=[pok])
            evac(po[:, :], pok, 512, A["kT"][hp * 128:(hp + 1) * 128, tsl], "kT")
        for b in range(4):
            vs = vst[st8["vc"] % 2]
            vsk = pfx + "vst%d" % (st8["vc"] % 2)
            st8["vc"] += 1
            for half in range(2):
                po, pok = next_po()
                for rc in range(2):
                    P.add("tensor", (lambda e, po=po, b=b, rc=rc, half=half, ct=ct: e.matmul(
                        po[:, :], ct[:, rc, b * 128:(b + 1) * 128], wuvb[:, rc, half * 8:(half + 1) * 8, :],
                        start=(rc == 0), stop=(rc == 1))), reads=[pfx + "wuvb", ctk], writes=[pok])
                if half == 0:
                    P.add("scalar", (lambda e, po=po, vs=vs, half=half: e.activation(
                        out=vs[:, half * 8:(half + 1) * 8, 0:64], in_=po[:, :].rearrange("p (h e) -> p h e", h=8), func=AF.Copy)),
                        reads=[pok], writes=[vsk])
                else:
                    P.add("vector", (lambda e, po=po, vs=vs, half=half: e.tensor_copy(
                        out=vs[:, half * 8:(half + 1) * 8, 0:64], in_=po[:, :].rearrange("p (h e) -> p h e", h=8))),
                        reads=[pok], writes=[vsk])
            r0 = t * 512 + b * 128
            P.dma("gpsimd", vsk + "o", A["vaug"][r0:r0 + 128, :], vs[:, :, :].rearrange("p h e -> p (h e)"), reads=[vsk], writes=["vaug"])


N_IT = 20
TOPK = 256


def phase_aattn(P, A, l, pfx):
    j = l // 2
    idf, idb = make_ident(P, pfx, A["ident"])
    onesb = P.sb(pfx + "onesb", [128, 128], BF16)
    P.add("vector", lambda e: e.memset(onesb[:, :], 1.0), writes=[pfx + "onesb"])
    onesf = P.sb(pfx + "onesf", [128, 128], F32)
    P.add("vector", lambda e: e.memset(onesf[:, :], 1.0), writes=[pfx + "onesf"])
    b31 = P.sb(pfx + "b31", [128, 16], F32)
    P.dma("sync", pfx + "b31", b31[:, :], A["rel_bias"][31, :].partition_broadcast(128), writes=[pfx + "b31"])
    BTf = P.sb(pfx + "BTf", [128, 256], F32)
    BT = P.sb(pfx + "BT", [128, 16, 256], BF16)
    for c in range(16):
        P.dma("sync", pfx + "BTf", BTf[:, :], A["biasT"][c], writes=[pfx + "BTf"])
        P.add("vector", (lambda e, c=c: e.tensor_scalar(out=BT[:, c, :], in0=BTf[:, :], scalar1=b31[:, c:c + 1],
                                                        scalar2=None, op0=ALU.subtract)),
              reads=[pfx + "BTf", pfx + "b31"], writes=[pfx + "BT"])
    cmask = P.sb(pfx + "cmask", [128, 128], F32)
    P.dma("sync", pfx + "cmask", cmask[:, :], A["cmask"], writes=[pfx + "cmask"])
    pw = P.sb(pfx + "pw", [128, N_IT + 1], F32)
    for i in range(N_IT + 1):
        P.add("vector", (lambda e, i=i: e.memset(pw[:, i:i + 1], float(2.0 ** -i))), writes=[pfx + "pw"])
    ik2 = P.sb(pfx + "ik2", [128, TL // 2], BF16)
    P.dma("sync", pfx + "ik2", ik2[0:64, :], A["ikT"][:, 0:TL // 2], reads=["ikT"], writes=[pfx + "ik2"])
    P.dma("sync", pfx + "ik2", ik2[64:128, :], A["ikT"][:, TL // 2:TL], reads=["ikT"], writes=[pfx + "ik2"])
    kTp = [P.sb(pfx + "kTp%d" % i, [128, TL], BF16) for i in range(2)]
    vp = [P.sb(pfx + "vp%d" % i, [128, NLB, 2, 65], BF16) for i in range(2)]
    qTp = [P.sb(pfx + "qTp%d" % i, [128, 256], BF16) for i in range(2)]
    otile = [P.sb(pfx + "otile%d" % i, [128, 1024], BF16) for i in range(2)]
    oTs = P.sb(pfx + "oTs", [128, 8, 256], BF16)
    rcp = P.sb(pfx + "rcp", [128, 4], F32)
    for i in range(2):
        P.add("gpsimd", (lambda e, i=i: e.memset(otile[i][:, :], 0.0)), writes=[pfx + "otile%d" % i])
    scores = P.sb(pfx + "scores", [128, TL], F32)
    NM = P.sb(pfx + "NM", [128, NLB, 256], BF16)
    iqt = [P.sb(pfx + "iqt%d" % i, [128, 8, 128], BF16) for i in range(2)]
    iwt = [P.sb(pfx + "iwt%d" % i, [128, 8], F32) for i in range(2)]
    rts = [P.sb(pfx + "rt%d" % i, [128, 512], F32) for i in range(3)]
    PT = [P.sb(pfx + "PT%d" % i, [128, 256], BF16) for i in range(6)]
    bs = P.sb(pfx + "bs", [128, 8 + N_IT + 1], F32)
    nd = P.sb(pfx + "nd", [128, 128], F32)
    ps_i = [P.ps(pfx + "pi%d" % i, [128, 512], F32) for i in range(2)]
    dscs = [P.sb(pfx + "dsc%d" % i, [128, TL], BF16) for i in range(2)]
    ps_s = [P.ps(pfx + "pss%d" % i, [128, 256], F32) for i in range(3)]
    acc_o = [P.ps(pfx + "acco%d" % i, [128, 65], F32) for i in range(2)]
    ps_tt = P.ps(pfx + "ptt", [128, 2, 128], BF16)
    iqv = A["iqT"].rearrange("(h d) t -> d h t", d=64)
    vav = A["vaug"].rearrange("(blk p) (h e) -> p blk h e", p=128, e=65)
    st8 = {"ic": 0, "rc": 0, "mc": 0, "sc": 0, "pc": 0, "oc": 0}
    skey = pfx + "scores"
    bkey = pfx + "bs"

    def idx(qb, db):
        nk = (qb + 1) * 128
        dbuf = dscs[db]
        dkey = pfx + "dsc%d" % db
        V = lambda fn, rd, wr, na=False: P.add("vector", fn, reads=rd, writes=wr, noattach=na)

        def scores_step():
            it = iqt[qb % 2]; itk = pfx + "iqt%d" % (qb % 2)
            P.dma("sync", itk, it[0:64, :, :], iqv[:, :, qb * 128:(qb + 1) * 128], reads=["iqT"], writes=[itk])
            P.dma("sync", itk, it[64:128, :, :], iqv[:, :, qb * 128:(qb + 1) * 128], reads=["iqT"], writes=[itk])
            wt = iwt[qb % 2]; wtk = pfx + "iwt%d" % (qb % 2)
            P.dma("sync", wtk, wt[:, :], A["iw"][qb * 128:(qb + 1) * 128, :], reads=["iw"], writes=[wtk])
            nst = (nk + 511) // 512
            for st in range(nst):
                wd = min(512, nk - st * 512)
                half = (st * 512) // (TL // 2)
                kc0 = st * 512 - half * (TL // 2)
                for h in range(8):
                    pi = ps_i[st8["ic"] % 2]; pik = pfx + "pi%d" % (st8["ic"] % 2)
                    st8["ic"] += 1
                    P.add("tensor", (lambda e, pi=pi, it=it, h=h, half=half, kc0=kc0, wd=wd: e.matmul(
                        pi[:, 0:wd], it[64 * half:64 * half + 64, h, :], ik2[64 * half:64 * half + 64, kc0:kc0 + wd],
                        start=True, stop=True)), reads=[itk, pfx + "ik2"], writes=[pik])
                    rt = rts[st8["rc"] % 3]; rk = pfx + "rt%d" % (st8["rc"] % 3)
                    st8["rc"] += 1
                    P.add("scalar", (lambda e, pi=pi, rt=rt, wd=wd: e.activation(out=rt[:, 0:wd], in_=pi[:, 0:wd], func=AF.Relu)),
                          reads=[pik], writes=[rk])
                    sl = slice(st * 512, st * 512 + wd)
                    if h == 0:
                        P.add("vector", (lambda e, rt=rt, wt=wt, sl=sl, wd=wd: e.tensor_scalar(
                            out=scores[:, sl], in0=rt[:, 0:wd], scalar1=wt[:, 0:1], scalar2=None, op0=ALU.mult)),
                            reads=[rk, wtk], writes=[skey])
                    else:
                        P.add("vector", (lambda e, rt=rt, wt=wt, sl=sl, wd=wd, h=h: e.scalar_tensor_tensor(
                            out=scores[:, sl], in0=rt[:, 0:wd], scalar=wt[:, h:h + 1], in1=scores[:, sl],
                            op0=ALU.mult, op1=ALU.add)), reads=[rk, wtk, skey], writes=[skey])

        def setup_step():
            V(lambda e: e.tensor_reduce(out=bs[:, 0:1], in_=scores[:, 0:nk], axis=AX.X, op=ALU.max, apply_absolute_value=True),
              [skey], [bkey])
            V(lambda e: e.tensor_scalar(out=bs[:, 1:2], in0=bs[:, 0:1], scalar1=1.0, scalar2=None, op0=ALU.add), [bkey], [bkey])
            V(lambda e: e.tensor_scalar(out=bs[:, 8:8 + N_IT + 1], in0=pw[:, :], scalar1=bs[:, 1:2], scalar2=None, op0=ALU.mult),
              [bkey, pfx + "pw"], [bkey])
            V(lambda e: e.tensor_tensor(out=scores[:, qb * 128:(qb + 1) * 128], in0=scores[:, qb * 128:(qb + 1) * 128],
                                        in1=cmask[:, :], op=ALU.add), [skey, pfx + "cmask"], [skey])
            V(lambda e: e.tensor_scalar(out=bs[:, 2:3], in0=bs[:, 1:2], scalar1=-1.0, scalar2=None, op0=ALU.mult), [bkey], [bkey])
            V(lambda e: e.tensor_tensor(out=bs[:, 3:4], in0=bs[:, 2:3], in1=bs[:, 8:9], op=ALU.add), [bkey], [bkey])

        def one_iter(i):
            V(lambda e: e.tensor_scalar(out=dbuf[:, 0:nk], in0=scores[:, 0:nk], scalar1=bs[:, 3:4], scalar2=None,
                                        op0=ALU.is_ge, op1=ALU.add, accum_out=bs[:, 4:5]), [skey, bkey], [dkey, bkey], True)
            V((lambda e, i=i: e.tensor_scalar(out=bs[:, 5:6], in0=bs[:, 4:5], scalar1=float(TOPK), scalar2=bs[:, 8 + i:9 + i],
                                              op0=ALU.is_ge, op1=ALU.mult)), [bkey], [bkey])
            V(lambda e: e.tensor_tensor(out=bs[:, 2:3], in0=bs[:, 2:3], in1=bs[:, 5:6], op=ALU.add), [bkey], [bkey])
            V((lambda e, i=i: e.tensor_tensor(out=bs[:, 3:4], in0=bs[:, 2:3], in1=bs[:, 9 + i:10 + i], op=ALU.add)), [bkey], [bkey])

        def dsc_step():
            V(lambda e: e.tensor_scalar(out=dbuf[:, 0:nk], in0=scores[:, 0:nk], scalar1=bs[:, 2:3], scalar2=None,
                                        op0=ALU.subtract), [skey, bkey], [dkey])

        return [scores_step, setup_step] + [(lambda i=i: one_iter(i)) for i in range(N_IT)] + [dsc_step]

    def nm_gen(qb, db):
        jj = qb % 2
        dbuf = dscs[db]
        dkey = pfx + "dsc%d" % db
        for sb in range(qb + 1):
            mi = st8["mc"] % 2
            st8["mc"] += 1
            pmk = pfx + "ptt"
            P.add("tensor", (lambda e, mi=mi, sb=sb: e.transpose(out=ps_tt[:, mi, :], in_=dbuf[:, sb * 128:(sb + 1) * 128],
                                                                identity=idb[:, :])), reads=[dkey, pfx + "idb"], writes=[pmk])
            P.add("vector", (lambda e, mi=mi, sb=sb, jj=jj: e.tensor_scalar(
                out=NM[:, sb, jj * 128:(jj + 1) * 128], in0=ps_tt[:, mi, :], scalar1=0.0, scalar2=float(NEGM),
                op0=ALU.is_lt, op1=ALU.mult)), reads=[pmk], writes=[pfx + "NM"])

    def attn(J, steps=()):
        steps = list(steps)
        nsteps0 = len(steps)
        ns = 2 * J + 2
        nk = ns * 128
        nh = DBG.get("aheads", 16)

        def load_pair(hp):
            kt = kTp[hp % 2]; ktk = pfx + "kTp%d" % (hp % 2)
            vt = vp[hp % 2]; vtk = pfx + "vp%d" % (hp % 2)
            qt = qTp[hp % 2]; qtk = pfx + "qTp%d" % (hp % 2)
            P.dma("sync", ktk, kt[:, 0:nk], A["kT"][hp * 128:(hp + 1) * 128, 0:nk], reads=["kT"], writes=[ktk])
            P.dma("sync", vtk, vt[:, 0:ns, :, :], vav[:, 0:ns, 2 * hp:2 * hp + 2, :], reads=["vaug"], writes=[vtk])
            P.dma("sync", qtk, qt[:, :], A["qT"][hp * 128:(hp + 1) * 128, J * 256:(J + 1) * 256], reads=["qT"], writes=[qtk])

        def qk(h, s):
            hp, hh = h // 2, h % 2
            kt = kTp[hp % 2]; ktk = pfx + "kTp%d" % (hp % 2)
            qt = qTp[hp % 2]; qtk = pfx + "qTp%d" % (hp % 2)
            k = s - 2 * J
            c0 = max(0, k) * 128
            near = k >= -1
            si = st8["sc"] % 3
            st8["sc"] += 1
            psk = pfx + "pss%d" % si
            P.add("tensor", (lambda e, si=si, s=s, hh=hh, c0=c0, kt=kt, qt=qt: e.matmul(
                ps_s[si][:, c0:256], kt[64 * hh:64 * hh + 64, s * 128:(s + 1) * 128], qt[64 * hh:64 * hh + 64, c0:256],
                start=True, stop=False)), reads=[ktk, qtk], writes=[psk], pe_attach=(s > 0))
            P.add("tensor", (lambda e, si=si, s=s, c0=c0, near=near: e.matmul(
                ps_s[si][:, c0:256], idb[:, :], NM[:, s, c0:256], start=False, stop=(not near))),
                reads=[pfx + "idb", pfx + "NM"], writes=[psk])
            if near:
                if k == -1:
                    P.add("tensor", (lambda e, si=si, h=h: e.matmul(ps_s[si][:, 0:128], idb[:, :], BT[:, h, 128:256],
                                                                   start=False, stop=True)),
                          reads=[pfx + "idb", pfx + "BT"], writes=[psk])
                else:
                    w = 256 - c0
                    P.add("tensor", (lambda e, si=si, h=h, c0=c0, w=w: e.matmul(ps_s[si][:, c0:c0 + w], idb[:, :], BT[:, h, 0:w],
                                                                               start=False, stop=True)),
                          reads=[pfx + "idb", pfx + "BT"], writes=[psk])
            pt = PT[st8["pc"] % 6]; ptk = pfx + "PT%d" % (st8["pc"] % 6)
            st8["pc"] += 1
            P.add("scalar", (lambda e, si=si, pt=pt, c0=c0, h=h: e.activation(
                out=pt[:, c0:256], in_=ps_s[si][:, c0:256], func=AF.Exp, bias=b31[:, h:h + 1], scale=1.0)),
                reads=[psk, pfx + "b31"], writes=[ptk])
            return pt, ptk, c0

        def pv(h, s, pre):
            pt, ptk, c0 = pre
            hp, hh = h // 2, h % 2
            vt = vp[hp % 2]; vtk = pfx + "vp%d" % (hp % 2)
            for jj in range(2):
                if jj * 128 < c0:
                    continue
                last = (s == ns - 1) if jj == 1 else (s == ns - 2)
                P.add("tensor", (lambda e, pt=pt, jj=jj, s=s, hh=hh, vt=vt, last=last: e.matmul(
                    acc_o[jj][:, :], pt[:, jj * 128:(jj + 1) * 128], vt[:, s, hh, :],
                    start=(s == 0), stop=last)), reads=[vtk, ptk], writes=[pfx + "acco%d" % jj])

        def finish_head(h):
            for jj in range(2):
                P.add("vector", (lambda e, jj=jj: e.reciprocal(out=rcp[:, jj:jj + 1], in_=acc_o[jj][:, 64:65])),
                      reads=[pfx + "acco%d" % jj], writes=[pfx + "rcp%d" % jj])
                P.add("vector", (lambda e, jj=jj, h=h: e.tensor_scalar(out=otile[jj][:, h * 64:(h + 1) * 64], in0=acc_o[jj][:, 0:64],
                                                                      scalar1=rcp[:, jj:jj + 1], scalar2=None, op0=ALU.mult)),
                      reads=[pfx + "acco%d" % jj, pfx + "rcp%d" % jj], writes=[pfx + "otile%d" % jj])
            nstep = (nsteps0 + nh - 1) // nh
            if h == nh - 1:
                nstep = len(steps)
            for _ in range(min(nstep, len(steps))):
                steps.pop(0)()

        load_pair(0)
        pairs = [(h, s) for h in range(nh) for s in range(ns)]
        LA = 2
        queue = [qk(*pairs[i]) for i in range(min(LA, len(pairs)))]
        for i, (h, s) in enumerate(pairs):
            if s == 0 and h % 2 == 0 and h + 2 < nh:
                load_pair(h // 2 + 1)
            if i + LA < len(pairs):
                queue.append(qk(*pairs[i + LA]))
            pv(h, s, queue.pop(0))
            if s == ns - 1:
                finish_head(h)
        for jj in range(2):
            for c in range(8):
                P.add("tensor", (lambda e, jj=jj, c=c: e.transpose(out=ps_tt[:, c % 2, :], in_=otile[jj][:, c * 128:(c + 1) * 128],
                                                                  identity=idb[:, :])),
                      reads=[pfx + "otile%d" % jj, pfx + "idb"], writes=[pfx + "ptt"])
                if c % 2 == 1:
                    P.add("scalar", (lambda e, jj=jj, c=c: e.activation(out=oTs[:, c - 1:c + 1, jj * 128:(jj + 1) * 128],
                                                                       in_=ps_tt[:, :, :], func=AF.Copy)),
                          reads=[pfx + "ptt"], writes=[pfx + "oTs"])
        P.dma("gpsimd", pfx + "oTso", A["oT"].rearrange("(c p) t -> p c t", p=128)[:, :, J * 256:(J + 1) * 256], oTs[:, :, :],
              reads=[pfx + "oTs"], writes=["oT"])

    NJ = DBG.get("AJ", TL // 256)
    for st_ in idx(0, 0):
        st_()
    nm_gen(0, 0)
    for st_ in idx(1, 1):
        st_()
    nm_gen(1, 1)
    for J in range(NJ):
        steps = (idx(2 * J + 2, 0) + idx(2 * J + 3, 1)) if J + 1 < NJ else []
        attn(J, steps)
        if J + 1 < NJ:
            nm_gen(2 * J + 2, 0)
            nm_gen(2 * J + 3, 1)


def _new_nc():
    return bass.Bass("TRN2", target_bir_lowering=False)


def build_single(phase_name, l=0):
    nc = _new_nc()
    A = {}
    ins, outs = [], []

    def din(name, shape, dt=F32):
        A[name] = nc.dram_tensor(name, list(shape), dt, kind="ExternalInput").ap()
        ins.append(name)

    def dout(name, shape, dt=F32):
        A[name] = nc.dram_tensor(name, list(shape), dt, kind="ExternalOutput").ap()
        outs.append(name)

    with ExitStack() as es:
        P = Prog(nc, es)
        make_epsc(P)
        if phase_name == "mod":
            din("ccol", [128, 8]); din("ada_w", [DEPTH, D, 6 * D]); din("ada_b", [DEPTH, 6 * D]); din("ada_bT", [128, 4 * 48])
            dout("modT", [128, 4 * 48]); dout("modrow", [DEPTH, 2, D])
            phase_mod(P, A)
            P.final_wait()
        elif phase_name == "mlp":
            din("ident", [128, 128]); din("modT", [128, 4 * 48]); din("modrow", [DEPTH, 2, D])
            din("ln_g", [DEPTH, 2, D]); din("ln_b", [DEPTH, 2, D])
            din("mlp_w1", [DEPTH, D, DFF]); din("mlp_w2", [DEPTH, DFF, D]); din("xin", [TL, D])
            dout("xout", [TL, D])
            phase_mlp(P, A, l, A["xin"], "xin", A["xout"], "xout", "ml_")
            P.final_wait()
        elif phase_name == "bproj":
            din("ident", [128, 128]); din("modT", [128, 4 * 48]); din("b_w_in", [2, D, 3072]); din("xin", [TL, D])
            dout("qT", [D, TL], BF16); dout("kT", [D, TL], BF16); dout("v", [TL, D], BF16)
            phase_bproj(P, A, l, A["xin"], "xin", "bp_")
            P.final_wait()
        elif phase_name == "battn":
            din("ident", [128, 128]); din("rel_bias", [32, 16]); din("biasT", [16, 128, 256]); din("b_lambda", [2, 4, 64])
            din("b_subln", [2, 128]); din("qT", [D, TL], BF16); din("kT", [D, TL], BF16); din("v", [TL, D], BF16)
            dout("oT", [D, TL], BF16)
            phase_battn(P, A, l, "ba_")
            P.final_wait()
        elif phase_name == "aproj":
            din("ident", [128, 128]); din("modT", [128, 4 * 48]); din("a_w_in", [2, D, A_IN]); din("a_w_uk", [2, 16, 64, 256])
            din("a_kv_norm", [2, 256]); din("xin", [TL, D])
            din("a_w_uv", [2, 16, 256, 64])
            dout("qT", [D, TL], BF16); dout("kT", [D, TL], BF16); dout("vaug", [TL, 16 * 65], BF16)
            dout("iqT", [512, TL], BF16); dout("ikT", [64, TL], BF16); dout("iw", [TL, 8], F32)
            phase_aproj(P, A, l, A["xin"], "xin", "ap_")
            P.final_wait()
        elif phase_name == "aattn":
            din("ident", [128, 128]); din("rel_bias", [32, 16]); din("biasT", [16, 128, 256]); din("cmask", [128, 128])
            din("qT", [D, TL], BF16); din("kT", [D, TL], BF16); din("vaug", [TL, 16 * 65], BF16)
            din("iqT", [512, TL], BF16); din("ikT", [64, TL], BF16); din("iw", [TL, 8], F32)
            dout("oT", [D, TL], BF16)
            phase_aattn(P, A, l, "aa_")
            P.final_wait()
        elif phase_name == "outln":
            din("modrow", [DEPTH, 2, D]); din("ln_g", [DEPTH, 2, D]); din("ln_b", [DEPTH, 2, D])
            din("w_o", [D, D]); din("oT", [D, TL], BF16); din("xin", [TL, D])
            dout("xout", [TL, D])
            phase_outln(P, A, l, A["w_o"], A["oT"], "oT", A["xin"], "xin", A["xout"], "xout", "ol_")
            P.final_wait()
        else:
            raise ValueError(phase_name)
        P.finish()
        nops = P.n
    return nc, ins, outs, nops


def _to_local(a_b, hf):
    s = a_b.shape
    return np.ascontiguousarray(a_b.reshape(32, 2, 128, *s[1:])[:, hf].reshape(TL, *s[1:]))


def _from_local(parts):
    s = parts[0].shape
    o = np.empty((32, 2, 128) + s[1:], parts[0].dtype)
    for hf in range(2):
        o[:, hf] = parts[hf].reshape(32, 128, *s[1:])
    return o.reshape(SEQ, *s[1:])


def run_phase(nc, ins, in_maps):
    res = run_bass_kernel_spmd(nc, [{k: m[k] for k in ins} for m in in_maps], core_ids=list(range(len(in_maps))))
    return res.results


def rel_bucket_np(dist):
    import math
    n = np.maximum(dist, 0)
    nf = np.maximum(n, 1).astype(np.float32)
    large = 16 + (np.log(nf / np.float32(16)) / np.float32(math.log(128 / 16)) * np.float32(16)).astype(np.int32)
    large = np.minimum(large, 31)
    return np.where(n < 16, n, large)


def make_biasT(rel_bias):
    s_ = np.arange(128)[:, None]
    q_ = np.arange(128)[None, :]
    dd = q_ - s_
    bd = rel_bucket_np(dd)
    bp = rel_bucket_np(dd + 128)
    out = np.empty((16, 128, 256), np.float32)
    for c in range(16):
        diag = rel_bias[:, c][bd]
        out[c, :, 0:128] = np.where(dd >= 0, diag, np.float32(NEGM))
        out[c, :, 128:256] = rel_bias[:, c][bp]
    return out


def build_fused(single=None):
    nc = _new_nc()
    A = {}
    NL = DEPTH if single is None else 1
    NM_ = 2 if single is None else 1

    def din(name, shape, dt=F32):
        A[name] = nc.dram_tensor(name, list(shape), dt, kind="ExternalInput").ap()

    def dint(name, shape, dt=F32):
        A[name] = nc.dram_tensor(name, list(shape), dt, kind="Internal").ap()

    din("x", [TL, D]); din("ccol", [128, 8]); din("ada_w", [NL, D, 6 * D]); din("ada_b", [NL, 6 * D])
    din("ada_bT", [128, NL * 48]); din("ident", [128, 128]); din("rel_bias", [32, 16]); din("biasT", [16, 128, 256])
    din("ln_g", [NL, 2, D]); din("ln_b", [NL, 2, D])
    if single is None or single % 2 == 0:
        din("cmask", [128, 128])
        din("a_w_in", [NM_, D, A_IN]); din("a_kv_norm", [NM_, 256]); din("a_w_uk", [NM_, 16, 64, 256])
        din("a_w_uv", [NM_, 16, 256, 64]); din("a_w_o", [NM_, D, D])
    if single is None or single % 2 == 1:
        din("b_w_in", [NM_, D, 3072]); din("b_lambda", [NM_, 4, 64]); din("b_subln", [NM_, 128]); din("b_w_o", [NM_, D, D])
    din("mlp_w1", [NL, D, DFF]); din("mlp_w2", [NL, DFF, D])
    A["out"] = nc.dram_tensor("out", [TL, D], F32, kind="ExternalOutput").ap()
    dint("modT", [128, NL * 48]); dint("modrow", [NL, 2, D]); dint("xA", [TL, D]); dint("xB", [TL, D])
    dint("vaug", [TL, 16 * 65], BF16)
    dint("iqT", [512, TL], BF16); dint("ikT", [64, TL], BF16); dint("iw", [TL, 8], F32)
    dint("qT", [D, TL], BF16); dint("kT", [D, TL], BF16); dint("v", [TL, D], BF16); dint("oT", [D, TL], BF16)
    nops = [0]

    def block(fn, outkeys):
        with ExitStack() as es:
            P = Prog(nc, es)
            make_epsc(P)
            fn(P)
            P.final_wait()
            P.finish()
            nops[0] += P.n

    block(lambda P: phase_mod(P, A, NL), ["modT", "modrow"])
    xcur = "x"
    llist = DBG.get("llist", list(range(DBG.get("layers", DEPTH))))
    if single is not None:
        llist = [0]
        DBG["true_l"] = single
    else:
        DBG.pop("true_l", None)
    for l in llist:
        j = l // 2
        kind_a = (l % 2 == 0) if single is None else (single % 2 == 0)
        if kind_a:
            block(lambda P: phase_aproj(P, A, l, A[xcur], "xin", "ap_"), ["qT", "kT", "vaug", "iqT", "ikT", "iw"])
            block(lambda P: phase_aattn(P, A, l, "aa_"), ["oT"])
            w_o = A["a_w_o"][j]
        else:
            block(lambda P: phase_bproj(P, A, l, A[xcur], "xin", "bp_"), ["qT", "kT", "v"])
            block(lambda P: phase_battn(P, A, l, "ba_"), ["oT"])
            w_o = A["b_w_o"][j]
        block(lambda P: phase_outln(P, A, l, w_o, A["oT"], "oT", A[xcur], "xin", A["xA"], "xout", "ol_"), ["xout"])
        last = (l == llist[-1])
        dst = "out" if last else "xB"
        block(lambda P: phase_mlp(P, A, l, A["xA"], "xin", A[dst], "xout", "ml_"), ["xout"])
        xcur = "xB"
    return nc, nops[0]


_CACHE = {}
FUSED = True


def kernel(**inputs):
    f32 = lambda a: np.ascontiguousarray(np.asarray(a, dtype=np.float32))
    x = f32(inputs["x"])
    c = f32(inputs["c"])
    rel_bias = f32(inputs["rel_bias"])
    W = {k: f32(inputs[k]) for k in ["ada_w", "ada_b", "ln_g", "ln_b", "a_w_in", "a_kv_norm", "a_w_uk", "a_w_uv", "a_w_o",
                                     "b_w_in", "b_lambda", "b_subln", "b_w_o", "mlp_w1", "mlp_w2"]}
    const = {
        "ident": np.eye(128, dtype=np.float32),
        "rel_bias": rel_bias,
        "biasT": make_biasT(rel_bias),
    }
    cmask = np.where(np.arange(128)[None, :] <= np.arange(128)[:, None], 0.0, -1e30).astype(np.float32)
    ccols = [np.ascontiguousarray(c[b].reshape(8, 128).T) for b in range(NB)]
    if FUSED:
        if "nc" not in _CACHE:
            _CACHE["nc"] = build_fused()
        nc, nops = _CACHE["nc"]
        shared = dict(const)
        shared.update(W)
        shared["ada_bT"] = np.ascontiguousarray(W["ada_b"].reshape(4 * 48, 128).T)
        shared["cmask"] = cmask
        in_maps = []
        for b in range(NB):
            m = dict(shared)
            m["x"] = x[b]
            m["ccol"] = ccols[b]
            in_maps.append(m)
        res = run_bass_kernel_spmd(nc, in_maps, core_ids=list(range(NB)))
        return np.stack([np.asarray(res.results[b]["out"], dtype=np.float32) for b in range(NB)], axis=0)
    xs = [x[b] for b in range(NB)]
    for L in range(DEPTH):
        key = "nc%d" % (L % 2)
        if ("L", L) not in _CACHE:
            _CACHE[("L", L)] = build_fused(single=L)
        nc, nops = _CACHE[("L", L)]
        j = L // 2
        shared = dict(const)
        for k in ["ada_w", "ada_b", "ln_g", "ln_b", "mlp_w1", "mlp_w2"]:
            shared[k] = np.ascontiguousarray(W[k][L:L + 1])
        shared["ada_bT"] = np.ascontiguousarray(W["ada_b"][L].reshape(48, 128).T)
        if L % 2 == 0:
            shared["cmask"] = cmask
            for k in ["a_w_in", "a_kv_norm", "a_w_uk", "a_w_uv", "a_w_o"]:
                shared[k] = np.ascontiguousarray(W[k][j:j + 1])
        else:
            for k in ["b_w_in", "b_lambda", "b_subln", "b_w_o"]:
                shared[k] = np.ascontiguousarray(W[k][j:j + 1])
        in_maps = []
        for b in range(NB):
            m = dict(shared)
            m["x"] = xs[b]
            m["ccol"] = ccols[b]
            in_maps.append(m)
        res = run_bass_kernel_spmd(nc, in_maps, core_ids=list(range(NB)))
        xs = [np.asarray(res.results[b]["out"], dtype=np.float32) for b in range(NB)]
    return np.stack(xs, axis=0)
```

```python
import numpy as np
from contextlib import ExitStack
import concourse.bass as bass
import concourse.mybir as mybir
from concourse.bass_utils import run_bass_kernel_spmd

F32 = mybir.dt.float32
BF16 = mybir.dt.bfloat16
U8 = mybir.dt.uint8
AF = mybir.ActivationFunctionType
ALU = mybir.AluOpType
AX = mybir.AxisListType

D = 1024
SEQ = 8192
NB = 4
DEPTH = 4
TL = 8192
NLB = 64
DFF = 4096
LN_EPS = 1e-5
DN_ALPHA = (2 * DEPTH) ** 0.25
A_IN = 1864
NEGM = -30000.0

DBG = {}
STATS = {}
ENGS = ["tensor", "vector", "scalar", "gpsimd", "sync"]
EPOCH = 30000
EPOCH_DMA = 1800


class Op:
    __slots__ = ("eng", "fn", "chan", "waits", "needs_inc", "val", "epoch", "isdma", "noattach")


class Prog:
    _uid = [0]

    def __init__(self, nc, es):
        self.nc = nc
        self.es = es
        Prog._uid[0] += 1
        self.uid = Prog._uid[0]
        self.ops = {e: [] for e in ENGS}
        self.chan_ops = {}
        self.lastw = {}
        self.readers = {}
        self.n = 0
        self.outkeys = []

    def sb(self, name, shape, dt):
        return self.es.enter_context(self.nc.sbuf_tensor("%s_u%d" % (name, self.uid), list(shape), dt))

    def ps(self, name, shape, dt=F32):
        esz = 4 if dt == F32 else 2
        n = 1
        for d in shape[1:]:
            n *= d
        per_bank = 2048 // esz
        tot = ((n + per_bank - 1) // per_bank) * per_bank
        h = self.es.enter_context(self.nc.psum_tensor("%s_u%d" % (name, self.uid), [128, tot], dt))
        ap = h[0:shape[0], 0:n]
        if len(shape) == 3:
            ap = ap.rearrange("p (a b) -> p a b", a=shape[1])
        return ap

    def add(self, eng, fn, reads=(), writes=(), dma=None, noattach=False, pe_attach=False):
        op = Op()
        op.noattach = noattach or (eng == "tensor" and not pe_attach)
        op.eng = eng
        op.fn = fn
        op.isdma = dma is not None
        op.chan = ("dma", dma) if dma is not None else eng
        op.needs_inc = op.isdma
        deps = []
        for k in reads:
            w = self.lastw.get(k)
            if w is not None:
                deps.append(w)
        for k in writes:
            w = self.lastw.get(k)
            if w is not None:
                deps.append(w)
            rd = self.readers.get(k)
            if rd:
                deps.extend(rd.values())
        waits = []
        seen = set()
        for d in deps:
            if id(d) in seen:
                continue
            seen.add(id(d))
            if (not d.isdma) and d.chan == eng and eng == "tensor":
                continue
            d.needs_inc = True
            waits.append(d)
        op.waits = waits
        for k in reads:
            self.readers.setdefault(k, {})[op.chan] = op
        for k in writes:
            self.lastw[k] = op
            self.readers[k] = {}
        self.ops[eng].append(op)
        self.chan_ops.setdefault(op.chan, []).append(op)
        self.n += 1
        return op

    DRAM_OUT = ("qT", "kT", "v", "vaug", "iqT", "ikT", "iw", "oT", "xout", "modT", "modrow")

    def dma(self, eng, chan, out, in_, reads=(), writes=()):
        w2 = []
        for k in writes:
            if k in Prog.DRAM_OUT:
                k = "%s#%d" % (k, len(self.outkeys))
                self.outkeys.append(k)
            w2.append(k)
        return self.add(eng, lambda e: e.dma_start(out=out, in_=in_), reads, w2, dma=chan)

    def final_wait(self, eng="gpsimd"):
        self.add(eng, None, reads=list(self.outkeys))

    def finish(self):
        nc = self.nc
        sems = {}
        for chan, lst in self.chan_ops.items():
            cnt = 0
            ep = EPOCH_DMA if chan[0] == "dma" else EPOCH
            for op in lst:
                if op.needs_inc:
                    op.epoch = cnt // ep
                    op.val = cnt % ep + 1
                    cnt += 1
                    key = (chan, op.epoch)
                    if key not in sems:
                        sems[key] = nc.alloc_semaphore(name="s%d_u%d" % (len(sems), self.uid))
        self.nsems = len(sems)
        with nc.Block() as block:
            self._emit(block, sems)
        nc.clear_and_free_semaphores(list(sems.values()))
        nc.all_engine_barrier()

    def _emit(self, block, sems):

        def emit(eng_name):
            ops = self.ops[eng_name]

            def body(e):
                waited = {}
                for op in ops:
                    need = {}
                    for d in op.waits:
                        cur = waited.get(d.chan)
                        if cur is not None and (cur[0] > d.epoch or (cur[0] == d.epoch and cur[1] >= d.val)):
                            continue
                        prev = need.get(d.chan)
                        if prev is None or (d.epoch, d.val) > (prev.epoch, prev.val):
                            need[d.chan] = d
                    need = list(need.values())
                    attach = None
                    if need and op.fn is not None and not op.noattach:
                        attach = need.pop()
                    for d in need:
                        e.wait_ge(sems[(d.chan, d.epoch)], d.val * (16 if d.isdma else 1))
                        waited[d.chan] = (d.epoch, d.val)
                        STATS[eng_name + "_wait"] = STATS.get(eng_name + "_wait", 0) + 1
                    if op.fn is None:
                        continue
                    ins = op.fn(e)
                    if attach is not None:
                        ins._wait_ge(sems[(attach.chan, attach.epoch)], attach.val * (16 if attach.isdma else 1))
                        waited[attach.chan] = (attach.epoch, attach.val)
                    STATS[eng_name] = STATS.get(eng_name, 0) + 1
                    if op.needs_inc:
                        ins.then_inc(sems[(op.chan, op.epoch)], 16 if op.isdma else 1)
            return body

        for en in ENGS:
            if self.ops[en]:
                getattr(block, en)(emit(en))


class Ctx:
    pass


def _rot(lst, i):
    return lst[i % len(lst)]


def load_mod_cols(P, pfx, modT_ap, l, which):
    nc = P.nc
    mt = P.sb(pfx + "modc", [128, 16], F32)
    base = l * 48 + which * 24
    P.dma("sync", pfx + "modc", mt[:, :], modT_ap[:, base:base + 16], reads=["modT"], writes=[pfx + "modc"])
    P.add("vector", lambda e: e.tensor_scalar(out=mt[:, 8:16], in0=mt[:, 8:16], scalar1=1.0, scalar2=None,
                                             op0=ALU.add), reads=[pfx + "modc"], writes=[pfx + "modc"])
    return mt


def load_bcast(P, name, src_ap_1d, n):
    t = P.sb(name, [128, n], F32)
    P.dma("sync", name, t[:, :], src_ap_1d.partition_broadcast(128), reads=["modrow"], writes=[name])
    return t


def make_ident(P, pfx, ident_ap):
    idf = P.sb(pfx + "idf", [128, 128], F32)
    P.dma("sync", pfx + "idf", idf[:, :], ident_ap, writes=[pfx + "idf"])
    idb = P.sb(pfx + "idb", [128, 128], BF16)
    P.add("vector", lambda e: e.tensor_copy(out=idb[:, :], in_=idf[:, :]), reads=[pfx + "idf"], writes=[pfx + "idb"])
    return idf, idb


def emit_hT(P, pfx, xblk_tiles, nblk, idf, modc, hT, col0, ps_tr_list, cnt):
    for c in range(8):
        pt, pk = _rot(ps_tr_list, cnt[0])
        cnt[0] += 1
        for b in range(nblk):
            xt, xk = xblk_tiles[b]
            P.add("tensor", (lambda e, pt=pt, xt=xt, b=b, c=c: e.transpose(
                out=pt[:, b * 128:(b + 1) * 128], in_=xt[:, c * 128:(c + 1) * 128], identity=idf[:, :])),
                reads=[xk, pfx + "idf"], writes=[pk])
        P.add("scalar", (lambda e, pt=pt, c=c: e.activation(
            out=hT[:, c, col0:col0 + nblk * 128], in_=pt[:, 0:nblk * 128], func=AF.Identity,
            bias=modc[:, c:c + 1], scale=modc[:, 8 + c:9 + c])),
            reads=[pk, pfx + "modc"], writes=[pfx + "hT"])


def emit_resid_ln(P, pfx, ps_y, ps_key, xt, xkey, gbc, lng, lnb, ot, okey, small, skey):
    nc = P.nc
    st, mv, rs = small
    P.add("vector", lambda e: e.tensor_tensor(out=ot[:, :].rearrange("p (a f) -> p a f", a=2), in0=ps_y[:, :, :],
                                             in1=gbc[:, :].rearrange("p (a f) -> p a f", a=2), op=ALU.mult),
          reads=[ps_key, pfx + "gbc"], writes=[okey])
    P.add("vector", lambda e: e.scalar_tensor_tensor(out=ot[:, :], in0=xt, scalar=float(DN_ALPHA), in1=ot[:, :],
                                                    op0=ALU.mult, op1=ALU.add),
          reads=[xkey], writes=[okey])
    P.add("vector", lambda e: e.bn_stats(out=st[:, 0, :], in_=ot[:, 0:512]), reads=[okey], writes=[skey])
    P.add("vector", lambda e: e.bn_stats(out=st[:, 1, :], in_=ot[:, 512:1024]), reads=[okey], writes=[skey])
    P.add("vector", lambda e: e.bn_aggr(out=mv[:, :], in_=st[:, :, :]), reads=[skey], writes=[skey])
    P.add("scalar", lambda e: e.activation(out=rs[:, 0:1], in_=mv[:, 1:2], func=AF.Sqrt, bias=P.epsc[:, 0:1], scale=1.0),
          reads=[skey, "epsc"], writes=[skey + "r"])
    P.add("vector", lambda e: e.reciprocal(out=rs[:, 1:2], in_=rs[:, 0:1]), reads=[skey + "r"], writes=[skey + "r2"])
    P.add("vector", lambda e: e.tensor_scalar(out=rs[:, 2:3], in0=mv[:, 0:1], scalar1=rs[:, 1:2], scalar2=-1.0,
                                             op0=ALU.mult, op1=ALU.mult), reads=[skey + "r2"], writes=[skey + "r2"])
    P.add("scalar", lambda e: e.activation(out=ot[:, :], in_=ot[:, :], func=AF.Identity, bias=rs[:, 2:3],
                                           scale=rs[:, 1:2]), reads=[skey + "r2", okey], writes=[okey])
    P.add("vector", lambda e: e.tensor_tensor(out=ot[:, :], in0=ot[:, :], in1=lng[:, :], op=ALU.mult),
          reads=[okey, pfx + "lng"], writes=[okey])
    P.add("gpsimd", lambda e: e.tensor_tensor(out=ot[:, :], in0=ot[:, :], in1=lnb[:, :], op=ALU.add),
          reads=[okey, pfx + "lnb"], writes=[okey])


def make_epsc(P):
    t = P.sb("epsc", [128, 1], F32)
    P.add("vector", lambda e: e.memset(t[:, :], LN_EPS), writes=["epsc"])
    P.epsc = t


def phase_mod(P, A, NL=DEPTH):
    nc = P.nc
    pfx = "md_"
    cT = P.sb(pfx + "cT", [128, 8], F32)
    P.dma("sync", pfx + "c", cT[:, :], A["ccol"], writes=[pfx + "cT"])
    sT = P.sb(pfx + "sT", [128, 8], F32)
    P.add("scalar", lambda e: e.activation(out=sT[:, :], in_=cT[:, :], func=AF.Silu), reads=[pfx + "cT"], writes=[pfx + "sT"])
    bT = P.sb(pfx + "bT", [128, NL * 48], F32)
    P.dma("sync", pfx + "b", bT[:, :], A["ada_bT"], writes=[pfx + "bT"])
    brow = P.sb(pfx + "brow", [1, NL * 2048], F32)
    for l in range(NL):
        for gi in range(2):
            P.dma("sync", pfx + "br", brow[:, (l * 2 + gi) * 1024:(l * 2 + gi + 1) * 1024],
                  A["ada_b"][l:l + 1, (2 + 3 * gi) * 1024:(3 + 3 * gi) * 1024], writes=[pfx + "brow"])
    modsb = P.sb(pfx + "modsb", [128, NL * 48], F32)
    rowsb = P.sb(pfx + "rowsb", [1, NL * 2048], F32)
    wp = [P.sb(pfx + "wp%d" % i, [128, 8, 512], F32) for i in range(3)]
    psc = P.ps(pfx + "psc", [128, NL * 48], F32)
    psr = [P.ps(pfx + "psr%d" % i, [1, 512], F32) for i in range(2)]
    P.add("vector", lambda e: e.memset(modsb[:, :], 0.0), writes=[pfx + "modsb"])
    P.add("vector", lambda e: e.memset(rowsb[:, :], 0.0), writes=[pfx + "rowsb"])
    it = 0
    for l in range(NL):
        for pc in range(12):
            w = wp[it % 3]
            wk = pfx + "wp%d" % (it % 3)
            src = A["ada_w"][l, :, pc * 512:(pc + 1) * 512].rearrange("(kk p) f -> p kk f", p=128)
            qeng = "sync" if it % 2 == 0 else "gpsimd"
            P.dma(qeng, wk + qeng, w[:, :, :], src, writes=[wk])
            seg = pc // 2
            if seg in (2, 5):
                pr = psr[it % 2]
                prk = pfx + "psr%d" % (it % 2)
                for kk in range(8):
                    P.add("tensor", (lambda e, pr=pr, w=w, kk=kk: e.matmul(
                        pr[:, :], sT[:, kk:kk + 1], w[:, kk, :], start=(kk == 0), stop=(kk == 7))),
                        reads=[wk, pfx + "sT"], writes=[prk])
                o0 = (l * 2 + seg // 3) * 1024 + (pc % 2) * 512
                P.add("vector", (lambda e, pr=pr, o0=o0: e.tensor_tensor(
                    out=rowsb[:, o0:o0 + 512], in0=pr[:, :], in1=brow[:, o0:o0 + 512], op=ALU.add)),
                    reads=[prk, pfx + "brow"], writes=[pfx + "rowsb"])
            else:
                for q in range(4):
                    col = l * 48 + pc * 4 + q
                    for kk in range(8):
                        P.add("tensor", (lambda e, w=w, kk=kk, q=q, col=col: e.matmul(
                            psc[:, col:col + 1], w[:, kk, q * 128:(q + 1) * 128], sT[:, kk:kk + 1],
                            start=(kk == 0), stop=(kk == 7))),
                            reads=[wk, pfx + "sT"], writes=[pfx + "psc"])
            it += 1
    for l in range(NL):
        for c0 in (l * 48, l * 48 + 24):
            P.add("vector", (lambda e, c0=c0: e.tensor_tensor(out=modsb[:, c0:c0 + 16], in0=psc[:, c0:c0 + 16],
                                                              in1=bT[:, c0:c0 + 16], op=ALU.add)),
                  reads=[pfx + "psc", pfx + "bT"], writes=[pfx + "modsb"])
    P.dma("sync", pfx + "o1", A["modT"], modsb[:, :], reads=[pfx + "modsb"], writes=["modT"])
    P.dma("sync", pfx + "o2", A["modrow"].rearrange("(o l) g f -> o (l g f)", o=1), rowsb[:, :], reads=[pfx + "rowsb"],
          writes=["modrow"])


def phase_mlp(P, A, l, xin, xin_key, xout, xout_key, pfx):
    nc = P.nc
    TT = 256
    NT = TL // TT
    idf, idb = make_ident(P, pfx, A["ident"])
    modc = load_mod_cols(P, pfx, A["modT"], l, 1)
    gbc = load_bcast(P, pfx + "gbc", A["modrow"][l, 1, :], 1024)
    P.add("vector", lambda e: e.tensor_scalar(out=gbc[:, :], in0=gbc[:, :], scalar1=1.0, scalar2=None, op0=ALU.add),
          reads=[pfx + "gbc"], writes=[pfx + "gbc"])
    lng = P.sb(pfx + "lng", [128, 1024], F32)
    P.dma("sync", pfx + "lng", lng[:, :], A["ln_g"][l, 1, :].partition_broadcast(128), writes=[pfx + "lng"])
    lnb = P.sb(pfx + "lnb", [128, 1024], F32)
    P.dma("sync", pfx + "lnb", lnb[:, :], A["ln_b"][l, 1, :].partition_broadcast(128), writes=[pfx + "lnb"])
    w1b = P.sb(pfx + "w1b", [128, 8, DFF], BF16)
    w2b = P.sb(pfx + "w2b", [128, 32, D], BF16)
    for kc in range(8):
        P.dma("gpsimd", pfx + "w1", w1b[:, kc, :], A["mlp_w1"][l, kc * 128:(kc + 1) * 128, :], writes=[pfx + "w1b"])
    w2v = A["mlp_w2"][l].rearrange("(kc p) f -> p kc f", p=128)
    for g in range(8):
        P.dma("gpsimd", pfx + "w2", w2b[:, g * 4:(g + 1) * 4, :], w2v[:, g * 4:(g + 1) * 4, :], writes=[pfx + "w2b"])
    xtr = [P.sb(pfx + "xtr%d" % i, [128, 1024], F32) for i in range(2)]
    xep = [P.sb(pfx + "xep%d" % i, [128, 1024], F32) for i in range(2)]
    ots = [P.sb(pfx + "ot%d" % i, [128, 1024], F32) for i in range(3)]
    hT = P.sb(pfx + "hT", [128, 8, TT], BF16)
    aT = P.sb(pfx + "aT", [128, 32, TT], BF16)
    rts = [P.sb(pfx + "rt%d" % i, [128, TT], F32) for i in range(3)]
    smalls = [(P.sb(pfx + "st%d" % i, [128, 2, 6], F32), P.sb(pfx + "mv%d" % i, [128, 2], F32),
               P.sb(pfx + "rs%d" % i, [128, 3], F32)) for i in range(3)]
    ps_tr = [(P.ps(pfx + "ptr%d" % i, [128, 512], F32), pfx + "ptr%d" % i) for i in range(2)]
    ps_a = [P.ps(pfx + "pa%d" % i, [128, 512], F32) for i in range(2)]
    ps_y = [P.ps(pfx + "py%d" % i, [128, 2, 512], F32) for i in range(2)]
    cnt = [0]
    nblk = TT // 128
    xcnt = [0]
    ecnt = [0]

    def do_tr(t):
        tiles = []
        for b in range(nblk):
            i = xcnt[0] % 2
            xcnt[0] += 1
            r0 = t * TT + b * 128
            P.dma("sync", pfx + "xtr%d" % i, xtr[i][:, :], xin[r0:r0 + 128, :], reads=[xin_key], writes=[pfx + "xtr%d" % i])
            tiles.append((xtr[i], pfx + "xtr%d" % i))
        emit_hT(P, pfx, tiles, nblk, idf, modc, hT, 0, ps_tr, cnt)

    def do_w1(t):
        for fc in range(32):
            pa = ps_a[fc % 2]
            pak = pfx + "pa%d" % (fc % 2)
            for kc in range(8):
                P.add("tensor", (lambda e, pa=pa, kc=kc, fc=fc: e.matmul(
                    pa[:, 0:TT], w1b[:, kc, fc * 128:(fc + 1) * 128], hT[:, kc, :], start=(kc == 0), stop=(kc == 7))),
                    reads=[pfx + "w1b", pfx + "hT"], writes=[pak])
            rt = rts[fc % 3]
            rk = pfx + "rt%d" % (fc % 3)
            P.add("scalar", (lambda e, pa=pa, rt=rt: e.activation(out=rt[:, :], in_=pa[:, 0:TT], func=AF.Relu)),
                  reads=[pak], writes=[rk])
            P.add("gpsimd", (lambda e, rt=rt, fc=fc: e.tensor_tensor(out=aT[:, fc, :], in0=rt[:, :], in1=rt[:, :], op=ALU.mult)),
                  reads=[rk], writes=[pfx + "aT"])

    def do_w2(t):
        for b in range(nblk):
            i = ecnt[0]
            ecnt[0] += 1
            py = ps_y[i % 2]
            pyk = pfx + "py%d" % (i % 2)
            for half in range(2):
                for fc in range(32):
                    P.add("tensor", (lambda e, py=py, half=half, fc=fc, b=b: e.matmul(
                        py[:, half, :], aT[:, fc, b * 128:(b + 1) * 128], w2b[:, fc, half * 512:(half + 1) * 512],
                        start=(fc == 0), stop=(fc == 31))),
                        reads=[pfx + "aT", pfx + "w2b"], writes=[pyk])
            r0 = t * TT + b * 128
            xe = xep[i % 2]
            xek = pfx + "xep%d" % (i % 2)
            P.dma("sync", xek, xe[:, :], xin[r0:r0 + 128, :], reads=[xin_key], writes=[xek])
            ot = ots[i % 3]
            ok = pfx + "ot%d" % (i % 3)
            emit_resid_ln(P, pfx, py, pyk, xe[:, :], xek, gbc, lng, lnb, ot, ok, smalls[i % 3], pfx + "sm%d" % (i % 3))
            P.dma("gpsimd", ok + "o", xout[r0:r0 + 128, :], ot[:, :], reads=[ok], writes=[xout_key])

    NT = DBG.get("NT", NT)
    do_tr(0)
    for t in range(NT):
        do_w1(t)
        if t + 1 < NT:
            do_tr(t + 1)
        do_w2(t)


def phase_outln(P, A, l, w_ap, oT, oT_key, xin, xin_key, xout, xout_key, pfx):
    gbc = load_bcast(P, pfx + "gbc", A["modrow"][l, 0, :], 1024)
    P.add("vector", lambda e: e.tensor_scalar(out=gbc[:, :], in0=gbc[:, :], scalar1=1.0, scalar2=None, op0=ALU.add),
          reads=[pfx + "gbc"], writes=[pfx + "gbc"])
    lng = P.sb(pfx + "lng", [128, 1024], F32)
    P.dma("sync", pfx + "lng", lng[:, :], A["ln_g"][l, 0, :].partition_broadcast(128), writes=[pfx + "lng"])
    lnb = P.sb(pfx + "lnb", [128, 1024], F32)
    P.dma("sync", pfx + "lnb", lnb[:, :], A["ln_b"][l, 0, :].partition_broadcast(128), writes=[pfx + "lnb"])
    wob = P.sb(pfx + "wob", [128, 8, D], BF16)
    P.dma("gpsimd", pfx + "wo", wob[:, :, :], w_ap.rearrange("(c p) f -> p c f", p=128), writes=[pfx + "wob"])
    ots = [P.sb(pfx + "ot%d" % i, [128, 1024], F32) for i in range(3)]
    xep = [P.sb(pfx + "xep%d" % i, [128, 1024], F32) for i in range(3)]
    oTt = [P.sb(pfx + "oTt%d" % i, [128, 8, 512], BF16) for i in range(2)]
    smalls = [(P.sb(pfx + "st%d" % i, [128, 2, 6], F32), P.sb(pfx + "mv%d" % i, [128, 2], F32),
               P.sb(pfx + "rs%d" % i, [128, 3], F32)) for i in range(3)]
    ps_y = [P.ps(pfx + "py%d" % i, [128, 2, 512], F32) for i in range(2)]
    oTv = oT.rearrange("(c p) t -> p c t", p=128)
    i = 0
    for t in range(DBG.get("OT", TL // 512)):
        ob = oTt[t % 2]
        obk = pfx + "oTt%d" % (t % 2)
        P.dma("sync", obk, ob[:, :, :], oTv[:, :, t * 512:(t + 1) * 512], reads=[oT_key], writes=[obk])
        for b in range(4):
            py = ps_y[i % 2]
            pyk = pfx + "py%d" % (i % 2)
            for half in range(2):
                for c in range(8):
                    P.add("tensor", (lambda e, py=py, half=half, c=c, b=b, ob=ob: e.matmul(
                        py[:, half, :], ob[:, c, b * 128:(b + 1) * 128], wob[:, c, half * 512:(half + 1) * 512],
                        start=(c == 0), stop=(c == 7))), reads=[obk, pfx + "wob"], writes=[pyk])
            r0 = t * 512 + b * 128
            xe = xep[i % 3]
            xek = pfx + "xep%d" % (i % 3)
            P.dma("sync", xek, xe[:, :], xin[r0:r0 + 128, :], reads=[xin_key], writes=[xek])
            ot = ots[i % 3]
            ok = pfx + "ot%d" % (i % 3)
            emit_resid_ln(P, pfx, py, pyk, xe[:, :], xek, gbc, lng, lnb, ot, ok, smalls[i % 3], pfx + "sm%d" % (i % 3))
            P.dma("gpsimd", ok + "o", xout[r0:r0 + 128, :], ot[:, :], reads=[ok], writes=[xout_key])
            i += 1


def phase_bproj(P, A, l, xin, xin_key, pfx):
    j = l // 2
    idf, idb = make_ident(P, pfx, A["ident"])
    modc = load_mod_cols(P, pfx, A["modT"], l, 0)
    wb = P.sb(pfx + "wb", [128, 8, 3072], BF16)
    wv = A["b_w_in"][j].rearrange("(c p) f -> p c f", p=128)
    for c in range(8):
        P.dma("gpsimd", pfx + "w", wb[:, c, :], wv[:, c, :], writes=[pfx + "wb"])
    xtr = [P.sb(pfx + "xtr%d" % i, [128, 1024], F32) for i in range(6)]
    hT = P.sb(pfx + "hT", [128, 8, 512], BF16)
    stg = [P.sb(pfx + "stg%d" % i, [128, 512], BF16) for i in range(4)]
    ps_tr = [(P.ps(pfx + "ptr%d" % i, [128, 512], F32), pfx + "ptr%d" % i) for i in range(2)]
    ps_o = [P.ps(pfx + "po%d" % i, [128, 512], F32) for i in range(4)]
    cnt = [0]
    xc = 0
    oc = 0
    for t in range(TL // 512):
        tiles = []
        for b in range(4):
            i = xc % 6
            xc += 1
            r0 = t * 512 + b * 128
            P.dma("sync", pfx + "xtr%d" % i, xtr[i][:, :], xin[r0:r0 + 128, :], reads=[xin_key], writes=[pfx + "xtr%d" % i])
            tiles.append((xtr[i], pfx + "xtr%d" % i))
        emit_hT(P, pfx, tiles, 4, idf, modc, hT, 0, ps_tr, cnt)
        for fo in range(16):
            po = ps_o[oc % 4]
            pok = pfx + "po%d" % (oc % 4)
            sg = stg[oc % 4]
            sgk = pfx + "stg%d" % (oc % 4)
            oc += 1
            for c in range(8):
                P.add("tensor", (lambda e, po=po, c=c, fo=fo: e.matmul(
                    po[:, :], wb[:, c, fo * 128:(fo + 1) * 128], hT[:, c, :], start=(c == 0), stop=(c == 7))),
                    reads=[pfx + "wb", pfx + "hT"], writes=[pok])
            sc = 0.125 if fo < 8 else 1.0
            P.add("scalar", (lambda e, po=po, sg=sg, sc=sc: e.activation(out=sg[:, :], in_=po[:, :], func=AF.Copy, scale=sc)),
                  reads=[pok], writes=[sgk])
            dst = A["qT"] if fo < 8 else A["kT"]
            dk = "qT" if fo < 8 else "kT"
            fr = (fo % 8) * 128
            P.dma("gpsimd", sgk + "o", dst[fr:fr + 128, t * 512:(t + 1) * 512], sg[:, :], reads=[sgk], writes=[dk])
        for b in range(4):
            for half in range(2):
                po = ps_o[oc % 4]
                pok = pfx + "po%d" % (oc % 4)
                sg = stg[oc % 4]
                sgk = pfx + "stg%d" % (oc % 4)
                oc += 1
                for c in range(8):
                    P.add("tensor", (lambda e, po=po, c=c, b=b, half=half: e.matmul(
                        po[:, :], hT[:, c, b * 128:(b + 1) * 128], wb[:, c, 2048 + half * 512:2048 + (half + 1) * 512],
                        start=(c == 0), stop=(c == 7))), reads=[pfx + "wb", pfx + "hT"], writes=[pok])
                P.add("vector", (lambda e, po=po, sg=sg: e.tensor_copy(out=sg[:, :], in_=po[:, :])), reads=[pok], writes=[sgk])
                r0 = t * 512 + b * 128
                P.dma("gpsimd", sgk + "o", A["v"][r0:r0 + 128, half * 512:(half + 1) * 512], sg[:, :], reads=[sgk], writes=["v"])


def phase_battn(P, A, l, pfx):
    j = l // 2
    import math
    lam_init = 0.8 - 0.6 * math.exp(-0.3 * DBG.get("true_l", l))
    idf, idb = make_ident(P, pfx, A["ident"])
    onesb = P.sb(pfx + "onesb", [128, 128], BF16)
    P.add("vector", lambda e: e.memset(onesb[:, :], 1.0), writes=[pfx + "onesb"])
    onesf = P.sb(pfx + "onesf", [128, 128], F32)
    P.add("vector", lambda e: e.memset(onesf[:, :], 1.0), writes=[pfx + "onesf"])
    b31 = P.sb(pfx + "b31", [128, 16], F32)
    P.dma("sync", pfx + "b31", b31[:, :], A["rel_bias"][31, :].partition_broadcast(128), writes=[pfx + "b31"])
    BT = P.sb(pfx + "BT", [128, 16, 256], F32)
    P.dma("sync", pfx + "BT", BT[:, :, :], A["biasT"].rearrange("c s q -> s c q"), writes=[pfx + "BT"])
    for c in range(16):
        P.add("vector", (lambda e, c=c: e.tensor_scalar(out=BT[:, c, :], in0=BT[:, c, :], scalar1=b31[:, c:c + 1],
                                                        scalar2=None, op0=ALU.subtract)),
              reads=[pfx + "BT", pfx + "b31"], writes=[pfx + "BT"])
    lamb = P.sb(pfx + "lamb", [128, 256], F32)
    P.dma("sync", pfx + "lamb", lamb[:, :], A["b_lambda"][j].rearrange("a d -> (a d)").partition_broadcast(128),
          writes=[pfx + "lamb"])
    lt = P.sb(pfx + "lt", [128, 128], F32)
    ls = P.sb(pfx + "ls", [128, 4], F32)
    P.add("vector", lambda e: e.tensor_tensor(out=lt[:, 0:64], in0=lamb[:, 0:64], in1=lamb[:, 64:128], op=ALU.mult),
          reads=[pfx + "lamb"], writes=[pfx + "lt"])
    P.add("vector", lambda e: e.tensor_tensor(out=lt[:, 64:128], in0=lamb[:, 128:192], in1=lamb[:, 192:256], op=ALU.mult),
          reads=[pfx + "lamb"], writes=[pfx + "lt"])
    P.add("vector", lambda e: e.reduce_sum(out=ls[:, 0:1], in_=lt[:, 0:64], axis=AX.X), reads=[pfx + "lt"], writes=[pfx + "ls"])
    P.add("vector", lambda e: e.reduce_sum(out=ls[:, 1:2], in_=lt[:, 64:128], axis=AX.X), reads=[pfx + "lt"], writes=[pfx + "ls"])
    P.add("scalar", lambda e: e.activation(out=ls[:, 2:4], in_=ls[:, 0:2], func=AF.Exp), reads=[pfx + "ls"], writes=[pfx + "ls2"])
    neglam = P.sb(pfx + "neglam", [128, 1], F32)
    P.add("vector", lambda e: e.tensor_scalar(out=neglam[:, :], in0=ls[:, 3:4], scalar1=float(lam_init), scalar2=ls[:, 2:3],
                                             op0=ALU.subtract, op1=ALU.subtract), reads=[pfx + "ls2"], writes=[pfx + "neglam"])
    sg = P.sb(pfx + "sg", [128, 1], F32)
    P.dma("sync", pfx + "sg", sg[:, :], A["b_subln"][j].rearrange("(p o) -> p o", o=1), writes=[pfx + "sg"])
    P.add("vector", lambda e: e.tensor_scalar(out=sg[:, :], in0=sg[:, :], scalar1=float(1.0 - lam_init), scalar2=None,
                                             op0=ALU.mult), reads=[pfx + "sg"], writes=[pfx + "sg"])
    kTh = [P.sb(pfx + "kTh%d" % i, [128, TL], BF16) for i in range(2)]
    qTh = [P.sb(pfx + "qTh%d" % i, [128, TL], BF16) for i in range(2)]
    Vh = [P.sb(pfx + "Vh%d" % i, [128, NLB, 128], BF16) for i in range(2)]
    PT = [P.sb(pfx + "PT%d" % i, [128, 512], BF16) for i in range(6)]
    r0t = P.sb(pfx + "r0t", [128, 512], F32)
    r1t = P.sb(pfx + "r1t", [128, 512], F32)
    o0t = P.sb(pfx + "o0t", [128, 512], F32)
    o1t = P.sb(pfx + "o1t", [128, 512], F32)
    sqt = P.sb(pfx + "sqt", [128, 512], F32)
    sacc = [P.sb(pfx + "sacc%d" % i, [128, 512], F32) for i in range(2)]
    oTs = [P.sb(pfx + "oTs%d" % i, [128, 512], BF16) for i in range(2)]
    ps_s = [P.ps(pfx + "pss%d" % i, [128, 512], F32) for i in range(3)]
    acc_o = [P.ps(pfx + "acco%d" % i, [128, 512], F32) for i in range(2)]
    acc_s = [P.ps(pfx + "accs%d" % i, [128, 512], F32) for i in range(2)]
    ps_ms = P.ps(pfx + "psms", [128, 512], F32)
    st = {"sc": 0, "pc": 0}
    oc = 0
    vv = A["v"].rearrange("(blk p) f -> p blk f", p=128)
    for h in range(DBG.get("heads", 8)):
        kt = kTh[h % 2]; ktk = pfx + "kTh%d" % (h % 2)
        qt = qTh[h % 2]; qtk = pfx + "qTh%d" % (h % 2)
        vt = Vh[h % 2]; vtk = pfx + "Vh%d" % (h % 2)
        P.dma("sync", ktk, kt[:, :], A["kT"][h * 128:(h + 1) * 128, :], reads=["kT"], writes=[ktk])
        P.dma("sync", qtk, qt[:, :], A["qT"][h * 128:(h + 1) * 128, :], reads=["qT"], writes=[qtk])
        for g in range(4):
            P.dma("sync", vtk, vt[:, g * 16:(g + 1) * 16, :], vv[:, g * 16:(g + 1) * 16, h * 128:(h + 1) * 128], reads=["v"], writes=[vtk])
        for J in range(DBG.get("J", TL // 512)):
            ns = 4 * J + 4

            def qk_b(m, s_):
                col = 2 * h + m
                k = s_ - 4 * J
                c0 = max(0, k) * 128
                near = k >= -1
                ps = ps_s[st["sc"] % 3]; psk = pfx + "pss%d" % (st["sc"] % 3)
                st["sc"] += 1
                P.add("tensor", (lambda e, ps=ps, kt=kt, qt=qt, m=m, s_=s_, c0=c0, J=J, near=near: e.matmul(
                    ps[:, c0:512], kt[64 * m:64 * m + 64, s_ * 128:(s_ + 1) * 128],
                    qt[64 * m:64 * m + 64, J * 512 + c0:J * 512 + 512], start=True, stop=(not near))),
                    reads=[ktk, qtk], writes=[psk], pe_attach=(s_ > 0))
                if near:
                    if k == -1:
                        P.add("tensor", (lambda e, ps=ps, col=col: e.matmul(
                            ps[:, 0:128], idf[:, :], BT[:, col, 128:256], start=False, stop=True)),
                            reads=[pfx + "idf", pfx + "BT"], writes=[psk])
                    else:
                        w = min(256, 512 - c0)
                        P.add("tensor", (lambda e, ps=ps, col=col, c0=c0, w=w: e.matmul(
                            ps[:, c0:c0 + w], idf[:, :], BT[:, col, 0:w], start=False, stop=True)),
                            reads=[pfx + "idf", pfx + "BT"], writes=[psk])
                pt = PT[st["pc"] % 6]; ptk = pfx + "PT%d" % (st["pc"] % 6)
                st["pc"] += 1
                P.add("scalar", (lambda e, ps=ps, pt=pt, c0=c0, col=col: e.activation(
                    out=pt[:, c0:512], in_=ps[:, c0:512], func=AF.Exp, bias=b31[:, col:col + 1], scale=1.0)),
                    reads=[psk, pfx + "b31"], writes=[ptk])
                return pt, ptk, c0

            def pv_b(m, s_, pre):
                pt, ptk, c0 = pre
                ao = acc_o[m]; aok = pfx + "acco%d" % m
                as_ = acc_s[m]; ask = pfx + "accs%d" % m
                P.add("tensor", (lambda e, ao=ao, vt=vt, pt=pt, s_=s_, c0=c0, ns=ns: e.matmul(
                    ao[:, c0:512], vt[:, s_, :], pt[:, c0:512], start=(s_ == 0), stop=(s_ == ns - 1))),
                    reads=[vtk, ptk], writes=[aok], pe_attach=(s_ > 0))
                sa = sacc[m]; sak = pfx + "sacc%d" % m
                if s_ == 0:
                    P.add("vector", (lambda e, sa=sa, pt=pt: e.tensor_copy(out=sa[:, :], in_=pt[:, :])),
                          reads=[ptk], writes=[sak])
                else:
                    P.add("vector", (lambda e, sa=sa, pt=pt, c0=c0: e.tensor_tensor(
                        out=sa[:, c0:512], in0=sa[:, c0:512], in1=pt[:, c0:512], op=ALU.add)),
                        reads=[ptk, sak], writes=[sak])
                if s_ == ns - 1:
                    P.add("tensor", (lambda e, as_=as_, sa=sa: e.matmul(
                        as_[:, :], onesf[:, :], sa[:, :], start=True, stop=True)),
                        reads=[pfx + "onesf", sak], writes=[ask])

            pairs = [(m, s_) for m in range(2) for s_ in range(ns)]
            LA = 2
            queue = [qk_b(*pairs[i_]) for i_ in range(min(LA, len(pairs)))]
            for i_, (m, s_) in enumerate(pairs):
                if i_ + LA < len(pairs):
                    queue.append(qk_b(*pairs[i_ + LA]))
                pv_b(m, s_, queue.pop(0))
            P.add("vector", lambda e: e.reciprocal(out=r0t[:, :], in_=acc_s[0][:, :]), reads=[pfx + "accs0"], writes=[pfx + "r0t"])
            P.add("vector", lambda e: e.reciprocal(out=r1t[:, :], in_=acc_s[1][:, :]), reads=[pfx + "accs1"], writes=[pfx + "r1t"])
            P.add("vector", lambda e: e.tensor_tensor(out=o0t[:, :], in0=acc_o[0][:, :], in1=r0t[:, :], op=ALU.mult),
                  reads=[pfx + "acco0", pfx + "r0t"], writes=[pfx + "o0t"])
            P.add("vector", lambda e: e.tensor_tensor(out=o1t[:, :], in0=acc_o[1][:, :], in1=r1t[:, :], op=ALU.mult),
                  reads=[pfx + "acco1", pfx + "r1t"], writes=[pfx + "o1t"])
            P.add("vector", lambda e: e.scalar_tensor_tensor(out=o0t[:, :], in0=o1t[:, :], scalar=neglam[:, 0:1], in1=o0t[:, :],
                                                            op0=ALU.mult, op1=ALU.add),
                  reads=[pfx + "o1t", pfx + "neglam"], writes=[pfx + "o0t"])
            P.add("gpsimd", lambda e: e.tensor_tensor(out=sqt[:, :], in0=o0t[:, :], in1=o0t[:, :], op=ALU.mult),
                  reads=[pfx + "o0t"], writes=[pfx + "sqt"])
            P.add("tensor", lambda e: e.matmul(ps_ms[:, :], onesf[:, :], sqt[:, :], start=True, stop=True),
                  reads=[pfx + "onesf", pfx + "sqt"], writes=[pfx + "psms"])
            P.add("scalar", lambda e: e.activation(out=r0t[:, :], in_=ps_ms[:, :], func=AF.Sqrt, bias=P.epsc[:, 0:1], scale=1.0 / 128.0),
                  reads=[pfx + "psms", "epsc"], writes=[pfx + "r0t"])
            P.add("vector", lambda e: e.reciprocal(out=r1t[:, :], in_=r0t[:, :]), reads=[pfx + "r0t"], writes=[pfx + "r1t"])
            P.add("vector", lambda e: e.tensor_tensor(out=o0t[:, :], in0=o0t[:, :], in1=r1t[:, :], op=ALU.mult),
                  reads=[pfx + "r1t"], writes=[pfx + "o0t"])
            os_ = oTs[oc % 2]; osk = pfx + "oTs%d" % (oc % 2)
            oc += 1
            P.add("vector", (lambda e, os_=os_: e.tensor_scalar(out=os_[:, :], in0=o0t[:, :], scalar1=sg[:, 0:1], scalar2=None,
                                                               op0=ALU.mult)), reads=[pfx + "o0t", pfx + "sg"], writes=[osk])
            P.dma("gpsimd", osk + "o", A["oT"][h * 128:(h + 1) * 128, J * 512:(J + 1) * 512], os_[:, :], reads=[osk], writes=["oT"])


def phase_aproj(P, A, l, xin, xin_key, pfx):
    j = l // 2
    idf, idb = make_ident(P, pfx, A["ident"])
    modc = load_mod_cols(P, pfx, A["modT"], l, 0)
    wb = P.sb(pfx + "wb", [128, 8, A_IN], BF16)
    wv = A["a_w_in"][j].rearrange("(c p) f -> p c f", p=128)
    for c in range(8):
        P.dma("gpsimd", pfx + "w", wb[:, c, :], wv[:, c, :], writes=[pfx + "wb"])
    wuk = P.sb(pfx + "wuk", [128, 8, 256], BF16)
    P.dma("gpsimd", pfx + "wuk", wuk[:, :, :], A["a_w_uk"][j].rearrange("(hp two) d r -> (two d) hp r", two=2),
          writes=[pfx + "wuk"])
    wuvb = P.sb(pfx + "wuvb", [128, 2, 16, 64], BF16)
    for rc in range(2):
        P.dma("gpsimd", pfx + "wuvb", wuvb[:, rc, :, :], A["a_w_uv"][j][:, rc * 128:(rc + 1) * 128, :].rearrange("h p e -> p h e"),
              writes=[pfx + "wuvb"])
    kvn = P.sb(pfx + "kvn", [128, 256], F32)
    P.dma("sync", pfx + "kvn", kvn[:, :], A["a_kv_norm"][j].partition_broadcast(128), writes=[pfx + "kvn"])
    xtr = [P.sb(pfx + "xtr%d" % i, [128, 1024], F32) for i in range(6)]
    hT = P.sb(pfx + "hT", [128, 8, 512], BF16)
    stg = [P.sb(pfx + "stg%d" % i, [128, 512], BF16) for i in range(4)]
    cst = [P.sb(pfx + "cst%d" % i, [128, 256], BF16) for i in range(2)]
    iwst = [P.sb(pfx + "iwst%d" % i, [128, 8], F32) for i in range(2)]
    cTs = [P.sb(pfx + "cTs%d" % i, [128, 2, 512], BF16) for i in range(2)]
    vst = [P.sb(pfx + "vst%d" % i, [128, 16, 65], BF16) for i in range(2)]
    for i in range(2):
        P.add("vector", (lambda e, i=i: e.memset(vst[i][:, :, 64:65], 1.0)), writes=[pfx + "vst%d" % i])
    wukT = P.sb(pfx + "wukT", [128, 2, 1024], BF16)
    junk = P.sb(pfx + "junk", [128, 256], F32)
    sm = [P.sb(pfx + "sm%d" % i, [128, 3], F32) for i in range(2)]
    ps_tr = [(P.ps(pfx + "ptr%d" % i, [128, 512], F32), pfx + "ptr%d" % i) for i in range(2)]
    ps_o = [P.ps(pfx + "po%d" % i, [128, 512], F32) for i in range(3)]
    ps_c = [P.ps(pfx + "pc%d" % i, [128, 264], F32) for i in range(2)]
    ps_t = P.ps(pfx + "pt", [128, 2, 128], BF16)
    cnt = [0]
    st8 = {"xc": 0, "oc": 0, "sc": 0, "bc": 0, "vc": 0}
    for hp in range(8):
        for rc in range(2):
            P.add("tensor", (lambda e, hp=hp, rc=rc: e.transpose(out=ps_t[:, rc, :], in_=wuk[:, hp, rc * 128:(rc + 1) * 128],
                                                                identity=idb[:, :])), reads=[pfx + "wuk", pfx + "idb"], writes=[pfx + "pt"])
        P.add("scalar", (lambda e, hp=hp: e.activation(out=wukT[:, :, hp * 128:(hp + 1) * 128], in_=ps_t[:, :, :], func=AF.Copy)),
              reads=[pfx + "pt"], writes=[pfx + "wukT"])

    def evac(po_ap, pok, width, dst_ap, dkey, parts=128, scale=None):
        i = st8["sc"] % 4
        st8["sc"] += 1
        sg = stg[i]
        sgk = pfx + "stg%d" % i
        if scale is not None or st8["sc"] % 2 == 0:
            P.add("scalar", (lambda e: e.activation(out=sg[0:parts, 0:width], in_=po_ap, func=AF.Copy,
                                                    scale=(1.0 if scale is None else scale))), reads=[pok], writes=[sgk])
        else:
            P.add("vector", (lambda e: e.tensor_copy(out=sg[0:parts, 0:width], in_=po_ap)), reads=[pok], writes=[sgk])
        P.dma("gpsimd", sgk + "o", dst_ap, sg[0:parts, 0:width], reads=[sgk], writes=[dkey])

    def next_po():
        po = ps_o[st8["oc"] % 3]
        pok = pfx + "po%d" % (st8["oc"] % 3)
        st8["oc"] += 1
        return po, pok

    for t in range(DBG.get("T", TL // 512)):
        tiles = []
        for b in range(4):
            i = st8["xc"] % 6
            st8["xc"] += 1
            r0 = t * 512 + b * 128
            P.dma("sync", pfx + "xtr%d" % i, xtr[i][:, :], xin[r0:r0 + 128, :], reads=[xin_key], writes=[pfx + "xtr%d" % i])
            tiles.append((xtr[i], pfx + "xtr%d" % i))
        emit_hT(P, pfx, tiles, 4, idf, modc, hT, 0, ps_tr, cnt)
        tsl = slice(t * 512, (t + 1) * 512)

        def proj_fm(col0, ncol):
            po, pok = next_po()
            for c in range(8):
                P.add("tensor", (lambda e, po=po, c=c: e.matmul(po[0:ncol, :], wb[:, c, col0:col0 + ncol], hT[:, c, :],
                                                               start=(c == 0), stop=(c == 7))),
                      reads=[pfx + "wb", pfx + "hT"], writes=[pok])
            return po, pok

        for fo in range(8):
            po, pok = proj_fm(fo * 128, 128)
            evac(po[:, :], pok, 512, A["qT"][fo * 128:(fo + 1) * 128, tsl], "qT", scale=0.125)
        for fo in range(4):
            po, pok = proj_fm(1280 + fo * 128, 128)
            evac(po[:, :], pok, 512, A["iqT"][fo * 128:(fo + 1) * 128, tsl], "iqT")
        po, pok = proj_fm(1792, 64)
        evac(po[0:64, :], pok, 512, A["ikT"][0:64, tsl], "ikT", parts=64)
        ct = cTs[t % 2]
        ctk = pfx + "cTs%d" % (t % 2)
        for b in range(4):
            bi = st8["bc"]
            st8["bc"] += 1
            pc = ps_c[bi % 2]
            pck = pfx + "pc%d" % (bi % 2)
            for c in range(8):
                P.add("tensor", (lambda e, pc=pc, c=c, b=b: e.matmul(pc[:, 0:256], hT[:, c, b * 128:(b + 1) * 128],
                                                                    wb[:, c, 1024:1280], start=(c == 0), stop=(c == 7))),
                      reads=[pfx + "wb", pfx + "hT"], writes=[pck])
            for c in range(8):
                P.add("tensor", (lambda e, pc=pc, c=c, b=b: e.matmul(pc[:, 256:264], hT[:, c, b * 128:(b + 1) * 128],
                                                                    wb[:, c, 1856:1864], start=(c == 0), stop=(c == 7))),
                      reads=[pfx + "wb", pfx + "hT"], writes=[pck])
            s3 = sm[bi % 2]
            s3k = pfx + "sm%d" % (bi % 2)
            P.add("scalar", (lambda e, pc=pc, s3=s3: e.activation(out=junk[:, :], in_=pc[:, 0:256], func=AF.Square,
                                                                 accum_out=s3[:, 0:1])), reads=[pck], writes=[pfx + "junk", s3k],
                  noattach=True)
            P.add("scalar", (lambda e, s3=s3: e.activation(out=s3[:, 1:2], in_=s3[:, 0:1], func=AF.Sqrt, bias=P.epsc[:, 0:1],
                                                          scale=1.0 / 256.0)), reads=[s3k, "epsc"], writes=[s3k])
            P.add("vector", (lambda e, s3=s3: e.reciprocal(out=s3[:, 2:3], in_=s3[:, 1:2])), reads=[s3k], writes=[s3k])
            cs = cst[bi % 2]
            csk = pfx + "cst%d" % (bi % 2)
            P.add("vector", (lambda e, pc=pc, s3=s3, cs=cs: e.scalar_tensor_tensor(
                out=cs[:, :], in0=pc[:, 0:256], scalar=s3[:, 2:3], in1=kvn[:, :], op0=ALU.mult, op1=ALU.mult)),
                reads=[pck, s3k, pfx + "kvn"], writes=[csk])
            r0 = t * 512 + b * 128
            iws = iwst[bi % 2]
            iwk = pfx + "iwst%d" % (bi % 2)
            P.add("vector", (lambda e, pc=pc, iws=iws: e.tensor_scalar(out=iws[:, :], in0=pc[:, 256:264],
                                                                      scalar1=float(8 ** -0.5 * 64 ** -0.5), scalar2=None,
                                                                      op0=ALU.mult)), reads=[pck], writes=[iwk])
            P.dma("gpsimd", iwk + "o", A["iw"][r0:r0 + 128, :], iws[:, :], reads=[iwk], writes=["iw"])
            for rc in range(2):
                P.add("tensor", (lambda e, cs=cs, rc=rc: e.transpose(out=ps_t[:, rc, :], in_=cs[:, rc * 128:(rc + 1) * 128],
                                                                    identity=idb[:, :])), reads=[csk, pfx + "idb"], writes=[pfx + "pt"])
            P.add("scalar", (lambda e, ct=ct, b=b: e.activation(out=ct[:, :, b * 128:(b + 1) * 128], in_=ps_t[:, :, :], func=AF.Copy)),
                  reads=[pfx + "pt"], writes=[ctk])
        for hp in range(8):
            po, pok = next_po()
            for rc in range(2):
                P.add("tensor", (lambda e, po=po, hp=hp, rc=rc, ct=ct: e.matmul(po[:, :], wukT[:, rc, hp * 128:(hp + 1) * 128],
                                                                               ct[:, rc, :], start=(rc == 0), stop=(rc == 1))),
                      reads=[pfx + "wukT", ctk], writes=[pok])
            evac(po[:, :], pok, 512, A["kT"][hp * 128:(hp + 1) * 128, tsl], "kT")
        for b in range(4):
            vs = vst[st8["vc"] % 2]
            vsk = pfx + "vst%d" % (st8["vc"] % 2)
            st8["vc"] += 1
            for half in range(2):
                po, pok = next_po()
                for rc in range(2):
                    P.add("tensor", (lambda e, po=po, b=b, rc=rc, half=half, ct=ct: e.matmul(
                        po[:, :], ct[:, rc, b * 128:(b + 1) * 128], wuvb[:, rc, half * 8:(half + 1) * 8, :],
                        start=(rc == 0), stop=(rc == 1))), reads=[pfx + "wuvb", ctk], writes=[pok])
                if half == 0:
                    P.add("scalar", (lambda e, po=po, vs=vs, half=half: e.activation(
                        out=vs[:, half * 8:(half + 1) * 8, 0:64], in_=po[:, :].rearrange("p (h e) -> p h e", h=8), func=AF.Copy)),
                        reads=[pok], writes=[vsk])
                else:
                    P.add("vector", (lambda e, po=po, vs=vs, half=half: e.tensor_copy(
                        out=vs[:, half * 8:(half + 1) * 8, 0:64], in_=po[:, :].rearrange("p (h e) -> p h e", h=8))),
                        reads=[pok], writes=[vsk])
            r0 = t * 512 + b * 128
            P.dma("gpsimd", vsk + "o", A["vaug"][r0:r0 + 128, :], vs[:, :, :].rearrange("p h e -> p (h e)"), reads=[vsk], writes=["vaug"])


N_IT = 20
TOPK = 256


def phase_aattn(P, A, l, pfx):
    j = l // 2
    idf, idb = make_ident(P, pfx, A["ident"])
    onesb = P.sb(pfx + "onesb", [128, 128], BF16)
    P.add("vector", lambda e: e.memset(onesb[:, :], 1.0), writes=[pfx + "onesb"])
    onesf = P.sb(pfx + "onesf", [128, 128], F32)
    P.add("vector", lambda e: e.memset(onesf[:, :], 1.0), writes=[pfx + "onesf"])
    b31 = P.sb(pfx + "b31", [128, 16], F32)
    P.dma("sync", pfx + "b31", b31[:, :], A["rel_bias"][31, :].partition_broadcast(128), writes=[pfx + "b31"])
    BTf = P.sb(pfx + "BTf", [128, 256], F32)
    BT = P.sb(pfx + "BT", [128, 16, 256], BF16)
    for c in range(16):
        P.dma("sync", pfx + "BTf", BTf[:, :], A["biasT"][c], writes=[pfx + "BTf"])
        P.add("vector", (lambda e, c=c: e.tensor_scalar(out=BT[:, c, :], in0=BTf[:, :], scalar1=b31[:, c:c + 1],
                                                        scalar2=None, op0=ALU.subtract)),
              reads=[pfx + "BTf", pfx + "b31"], writes=[pfx + "BT"])
    cmask = P.sb(pfx + "cmask", [128, 128], F32)
    P.dma("sync", pfx + "cmask", cmask[:, :], A["cmask"], writes=[pfx + "cmask"])
    pw = P.sb(pfx + "pw", [128, N_IT + 1], F32)
    for i in range(N_IT + 1):
        P.add("vector", (lambda e, i=i: e.memset(pw[:, i:i + 1], float(2.0 ** -i))), writes=[pfx + "pw"])
    ik2 = P.sb(pfx + "ik2", [128, TL // 2], BF16)
    P.dma("sync", pfx + "ik2", ik2[0:64, :], A["ikT"][:, 0:TL // 2], reads=["ikT"], writes=[pfx + "ik2"])
    P.dma("sync", pfx + "ik2", ik2[64:128, :], A["ikT"][:, TL // 2:TL], reads=["ikT"], writes=[pfx + "ik2"])
    kTp = [P.sb(pfx + "kTp%d" % i, [128, TL], BF16) for i in range(2)]
    vp = [P.sb(pfx + "vp%d" % i, [128, NLB, 2, 65], BF16) for i in range(2)]
    qTp = [P.sb(pfx + "qTp%d" % i, [128, 256], BF16) for i in range(2)]
    otile = [P.sb(pfx + "otile%d" % i, [128, 1024], BF16) for i in range(2)]
    oTs = P.sb(pfx + "oTs", [128, 8, 256], BF16)
    rcp = P.sb(pfx + "rcp", [128, 4], F32)
    for i in range(2):
        P.add("gpsimd", (lambda e, i=i: e.memset(otile[i][:, :], 0.0)), writes=[pfx + "otile%d" % i])
    scores = P.sb(pfx + "scores", [128, TL], F32)
    NM = P.sb(pfx + "NM", [128, NLB, 256], BF16)
    iqt = [P.sb(pfx + "iqt%d" % i, [128, 8, 128], BF16) for i in range(2)]
    iwt = [P.sb(pfx + "iwt%d" % i, [128, 8], F32) for i in range(2)]
    rts = [P.sb(pfx + "rt%d" % i, [128, 512], F32) for i in range(3)]
    PT = [P.sb(pfx + "PT%d" % i, [128, 256], BF16) for i in range(6)]
    bs = P.sb(pfx + "bs", [128, 8 + N_IT + 1], F32)
    nd = P.sb(pfx + "nd", [128, 128], F32)
    ps_i = [P.ps(pfx + "pi%d" % i, [128, 512], F32) for i in range(2)]
    dscs = [P.sb(pfx + "dsc%d" % i, [128, TL], BF16) for i in range(2)]
    ps_s = [P.ps(pfx + "pss%d" % i, [128, 256], F32) for i in range(3)]
    acc_o = [P.ps(pfx + "acco%d" % i, [128, 65], F32) for i in range(2)]
    ps_tt = P.ps(pfx + "ptt", [128, 2, 128], BF16)
    iqv = A["iqT"].rearrange("(h d) t -> d h t", d=64)
    vav = A["vaug"].rearrange("(blk p) (h e) -> p blk h e", p=128, e=65)
    st8 = {"ic": 0, "rc": 0, "mc": 0, "sc": 0, "pc": 0, "oc": 0}
    skey = pfx + "scores"
    bkey = pfx + "bs"

    def idx(qb, db):
        nk = (qb + 1) * 128
        dbuf = dscs[db]
        dkey = pfx + "dsc%d" % db
        V = lambda fn, rd, wr, na=False: P.add("vector", fn, reads=rd, writes=wr, noattach=na)

        def scores_step():
            it = iqt[qb % 2]; itk = pfx + "iqt%d" % (qb % 2)
            P.dma("sync", itk, it[0:64, :, :], iqv[:, :, qb * 128:(qb + 1) * 128], reads=["iqT"], writes=[itk])
            P.dma("sync", itk, it[64:128, :, :], iqv[:, :, qb * 128:(qb + 1) * 128], reads=["iqT"], writes=[itk])
            wt = iwt[qb % 2]; wtk = pfx + "iwt%d" % (qb % 2)
            P.dma("sync", wtk, wt[:, :], A["iw"][qb * 128:(qb + 1) * 128, :], reads=["iw"], writes=[wtk])
            nst = (nk + 511) // 512
            for st in range(nst):
                wd = min(512, nk - st * 512)
                half = (st * 512) // (TL // 2)
                kc0 = st * 512 - half * (TL // 2)
                for h in range(8):
                    pi = ps_i[st8["ic"] % 2]; pik = pfx + "pi%d" % (st8["ic"] % 2)
                    st8["ic"] += 1
                    P.add("tensor", (lambda e, pi=pi, it=it, h=h, half=half, kc0=kc0, wd=wd: e.matmul(
                        pi[:, 0:wd], it[64 * half:64 * half + 64, h, :], ik2[64 * half:64 * half + 64, kc0:kc0 + wd],
                        start=True, stop=True)), reads=[itk, pfx + "ik2"], writes=[pik])
                    rt = rts[st8["rc"] % 3]; rk = pfx + "rt%d" % (st8["rc"] % 3)
                    st8["rc"] += 1
                    P.add("scalar", (lambda e, pi=pi, rt=rt, wd=wd: e.activation(out=rt[:, 0:wd], in_=pi[:, 0:wd], func=AF.Relu)),
                          reads=[pik], writes=[rk])
                    sl = slice(st * 512, st * 512 + wd)
                    if h == 0:
                        P.add("vector", (lambda e, rt=rt, wt=wt, sl=sl, wd=wd: e.tensor_scalar(
                            out=scores[:, sl], in0=rt[:, 0:wd], scalar1=wt[:, 0:1], scalar2=None, op0=ALU.mult)),
                            reads=[rk, wtk], writes=[skey])
                    else:
                        P.add("vector", (lambda e, rt=rt, wt=wt, sl=sl, wd=wd, h=h: e.scalar_tensor_tensor(
                            out=scores[:, sl], in0=rt[:, 0:wd], scalar=wt[:, h:h + 1], in1=scores[:, sl],
                            op0=ALU.mult, op1=ALU.add)), reads=[rk, wtk, skey], writes=[skey])

        def setup_step():
            V(lambda e: e.tensor_reduce(out=bs[:, 0:1], in_=scores[:, 0:nk], axis=AX.X, op=ALU.max, apply_absolute_value=True),
              [skey], [bkey])
            V(lambda e: e.tensor_scalar(out=bs[:, 1:2], in0=bs[:, 0:1], scalar1=1.0, scalar2=None, op0=ALU.add), [bkey], [bkey])
            V(lambda e: e.tensor_scalar(out=bs[:, 8:8 + N_IT + 1], in0=pw[:, :], scalar1=bs[:, 1:2], scalar2=None, op0=ALU.mult),
              [bkey, pfx + "pw"], [bkey])
            V(lambda e: e.tensor_tensor(out=scores[:, qb * 128:(qb + 1) * 128], in0=scores[:, qb * 128:(qb + 1) * 128],
                                        in1=cmask[:, :], op=ALU.add), [skey, pfx + "cmask"], [skey])
            V(lambda e: e.tensor_scalar(out=bs[:, 2:3], in0=bs[:, 1:2], scalar1=-1.0, scalar2=None, op0=ALU.mult), [bkey], [bkey])
            V(lambda e: e.tensor_tensor(out=bs[:, 3:4], in0=bs[:, 2:3], in1=bs[:, 8:9], op=ALU.add), [bkey], [bkey])

        def one_iter(i):
            V(lambda e: e.tensor_scalar(out=dbuf[:, 0:nk], in0=scores[:, 0:nk], scalar1=bs[:, 3:4], scalar2=None,
                                        op0=ALU.is_ge, op1=ALU.add, accum_out=bs[:, 4:5]), [skey, bkey], [dkey, bkey], True)
            V((lambda e, i=i: e.tensor_scalar(out=bs[:, 5:6], in0=bs[:, 4:5], scalar1=float(TOPK), scalar2=bs[:, 8 + i:9 + i],
                                              op0=ALU.is_ge, op1=ALU.mult)), [bkey], [bkey])
            V(lambda e: e.tensor_tensor(out=bs[:, 2:3], in0=bs[:, 2:3], in1=bs[:, 5:6], op=ALU.add), [bkey], [bkey])
            V((lambda e, i=i: e.tensor_tensor(out=bs[:, 3:4], in0=bs[:, 2:3], in1=bs[:, 9 + i:10 + i], op=ALU.add)), [bkey], [bkey])

        def dsc_step():
            V(lambda e: e.tensor_scalar(out=dbuf[:, 0:nk], in0=scores[:, 0:nk], scalar1=bs[:, 2:3], scalar2=None,
                                        op0=ALU.subtract), [skey, bkey], [dkey])

        return [scores_step, setup_step] + [(lambda i=i: one_iter(i)) for i in range(N_IT)] + [dsc_step]

    def nm_gen(qb, db):
        jj = qb % 2
        dbuf = dscs[db]
        dkey = pfx + "dsc%d" % db
        for sb in range(qb + 1):
            mi = st8["mc"] % 2
            st8["mc"] += 1
            pmk = pfx + "ptt"
            P.add("tensor", (lambda e, mi=mi, sb=sb: e.transpose(out=ps_tt[:, mi, :], in_=dbuf[:, sb * 128:(sb + 1) * 128],
                                                                identity=idb[:, :])), reads=[dkey, pfx + "idb"], writes=[pmk])
            P.add("vector", (lambda e, mi=mi, sb=sb, jj=jj: e.tensor_scalar(
                out=NM[:, sb, jj * 128:(jj + 1) * 128], in0=ps_tt[:, mi, :], scalar1=0.0, scalar2=float(NEGM),
                op0=ALU.is_lt, op1=ALU.mult)), reads=[pmk], writes=[pfx + "NM"])

    def attn(J, steps=()):
        steps = list(steps)
        nsteps0 = len(steps)
        ns = 2 * J + 2
        nk = ns * 128
        nh = DBG.get("aheads", 16)

        def load_pair(hp):
            kt = kTp[hp % 2]; ktk = pfx + "kTp%d" % (hp % 2)
            vt = vp[hp % 2]; vtk = pfx + "vp%d" % (hp % 2)
            qt = qTp[hp % 2]; qtk = pfx + "qTp%d" % (hp % 2)
            P.dma("sync", ktk, kt[:, 0:nk], A["kT"][hp * 128:(hp + 1) * 128, 0:nk], reads=["kT"], writes=[ktk])
            P.dma("sync", vtk, vt[:, 0:ns, :, :], vav[:, 0:ns, 2 * hp:2 * hp + 2, :], reads=["vaug"], writes=[vtk])
            P.dma("sync", qtk, qt[:, :], A["qT"][hp * 128:(hp + 1) * 128, J * 256:(J + 1) * 256], reads=["qT"], writes=[qtk])

        def qk(h, s):
            hp, hh = h // 2, h % 2
            kt = kTp[hp % 2]; ktk = pfx + "kTp%d" % (hp % 2)
            qt = qTp[hp % 2]; qtk = pfx + "qTp%d" % (hp % 2)
            k = s - 2 * J
            c0 = max(0, k) * 128
            near = k >= -1
            si = st8["sc"] % 3
            st8["sc"] += 1
            psk = pfx + "pss%d" % si
            P.add("tensor", (lambda e, si=si, s=s, hh=hh, c0=c0, kt=kt, qt=qt: e.matmul(
                ps_s[si][:, c0:256], kt[64 * hh:64 * hh + 64, s * 128:(s + 1) * 128], qt[64 * hh:64 * hh + 64, c0:256],
                start=True, stop=False)), reads=[ktk, qtk], writes=[psk], pe_attach=(s > 0))
            P.add("tensor", (lambda e, si=si, s=s, c0=c0, near=near: e.matmul(
                ps_s[si][:, c0:256], idb[:, :], NM[:, s, c0:256], start=False, stop=(not near))),
                reads=[pfx + "idb", pfx + "NM"], writes=[psk])
            if near:
                if k == -1:
                    P.add("tensor", (lambda e, si=si, h=h: e.matmul(ps_s[si][:, 0:128], idb[:, :], BT[:, h, 128:256],
                                                                   start=False, stop=True)),
                          reads=[pfx + "idb", pfx + "BT"], writes=[psk])
                else:
                    w = 256 - c0
                    P.add("tensor", (lambda e, si=si, h=h, c0=c0, w=w: e.matmul(ps_s[si][:, c0:c0 + w], idb[:, :], BT[:, h, 0:w],
                                                                               start=False, stop=True)),
                          reads=[pfx + "idb", pfx + "BT"], writes=[psk])
            pt = PT[st8["pc"] % 6]; ptk = pfx + "PT%d" % (st8["pc"] % 6)
            st8["pc"] += 1
            P.add("scalar", (lambda e, si=si, pt=pt, c0=c0, h=h: e.activation(
                out=pt[:, c0:256], in_=ps_s[si][:, c0:256], func=AF.Exp, bias=b31[:, h:h + 1], scale=1.0)),
                reads=[psk, pfx + "b31"], writes=[ptk])
            return pt, ptk, c0

        def pv(h, s, pre):
            pt, ptk, c0 = pre
            hp, hh = h // 2, h % 2
            vt = vp[hp % 2]; vtk = pfx + "vp%d" % (hp % 2)
            for jj in range(2):
                if jj * 128 < c0:
                    continue
                last = (s == ns - 1) if jj == 1 else (s == ns - 2)
                P.add("tensor", (lambda e, pt=pt, jj=jj, s=s, hh=hh, vt=vt, last=last: e.matmul(
                    acc_o[jj][:, :], pt[:, jj * 128:(jj + 1) * 128], vt[:, s, hh, :],
                    start=(s == 0), stop=last)), reads=[vtk, ptk], writes=[pfx + "acco%d" % jj])

        def finish_head(h):
            for jj in range(2):
                P.add("vector", (lambda e, jj=jj: e.reciprocal(out=rcp[:, jj:jj + 1], in_=acc_o[jj][:, 64:65])),
                      reads=[pfx + "acco%d" % jj], writes=[pfx + "rcp%d" % jj])
                P.add("vector", (lambda e, jj=jj, h=h: e.tensor_scalar(out=otile[jj][:, h * 64:(h + 1) * 64], in0=acc_o[jj][:, 0:64],
                                                                      scalar1=rcp[:, jj:jj + 1], scalar2=None, op0=ALU.mult)),
                      reads=[pfx + "acco%d" % jj, pfx + "rcp%d" % jj], writes=[pfx + "otile%d" % jj])
            nstep = (nsteps0 + nh - 1) // nh
            if h == nh - 1:
                nstep = len(steps)
            for _ in range(min(nstep, len(steps))):
                steps.pop(0)()

        load_pair(0)
        pairs = [(h, s) for h in range(nh) for s in range(ns)]
        LA = 2
        queue = [qk(*pairs[i]) for i in range(min(LA, len(pairs)))]
        for i, (h, s) in enumerate(pairs):
            if s == 0 and h % 2 == 0 and h + 2 < nh:
                load_pair(h // 2 + 1)
            if i + LA < len(pairs):
                queue.append(qk(*pairs[i + LA]))
            pv(h, s, queue.pop(0))
            if s == ns - 1:
                finish_head(h)
        for jj in range(2):
            for c in range(8):
                P.add("tensor", (lambda e, jj=jj, c=c: e.transpose(out=ps_tt[:, c % 2, :], in_=otile[jj][:, c * 128:(c + 1) * 128],
                                                                  identity=idb[:, :])),
                      reads=[pfx + "otile%d" % jj, pfx + "idb"], writes=[pfx + "ptt"])
                if c % 2 == 1:
                    P.add("scalar", (lambda e, jj=jj, c=c: e.activation(out=oTs[:, c - 1:c + 1, jj * 128:(jj + 1) * 128],
                                                                       in_=ps_tt[:, :, :], func=AF.Copy)),
                          reads=[pfx + "ptt"], writes=[pfx + "oTs"])
        P.dma("gpsimd", pfx + "oTso", A["oT"].rearrange("(c p) t -> p c t", p=128)[:, :, J * 256:(J + 1) * 256], oTs[:, :, :],
              reads=[pfx + "oTs"], writes=["oT"])

    NJ = DBG.get("AJ", TL // 256)
    for st_ in idx(0, 0):
        st_()
    nm_gen(0, 0)
    for st_ in idx(1, 1):
        st_()
    nm_gen(1, 1)
    for J in range(NJ):
        steps = (idx(2 * J + 2, 0) + idx(2 * J + 3, 1)) if J + 1 < NJ else []
        attn(J, steps)
        if J + 1 < NJ:
            nm_gen(2 * J + 2, 0)
            nm_gen(2 * J + 3, 1)


def _new_nc():
    return bass.Bass("TRN2", target_bir_lowering=False)


def build_single(phase_name, l=0):
    nc = _new_nc()
    A = {}
    ins, outs = [], []

    def din(name, shape, dt=F32):
        A[name] = nc.dram_tensor(name, list(shape), dt, kind="ExternalInput").ap()
        ins.append(name)

    def dout(name, shape, dt=F32):
        A[name] = nc.dram_tensor(name, list(shape), dt, kind="ExternalOutput").ap()
        outs.append(name)

    with ExitStack() as es:
        P = Prog(nc, es)
        make_epsc(P)
        if phase_name == "mod":
            din("ccol", [128, 8]); din("ada_w", [DEPTH, D, 6 * D]); din("ada_b", [DEPTH, 6 * D]); din("ada_bT", [128, 4 * 48])
            dout("modT", [128, 4 * 48]); dout("modrow", [DEPTH, 2, D])
            phase_mod(P, A)
            P.final_wait()
        elif phase_name == "mlp":
            din("ident", [128, 128]); din("modT", [128, 4 * 48]); din("modrow", [DEPTH, 2, D])
            din("ln_g", [DEPTH, 2, D]); din("ln_b", [DEPTH, 2, D])
            din("mlp_w1", [DEPTH, D, DFF]); din("mlp_w2", [DEPTH, DFF, D]); din("xin", [TL, D])
            dout("xout", [TL, D])
            phase_mlp(P, A, l, A["xin"], "xin", A["xout"], "xout", "ml_")
            P.final_wait()
        elif phase_name == "bproj":
            din("ident", [128, 128]); din("modT", [128, 4 * 48]); din("b_w_in", [2, D, 3072]); din("xin", [TL, D])
            dout("qT", [D, TL], BF16); dout("kT", [D, TL], BF16); dout("v", [TL, D], BF16)
            phase_bproj(P, A, l, A["xin"], "xin", "bp_")
            P.final_wait()
        elif phase_name == "battn":
            din("ident", [128, 128]); din("rel_bias", [32, 16]); din("biasT", [16, 128, 256]); din("b_lambda", [2, 4, 64])
            din("b_subln", [2, 128]); din("qT", [D, TL], BF16); din("kT", [D, TL], BF16); din("v", [TL, D], BF16)
            dout("oT", [D, TL], BF16)
            phase_battn(P, A, l, "ba_")
            P.final_wait()
        elif phase_name == "aproj":
            din("ident", [128, 128]); din("modT", [128, 4 * 48]); din("a_w_in", [2, D, A_IN]); din("a_w_uk", [2, 16, 64, 256])
            din("a_kv_norm", [2, 256]); din("xin", [TL, D])
            din("a_w_uv", [2, 16, 256, 64])
            dout("qT", [D, TL], BF16); dout("kT", [D, TL], BF16); dout("vaug", [TL, 16 * 65], BF16)
            dout("iqT", [512, TL], BF16); dout("ikT", [64, TL], BF16); dout("iw", [TL, 8], F32)
            phase_aproj(P, A, l, A["xin"], "xin", "ap_")
            P.final_wait()
        elif phase_name == "aattn":
            din("ident", [128, 128]); din("rel_bias", [32, 16]); din("biasT", [16, 128, 256]); din("cmask", [128, 128])
            din("qT", [D, TL], BF16); din("kT", [D, TL], BF16); din("vaug", [TL, 16 * 65], BF16)
            din("iqT", [512, TL], BF16); din("ikT", [64, TL], BF16); din("iw", [TL, 8], F32)
            dout("oT", [D, TL], BF16)
            phase_aattn(P, A, l, "aa_")
            P.final_wait()
        elif phase_name == "outln":
            din("modrow", [DEPTH, 2, D]); din("ln_g", [DEPTH, 2, D]); din("ln_b", [DEPTH, 2, D])
            din("w_o", [D, D]); din("oT", [D, TL], BF16); din("xin", [TL, D])
            dout("xout", [TL, D])
            phase_outln(P, A, l, A["w_o"], A["oT"], "oT", A["xin"], "xin", A["xout"], "xout", "ol_")
            P.final_wait()
        else:
            raise ValueError(phase_name)
        P.finish()
        nops = P.n
    return nc, ins, outs, nops


def _to_local(a_b, hf):
    s = a_b.shape
    return np.ascontiguousarray(a_b.reshape(32, 2, 128, *s[1:])[:, hf].reshape(TL, *s[1:]))


def _from_local(parts):
    s = parts[0].shape
    o = np.empty((32, 2, 128) + s[1:], parts[0].dtype)
    for hf in range(2):
        o[:, hf] = parts[hf].reshape(32, 128, *s[1:])
    return o.reshape(SEQ, *s[1:])


def run_phase(nc, ins, in_maps):
    res = run_bass_kernel_spmd(nc, [{k: m[k] for k in ins} for m in in_maps], core_ids=list(range(len(in_maps))))
    return res.results


def rel_bucket_np(dist):
    import math
    n = np.maximum(dist, 0)
    nf = np.maximum(n, 1).astype(np.float32)
    large = 16 + (np.log(nf / np.float32(16)) / np.float32(math.log(128 / 16)) * np.float32(16)).astype(np.int32)
    large = np.minimum(large, 31)
    return np.where(n < 16, n, large)


def make_biasT(rel_bias):
    s_ = np.arange(128)[:, None]
    q_ = np.arange(128)[None, :]
    dd = q_ - s_
    bd = rel_bucket_np(dd)
    bp = rel_bucket_np(dd + 128)
    out = np.empty((16, 128, 256), np.float32)
    for c in range(16):
        diag = rel_bias[:, c][bd]
        out[c, :, 0:128] = np.where(dd >= 0, diag, np.float32(NEGM))
        out[c, :, 128:256] = rel_bias[:, c][bp]
    return out


def build_fused(single=None):
    nc = _new_nc()
    A = {}
    NL = DEPTH if single is None else 1
    NM_ = 2 if single is None else 1

    def din(name, shape, dt=F32):
        A[name] = nc.dram_tensor(name, list(shape), dt, kind="ExternalInput").ap()

    def dint(name, shape, dt=F32):
        A[name] = nc.dram_tensor(name, list(shape), dt, kind="Internal").ap()

    din("x", [TL, D]); din("ccol", [128, 8]); din("ada_w", [NL, D, 6 * D]); din("ada_b", [NL, 6 * D])
    din("ada_bT", [128, NL * 48]); din("ident", [128, 128]); din("rel_bias", [32, 16]); din("biasT", [16, 128, 256])
    din("ln_g", [NL, 2, D]); din("ln_b", [NL, 2, D])
    if single is None or single % 2 == 0:
        din("cmask", [128, 128])
        din("a_w_in", [NM_, D, A_IN]); din("a_kv_norm", [NM_, 256]); din("a_w_uk", [NM_, 16, 64, 256])
        din("a_w_uv", [NM_, 16, 256, 64]); din("a_w_o", [NM_, D, D])
    if single is None or single % 2 == 1:
        din("b_w_in", [NM_, D, 3072]); din("b_lambda", [NM_, 4, 64]); din("b_subln", [NM_, 128]); din("b_w_o", [NM_, D, D])
    din("mlp_w1", [NL, D, DFF]); din("mlp_w2", [NL, DFF, D])
    A["out"] = nc.dram_tensor("out", [TL, D], F32, kind="ExternalOutput").ap()
    dint("modT", [128, NL * 48]); dint("modrow", [NL, 2, D]); dint("xA", [TL, D]); dint("xB", [TL, D])
    dint("vaug", [TL, 16 * 65], BF16)
    dint("iqT", [512, TL], BF16); dint("ikT", [64, TL], BF16); dint("iw", [TL, 8], F32)
    dint("qT", [D, TL], BF16); dint("kT", [D, TL], BF16); dint("v", [TL, D], BF16); dint("oT", [D, TL], BF16)
    nops = [0]

    def block(fn, outkeys):
        with ExitStack() as es:
            P = Prog(nc, es)
            make_epsc(P)
            fn(P)
            P.final_wait()
            P.finish()
            nops[0] += P.n

    block(lambda P: phase_mod(P, A, NL), ["modT", "modrow"])
    xcur = "x"
    llist = DBG.get("llist", list(range(DBG.get("layers", DEPTH))))
    if single is not None:
        llist = [0]
        DBG["true_l"] = single
    else:
        DBG.pop("true_l", None)
    for l in llist:
        j = l // 2
        kind_a = (l % 2 == 0) if single is None else (single % 2 == 0)
        if kind_a:
            block(lambda P: phase_aproj(P, A, l, A[xcur], "xin", "ap_"), ["qT", "kT", "vaug", "iqT", "ikT", "iw"])
            block(lambda P: phase_aattn(P, A, l, "aa_"), ["oT"])
            w_o = A["a_w_o"][j]
        else:
            block(lambda P: phase_bproj(P, A, l, A[xcur], "xin", "bp_"), ["qT", "kT", "v"])
            block(lambda P: phase_battn(P, A, l, "ba_"), ["oT"])
            w_o = A["b_w_o"][j]
        block(lambda P: phase_outln(P, A, l, w_o, A["oT"], "oT", A[xcur], "xin", A["xA"], "xout", "ol_"), ["xout"])
        last = (l == llist[-1])
        dst = "out" if last else "xB"
        block(lambda P: phase_mlp(P, A, l, A["xA"], "xin", A[dst], "xout", "ml_"), ["xout"])
        xcur = "xB"
    return nc, nops[0]


_CACHE = {}
FUSED = True


def kernel(**inputs):
    f32 = lambda a: np.ascontiguousarray(np.asarray(a, dtype=np.float32))
    x = f32(inputs["x"])
    c = f32(inputs["c"])
    rel_bias = f32(inputs["rel_bias"])
    W = {k: f32(inputs[k]) for k in ["ada_w", "ada_b", "ln_g", "ln_b", "a_w_in", "a_kv_norm", "a_w_uk", "a_w_uv", "a_w_o",
                                     "b_w_in", "b_lambda", "b_subln", "b_w_o", "mlp_w1", "mlp_w2"]}
    const = {
        "ident": np.eye(128, dtype=np.float32),
        "rel_bias": rel_bias,
        "biasT": make_biasT(rel_bias),
    }
    cmask = np.where(np.arange(128)[None, :] <= np.arange(128)[:, None], 0.0, -1e30).astype(np.float32)
    ccols = [np.ascontiguousarray(c[b].reshape(8, 128).T) for b in range(NB)]
    if FUSED:
        if "nc" not in _CACHE:
            _CACHE["nc"] = build_fused()
        nc, nops = _CACHE["nc"]
        shared = dict(const)
        shared.update(W)
        shared["ada_bT"] = np.ascontiguousarray(W["ada_b"].reshape(4 * 48, 128).T)
        shared["cmask"] = cmask
        in_maps = []
        for b in range(NB):
            m = dict(shared)
            m["x"] = x[b]
            m["ccol"] = ccols[b]
            in_maps.append(m)
        res = run_bass_kernel_spmd(nc, in_maps, core_ids=list(range(NB)))
        return np.stack([np.asarray(res.results[b]["out"], dtype=np.float32) for b in range(NB)], axis=0)
    xs = [x[b] for b in range(NB)]
    for L in range(DEPTH):
        key = "nc%d" % (L % 2)
        if ("L", L) not in _CACHE:
            _CACHE[("L", L)] = build_fused(single=L)
        nc, nops = _CACHE[("L", L)]
        j = L // 2
        shared = dict(const)
        for k in ["ada_w", "ada_b", "ln_g", "ln_b", "mlp_w1", "mlp_w2"]:
            shared[k] = np.ascontiguousarray(W[k][L:L + 1])
        shared["ada_bT"] = np.ascontiguousarray(W["ada_b"][L].reshape(48, 128).T)
        if L % 2 == 0:
            shared["cmask"] = cmask
            for k in ["a_w_in", "a_kv_norm", "a_w_uk", "a_w_uv", "a_w_o"]:
                shared[k] = np.ascontiguousarray(W[k][j:j + 1])
        else:
            for k in ["b_w_in", "b_lambda", "b_subln", "b_w_o"]:
                shared[k] = np.ascontiguousarray(W[k][j:j + 1])
        in_maps = []
        for b in range(NB):
            m = dict(shared)
            m["x"] = xs[b]
            m["ccol"] = ccols[b]
            in_maps.append(m)
        res = run_bass_kernel_spmd(nc, in_maps, core_ids=list(range(NB)))
        xs = [np.asarray(res.results[b]["out"], dtype=np.float32) for b in range(NB)]
    return np.stack(xs, axis=0)
```
